# Optimizing a Trainium2 kernel written in Bass

```python
import math
import jax
import jax.numpy as jnp
from jax import lax
import numpy as np

D_MODEL = 1024
BATCH = 4
SEQ = 8192
DEPTH = 2

GRID_W = 64
CTX_LEN = 256
EPS = 1e-6
HEAD_DIM = 64
D_MIX = D_MODEL
W_GROUP = D_MIX // 4
SSD_HEADS = W_GROUP // HEAD_DIM
SSD_GROUPS = 2
SSD_STATE = 64
SSD_CONV = 5
SSD_CHUNK = 128
SSD_XBC = W_GROUP + 2 * SSD_GROUPS * SSD_STATE
COLS_A = W_GROUP + SSD_XBC + 2 * SSD_HEADS
HY_CH = W_GROUP
HY_ORDER = 2
HY_SHORT = 3
HY_EMB = 33
HY_BANDS = (HY_EMB - 1) // 2
HY_FILT = 64
HY_TARGET = 1e-2
HY_FAST_PCT = 0.3
HY_SLOW_PCT = 1.5
HY_MAX_DECAY = math.log(HY_TARGET) / HY_FAST_PCT
HY_MIN_DECAY = math.log(HY_TARGET) / HY_SLOW_PCT
COLS_B = (HY_ORDER + 1) * HY_CH
ATT_HEADS = W_GROUP // HEAD_DIM
ATT_KV = 2
COLS_ATT = ATT_HEADS * HEAD_DIM + 2 * ATT_KV * HEAD_DIM
WINDOW = 128
BLOCK = 128
ROPE_THETA = 10000.0
OFF_B = COLS_A
OFF_C = OFF_B + COLS_B
OFF_D = OFF_C + COLS_ATT
D_IN = OFF_D + COLS_ATT
D_FF = 4 * D_MODEL

kernel_name = 'hybrid_parallel_groups_flow_block'


def rms_norm(x, g):
    xf = x.astype(jnp.float32)
    y = xf * lax.rsqrt(jnp.mean(xf * xf, axis=-1, keepdims=True) + EPS)
    return (y * g.astype(jnp.float32)).astype(x.dtype)


def dw_conv_centred(u, w, b):
    k = w.shape[0]
    out = lax.conv_general_dilated(
        u, w[:, None, :].astype(u.dtype), window_strides=(1,), padding=[(k // 2, k // 2)],
        dimension_numbers=('NWC', 'WIO', 'NWC'), feature_group_count=u.shape[-1])
    return out + b.astype(u.dtype)


def rope_tables(rows):
    row = jnp.repeat(jnp.arange(rows, dtype=jnp.float32), GRID_W)
    col = jnp.tile(jnp.arange(GRID_W, dtype=jnp.float32), rows)
    n_freq = HEAD_DIM // 4
    inv = ROPE_THETA ** (-jnp.arange(n_freq, dtype=jnp.float32) / n_freq)
    ang = jnp.concatenate([row[:, None] * inv, col[:, None] * inv], axis=-1)
    return jnp.cos(ang), jnp.sin(ang)


def apply_rope(t, cos, sin):
    tf = t.astype(jnp.float32)
    t1, t2 = jnp.split(tf, 2, axis=-1)
    c = cos[None, :, None, :]
    s = sin[None, :, None, :]
    return jnp.concatenate([t1 * c - t2 * s, t1 * s + t2 * c], axis=-1).astype(t.dtype)


def ssd_scan(xs, dt, a_h, bm, cm, h0):
    b, n, nh, hp = xs.shape
    nchunk = n // SSD_CHUNK
    rep = nh // SSD_GROUPS
    bh = jnp.repeat(bm, rep, axis=2).reshape(b, nchunk, SSD_CHUNK, nh, SSD_STATE)
    ch = jnp.repeat(cm, rep, axis=2).reshape(b, nchunk, SSD_CHUNK, nh, SSD_STATE)
    xc = xs.reshape(b, nchunk, SSD_CHUNK, nh, hp)
    dtc = dt.reshape(b, nchunk, SSD_CHUNK, nh)
    a_cum = jnp.cumsum(dtc * a_h, axis=2)
    seg = a_cum[:, :, :, None, :] - a_cum[:, :, None, :, :]
    lower = jnp.tril(jnp.ones((SSD_CHUNK, SSD_CHUNK), dtype=bool))[None, None, :, :, None]
    decay_in = jnp.exp(jnp.where(lower, seg, -jnp.inf))
    scores = jnp.einsum('bcihn,bcjhn->bcijh', ch, bh) * decay_in
    y_diag = jnp.einsum('bcijh,bcjh,bcjhp->bcihp', scores, dtc, xc)
    decay_to_end = jnp.exp(a_cum[:, :, -1:, :] - a_cum)
    chunk_states = jnp.einsum('bcjhn,bcjh,bcjhp->bchpn', bh, decay_to_end * dtc, xc)
    chunk_decay = jnp.exp(a_cum[:, :, -1, :])

    def step(h, inp):
        st, dec = inp
        return dec[:, :, None, None] * h + st, h

    h_final, h_prev = lax.scan(step, h0, (jnp.moveaxis(chunk_states, 1, 0), jnp.moveaxis(chunk_decay, 1, 0)))
    h_prev = jnp.moveaxis(h_prev, 0, 1)
    y_off = jnp.einsum('bcihn,bchpn,bcih->bcihp', ch, h_prev, jnp.exp(a_cum))
    return (y_diag + y_off).reshape(b, n, nh, hp), h_final


def _maybe_flip(t, rev):
    return jnp.flip(t, axis=1) if rev else t


def _ssd_prep(a, conv_w, conv_b):
    b, n, _ = a.shape
    gn = SSD_GROUPS * SSD_STATE
    z = a[..., :W_GROUP].astype(jnp.float32)
    xbc = jax.nn.silu(dw_conv_centred(a[..., W_GROUP:W_GROUP + SSD_XBC], conv_w, conv_b)).astype(jnp.float32)
    xs = xbc[..., :W_GROUP].reshape(b, n, SSD_HEADS, HEAD_DIM)
    bm = xbc[..., W_GROUP:W_GROUP + gn].reshape(b, n, SSD_GROUPS, SSD_STATE)
    cm = xbc[..., W_GROUP + gn:].reshape(b, n, SSD_GROUPS, SSD_STATE)
    dt_raw = a[..., W_GROUP + SSD_XBC:].astype(jnp.float32).reshape(b, n, 2, SSD_HEADS)
    return z, xs, bm, cm, dt_raw


def ssd_branch(a_lat, a_ctx, conv_w, conv_b, a_log, dt_bias, d_skip, norm_g, need_ctx):
    zl, xl, bl, cl, dtl = _ssd_prep(a_lat, conv_w, conv_b)
    zc, xc, bc, cc, dtc = _ssd_prep(a_ctx, conv_w, conv_b)
    b, n = a_lat.shape[:2]
    n_ctx = a_ctx.shape[1]
    skip = d_skip.astype(jnp.float32)[:, None]
    yl = skip * xl
    yc = skip * xc
    for d in range(2):
        rev = d == 1
        a_h = -jnp.exp(a_log[d].astype(jnp.float32))
        bias = dt_bias[d].astype(jnp.float32)
        dt_c = jax.nn.softplus(dtc[:, :, d] + bias)
        dt_l = jax.nn.softplus(dtl[:, :, d] + bias)
        h0 = jnp.zeros((b, SSD_HEADS, HEAD_DIM, SSD_STATE), jnp.float32)
        yc_d, h_ctx = ssd_scan(_maybe_flip(xc, rev), _maybe_flip(dt_c, rev), a_h,
                               _maybe_flip(bc, rev), _maybe_flip(cc, rev), h0)
        yl_d, _ = ssd_scan(_maybe_flip(xl, rev), _maybe_flip(dt_l, rev), a_h,
                           _maybe_flip(bl, rev), _maybe_flip(cl, rev), h_ctx)
        yl = yl + _maybe_flip(yl_d, rev)
        if need_ctx:
            yc = yc + _maybe_flip(yc_d, rev)
    out_l = rms_norm(yl.reshape(b, n, W_GROUP) * jax.nn.silu(zl), norm_g).astype(a_lat.dtype)
    if not need_ctx:
        return out_l, None
    out_c = rms_norm(yc.reshape(b, n_ctx, W_GROUP) * jax.nn.silu(zc), norm_g).astype(a_ctx.dtype)
    return out_l, out_c


def hyena_spectra(n, w1, b1, freq1, w2, b2, freq2, w3, b3):
    f32 = jnp.float32
    pos = jnp.arange(n, dtype=f32)
    t = jnp.linspace(0.0, 1.0, n, dtype=f32)
    f = jnp.linspace(1e-4, HY_BANDS - 1, HY_BANDS, dtype=f32)
    ang = 2.0 * math.pi * pos[:, None] * f[None, :] / n
    z = jnp.concatenate([t[:, None], jnp.cos(ang), -jnp.sin(ang)], axis=-1)
    h = jnp.sin(freq1.astype(f32) * (z @ w1.astype(f32) + b1.astype(f32)))
    h = jnp.sin(freq2.astype(f32) * (h @ w2.astype(f32) + b2.astype(f32)))
    k = (h @ w3.astype(f32) + b3.astype(f32)).reshape(n, HY_ORDER, 2, HY_CH)
    deltas = jnp.abs(jnp.linspace(HY_MIN_DECAY, HY_MAX_DECAY, HY_CH, dtype=f32))
    k = k * jnp.exp(-t[:, None] * deltas[None, :])[:, None, None, :]
    k = k * lax.rsqrt(jnp.sum(k * k, axis=(0, 2), keepdims=True) + EPS)
    k_fwd, k_bwd = k[:, :, 0], k[:, :, 1]
    full = jnp.concatenate([k_fwd, jnp.zeros((1, HY_ORDER, HY_CH), f32), jnp.flip(k_bwd[1:], axis=0)], axis=0)
    return jnp.fft.rfft(full, axis=0)


def hyena_seq(u, conv_w, conv_b, spec, bias):
    n = u.shape[1]
    u = dw_conv_centred(u, conv_w, conv_b).astype(jnp.float32)
    v, x1, x2 = jnp.split(u, 3, axis=-1)
    gates = (x1, x2)
    bias = bias.astype(jnp.float32)
    z = v
    for o in range(HY_ORDER):
        zf = jnp.fft.irfft(jnp.fft.rfft(z, n=2 * n, axis=1) * spec[:, o], n=2 * n, axis=1)[:, :n]
        z = gates[o] * (zf + z * bias[o])
    return z


def split_qkv(p):
    b, n, _ = p.shape
    nq = ATT_HEADS * HEAD_DIM
    nk = ATT_KV * HEAD_DIM
    q = p[..., :nq].reshape(b, n, ATT_HEADS, HEAD_DIM)
    k = p[..., nq:nq + nk].reshape(b, n, ATT_KV, HEAD_DIM)
    v = p[..., nq + nk:].reshape(b, n, ATT_KV, HEAD_DIM)
    return q, k, v


def window_attention(q, k, v, kc, vc, sink):
    b, n, nh, hd = q.shape
    nb = n // BLOCK
    rep = nh // ATT_KV
    scale = hd ** -0.5
    qb = q.reshape(b, nb, BLOCK, ATT_KV, rep, hd)

    def band(t):
        tp = jnp.pad(t, ((0, 0), (BLOCK, BLOCK), (0, 0), (0, 0))).reshape(b, nb + 2, BLOCK, ATT_KV, hd)
        return jnp.concatenate([tp[:, :-2], tp[:, 1:-1], tp[:, 2:]], axis=2)

    kb, vb = band(k), band(v)
    qpos = jnp.arange(nb)[:, None] * BLOCK + jnp.arange(BLOCK)[None, :]
    kpos = jnp.arange(nb)[:, None] * BLOCK - BLOCK + jnp.arange(3 * BLOCK)[None, :]
    valid = ((jnp.abs(qpos[:, :, None] - kpos[:, None, :]) <= WINDOW)
             & (kpos[:, None, :] >= 0) & (kpos[:, None, :] < n))
    s_loc = jnp.einsum('bnqgrd,bnkgd->bngrqk', qb, kb, preferred_element_type=jnp.float32) * scale
    s_loc = jnp.where(valid[None, :, None, None], s_loc, -jnp.inf)
    s_ctx = jnp.einsum('bnqgrd,bcgd->bngrqc', qb, kc, preferred_element_type=jnp.float32) * scale
    snk = jnp.broadcast_to(sink.astype(jnp.float32).reshape(1, 1, ATT_KV, rep, 1, 1), s_loc.shape[:-1] + (1,))
    p = jax.nn.softmax(jnp.concatenate([s_loc, s_ctx, snk], axis=-1), axis=-1).astype(v.dtype)
    kl = 3 * BLOCK
    o = (jnp.einsum('bngrqk,bnkgd->bnqgrd', p[..., :kl], vb)
         + jnp.einsum('bngrqc,bcgd->bnqgrd', p[..., kl:kl + kc.shape[1]], vc))
    return o.reshape(b, n, nh * hd)


def dense_attention(q, k, v, kc, vc):
    b, n, nh, hd = q.shape
    nb = n // BLOCK
    rep = nh // ATT_KV
    k_all = jnp.concatenate([k, kc], axis=1)
    v_all = jnp.concatenate([v, vc], axis=1)
    qb = jnp.moveaxis(q.reshape(b, nb, BLOCK, ATT_KV, rep, hd), 1, 0)

    def one_block(q_blk):
        s = jnp.einsum('bqgrd,bkgd->bgrqk', q_blk, k_all, preferred_element_type=jnp.float32) * hd ** -0.5
        p = jax.nn.softmax(s, axis=-1).astype(v_all.dtype)
        return jnp.einsum('bgrqk,bkgd->bqgrd', p, v_all)

    o = lax.map(one_block, qb)
    return jnp.moveaxis(o, 0, 1).reshape(b, n, nh * hd)


def context_attention(q, k, v, sink):
    b, n, nh, hd = q.shape
    rep = nh // ATT_KV
    nk = k.shape[1]
    s = jnp.einsum('bqgrd,bkgd->bgrqk', q.reshape(b, n, ATT_KV, rep, hd), k,
                   preferred_element_type=jnp.float32) * hd ** -0.5
    if sink is not None:
        snk = jnp.broadcast_to(sink.astype(jnp.float32).reshape(1, ATT_KV, rep, 1, 1), s.shape[:-1] + (1,))
        s = jnp.concatenate([s, snk], axis=-1)
    p = jax.nn.softmax(s, axis=-1)[..., :nk].astype(v.dtype)
    return jnp.einsum('bgrqk,bkgd->bqgrd', p, v).reshape(b, n, nh * hd)


def token_mixers(h, hc, w_in, w_out, ssd_conv_w, ssd_conv_b, ssd_a_log, ssd_dt_bias, ssd_d, ssd_norm,
                 hy_conv_w, hy_conv_b, hy_filter, hy_bias, attn_sink, q_norm, k_norm, cos, sin, need_ctx):
    n, n_ctx = h.shape[1], hc.shape[1]
    proj = h @ w_in
    projc = hc @ w_in
    ya, yac = ssd_branch(proj[..., :OFF_B], projc[..., :OFF_B], ssd_conv_w, ssd_conv_b, ssd_a_log,
                         ssd_dt_bias, ssd_d, ssd_norm, need_ctx)
    yb = hyena_seq(proj[..., OFF_B:OFF_C], hy_conv_w, hy_conv_b, hyena_spectra(n, *hy_filter), hy_bias)
    qw, kw, vw = split_qkv(proj[..., OFF_C:OFF_D])
    qwc, kwc, vwc = split_qkv(projc[..., OFF_C:OFF_D])
    yw = window_attention(apply_rope(qw, cos, sin), apply_rope(kw, cos, sin), vw, kwc, vwc, attn_sink)
    qd, kd, vd = split_qkv(proj[..., OFF_D:])
    qdc, kdc, vdc = split_qkv(projc[..., OFF_D:])
    kdc = rms_norm(kdc, k_norm)
    yd = dense_attention(apply_rope(rms_norm(qd, q_norm), cos, sin), apply_rope(rms_norm(kd, k_norm), cos, sin),
                         vd, kdc, vdc)
    dt = h.dtype
    y = jnp.concatenate([ya, yb.astype(dt), yw, yd], axis=-1) @ w_out
    if not need_ctx:
        return y, None
    ybc = hyena_seq(projc[..., OFF_B:OFF_C], hy_conv_w, hy_conv_b, hyena_spectra(n_ctx, *hy_filter), hy_bias)
    ywc = context_attention(qwc, kwc, vwc, attn_sink)
    ydc = context_attention(rms_norm(qdc, q_norm), kdc, vdc, None)
    yc = jnp.concatenate([yac, ybc.astype(dt), ywc, ydc], axis=-1) @ w_out
    return y, yc


def squared_relu_mlp(h, w1, w2):
    return jnp.square(jax.nn.relu(h @ w1)) @ w2


def setup_inputs(seed: int = 0) -> dict:
    key = jax.random.key(seed)
    ks = jax.random.split(key, 40)
    f32 = jnp.float32

    def nrm(k, shape, scale):
        return jax.random.normal(k, shape, f32) * scale

    def gain(k, shape):
        return 1.0 + 0.05 * jax.random.normal(k, shape, f32)

    dt_init = jnp.exp(jax.random.uniform(ks[15], (DEPTH, 2, SSD_HEADS), f32, math.log(1e-3), math.log(1e-1)))
    return {
        'x': nrm(ks[0], (BATCH, SEQ, D_MODEL), 1.0),
        'c': nrm(ks[1], (BATCH, D_MODEL), 1.0),
        'ctx': nrm(ks[2], (BATCH, CTX_LEN, D_MODEL), 1.0),
        'c_ctx': nrm(ks[3], (D_MODEL,), 1.0),
        'w_mod': nrm(ks[4], (DEPTH, D_MODEL, 6 * D_MODEL), 0.5 * D_MODEL ** -0.5),
        'b_mod': nrm(ks[5], (DEPTH, 6 * D_MODEL), 0.02),
        'norm_mix_pre': gain(ks[6], (DEPTH, D_MODEL)),
        'norm_mix_post': gain(ks[7], (DEPTH, D_MODEL)),
        'norm_mlp_pre': gain(ks[8], (DEPTH, D_MODEL)),
        'norm_mlp_post': gain(ks[9], (DEPTH, D_MODEL)),
        'w_in': nrm(ks[10], (DEPTH, D_MODEL, D_IN), D_MODEL ** -0.5),
        'w_out': nrm(ks[11], (DEPTH, D_MIX, D_MODEL), D_MIX ** -0.5),
        'ssd_conv_w': nrm(ks[12], (DEPTH, SSD_CONV, SSD_XBC), SSD_CONV ** -0.5),
        'ssd_conv_b': nrm(ks[13], (DEPTH, SSD_XBC), 0.02),
        'ssd_a_log': jnp.log(jax.random.uniform(ks[14], (DEPTH, 2, SSD_HEADS), f32, 1.0, 16.0)),
        'ssd_dt_bias': dt_init + jnp.log(-jnp.expm1(-dt_init)),
        'ssd_d': gain(ks[16], (DEPTH, SSD_HEADS)),
        'ssd_norm': gain(ks[17], (DEPTH, W_GROUP)),
        'hy_conv_w': nrm(ks[18], (DEPTH, HY_SHORT, COLS_B), HY_SHORT ** -0.5),
        'hy_conv_b': nrm(ks[19], (DEPTH, COLS_B), 0.02),
        'hy_w1': nrm(ks[20], (DEPTH, HY_EMB, HY_FILT), HY_EMB ** -0.5),
        'hy_b1': nrm(ks[21], (DEPTH, HY_FILT), 0.1),
        'hy_freq1': gain(ks[22], (DEPTH, HY_FILT)),
        'hy_w2': nrm(ks[23], (DEPTH, HY_FILT, HY_FILT), HY_FILT ** -0.5),
        'hy_b2': nrm(ks[24], (DEPTH, HY_FILT), 0.1),
        'hy_freq2': gain(ks[25], (DEPTH, HY_FILT)),
        'hy_w3': nrm(ks[26], (DEPTH, HY_FILT, 2 * HY_ORDER * HY_CH), HY_FILT ** -0.5),
        'hy_b3': nrm(ks[27], (DEPTH, 2 * HY_ORDER * HY_CH), 0.02),
        'hy_bias': nrm(ks[28], (DEPTH, HY_ORDER, HY_CH), 0.5),
        'attn_sink': nrm(ks[29], (DEPTH, ATT_HEADS), 0.5),
        'q_norm': gain(ks[30], (DEPTH, HEAD_DIM)),
        'k_norm': gain(ks[31], (DEPTH, HEAD_DIM)),
        'mlp_w1': nrm(ks[32], (DEPTH, D_MODEL, D_FF), D_MODEL ** -0.5),
        'mlp_w2': nrm(ks[33], (DEPTH, D_FF, D_MODEL), D_FF ** -0.5),
    }


def reference(x, c, ctx, c_ctx, w_mod, b_mod, norm_mix_pre, norm_mix_post, norm_mlp_pre, norm_mlp_post,
              w_in, w_out, ssd_conv_w, ssd_conv_b, ssd_a_log, ssd_dt_bias, ssd_d, ssd_norm,
              hy_conv_w, hy_conv_b, hy_w1, hy_b1, hy_freq1, hy_w2, hy_b2, hy_freq2, hy_w3, hy_b3, hy_bias,
              attn_sink, q_norm, k_norm, mlp_w1, mlp_w2):
    rows = x.shape[1] // GRID_W
    cos, sin = rope_tables(rows)
    for i in range(DEPTH):
        need_ctx = i < DEPTH - 1
        mod = jax.nn.silu(c) @ w_mod[i] + b_mod[i]
        mod_c = jax.nn.silu(c_ctx) @ w_mod[i] + b_mod[i]
        sh1, sc1, g1, sh2, sc2, g2 = jnp.split(mod[:, None, :], 6, axis=-1)
        csh1, csc1, cg1, csh2, csc2, cg2 = jnp.split(mod_c, 6, axis=-1)
        h = rms_norm(x, norm_mix_pre[i]) * (1.0 + sc1) + sh1
        hc = rms_norm(ctx, norm_mix_pre[i]) * (1.0 + csc1) + csh1
        hy_filter = (hy_w1[i], hy_b1[i], hy_freq1[i], hy_w2[i], hy_b2[i], hy_freq2[i], hy_w3[i], hy_b3[i])
        y, yc = token_mixers(h, hc, w_in[i], w_out[i], ssd_conv_w[i], ssd_conv_b[i], ssd_a_log[i],
                             ssd_dt_bias[i], ssd_d[i], ssd_norm[i], hy_conv_w[i], hy_conv_b[i], hy_filter,
                             hy_bias[i], attn_sink[i], q_norm[i], k_norm[i], cos, sin, need_ctx)
        x = x + g1 * rms_norm(y, norm_mix_post[i])
        h = rms_norm(x, norm_mlp_pre[i]) * (1.0 + sc2) + sh2
        x = x + g2 * rms_norm(squared_relu_mlp(h, mlp_w1[i], mlp_w2[i]), norm_mlp_post[i])
        if need_ctx:
            ctx = ctx + cg1 * rms_norm(yc, norm_mix_post[i])
            hc = rms_norm(ctx, norm_mlp_pre[i]) * (1.0 + csc2) + csh2
            ctx = ctx + cg2 * rms_norm(squared_relu_mlp(hc, mlp_w1[i], mlp_w2[i]), norm_mlp_post[i])
    return x
```

```python
import os
import sys
import time
import math
from contextlib import ExitStack
import numpy as np
import concourse.bass as bass
import concourse.mybir as mybir
from concourse.bass_utils import run_bass_kernel_spmd


F32 = mybir.dt.float32
BF16 = mybir.dt.bfloat16
ALU = mybir.AluOpType
AF = mybir.ActivationFunctionType
AX = mybir.AxisListType


class Buf:
    __slots__ = ("name", "w", "r")

    def __init__(self, name):
        self.name = name
        self.w = None
        self.r = []


class Prog:
    ENG = ("pe", "act", "dve", "pool", "sp")
    NDMA = 8

    def __init__(self, nc, stack):
        self.nc = nc
        self.ops = {e: [] for e in self.ENG}
        self.sems = []
        self.semval = []
        self.known = {e: {} for e in self.ENG}
        self.esem = {}
        for e in ("pe", "act", "dve", "pool"):
            self.esem[e] = self._newsem(stack, "s_" + e)
        self.dsem = {}
        self.dcnt = {}
        for e in ("sp", "act", "pool"):
            self.dsem[e] = [self._newsem(stack, "d_%s%d" % (e, i)) for i in range(self.NDMA)]
            self.dcnt[e] = 0
        self.nbuf = 0

    def _newsem(self, stack, name):
        h = stack.enter_context(self.nc.semaphore(name))
        self.sems.append(h)
        self.semval.append(0)
        return len(self.sems) - 1

    def buf(self, name=None):
        self.nbuf += 1
        return Buf(name or "b%d" % self.nbuf)

    def bufs(self, n, name="b"):
        return [self.buf("%s%d" % (name, i)) for i in range(n)]

    def _deps(self, eng, reads, writes):
        need = {}
        def add(d):
            if d is None:
                return
            s, v = d
            if need.get(s, 0) < v:
                need[s] = v
        for b in reads:
            add(b.w)
        for b in writes:
            add(b.w)
            for d in b.r:
                add(d)
        kn = self.known[eng]
        waits = []
        for s, v in need.items():
            if kn.get(s, 0) < v:
                kn[s] = v
                waits.append((s, v))
        return waits

    def op(self, eng, fn, reads=(), writes=()):
        waits = self._deps(eng, reads, writes)
        s = self.esem[eng]
        self.semval[s] += 1
        tok = (s, self.semval[s])
        self.ops[eng].append((fn, waits, (s, 1)))
        for b in reads:
            b.r.append(tok)
        for b in writes:
            b.w = tok
            b.r = []
        return tok

    def dma(self, q, out, in_, reads=(), writes=(), **kw):
        waits = self._deps(q, reads, writes)
        i = self.dcnt[q]
        self.dcnt[q] += 1
        s = self.dsem[q][i % self.NDMA]
        prev = self.semval[s]
        if prev > 0 and self.known[q].get(s, 0) < prev:
            self.known[q][s] = prev
            waits.append((s, prev))
        self.semval[s] += 16
        tok = (s, self.semval[s])
        self.ops[q].append((lambda e: e.dma_start(out=out, in_=in_, **kw), waits, (s, 16)))
        for b in reads:
            b.r.append(tok)
        for b in writes:
            b.w = tok
            b.r = []
        return tok

    def finish(self, eng="sp", bufs=()):
        waits = self._deps(eng, bufs, ())
        for q in self.dsem:
            for s in self.dsem[q]:
                v = self.semval[s]
                if v > 0 and self.known[eng].get(s, 0) < v:
                    self.known[eng][s] = v
                    waits.append((s, v))
        self.ops[eng].append((None, waits, None))

    def emit(self):
        nc = self.nc
        hmap = {"pe": "tensor", "act": "scalar", "dve": "vector", "pool": "gpsimd", "sp": "sync"}
        with nc.Block() as block:
            for e in self.ENG:
                ops = self.ops[e]
                sems = self.sems

                def body(h, ops=ops):
                    for fn, waits, inc in ops:
                        for s, v in waits:
                            h.wait_ge(sems[s], v)
                        if fn is not None:
                            ins = fn(h)
                            ins.then_inc(sems[inc[0]], inc[1])
                getattr(block, hmap[e])(body)


D = 1024
DIN = 2568
EPS = 1e-6


def load_bcast_row(P, q, dst_ap, src_row_ap, n, wbuf):
    P.dma(q, dst_ap, src_row_ap.partition_broadcast(128), writes=[wbuf])


def mod_scratch(P, nc, st, ncols):
    S = {}
    S["c_sb"] = st.enter_context(nc.sbuf_tensor("c_sb", [128, 8], F32))
    S["c_sg"] = st.enter_context(nc.sbuf_tensor("c_sg", [128, 8], F32))
    S["c_bc"] = st.enter_context(nc.sbuf_tensor("c_bc", [128, 8, 128], F32))
    S["bm"] = st.enter_context(nc.sbuf_tensor("bm", [128, ncols], F32))
    S["wst"] = [st.enter_context(nc.sbuf_tensor("wmst%d" % i, [128, 8, 512], F32)) for i in range(2)]
    S["ps"] = [st.enter_context(nc.psum_tensor("modps%d" % i, [128, 512], F32)) for i in range(2)]
    S["b"] = P.bufs(4, "modb")
    S["b_w"] = P.bufs(2, "wmst")
    S["b_ps"] = P.bufs(2, "modps")
    return S


def build_mod(P, nc, S, cvec, w_mod, b_mod, col0, ncols, out_tile, out_buf):
    c_sb, c_sg, c_bc, bm, wst, ps = S["c_sb"], S["c_sg"], S["c_bc"], S["bm"], S["wst"], S["ps"]
    b_c, b_cs, b_bc, b_bm = S["b"]
    b_w, b_ps = S["b_w"], S["b_ps"]
    P.dma("sp", c_sb[:], cvec.rearrange("(k p) -> p k", p=128), writes=[b_c], allow_slow_non_contiguous=True)
    P.dma("sp", bm[:, 0:ncols], b_mod[col0:col0 + ncols].partition_broadcast(128), writes=[b_bm])
    P.op("act", lambda e: e.activation(out=c_sg[:], in_=c_sb[:], func=AF.Sigmoid), reads=[b_c], writes=[b_cs])
    P.op("dve", lambda e: e.tensor_tensor(out=c_sg[:], in0=c_sg[:], in1=c_sb[:], op=ALU.mult), reads=[b_c, b_cs], writes=[b_cs])
    P.op("dve", lambda e: e.tensor_copy(out=c_bc[:], in_=c_sg[:].unsqueeze(2).to_broadcast([128, 8, 128])), reads=[b_cs], writes=[b_bc])
    wv = w_mod.rearrange("(k p) n -> p k n", p=128)
    for j in range(ncols // 512):
        s = j % 2
        P.dma("sp", wst[s][:], wv[:, :, col0 + j * 512: col0 + (j + 1) * 512], writes=[b_w[s]])
        for k in range(8):
            P.op("pe", lambda e, k=k, s=s: e.matmul(ps[s][:], lhsT=c_bc[:, k, :], rhs=wst[s][:, k, :], start=(k == 0), stop=(k == 7)),
                 reads=[b_bc, b_w[s]], writes=[b_ps[s]])
        P.op("dve", lambda e, j=j, s=s: e.tensor_tensor(out=out_tile[:, j * 512:(j + 1) * 512], in0=ps[s][:], in1=bm[:, j * 512:(j + 1) * 512], op=ALU.add),
             reads=[b_ps[s], b_bm], writes=[out_buf])


def build_k1(ntiles_lat, ntiles_ctx):
    nc = bass.Bass("TRN2", target_bir_lowering=False)
    NT = ntiles_lat + ntiles_ctx
    x = nc.dram_tensor("x", [NT * 128, D], F32, kind="ExternalInput").ap()
    cv = nc.dram_tensor("cv", [D], F32, kind="ExternalInput").ap()
    cctx = nc.dram_tensor("cctx", [D], F32, kind="ExternalInput").ap()
    w_mod = nc.dram_tensor("w_mod", [D, 2048], F32, kind="ExternalInput").ap()
    b_mod = nc.dram_tensor("b_mod", [2048], F32, kind="ExternalInput").ap()
    g_pre = nc.dram_tensor("g_pre", [D], F32, kind="ExternalInput").ap()
    w_in = nc.dram_tensor("w_in", [D, DIN], F32, kind="ExternalInput").ap()
    proj = nc.dram_tensor("proj", [NT * 128, DIN], F32, kind="ExternalOutput").ap()
    ident_d = nc.dram_tensor("ident", [128, 128], F32, kind="ExternalInput").ap()
    with ExitStack() as st:
        P = Prog(nc, st)
        sb = lambda name, shape, dt=F32: st.enter_context(nc.sbuf_tensor(name, shape, dt))
        ident_f = sb("ident_f", [128, 128])
        ident = sb("ident_b", [128, 128], BF16)
        b_id = P.buf("ident")
        P.dma("sp", ident_f[:], ident_d, writes=[b_id])
        P.op("dve", lambda e: e.tensor_copy(out=ident[:], in_=ident_f[:]), reads=[b_id], writes=[b_id])
        modl = sb("modl", [128, 2048]); b_modl = P.buf("modl")
        modc = sb("modc", [128, 2048]); b_modc = P.buf("modc")
        if True:
            st2 = st
            MS = mod_scratch(P, nc, st, 2048)
            build_mod(P, nc, MS, cv, w_mod, b_mod, 0, 2048, modl, b_modl)
            build_mod(P, nc, MS, cctx, w_mod, b_mod, 0, 2048, modc, b_modc)
            gp = sb("gp", [128, D]); b_gp = P.buf("gp")
            P.dma("sp", gp[:], g_pre.partition_broadcast(128), writes=[b_gp])
            for m, bm_ in ((modl, b_modl), (modc, b_modc)):
                P.op("dve", lambda e, m=m: e.scalar_tensor_tensor(out=m[:, 1024:2048], in0=m[:, 1024:2048], scalar=1.0, in1=gp[:], op0=ALU.add, op1=ALU.mult),
                     reads=[bm_, b_gp], writes=[bm_])
            w_bf = sb("w_bf", [128, 8, DIN], BF16); b_wbf = P.buf("wbf")
            wst = [st2.enter_context(nc.sbuf_tensor("wst%d" % i, [128, DIN], F32)) for i in range(2)]
            b_wst = P.bufs(2, "wst")
            wv = w_in.rearrange("(k p) n -> p k n", p=128)
            for k in range(8):
                s = k % 2
                P.dma("pool", wst[s][:], wv[:, k, :], writes=[b_wst[s]])
                P.op("pool", lambda e, k=k, s=s: e.tensor_copy(out=w_bf[:, k, :], in_=wst[s][:]), reads=[b_wst[s]], writes=[b_wbf])
        epsb = sb("epsb", [128, 1])
        b_eps = P.buf("eps")
        P.op("dve", lambda e: e.memset(epsb[:], EPS), writes=[b_eps])
        NB = 2
        xt = [sb("xt%d" % i, [128, D]) for i in range(NB)]; b_xt = P.bufs(NB, "xt")
        sq = sb("sq", [128, D]); b_sq = P.buf("sq")
        ss = [sb("ss%d" % i, [128, 1]) for i in range(NB)]; b_ss = P.bufs(NB, "ss")
        hb = [sb("hb%d" % i, [128, D], BF16) for i in range(NB)]; b_hb = P.bufs(NB, "hb")
        hT = [sb("hT%d" % i, [128, 8, 128], BF16) for i in range(NB)]; b_hT = P.bufs(NB, "hT")
        tps = [st.enter_context(nc.psum_tensor("tps%d" % i, [128, 8, 128], BF16)) for i in range(2)]; b_tps = P.bufs(2, "tps")
        ops_ = [st.enter_context(nc.psum_tensor("ops%d" % i, [128, 512], F32)) for i in range(4)]; b_ops = P.bufs(4, "ops")
        ob = [sb("ob%d" % i, [128, DIN]) for i in range(NB)]; b_ob = P.bufs(NB, "ob")
        b_out = P.buf("out")
        colch = [(c0, min(512, DIN - c0)) for c0 in range(0, DIN, 512)]
        pi = 0
        for t in range(NT):
            s = t % NB
            mod = modl if t < ntiles_lat else modc
            bmod = b_modl if t < ntiles_lat else b_modc
            P.dma("sp", xt[s][:], x[t * 128:(t + 1) * 128, :], writes=[b_xt[s]])
            P.op("act", lambda e, s=s: e.activation(out=sq[:], in_=xt[s][:], func=AF.Square, scale=float(D ** -0.5), accum_out=ss[s][:]),
                 reads=[b_xt[s]], writes=[b_sq, b_ss[s]])
            P.op("act", lambda e, s=s: e.activation(out=ss[s][:], in_=ss[s][:], func=AF.Sqrt, bias=epsb[:, 0:1]),
                 reads=[b_ss[s], b_eps], writes=[b_ss[s]])
            P.op("dve", lambda e, s=s: e.reciprocal(out=ss[s][:], in_=ss[s][:]),
                 reads=[b_ss[s]], writes=[b_ss[s]])
            P.op("dve", lambda e, s=s, mod=mod: e.scalar_tensor_tensor(out=xt[s][:], in0=xt[s][:], scalar=ss[s][:, 0:1], in1=mod[:, 1024:2048], op0=ALU.mult, op1=ALU.mult),
                 reads=[b_xt[s], b_ss[s], bmod], writes=[b_xt[s]])
            P.op("dve", lambda e, s=s, mod=mod: e.tensor_tensor(out=hb[s][:], in0=xt[s][:], in1=mod[:, 0:1024], op=ALU.add),
                 reads=[b_xt[s], bmod], writes=[b_hb[s]])
            tp = t % 2
            for k in range(8):
                P.op("pe", lambda e, k=k, s=s, tp=tp: e.transpose(tps[tp][:, k, :], hb[s][:, k * 128:(k + 1) * 128], ident[:]),
                     reads=[b_hb[s], b_id], writes=[b_tps[tp]])
            P.op("act", lambda e, s=s, tp=tp: e.copy(out=hT[s][:], in_=tps[tp][:]), reads=[b_tps[tp]], writes=[b_hT[s]])
            for (c0, cw) in colch:
                p = pi % 4; pi += 1
                for k in range(8):
                    P.op("pe", lambda e, k=k, s=s, p=p, c0=c0, cw=cw: e.matmul(ops_[p][:, 0:cw], lhsT=hT[s][:, k, :], rhs=w_bf[:, k, c0:c0 + cw], start=(k == 0), stop=(k == 7)),
                         reads=[b_hT[s], b_wbf], writes=[b_ops[p]])
                eng = "act" if (pi % 2) else "dve"
                if eng == "act":
                    P.op("act", lambda e, s=s, p=p, c0=c0, cw=cw: e.copy(out=ob[s][:, c0:c0 + cw], in_=ops_[p][:, 0:cw]), reads=[b_ops[p]], writes=[b_ob[s]])
                else:
                    P.op("dve", lambda e, s=s, p=p, c0=c0, cw=cw: e.tensor_copy(out=ob[s][:, c0:c0 + cw], in_=ops_[p][:, 0:cw]), reads=[b_ops[p]], writes=[b_ob[s]])
            P.dma("sp", proj[t * 128:(t + 1) * 128, :], ob[s][:], reads=[b_ob[s]], writes=[b_out])
        P.finish("sp", [b_out])
        P.emit()
    return nc


EPS = 1e-6
HD = 64


def build_kc(NL, NC):
    nc = bass.Bass("TRN2", target_bir_lowering=False)
    NT = NL + NC
    nlt, nct, nkt = NL // 128, NC // 128, NT // 128
    din = lambda n, s: nc.dram_tensor(n, s, F32, kind="ExternalInput").ap()
    qT = {b: din("qT_" + b, [2, 64, NT]) for b in "dw"}
    kT = {b: din("kT_" + b, [64, NT]) for b in "dw"}
    vv = {b: din("v_" + b, [NT, 64]) for b in "dw"}
    cos_d = din("cos2", [64, NL]); sin_d = din("sin2", [64, NL])
    Rm_d = din("Rm", [64, 64]); ones_d = din("ones64", [64, 64]); ident_d = din("ident", [128, 128])
    mprev_d = din("mprev", [128, 128]); mnext_d = din("mnext", [128, 128])
    qn_d = din("qn", [64]); kn_d = din("kn", [64]); sink_d = din("sink", [2])
    y = {b: nc.dram_tensor("y_" + b, [NT, 128], F32, kind="ExternalOutput").ap() for b in "dw"}
    with ExitStack() as st:
        P = Prog(nc, st)
        sb = lambda name, shape, dt=F32: st.enter_context(nc.sbuf_tensor("s_" + name, shape, dt))
        ps = lambda name, shape, dt=F32: st.enter_context(nc.psum_tensor("p_" + name, shape, dt))
        b_c = P.buf("consts")
        cos2 = sb("cos2", [64, NL]); sin2 = sb("sin2", [64, NL])
        Rm = sb("Rm", [64, 64]); ones64 = sb("ones64", [64, 64]); ident = sb("ident", [128, 128])
        mpn_f = sb("mpn_f", [128, 2, 128]); mpn = sb("mpn", [128, 2, 128], BF16)
        gq = sb("gq", [64, 1]); gk = sb("gk", [64, 1]); sk = sb("sk", [128, 2]); epsb = sb("epsb", [128, 1])
        P.dma("sp", cos2[:], cos_d, writes=[b_c]); P.dma("sp", sin2[:], sin_d, writes=[b_c])
        P.dma("sp", Rm[:], Rm_d, writes=[b_c]); P.dma("sp", ones64[:], ones_d, writes=[b_c]); P.dma("sp", ident[:], ident_d, writes=[b_c])
        P.dma("sp", mpn_f[:, 0, :], mprev_d, writes=[b_c]); P.dma("sp", mpn_f[:, 1, :], mnext_d, writes=[b_c])
        P.dma("sp", gq[:], qn_d.rearrange("(p o) -> p o", o=1), writes=[b_c]); P.dma("sp", gk[:], kn_d.rearrange("(p o) -> p o", o=1), writes=[b_c])
        P.dma("sp", sk[:], sink_d.partition_broadcast(128), writes=[b_c])
        P.op("dve", lambda e: e.tensor_copy(out=mpn[:], in_=mpn_f[:]), reads=[b_c], writes=[b_c])
        P.op("dve", lambda e: e.memset(epsb[:], EPS), reads=[b_c], writes=[b_c])
        P.op("act", lambda e: e.activation(out=sk[:], in_=sk[:], func=AF.Exp), reads=[b_c], writes=[b_c])
        QT = sb("QT", [64, 2, NT], BF16); b_QT = P.buf("QT")
        KT = sb("KT", [64, NT], BF16); b_KT = P.buf("KT")
        VA = sb("VA", [128, nkt, 65], BF16); b_VA = P.buf("VA")
        vst = sb("vst", [128, nkt, 64]); b_vst = P.buf("vst")
        stg = [sb("stg%d" % i, [64, 512]) for i in range(2)]; b_stg = P.bufs(2, "stg")
        sqb = sb("sqb", [64, 512]); b_sq = P.buf("sqb")
        rsb = sb("rsb", [64, 512]); b_rs = P.buf("rsb")
        t1b = sb("t1b", [64, 512]); b_t1 = P.buf("t1b")
        t2b = sb("t2b", [64, 512]); b_t2 = P.buf("t2b")
        pps = [ps("pps%d" % i, [64, 512]) for i in range(2)]; b_pps = P.bufs(2, "pps")
        sps = [ps("sps%d" % i, [128, 512]) for i in range(3)]; b_sps = P.bufs(3, "sps")
        ops_ = [ps("ops%d" % i, [128, 512]) for i in range(2)]; b_ops = P.bufs(2, "ops")
        tps = ps("tps", [128, 4, 128]); b_tps = P.buf("tps")
        ptb = [sb("ptb%d" % i, [128, 512], BF16) for i in range(3)]; b_pt = P.bufs(3, "ptb")
        osb = sb("osb", [65, 512]); b_osb = P.buf("osb")
        rd = sb("rd", [128, 4, 1]); b_rd = P.buf("rd")
        otb = [sb("otb%d" % i, [128, 4, 64]) for i in range(2)]; b_ot = P.bufs(2, "otb")
        b_y = P.buf("y")
        cnt = {"stg": 0, "s": 0, "o": 0, "ot": 0, "pp": 0}

        def prep(src_ap, dst_ap, ncols, col0, norm_gain, rope):
            for c0 in range(0, ncols, 512):
                cw = min(512, ncols - c0)
                s = cnt["stg"] % 2; cnt["stg"] += 1
                P.dma("sp", stg[s][:, 0:cw], src_ap[:, c0:c0 + cw], writes=[b_stg[s]])
                cur = stg[s]; bcur = b_stg[s]
                if norm_gain is not None:
                    pp = cnt["pp"] % 2; cnt["pp"] += 1
                    P.op("act", lambda e, s=s, cw=cw: e.activation(out=sqb[:, 0:cw], in_=stg[s][:, 0:cw], func=AF.Square), reads=[b_stg[s]], writes=[b_sq])
                    P.op("pe", lambda e, pp=pp, cw=cw: e.matmul(pps[pp][:, 0:cw], lhsT=ones64[:], rhs=sqb[:, 0:cw], start=True, stop=True), reads=[b_sq, b_c], writes=[b_pps[pp]])
                    P.op("act", lambda e, pp=pp, cw=cw: e.activation(out=rsb[:, 0:cw], in_=pps[pp][:, 0:cw], func=AF.Sqrt, bias=epsb[0:64, 0:1], scale=1.0 / 64), reads=[b_pps[pp], b_c], writes=[b_rs])
                    P.op("dve", lambda e, cw=cw: e.reciprocal(out=rsb[:, 0:cw], in_=rsb[:, 0:cw]), reads=[b_rs], writes=[b_rs])
                    P.op("dve", lambda e, s=s, cw=cw, g=norm_gain: e.scalar_tensor_tensor(out=stg[s][:, 0:cw], in0=stg[s][:, 0:cw], scalar=g[:, 0:1], in1=rsb[:, 0:cw], op0=ALU.mult, op1=ALU.mult),
                         reads=[b_stg[s], b_rs, b_c], writes=[b_stg[s]])
                if rope:
                    pp = cnt["pp"] % 2; cnt["pp"] += 1
                    P.op("pe", lambda e, pp=pp, s=s, cw=cw: e.matmul(pps[pp][:, 0:cw], lhsT=Rm[:], rhs=stg[s][:, 0:cw], start=True, stop=True), reads=[b_stg[s], b_c], writes=[b_pps[pp]])
                    P.op("pool", lambda e, s=s, cw=cw, c0=c0: e.tensor_tensor(out=t1b[:, 0:cw], in0=stg[s][:, 0:cw], in1=cos2[:, col0 + c0:col0 + c0 + cw], op=ALU.mult), reads=[b_stg[s], b_c], writes=[b_t1])
                    P.op("dve", lambda e, pp=pp, cw=cw, c0=c0: e.tensor_tensor(out=t2b[:, 0:cw], in0=pps[pp][:, 0:cw], in1=sin2[:, col0 + c0:col0 + c0 + cw], op=ALU.mult), reads=[b_pps[pp], b_c], writes=[b_t2])
                    P.op("dve", lambda e, cw=cw, c0=c0: e.tensor_tensor(out=dst_ap[:, c0:c0 + cw], in0=t1b[:, 0:cw], in1=t2b[:, 0:cw], op=ALU.add), reads=[b_t1, b_t2], writes=[dst_buf[0]])
                else:
                    P.op("dve", lambda e, s=s, cw=cw, c0=c0: e.tensor_copy(out=dst_ap[:, c0:c0 + cw], in_=stg[s][:, 0:cw]), reads=[b_stg[s]], writes=[dst_buf[0]])

        dst_buf = [None]

        def attn_group(qcols, N, ktiles, sink_ap, out_cb):
            o = cnt["o"] % 2; cnt["o"] += 1
            nk = len(ktiles)
            for i, (kt, va, mask) in enumerate(ktiles):
                s = cnt["s"] % 3; cnt["s"] += 1
                P.op("pe", lambda e, s=s, kt=kt: e.matmul(sps[s][:, 0:N], lhsT=kt, rhs=qcols, start=True, stop=True), reads=[b_KT, b_QT], writes=[b_sps[s]])
                P.op("act", lambda e, s=s: e.activation(out=ptb[s][:, 0:N], in_=sps[s][:, 0:N], func=AF.Exp, scale=0.125), reads=[b_sps[s]], writes=[b_pt[s]])
                if mask is not None:
                    P.op("dve", lambda e, s=s, mask=mask: e.tensor_tensor(out=ptb[s][:, 0:N], in0=ptb[s][:, 0:N], in1=mask, op=ALU.mult), reads=[b_pt[s], b_c], writes=[b_pt[s]])
                P.op("pe", lambda e, s=s, va=va, i=i: e.matmul(ops_[o][0:65, 0:N], lhsT=va, rhs=ptb[s][:, 0:N], start=(i == 0), stop=(i == nk - 1)), reads=[b_pt[s], b_VA], writes=[b_ops[o]])
            nj = N // 128
            P.op("act", lambda e: e.copy(out=osb[:, 0:N], in_=ops_[o][0:65, 0:N]), reads=[b_ops[o]], writes=[b_osb])
            for j in range(nj):
                P.op("pe", lambda e, j=j: e.transpose(tps[:, j, 0:65], osb[0:65, j * 128:(j + 1) * 128], ident[0:65, 0:65]), reads=[b_osb, b_c], writes=[b_tps])
            if sink_ap is not None:
                P.op("dve", lambda e: e.tensor_tensor(out=rd[:, 0:nj, :], in0=tps[:, 0:nj, 64:65], in1=sink_ap, op=ALU.add), reads=[b_tps, b_c], writes=[b_rd])
                P.op("dve", lambda e: e.reciprocal(out=rd[:, 0:nj, :], in_=rd[:, 0:nj, :]), reads=[b_rd], writes=[b_rd])
            else:
                P.op("dve", lambda e: e.reciprocal(out=rd[:, 0:nj, :], in_=tps[:, 0:nj, 64:65]), reads=[b_tps], writes=[b_rd])
            t = cnt["ot"] % 2; cnt["ot"] += 1
            P.op("dve", lambda e, t=t: e.tensor_tensor(out=otb[t][:, 0:nj, :], in0=tps[:, 0:nj, 0:64], in1=rd[:, 0:nj, :].to_broadcast([128, nj, 64]), op=ALU.mult), reads=[b_tps, b_rd], writes=[b_ot[t]])
            out_cb(otb[t], b_ot[t])

        STAGE = int(os.environ.get("STAGE", "9"))
        for br in "dw":
            dst_buf[0] = b_QT
            for h in range(2):
                prep(qT[br][h][:, 0:NL], QT[:, h, 0:NL], NL, 0, gq if br == "d" else None, True)
                prep(qT[br][h][:, NL:NT], QT[:, h, NL:NT], NC, 0, gq if br == "d" else None, False)
            dst_buf[0] = b_KT
            prep(kT[br][:, 0:NL], KT[:, 0:NL], NL, 0, gk if br == "d" else None, True)
            prep(kT[br][:, NL:NT], KT[:, NL:NT], NC, 0, gk if br == "d" else None, False)
            P.dma("sp", vst[:], vv[br].rearrange("(t p) d -> p t d", p=128), writes=[b_vst])
            P.op("pool", lambda e: e.memset(VA[:, :, 64:65], 1.0), writes=[b_VA])
            P.op("pool", lambda e: e.tensor_copy(out=VA[:, :, 0:64], in_=vst[:]), reads=[b_vst], writes=[b_VA])
            yb = y[br]
            ctx_tiles = [(KT[:, NL + c * 128: NL + (c + 1) * 128], VA[:, nlt + c, :], None) for c in range(nct)]
            if STAGE < 2:
                P.dma("sp", yb[0:64, 0:64], stg[0][:, 0:64], reads=[b_stg[0]], writes=[b_y])
                continue
            if br == "d" and STAGE != 3:
                all_tiles = [(KT[:, c * 128:(c + 1) * 128], VA[:, c, :], None) for c in range(nkt)]
                for h in range(2):
                    for q0 in range(0, NL, 512):
                        def cb(ot, bo, h=h, q0=q0):
                            P.dma("sp", yb[q0:q0 + 512, h * 64:(h + 1) * 64].rearrange("(j p) d -> p j d", p=128), ot[:, 0:4, :], reads=[bo], writes=[b_y])
                        attn_group(QT[:, h, q0:q0 + 512], 512, all_tiles, None, cb)
            elif br == "w" and STAGE >= 3:
                for n in range(nlt):
                    tiles = []
                    if n > 0:
                        tiles.append((KT[:, (n - 1) * 128:n * 128], VA[:, n - 1, :], mpn[:, 0, :]))
                    tiles.append((KT[:, n * 128:(n + 1) * 128], VA[:, n, :], None))
                    if n < nlt - 1:
                        tiles.append((KT[:, (n + 1) * 128:(n + 2) * 128], VA[:, n + 1, :], mpn[:, 1, :]))
                    tiles += ctx_tiles
                    for h in range(2):
                        def cb(ot, bo, n=n, h=h):
                            P.dma("sp", yb[n * 128:(n + 1) * 128, h * 64:(h + 1) * 64], ot[:, 0, :], reads=[bo], writes=[b_y])
                        attn_group(QT[:, h, n * 128:(n + 1) * 128], 128, tiles, sk[:, h:h + 1].unsqueeze(1), cb)
            for h in range(2 if STAGE >= 4 else 0):
                def cb(ot, bo, h=h):
                    P.dma("sp", yb[NL:NT, h * 64:(h + 1) * 64].rearrange("(j p) d -> p j d", p=128), ot[:, 0:nct, :], reads=[bo], writes=[b_y])
                snk = sk[:, h:h + 1].unsqueeze(1).to_broadcast([128, nct, 1]) if br == "w" else None
                attn_group(QT[:, h, NL:NT], NC, ctx_tiles, snk, cb)
        P.finish("sp", [b_y])
        P.emit()
    return nc


def rope_tables_np(NL, GW=64):
    t = np.arange(NL)
    row = (t // GW).astype(np.float32); col = (t % GW).astype(np.float32)
    inv = (10000.0 ** (-np.arange(16, dtype=np.float32) / 16)).astype(np.float32)
    ang = np.concatenate([row[:, None] * inv, col[:, None] * inv], -1)
    return np.cos(ang).astype(np.float32), np.sin(ang).astype(np.float32)


def kc_consts(NL):
    cos, sin = rope_tables_np(NL)
    cos2 = np.ascontiguousarray(np.concatenate([cos, cos], -1).T)
    sin2 = np.ascontiguousarray(np.concatenate([sin, sin], -1).T)
    Rm = np.zeros((64, 64), np.float32)
    for m in range(32):
        Rm[m + 32, m] = -1.0
        Rm[m, m + 32] = 1.0
    kl = np.arange(128)[:, None]; ql = np.arange(128)[None, :]
    return {"cos2": cos2, "sin2": sin2, "Rm": Rm, "ones64": np.ones((64, 64), np.float32), "ident": np.eye(128, dtype=np.float32),
            "mprev": (kl >= ql).astype(np.float32), "mnext": (kl <= ql).astype(np.float32)}


NEG = -30000.0


def build_ka(NC_, NL):
    nc = bass.Bass("TRN2", target_bir_lowering=False)
    NT = NC_ + NL
    nt = NT // 128
    NP = NT + 8
    din = lambda n, s: nc.dram_tensor(n, s, F32, kind="ExternalInput").ap()
    xr = din("xr", [2, 128, NP]); br = din("br", [2, 64, NP]); cr = din("cr", [2, 64, NP])
    cwx = din("cwx", [2, 128, 6]); cwb = din("cwb", [2, 64, 6]); cwc = din("cwc", [2, 64, 6])
    dtr = din("dtr", [2, 2, NT])
    alog = din("alog", [2, 2]); dtb = din("dtb", [2, 2])
    U_d = din("U", [128, 128]); ones_d = din("ones", [128, 128]); SL_d = din("SL", [128, 128]); ident_d = din("ident", [128, 128])
    mask_d = din("masks", [4, 128, 512])
    yo = nc.dram_tensor("y", [2, NT, 128], F32, kind="ExternalOutput").ap()
    xso = nc.dram_tensor("xs", [NT, 128], F32, kind="ExternalOutput").ap()
    scr = nc.dram_tensor("scr", [4, NT], F32, kind="ExternalOutput").ap()
    segs = [(0, NC_), (NC_, NL)]
    with ExitStack() as st:
        P = Prog(nc, st)
        sb = lambda name, shape, dt=F32: st.enter_context(nc.sbuf_tensor("s_" + name, shape, dt))
        ps = lambda name, shape, dt=F32: st.enter_context(nc.psum_tensor("p_" + name, shape, dt))
        b_c = P.buf("c")
        U = sb("U", [128, 128]); ones = sb("ones", [128, 128]); SL = sb("SL", [128, 128]); ident = sb("ident", [128, 128])
        masks = sb("masks", [128, 4, 512])
        for t_, d_ in ((U, U_d), (ones, ones_d), (SL, SL_d), (ident, ident_d)):
            P.dma("sp", t_[:], d_, writes=[b_c])
        P.dma("sp", masks[:], mask_d.rearrange("r p n -> p r n"), writes=[b_c])
        bigA = sb("bigA", [128, NP]); b_bigA = P.buf("bigA")
        bigB = sb("bigB", [128, NT]); b_bigB = P.buf("bigB")
        sgm = sb("sgm", [128, 2048]); b_sgm = P.buf("sgm")
        cw = sb("cw", [128, 6]); b_cw = P.buf("cw")
        X = sb("X", [128, nt, 128]); b_X = P.buf("X")
        Xdt = sb("Xdt", [128, nt, 64], BF16); b_Xdt = P.buf("Xdt")
        BT = sb("BT", [64, NT], BF16); b_BT = P.buf("BT")
        CT = sb("CT", [64, NT], BF16); b_CT = P.buf("CT")
        dt_ = sb("dt", [128, nt]); dta = sb("dta", [128, nt]); acum = sb("acum", [128, nt]); nacum = sb("nacum", [128, nt]); dsl = sb("dsl", [128, nt])
        dtaT = sb("dtaT", [128, 128]); acT = sb("acT", [128, 128])
        b_dt = P.buf("dt")
        sc2 = sb("sc2", [128, 2]); b_sc = P.buf("sc2")
        tps = ps("tps", [128, 512]); b_tps = P.buf("tps")
        aps = ps("aps", [128, 128]); b_aps = P.buf("aps")
        sps = [ps("sps%d" % i, [128, 512]) for i in range(2)]; b_sps = P.bufs(2, "sps")
        ops_ = ps("ops", [64, 512]); b_ops = P.buf("ops")
        Lb = [sb("Lb%d" % i, [128, 512]) for i in range(2)]; b_L = P.bufs(2, "L")
        Mb = [sb("Mb%d" % i, [128, 512], BF16) for i in range(2)]; b_M = P.bufs(2, "M")
        osb = sb("osb", [64, 512]); b_osb = P.buf("osb")
        ot = [sb("ot%d" % i, [128, 4, 64]) for i in range(2)]; b_ot = P.bufs(2, "ot")
        b_y = P.buf("y"); b_scr = P.buf("scr")
        cnt = {"s": 0, "l": 0, "ot": 0}

        def conv_silu(raw_ap, cw_ap, npart, out_fn):
            P.dma("sp", bigA[0:npart, :], raw_ap, writes=[b_bigA])
            P.dma("sp", cw[0:npart, :], cw_ap, writes=[b_cw])
            for si, (t0, ln) in enumerate(segs):
                p0 = t0 + 2 + 4 * si
                for c0 in range(0, ln, 2048):
                    w = min(2048, ln - c0)
                    o = bigB[0:npart, t0 + c0:t0 + c0 + w]
                    P.op("dve", lambda e, o=o, p0=p0, c0=c0, w=w: e.tensor_scalar(out=o, in0=bigA[0:npart, p0 + c0 - 2:p0 + c0 - 2 + w], scalar1=cw[0:npart, 0:1], scalar2=cw[0:npart, 5:6], op0=ALU.mult, op1=ALU.add),
                         reads=[b_bigA, b_cw], writes=[b_bigB])
                    for k in range(1, 5):
                        P.op("dve", lambda e, o=o, p0=p0, c0=c0, w=w, k=k: e.scalar_tensor_tensor(out=o, in0=bigA[0:npart, p0 + c0 - 2 + k:p0 + c0 - 2 + k + w], scalar=cw[0:npart, k:k + 1], in1=o, op0=ALU.mult, op1=ALU.add),
                             reads=[b_bigA, b_cw, b_bigB], writes=[b_bigB])
                    P.op("act", lambda e, o=o, w=w: e.activation(out=sgm[0:npart, 0:w], in_=o, func=AF.Sigmoid), reads=[b_bigB], writes=[b_sgm])
                    P.op("dve", lambda e, o=o, w=w: e.tensor_tensor(out=o, in0=o, in1=sgm[0:npart, 0:w], op=ALU.mult), reads=[b_bigB, b_sgm], writes=[b_bigB])
            out_fn()

        for d in range(2):
            conv_silu(br[d], cwb[d], 64, lambda: P.op("pool", lambda e: e.tensor_copy(out=BT[:], in_=bigB[0:64, :]), reads=[b_bigB], writes=[b_BT]))
            conv_silu(cr[d], cwc[d], 64, lambda: P.op("pool", lambda e: e.tensor_copy(out=CT[:], in_=bigB[0:64, :]), reads=[b_bigB], writes=[b_CT]))
            def xout():
                for c in range(nt):
                    g = c % 4
                    P.op("pe", lambda e, c=c, g=g: e.transpose(tps[:, g * 128:(g + 1) * 128], bigB[:, c * 128:(c + 1) * 128], ident[:]), reads=[b_bigB, b_c], writes=[b_tps])
                    if g == 3 or c == nt - 1:
                        c0 = c - g
                        P.op("act", lambda e, c0=c0, g=g: e.copy(out=X[:, c0:c0 + g + 1, :], in_=tps[:, 0:(g + 1) * 128].rearrange("p (a n) -> p a n", n=128)), reads=[b_tps], writes=[b_X])
                if d == 0:
                    P.dma("sp", xso.rearrange("(c p) n -> p c n", p=128), X[:], reads=[b_X], writes=[b_y])
            conv_silu(xr[d], cwx[d], 128, xout)
            for h in range(2):
                P.dma("sp", dt_[:], dtr[d, h].rearrange("(c p) -> p c", p=128), writes=[b_dt], allow_slow_non_contiguous=True)
                P.dma("sp", sc2[:, 0:1], dtb[d, h:h + 1].partition_broadcast(128), writes=[b_sc])
                P.dma("sp", sc2[:, 1:2], alog[d, h:h + 1].partition_broadcast(128), writes=[b_sc])
                P.op("act", lambda e: e.activation(out=sc2[:, 1:2], in_=sc2[:, 1:2], func=AF.Exp), reads=[b_sc], writes=[b_sc])
                P.op("act", lambda e: e.activation(out=dt_[:], in_=dt_[:], func=AF.Exp, bias=sc2[:, 0:1]), reads=[b_dt, b_sc], writes=[b_dt])
                P.op("act", lambda e: e.activation(out=dt_[:], in_=dt_[:], func=AF.Ln, bias=ones[:, 0:1]), reads=[b_dt, b_c], writes=[b_dt])
                P.op("dve", lambda e: e.tensor_scalar(out=dta[:], in0=dt_[:], scalar1=sc2[:, 1:2], scalar2=-1.0, op0=ALU.mult, op1=ALU.mult), reads=[b_dt, b_sc], writes=[b_dt])
                P.op("pe", lambda e: e.transpose(aps[0:nt, :], dta[:], ident[:]), reads=[b_dt, b_c], writes=[b_aps])
                P.op("act", lambda e: e.copy(out=dtaT[0:nt, :], in_=aps[0:nt, :]), reads=[b_aps], writes=[b_dt])
                P.op("pe", lambda e: e.matmul(aps[:, 0:nt], lhsT=dtaT[0:nt, :], rhs=SL[0:nt, 0:nt], start=True, stop=True), reads=[b_dt, b_c], writes=[b_aps])
                P.op("act", lambda e: e.copy(out=dsl[:], in_=aps[:, 0:nt]), reads=[b_aps], writes=[b_dt])
                P.op("pe", lambda e: e.matmul(aps[:, 0:nt], lhsT=U[:], rhs=dta[:], start=True, stop=False), reads=[b_dt, b_c], writes=[b_aps])
                P.op("pe", lambda e: e.matmul(aps[:, 0:nt], lhsT=ones[:], rhs=dsl[:], start=False, stop=True), reads=[b_dt, b_c], writes=[b_aps])
                P.op("act", lambda e: e.copy(out=acum[:], in_=aps[:, 0:nt]), reads=[b_aps], writes=[b_dt])
                P.op("dve", lambda e: e.tensor_scalar(out=nacum[:], in0=acum[:], scalar1=-1.0, scalar2=0.0, op0=ALU.mult, op1=ALU.add), reads=[b_dt], writes=[b_dt])
                P.op("pe", lambda e: e.transpose(aps[0:nt, :], acum[:], ident[:]), reads=[b_dt, b_c], writes=[b_aps])
                P.op("act", lambda e: e.copy(out=acT[0:nt, :], in_=aps[0:nt, :]), reads=[b_aps], writes=[b_dt])
                si = d * 2 + h
                P.dma("sp", scr[si].rearrange("(c p) -> c p", p=128), acT[0:nt, :], reads=[b_dt], writes=[b_scr])
                P.dma("sp", bigB[:], scr[si].partition_broadcast(128), reads=[b_scr], writes=[b_bigB])
                P.op("dve", lambda e, h=h: e.tensor_tensor(out=Xdt[:], in0=X[:, :, h * 64:(h + 1) * 64], in1=dt_[:].unsqueeze(2).to_broadcast([128, nt, 64]), op=ALU.mult), reads=[b_X, b_dt], writes=[b_Xdt])
                def chunk(d, h, q0):
                    W = min(512, NT - q0)
                    cmax = (q0 + W) // 128 - 1
                    for c in range(cmax + 1):
                        s = cnt["s"] % 2; cnt["s"] += 1
                        P.op("pe", lambda e, s=s, c=c: e.matmul(sps[s][:, 0:W], lhsT=BT[:, c * 128:(c + 1) * 128], rhs=CT[:, q0:q0 + W], start=True, stop=True), reads=[b_BT, b_CT], writes=[b_sps[s]])
                        l = cnt["l"] % 2; cnt["l"] += 1
                        r = c - q0 // 128
                        if r >= 0:
                            P.op("pool", lambda e, l=l, r=r: e.tensor_tensor(out=Lb[l][:, 0:W], in0=bigB[:, q0:q0 + W], in1=masks[:, r, 0:W], op=ALU.add), reads=[b_bigB, b_c], writes=[b_L[l]])
                            P.op("act", lambda e, l=l, c=c: e.activation(out=Lb[l][:, 0:W], in_=Lb[l][:, 0:W], func=AF.Exp, bias=nacum[:, c:c + 1]), reads=[b_L[l], b_dt], writes=[b_L[l]])
                        else:
                            P.op("act", lambda e, l=l, c=c: e.activation(out=Lb[l][:, 0:W], in_=bigB[:, q0:q0 + W], func=AF.Exp, bias=nacum[:, c:c + 1]), reads=[b_bigB, b_dt], writes=[b_L[l]])
                        P.op("dve", lambda e, l=l, s=s: e.tensor_tensor(out=Mb[l][:, 0:W], in0=sps[s][:, 0:W], in1=Lb[l][:, 0:W], op=ALU.mult), reads=[b_sps[s], b_L[l]], writes=[b_M[l]])
                        P.op("pe", lambda e, l=l, c=c: e.matmul(ops_[:, 0:W], lhsT=Xdt[:, c, :], rhs=Mb[l][:, 0:W], start=(c == 0), stop=(c == cmax)), reads=[b_M[l], b_Xdt], writes=[b_ops])
                    nj = W // 128
                    P.op("act", lambda e: e.copy(out=osb[:, 0:W], in_=ops_[:, 0:W]), reads=[b_ops], writes=[b_osb])
                    for j in range(nj):
                        P.op("pe", lambda e, j=j: e.transpose(tps[:, j * 64:(j + 1) * 64], osb[:, j * 128:(j + 1) * 128], ident[0:64, 0:64]), reads=[b_osb, b_c], writes=[b_tps])
                    t = cnt["ot"] % 2; cnt["ot"] += 1
                    P.op("dve", lambda e, t=t: e.tensor_copy(out=ot[t][:, 0:nj, :], in_=tps[:, 0:nj * 64].rearrange("p (a n) -> p a n", n=64)), reads=[b_tps], writes=[b_ot[t]])
                    P.dma("sp", yo[d, q0:q0 + W, h * 64:(h + 1) * 64].rearrange("(j p) n -> p j n", p=128), ot[t][:, 0:nj, :], reads=[b_ot[t]], writes=[b_y])
                for q0 in range(0, NT, 512):
                    chunk(d, h, q0)
        P.finish("sp", [b_y])
        P.emit()
    return nc


def ka_consts():
    k = np.arange(128)
    U = (k[:, None] <= k[None, :]).astype(np.float32)
    SL = (k[:, None] < k[None, :]).astype(np.float32)
    masks = np.zeros((4, 128, 512), np.float32)
    i = np.arange(512)
    for r in range(4):
        masks[r] = np.where(i[None, :] >= r * 128 + k[:, None], 0.0, NEG)
    return {"U": U, "SL": SL, "ones": np.ones((128, 128), np.float32), "ident": np.eye(128, dtype=np.float32), "masks": masks}


def pad_seq(a, NC_, NL):
    z = np.zeros(a.shape[:-1] + (2,), a.dtype)
    return np.concatenate([z, a[..., :NC_], z, z, a[..., NC_:], z], -1)


def build_kb(n, NCH=128, GC=16):
    nc = bass.Bass("TRN2", target_bir_lowering=False)
    NA = 2 * n // 128
    NAD = n // 128
    N = 2 * n
    QC = 4
    din = lambda nm, s: nc.dram_tensor(nm, s, F32, kind="ExternalInput").ap()
    raw_d = din("raw", [3, NAD, NCH, 130])
    cw_d = din("cw", [3 * NCH * 4]); hb_d = din("hb", [2 * NCH])
    full_d = din("full", [2, NA, NCH, 128])
    F1_d = din("F1", [NA, 2 * NA]); TW_d = din("TW", [128, 2, NA]); TWI_d = din("TWI", [NA, 2, 128])
    CS_d = din("CS", [128, 256]); nSC_d = din("nSC", [128, 256]); C_d = din("C128", [128, 128]); S_d = din("S128", [128, 128]); nS_d = din("nS128", [128, 128])
    CI_d = din("CI", [NA, NAD]); nSI_d = din("nSI", [NA, NAD])
    yo = nc.dram_tensor("y", [NAD, NCH, 128], F32, kind="ExternalOutput").ap()
    with ExitStack() as st:
        P = Prog(nc, st)
        sb = lambda name, shape, dt=F32: st.enter_context(nc.sbuf_tensor("s_" + name, shape, dt))
        ps = lambda name, shape, dt=F32: st.enter_context(nc.psum_tensor("p_" + name, shape, dt))
        b_c = P.buf("c")
        F1 = sb("F1", [NA, 2 * NA]); TW = sb("TW", [128, 2, NA]); TWI = sb("TWI", [NA, 2, 128])
        CS = sb("CS", [128, 256]); nSC = sb("nSC", [128, 256]); C128 = sb("C128", [128, 128]); S128 = sb("S128", [128, 128]); nS128 = sb("nS128", [128, 128])
        CI = sb("CI", [NA, NAD]); nSI = sb("nSI", [NA, NAD])
        cw = sb("cw", [NAD, 3, NCH, 4]); hb = sb("hb", [NAD, 2, NCH])
        for t_, d_ in ((F1, F1_d), (TW, TW_d), (TWI, TWI_d), (CS, CS_d), (nSC, nSC_d), (C128, C_d), (S128, S_d), (nS128, nS_d), (CI, CI_d), (nSI, nSI_d)):
            P.dma("sp", t_[:], d_, writes=[b_c])
        P.dma("sp", cw[:].rearrange("p a b c -> p (a b c)"), cw_d.partition_broadcast(NAD), writes=[b_c])
        P.dma("sp", hb[:].rearrange("p a b -> p (a b)"), hb_d.partition_broadcast(NAD), writes=[b_c])
        raw = sb("raw", [NAD, 3, GC, 130]); b_raw = P.buf("raw")
        u = sb("u", [NAD, 3, GC, 128]); b_u = P.buf("u")
        tmpc = sb("tmpc", [NAD, GC, 128]); b_tmpc = P.buf("tmpc")
        fl = sb("fl", [NA, 2, GC, 128]); b_fl = P.buf("fl")
        H = sb("H", [128, 2, QC, NA]); b_H = P.buf("H")
        Yp = sb("Yp", [128, 2, QC, NA]); b_Yp = P.buf("Yp")
        Zs = sb("Zs", [128, 2, QC, NA]); b_Zs = P.buf("Zs")
        Vp = sb("Vp", [NA, 2, QC, 128]); b_Vp = P.buf("Vp")
        t1 = sb("t1", [128, QC, max(NA, 128)]); t2 = sb("t2", [128, QC, max(NA, 128)]); b_t1 = P.buf("t1"); b_t2 = P.buf("t2")
        z1 = sb("z1", [NAD, GC, 128]); b_z1 = P.buf("z1")
        og = sb("og", [NAD, GC, 128]); b_og = P.buf("og")
        yps = ps("yps", [128, QC, 2, NA]); b_yps = P.buf("yps")
        xps = ps("xps", [128, 2, QC, NA]); b_xps = P.buf("xps")
        vps = ps("vps", [NA, QC, 2, 128]); b_vps = P.buf("vps")
        ops_ = ps("ops", [NAD, QC, 128]); b_ops = P.buf("ops")
        b_y = P.buf("y")

        def cmul(eng_out_re, eng_out_im, are, aim, bre, bim, conj_b, pn, w, rbufs, obuf):
            sgn_im = -1.0 if conj_b else 1.0
            P.op("dve", lambda e: e.tensor_tensor(out=t1[0:pn, :, 0:w], in0=are, in1=bre, op=ALU.mult), reads=rbufs, writes=[b_t1])
            P.op("dve", lambda e: e.tensor_tensor(out=t2[0:pn, :, 0:w], in0=aim, in1=bim, op=ALU.mult), reads=rbufs, writes=[b_t2])
            P.op("dve", lambda e: e.tensor_tensor(out=eng_out_re, in0=t1[0:pn, :, 0:w], in1=t2[0:pn, :, 0:w], op=(ALU.add if conj_b else ALU.subtract)), reads=[b_t1, b_t2], writes=[obuf])
            P.op("dve", lambda e: e.tensor_tensor(out=t1[0:pn, :, 0:w], in0=aim, in1=bre, op=ALU.mult), reads=rbufs, writes=[b_t1])
            P.op("dve", lambda e: e.tensor_tensor(out=t2[0:pn, :, 0:w], in0=are, in1=bim, op=ALU.mult), reads=rbufs, writes=[b_t2])
            P.op("dve", lambda e: e.tensor_tensor(out=eng_out_im, in0=t1[0:pn, :, 0:w], in1=t2[0:pn, :, 0:w], op=(ALU.subtract if conj_b else ALU.add)), reads=[b_t1, b_t2], writes=[obuf])

        def fwd(src_fn, K, src_bufs):
            for c in range(QC):
                P.op("pe", lambda e, c=c: e.matmul(yps[:, c, :, :].rearrange("p a b -> p (a b)"), lhsT=src_fn(c), rhs=F1[0:K, :], start=True, stop=True), reads=src_bufs + [b_c], writes=[b_yps])
            twc = TW[:, 0, :].unsqueeze(1).to_broadcast([128, QC, NA]); tws = TW[:, 1, :].unsqueeze(1).to_broadcast([128, QC, NA])
            cmul(Yp[:, 0, :, :], Yp[:, 1, :, :], yps[:, :, 0, :], yps[:, :, 1, :], twc, tws, True, 128, NA, [b_yps, b_c], b_Yp)
            yre = Yp[:, 0, :, :].rearrange("p a b -> p (a b)"); yim = Yp[:, 1, :, :].rearrange("p a b -> p (a b)")
            xre = xps[:, 0, :, :].rearrange("p a b -> p (a b)"); xim = xps[:, 1, :, :].rearrange("p a b -> p (a b)")
            P.op("pe", lambda e: e.matmul(xre, lhsT=C128[:], rhs=yre, start=True, stop=False), reads=[b_Yp, b_c], writes=[b_xps])
            P.op("pe", lambda e: e.matmul(xre, lhsT=S128[:], rhs=yim, start=False, stop=True), reads=[b_Yp, b_c], writes=[b_xps])
            P.op("pe", lambda e: e.matmul(xim, lhsT=C128[:], rhs=yim, start=True, stop=False), reads=[b_Yp, b_c], writes=[b_xps])
            P.op("pe", lambda e: e.matmul(xim, lhsT=nS128[:], rhs=yre, start=False, stop=True), reads=[b_Yp, b_c], writes=[b_xps])

        def long_conv(zsrc_fn, zbufs, o, q0, gate_fn):
            fwd(lambda c: fl[:, o, q0 + c, :], NA, [b_fl])
            P.op("act", lambda e: e.copy(out=H[:], in_=xps[:]), reads=[b_xps], writes=[b_H])
            fwd(zsrc_fn, NAD, zbufs)
            cmul(Zs[:, 0, :, :], Zs[:, 1, :, :], xps[:, 0, :, :], xps[:, 1, :, :], H[:, 0, :, :], H[:, 1, :, :], False, 128, NA, [b_xps, b_H], b_Zs)
            for c in range(QC):
                vv = vps[:, c, :, :].rearrange("p a b -> p (a b)")
                P.op("pe", lambda e, c=c, vv=vv: e.matmul(vv, lhsT=Zs[:, 0, c, :], rhs=CS[:], start=True, stop=False), reads=[b_Zs, b_c], writes=[b_vps])
                P.op("pe", lambda e, c=c, vv=vv: e.matmul(vv, lhsT=Zs[:, 1, c, :], rhs=nSC[:], start=False, stop=True), reads=[b_Zs, b_c], writes=[b_vps])
            twc = TWI[:, 0, :].unsqueeze(1).to_broadcast([NA, QC, 128]); tws = TWI[:, 1, :].unsqueeze(1).to_broadcast([NA, QC, 128])
            cmul(Vp[:, 0, :, :], Vp[:, 1, :, :], vps[:, :, 0, :], vps[:, :, 1, :], twc, tws, False, NA, 128, [b_vps, b_c], b_Vp)
            oo = ops_[:].rearrange("p a b -> p (a b)")
            P.op("pe", lambda e: e.matmul(oo, lhsT=CI[:], rhs=Vp[:, 0, :, :].rearrange("p a b -> p (a b)"), start=True, stop=False), reads=[b_Vp, b_c], writes=[b_ops])
            P.op("pe", lambda e: e.matmul(oo, lhsT=nSI[:], rhs=Vp[:, 1, :, :].rearrange("p a b -> p (a b)"), start=False, stop=True), reads=[b_Vp, b_c], writes=[b_ops])
            gate_fn()

        def group(g0):
            for s in range(3):
                P.dma("sp", raw[:, s, :, :], raw_d[s, :, g0:g0 + GC, :], writes=[b_raw])
            P.dma("sp", fl[:], full_d[:, :, g0:g0 + GC, :].rearrange("o a c r -> a o c r"), writes=[b_fl])
            for s in range(3):
                wk = lambda k, s=s: cw[:, s, g0:g0 + GC, k:k + 1].to_broadcast([NAD, GC, 128])
                us = u[:, s, :, :]
                P.op("dve", lambda e, s=s, us=us, wk=wk: e.tensor_tensor(out=us, in0=raw[:, s, :, 0:128], in1=wk(0), op=ALU.mult), reads=[b_raw, b_c], writes=[b_u])
                for k in (1, 2):
                    P.op("pool", lambda e, s=s, k=k, wk=wk: e.tensor_tensor(out=tmpc[:], in0=raw[:, s, :, k:k + 128], in1=wk(k), op=ALU.mult), reads=[b_raw, b_c], writes=[b_tmpc])
                    P.op("dve", lambda e, us=us: e.tensor_tensor(out=us, in0=us, in1=tmpc[:], op=ALU.add), reads=[b_u, b_tmpc], writes=[b_u])
                P.op("dve", lambda e, us=us, wk=wk: e.tensor_tensor(out=us, in0=us, in1=wk(3), op=ALU.add), reads=[b_u, b_c], writes=[b_u])
            for q0 in range(0, GC, QC):
                def gate0(q0=q0):
                    bb = hb[:, 0, g0 + q0:g0 + q0 + QC].unsqueeze(2).to_broadcast([NAD, QC, 128])
                    zq = z1[:, q0:q0 + QC, :]
                    P.op("dve", lambda e: e.tensor_tensor(out=zq, in0=u[:, 0, q0:q0 + QC, :], in1=bb, op=ALU.mult), reads=[b_u, b_c], writes=[b_z1])
                    P.op("dve", lambda e: e.scalar_tensor_tensor(out=zq, in0=ops_[:], scalar=1.0 / N, in1=zq, op0=ALU.mult, op1=ALU.add), reads=[b_ops, b_z1], writes=[b_z1])
                    P.op("dve", lambda e: e.tensor_tensor(out=zq, in0=zq, in1=u[:, 1, q0:q0 + QC, :], op=ALU.mult), reads=[b_u, b_z1], writes=[b_z1])
                long_conv(lambda c, q0=q0: u[:, 0, q0 + c, :], [b_u], 0, q0, gate0)

                def gate1(q0=q0):
                    bb = hb[:, 1, g0 + q0:g0 + q0 + QC].unsqueeze(2).to_broadcast([NAD, QC, 128])
                    oq = og[:, q0:q0 + QC, :]
                    P.op("dve", lambda e: e.tensor_tensor(out=oq, in0=z1[:, q0:q0 + QC, :], in1=bb, op=ALU.mult), reads=[b_z1, b_c], writes=[b_og])
                    P.op("dve", lambda e: e.scalar_tensor_tensor(out=oq, in0=ops_[:], scalar=1.0 / N, in1=oq, op0=ALU.mult, op1=ALU.add), reads=[b_ops, b_og], writes=[b_og])
                    P.op("dve", lambda e: e.tensor_tensor(out=oq, in0=oq, in1=u[:, 2, q0:q0 + QC, :], op=ALU.mult), reads=[b_u, b_og], writes=[b_og])
                long_conv(lambda c, q0=q0: z1[:, q0 + c, :], [b_z1], 1, q0, gate1)
            P.dma("sp", yo[:, g0:g0 + GC, :], og[:], reads=[b_og], writes=[b_y])

        for g0 in range(0, NCH, GC):
            group(g0)
        P.finish("sp", [b_y])
        P.emit()
    return nc


def kb_consts(n):
    NA = 2 * n // 128; NAD = n // 128; N = 2 * n
    a = np.arange(NA)[:, None]; k1 = np.arange(NA)[None, :]
    ang = 2 * np.pi * a * k1 / NA
    F1 = np.concatenate([np.cos(ang), -np.sin(ang)], 1)
    r = np.arange(128)[:, None]
    th = 2 * np.pi * r * k1 / N
    TW = np.stack([np.cos(th), np.sin(th)], 1)
    TWI = np.stack([np.cos(th).T, np.sin(th).T], 1)
    p = np.arange(128)
    a128 = 2 * np.pi * p[:, None] * p[None, :] / 128
    C = np.cos(a128); S = np.sin(a128)
    angI = 2 * np.pi * np.arange(NA)[:, None] * np.arange(NAD)[None, :] / NA
    f = lambda x: np.ascontiguousarray(x.astype(np.float32))
    return {"F1": f(F1), "TW": f(TW), "TWI": f(TWI), "CS": f(np.concatenate([C, S], 1)), "nSC": f(np.concatenate([-S, C], 1)),
            "C128": f(C), "S128": f(S), "nS128": f(-S), "CI": f(np.cos(angI)), "nSI": f(-np.sin(angI))}


def kb_pack_raw(uraw, n):
    NAD = n // 128
    p = np.pad(uraw, ((0, 0), (0, 0), (1, 1)))
    idx = (np.arange(NAD)[:, None] * 128 + np.arange(130)[None, :])
    g = p[:, :, idx]
    return np.ascontiguousarray(g.transpose(0, 2, 1, 3))


def kb_pack_full(k, n):
    NA = 2 * n // 128
    kf, kb = k[:, 0], k[:, 1]
    full = np.concatenate([kf, np.zeros_like(kf[..., :1]), kb[..., :0:-1]], -1)
    return np.ascontiguousarray(full.reshape(2, -1, NA, 128).transpose(0, 2, 1, 3))


EPS = 1e-6
TWO_PI = 2.0 * math.pi


def build_kf1(n):
    nc = bass.Bass("TRN2", target_bir_lowering=False)
    din = lambda nm, s: nc.dram_tensor(nm, s, F32, kind="ExternalInput").ap()
    zT_d = din("zT", [33, n]); dec_d = din("decay", [128, n])
    w1_d = din("w1", [33, 64]); w2_d = din("w2", [64, 64]); w3_d = din("w3", [64, 128])
    fb1_d = din("fb1", [64, 2]); fb2_d = din("fb2", [64, 2]); b3_d = din("b3", [128, 1])
    pm_d = din("pm", [128, 128])
    ko = nc.dram_tensor("k", [128, n], F32, kind="ExternalOutput").ap()
    CW = min(512, n)
    nch = n // CW
    with ExitStack() as st:
        P = Prog(nc, st)
        sb = lambda name, shape, dt=F32: st.enter_context(nc.sbuf_tensor("s_" + name, shape, dt))
        ps = lambda name, shape, dt=F32: st.enter_context(nc.psum_tensor("p_" + name, shape, dt))
        b_c = P.buf("c")
        zT = sb("zT", [33, n]); dec = sb("dec", [128, n]); w1 = sb("w1", [33, 64]); w2 = sb("w2", [64, 64]); w3 = sb("w3", [64, 128])
        fb1 = sb("fb1", [64, 2]); fb2 = sb("fb2", [64, 2]); b3 = sb("b3", [128, 1]); pm = sb("pm", [128, 128]); epsb = sb("epsb", [128, 1])
        for t_, d_ in ((zT, zT_d), (dec, dec_d), (w1, w1_d), (w2, w2_d), (w3, w3_d), (fb1, fb1_d), (fb2, fb2_d), (b3, b3_d), (pm, pm_d)):
            P.dma("sp", t_[:], d_, writes=[b_c])
        for fb in (fb1, fb2):
            P.op("dve", lambda e, fb=fb: e.tensor_scalar(out=fb[:, 1:2], in0=fb[:, 1:2], scalar1=fb[:, 0:1], scalar2=16.0 * math.pi, op0=ALU.mult, op1=ALU.add), reads=[b_c], writes=[b_c])
        P.op("dve", lambda e: e.memset(epsb[:], EPS), reads=[b_c], writes=[b_c])
        kT = sb("kT", [128, n]); b_k = P.buf("kT")
        ssq = sb("ssq", [128, nch + 2]); b_ss = P.buf("ss")
        sq = sb("sq", [128, CW]); b_sq = P.buf("sq")
        h1 = [sb("h1_%d" % i, [64, CW]) for i in range(2)]; b_h1 = P.bufs(2, "h1")
        h2 = [sb("h2_%d" % i, [64, CW]) for i in range(2)]; b_h2 = P.bufs(2, "h2")
        pp = [ps("pp%d" % i, [128, CW]) for i in range(4)]; b_pp = P.bufs(4, "pp")
        pi = [0]

        I32 = mybir.dt.int32
        ki = sb("ki", [64, CW], I32); kf = sb("kf", [64, CW]); b_ki = P.buf("ki")

        def sin_layer(src_ps, fb, dst, b_src, b_dst, w):
            P.op("dve", lambda e: e.tensor_scalar(out=dst[:, 0:w], in0=src_ps[0:64, 0:w], scalar1=fb[:, 0:1], scalar2=fb[:, 1:2], op0=ALU.mult, op1=ALU.add), reads=[b_src, b_c], writes=[b_dst])
            P.op("dve", lambda e: e.tensor_scalar(out=ki[:, 0:w], in0=dst[:, 0:w], scalar1=1.0 / TWO_PI, scalar2=0.0, op0=ALU.mult, op1=ALU.add), reads=[b_dst], writes=[b_ki])
            P.op("dve", lambda e: e.tensor_copy(out=kf[:, 0:w], in_=ki[:, 0:w]), reads=[b_ki], writes=[b_ki])
            P.op("dve", lambda e: e.scalar_tensor_tensor(out=dst[:, 0:w], in0=kf[:, 0:w], scalar=-TWO_PI, in1=dst[:, 0:w], op0=ALU.mult, op1=ALU.add), reads=[b_ki, b_dst], writes=[b_dst])
            P.op("dve", lambda e: e.tensor_scalar(out=kf[:, 0:w], in0=dst[:, 0:w], scalar1=math.pi, scalar2=-TWO_PI, op0=ALU.is_gt, op1=ALU.mult), reads=[b_dst, b_ki], writes=[b_ki])
            P.op("dve", lambda e: e.tensor_tensor(out=dst[:, 0:w], in0=dst[:, 0:w], in1=kf[:, 0:w], op=ALU.add), reads=[b_ki, b_dst], writes=[b_dst])
            P.op("act", lambda e: e.activation(out=dst[:, 0:w], in_=dst[:, 0:w], func=AF.Sin), reads=[b_dst], writes=[b_dst])

        def chunk(j):
            c0 = j * CW
            s = j % 2
            a = pi[0] % 4; pi[0] += 1
            P.op("pe", lambda e: e.matmul(pp[a][0:64, :], lhsT=w1[:], rhs=zT[:, c0:c0 + CW], start=True, stop=True), reads=[b_c], writes=[b_pp[a]])
            sin_layer(pp[a], fb1, h1[s], b_pp[a], b_h1[s], CW)
            a2 = pi[0] % 4; pi[0] += 1
            P.op("pe", lambda e: e.matmul(pp[a2][0:64, :], lhsT=w2[:], rhs=h1[s][:], start=True, stop=True), reads=[b_c, b_h1[s]], writes=[b_pp[a2]])
            sin_layer(pp[a2], fb2, h2[s], b_pp[a2], b_h2[s], CW)
            a3 = pi[0] % 4; pi[0] += 1
            P.op("pe", lambda e: e.matmul(pp[a3][:, :], lhsT=w3[:], rhs=h2[s][:], start=True, stop=True), reads=[b_c, b_h2[s]], writes=[b_pp[a3]])
            P.op("dve", lambda e: e.scalar_tensor_tensor(out=kT[:, c0:c0 + CW], in0=pp[a3][:, :], scalar=b3[:, 0:1], in1=dec[:, c0:c0 + CW], op0=ALU.add, op1=ALU.mult), reads=[b_pp[a3], b_c], writes=[b_k])
            P.op("act", lambda e: e.activation(out=sq[:], in_=kT[:, c0:c0 + CW], func=AF.Square, accum_out=ssq[:, j:j + 1]), reads=[b_k], writes=[b_sq, b_ss])

        for j in range(nch):
            chunk(j)
        P.op("dve", lambda e: e.tensor_reduce(out=ssq[:, nch:nch + 1], in_=ssq[:, 0:nch], axis=AX.X, op=ALU.add), reads=[b_ss], writes=[b_ss])
        P.op("pe", lambda e: e.matmul(pp[0][:, 0:1], lhsT=pm[:], rhs=ssq[:, nch:nch + 1], start=True, stop=True), reads=[b_ss, b_c], writes=[b_pp[0]])
        P.op("act", lambda e: e.activation(out=ssq[:, nch + 1:nch + 2], in_=pp[0][:, 0:1], func=AF.Sqrt, bias=epsb[:, 0:1]), reads=[b_pp[0], b_c], writes=[b_ss])
        P.op("dve", lambda e: e.reciprocal(out=ssq[:, nch + 1:nch + 2], in_=ssq[:, nch + 1:nch + 2]), reads=[b_ss], writes=[b_ss])
        b_o = P.buf("o")
        for c0 in range(0, n, 2048):
            w = min(2048, n - c0)
            P.op("dve", lambda e, c0=c0, w=w: e.tensor_scalar(out=kT[:, c0:c0 + w], in0=kT[:, c0:c0 + w], scalar1=ssq[:, nch + 1:nch + 2], scalar2=0.0, op0=ALU.mult, op1=ALU.add), reads=[b_k, b_ss], writes=[b_k])
        P.dma("sp", ko, kT[:], reads=[b_k], writes=[b_o])
        P.finish("sp", [b_o])
        P.emit()
    return nc


def hy_tables(n):
    f32 = np.float32
    pos = np.arange(n, dtype=f32)
    t = np.linspace(0.0, 1.0, n, dtype=f32)
    f = np.linspace(1e-4, 15.0, 16, dtype=f32)
    ang = (f32(2.0 * math.pi) * pos[:, None] * f[None, :] / f32(n)).astype(f32)
    z = np.concatenate([t[:, None], np.cos(ang), -np.sin(ang)], -1).astype(f32)
    mx = math.log(1e-2) / 0.3; mn = math.log(1e-2) / 1.5
    deltas = np.abs(np.linspace(mn, mx, 256, dtype=f32))
    decay = np.exp(-t[:, None] * deltas[None, :]).astype(f32)
    return np.ascontiguousarray(z.T), np.ascontiguousarray(decay.T)


def kf1_inputs(n, hy_w1, hy_b1, hy_f1, hy_w2, hy_b2, hy_f2, hy_w3, hy_b3):
    zT, decT = hy_tables(n)
    k64 = np.arange(128)
    pm = (k64[:, None] % 64 == k64[None, :] % 64).astype(np.float32)
    maps = []
    for core in range(8):
        o, cr = core // 4, core % 4
        cols = np.concatenate([o * 512 + d * 256 + cr * 64 + np.arange(64) for d in range(2)])
        chs = np.concatenate([cr * 64 + np.arange(64)] * 2)
        maps.append({"zT": zT, "decay": np.ascontiguousarray(decT[chs]), "w1": np.ascontiguousarray(hy_w1), "w2": np.ascontiguousarray(hy_w2),
                     "w3": np.ascontiguousarray(hy_w3[:, cols]), "fb1": np.ascontiguousarray(np.stack([hy_f1, hy_b1], -1)),
                     "fb2": np.ascontiguousarray(np.stack([hy_f2, hy_b2], -1)), "b3": np.ascontiguousarray(hy_b3[cols][:, None]), "pm": pm})
    return maps


def kf1_gather(results, n):
    k = np.zeros((2, 2, 256, n), np.float32)
    for core in range(8):
        o, cr = core // 4, core % 4
        r = results[core]["k"]
        for d in range(2):
            k[o, d, cr * 64:(cr + 1) * 64] = r[d * 64:(d + 1) * 64]
    return k


D = 1024
DFF = 4096
EPS = 1e-6


class ModCalc:
    def __init__(self, P, nc, sb, ps):
        self.P, self.nc = P, nc
        self.c_sb = sb("mc_c", [128, 8]); self.c_sg = sb("mc_sg", [128, 8]); self.c_bc = sb("mc_bc", [128, 8, 128])
        self.bm = sb("mc_bm", [128, 512]); self.wst = sb("mc_w", [128, 8, 512])
        self.ps = ps("mc_ps", [128, 512])
        self.b = P.bufs(6, "mc")

    def set_c(self, cvec):
        P = self.P
        b_c, b_cs, b_bc = self.b[0:3]
        c_sb, c_sg, c_bc = self.c_sb, self.c_sg, self.c_bc
        P.dma("sp", c_sb[:], cvec.rearrange("(k p) -> p k", p=128), writes=[b_c], allow_slow_non_contiguous=True)
        P.op("act", lambda e: e.activation(out=c_sg[:], in_=c_sb[:], func=AF.Sigmoid), reads=[b_c], writes=[b_cs])
        P.op("dve", lambda e: e.tensor_tensor(out=c_sg[:], in0=c_sg[:], in1=c_sb[:], op=ALU.mult), reads=[b_c, b_cs], writes=[b_cs])
        P.op("dve", lambda e: e.tensor_copy(out=c_bc[:], in_=c_sg[:].unsqueeze(2).to_broadcast([128, 8, 128])), reads=[b_cs], writes=[b_bc])

    def calc(self, w_mod, b_mod, col0, ncols, out_ap_fn, out_buf):
        P = self.P
        b_bc, b_bm, b_w, b_ps = self.b[2:6]
        wv = w_mod.rearrange("(k p) n -> p k n", p=128)
        for j in range(ncols // 512):
            c = col0 + j * 512
            P.dma("sp", self.wst[:], wv[:, :, c:c + 512], writes=[b_w])
            P.dma("sp", self.bm[:], b_mod[c:c + 512].partition_broadcast(128), writes=[b_bm])
            for k in range(8):
                P.op("pe", lambda e, k=k: e.matmul(self.ps[:], lhsT=self.c_bc[:, k, :], rhs=self.wst[:, k, :], start=(k == 0), stop=(k == 7)),
                     reads=[b_bc, b_w], writes=[b_ps])
            P.op("dve", lambda e, j=j: e.tensor_tensor(out=out_ap_fn(j), in0=self.ps[:], in1=self.bm[:], op=ALU.add), reads=[b_ps, b_bm], writes=[out_buf])


def rstd_from_ss(P, ss_ap, b_ss, epsb, b_eps):
    P.op("act", lambda e: e.activation(out=ss_ap, in_=ss_ap, func=AF.Sqrt, bias=epsb[:, 0:1]), reads=[b_ss, b_eps], writes=[b_ss])
    P.op("dve", lambda e: e.reciprocal(out=ss_ap, in_=ss_ap), reads=[b_ss], writes=[b_ss])


def build_k3a(ntl, ntc):
    nc = bass.Bass("TRN2", target_bir_lowering=False)
    NT = ntl + ntc
    NTK = NT * 128
    din = lambda n, s: nc.dram_tensor(n, s, F32, kind="ExternalInput").ap()
    x = din("x", [NTK, D])
    yf = din("yf", [NTK, 256]); yb = din("yb", [NTK, 256]); xs = din("xs", [NTK, 256]); zz = din("z", [NTK, 256])
    mixT = din("mixT", [768, NTK])
    cv = din("cv", [D]); cctx = din("cctx", [D]); w_mod = din("w_mod", [D, 1024]); b_mod = din("b_mod", [1024])
    g_post = din("g_post", [D]); skip_d = din("skip", [256]); ssdn_d = din("ssdn", [256])
    w_out = din("w_out", [D, D]); ident_d = din("ident", [128, 128])
    xo = nc.dram_tensor("xo", [NTK, D], F32, kind="ExternalOutput").ap()
    with ExitStack() as st:
        P = Prog(nc, st)
        sb = lambda name, shape, dt=F32: st.enter_context(nc.sbuf_tensor("s_" + name, shape, dt))
        ps = lambda name, shape, dt=F32: st.enter_context(nc.psum_tensor("p_" + name, shape, dt))
        b_c = P.buf("c")
        ident_f = sb("ident_f", [128, 128]); ident = sb("identb", [128, 128], BF16)
        epsb = sb("epsb", [128, 1]); gp = sb("gp", [128, D]); skipb = sb("skipb", [128, 256]); ssdn = sb("ssdn", [128, 256])
        P.dma("sp", ident_f[:], ident_d, writes=[b_c])
        P.dma("sp", gp[:], g_post.partition_broadcast(128), writes=[b_c])
        P.dma("sp", skipb[:], skip_d.partition_broadcast(128), writes=[b_c])
        P.dma("sp", ssdn[:], ssdn_d.partition_broadcast(128), writes=[b_c])
        P.op("dve", lambda e: e.tensor_copy(out=ident[:], in_=ident_f[:]), reads=[b_c], writes=[b_c])
        P.op("dve", lambda e: e.memset(epsb[:], EPS), reads=[b_c], writes=[b_c])
        wob = sb("wob", [128, 8, D], BF16); b_wo = P.buf("wo")
        wst = [sb("wst%d" % i, [128, D]) for i in range(2)]; b_wst = P.bufs(2, "wst")
        wv = w_out.rearrange("(k p) n -> p k n", p=128)
        for k in range(8):
            s = k % 2
            P.dma("pool", wst[s][:], wv[:, k, :], writes=[b_wst[s]])
            P.op("pool", lambda e, k=k, s=s: e.tensor_copy(out=wob[:, k, :], in_=wst[s][:]), reads=[b_wst[s]], writes=[b_wo])
        MC = ModCalc(P, nc, sb, ps)
        G1 = sb("G1", [128, D]); b_G1 = P.buf("G1")
        xt = [sb("xt%d" % i, [128, D]) for i in range(2)]; b_xt = P.bufs(2, "xt")
        sq = sb("sq", [128, D]); b_sq = P.buf("sq")
        a4 = [[sb("a%d_%d" % (j, i), [128, 256]) for j in range(4)] for i in range(2)]; b_a4 = P.bufs(2, "a4")
        sg = sb("sg", [128, 256]); b_sg = P.buf("sg")
        ss = sb("ss", [128, 4]); b_ss = P.buf("ss")
        tnb = sb("tnb", [128, 256], BF16); b_tn = P.buf("tn")
        mst = [sb("mst%d" % i, [128, 6, 128]) for i in range(2)]; b_mst = P.bufs(2, "mst")
        mT = [sb("mT%d" % i, [128, 8, 128], BF16) for i in range(2)]; b_mT = P.bufs(2, "mT")
        tps = ps("tps", [128, 2, 128], BF16); b_tps = P.buf("tps")
        yps = [ps("yps%d" % i, [128, 2, 512]) for i in range(2)]; b_yps = P.bufs(2, "yps")
        tmp = sb("tmp", [128, D]); b_tmp = P.buf("tmp")
        ob = [sb("ob%d" % i, [128, D]) for i in range(2)]; b_ob = P.bufs(2, "ob")
        b_out = P.buf("out")
        for seg, (t0, t1, cvec) in enumerate(((0, ntl, cv), (ntl, NT, cctx))):
            MC.set_c(cvec)
            MC.calc(w_mod, b_mod, 0, 1024, lambda j: G1[:, j * 512:(j + 1) * 512], b_G1)
            P.op("dve", lambda e: e.tensor_tensor(out=G1[:], in0=G1[:], in1=gp[:], op=ALU.mult), reads=[b_G1, b_c], writes=[b_G1])
            for t in range(t0, t1):
                s = t % 2
                tok = slice(t * 128, (t + 1) * 128)
                P.dma("sp", xt[s][:], x[tok, :], writes=[b_xt[s]])
                for j, src in enumerate((xs, yf, yb, zz)):
                    P.dma("sp", a4[s][j][:], src[tok, :], writes=[b_a4[s]])
                P.dma("sp", mst[s][:], mixT[:, tok].rearrange("(k p) t -> p k t", p=128), writes=[b_mst[s]])
                A = a4[s]
                P.op("dve", lambda e, A=A: e.tensor_tensor(out=A[0][:], in0=A[0][:], in1=skipb[:], op=ALU.mult), reads=[b_a4[s], b_c], writes=[b_a4[s]])
                P.op("dve", lambda e, A=A: e.tensor_tensor(out=A[0][:], in0=A[0][:], in1=A[1][:], op=ALU.add), reads=[b_a4[s]], writes=[b_a4[s]])
                P.op("dve", lambda e, A=A: e.tensor_tensor(out=A[0][:], in0=A[0][:], in1=A[2][:], op=ALU.add), reads=[b_a4[s]], writes=[b_a4[s]])
                P.op("act", lambda e, A=A: e.activation(out=sg[:], in_=A[3][:], func=AF.Sigmoid), reads=[b_a4[s]], writes=[b_sg])
                P.op("dve", lambda e, A=A: e.tensor_tensor(out=sg[:], in0=sg[:], in1=A[3][:], op=ALU.mult), reads=[b_a4[s], b_sg], writes=[b_sg])
                P.op("dve", lambda e, A=A: e.tensor_tensor(out=A[0][:], in0=A[0][:], in1=sg[:], op=ALU.mult), reads=[b_a4[s], b_sg], writes=[b_a4[s]])
                P.op("act", lambda e, A=A: e.activation(out=sq[:, 0:256], in_=A[0][:], func=AF.Square, scale=1.0 / 16, accum_out=ss[:, 0:1]), reads=[b_a4[s]], writes=[b_sq, b_ss])
                rstd_from_ss(P, ss[:, 0:1], b_ss, epsb, b_c)
                P.op("dve", lambda e, A=A: e.scalar_tensor_tensor(out=tnb[:], in0=A[0][:], scalar=ss[:, 0:1], in1=ssdn[:], op0=ALU.mult, op1=ALU.mult), reads=[b_a4[s], b_ss, b_c], writes=[b_tn])
                for k in range(2):
                    P.op("pe", lambda e, k=k: e.transpose(tps[:, k, :], tnb[:, k * 128:(k + 1) * 128], ident[:]), reads=[b_tn, b_c], writes=[b_tps])
                P.op("act", lambda e, s=s: e.copy(out=mT[s][:, 0:2, :], in_=tps[:]), reads=[b_tps], writes=[b_mT[s]])
                P.op("pool", lambda e, s=s: e.tensor_copy(out=mT[s][:, 2:8, :], in_=mst[s][:]), reads=[b_mst[s]], writes=[b_mT[s]])
                for c in range(2):
                    for k in range(8):
                        P.op("pe", lambda e, k=k, c=c, s=s: e.matmul(yps[s][:, c, :], lhsT=mT[s][:, k, :], rhs=wob[:, k, c * 512:(c + 1) * 512], start=(k == 0), stop=(k == 7)),
                             reads=[b_mT[s], b_wo], writes=[b_yps[s]])
                for c in range(2):
                    P.op("act", lambda e, c=c, s=s: e.activation(out=sq[:, c * 512:(c + 1) * 512], in_=yps[s][:, c, :], func=AF.Square, scale=1.0 / 32, accum_out=ss[:, 1 + c:2 + c]), reads=[b_yps[s]], writes=[b_sq, b_ss])
                P.op("dve", lambda e: e.tensor_tensor(out=ss[:, 3:4], in0=ss[:, 1:2], in1=ss[:, 2:3], op=ALU.add), reads=[b_ss], writes=[b_ss])
                rstd_from_ss(P, ss[:, 3:4], b_ss, epsb, b_c)
                for c in range(2):
                    P.op("dve", lambda e, c=c, s=s: e.scalar_tensor_tensor(out=tmp[:, c * 512:(c + 1) * 512], in0=yps[s][:, c, :], scalar=ss[:, 3:4], in1=G1[:, c * 512:(c + 1) * 512], op0=ALU.mult, op1=ALU.mult),
                         reads=[b_yps[s], b_ss, b_G1], writes=[b_tmp])
                P.op("pool", lambda e, s=s: e.tensor_tensor(out=ob[s][:], in0=tmp[:], in1=xt[s][:], op=ALU.add), reads=[b_tmp, b_xt[s]], writes=[b_ob[s]])
                P.dma("sp", xo[tok, :], ob[s][:], reads=[b_ob[s]], writes=[b_out])
        P.finish("sp", [b_out])
        P.emit()
    return nc


def build_k3b(ntl, ntc):
    nc = bass.Bass("TRN2", target_bir_lowering=False)
    NT = ntl + ntc
    NTK = NT * 128
    din = lambda n, s: nc.dram_tensor(n, s, F32, kind="ExternalInput").ap()
    x = din("x", [NTK, D])
    cv = din("cv", [D]); cctx = din("cctx", [D]); w_mod = din("w_mod", [D, 3072]); b_mod = din("b_mod", [3072])
    g_pre = din("g_pre", [D]); g_post = din("g_post", [D])
    w1 = din("w1", [D, DFF]); w2 = din("w2", [DFF, D]); ident_d = din("ident", [128, 128])
    xo = nc.dram_tensor("xo", [NTK, D], F32, kind="ExternalOutput").ap()
    G = 1
    with ExitStack() as st:
        P = Prog(nc, st)
        sb = lambda name, shape, dt=F32: st.enter_context(nc.sbuf_tensor("s_" + name, shape, dt))
        ps = lambda name, shape, dt=F32: st.enter_context(nc.psum_tensor("p_" + name, shape, dt))
        b_c = P.buf("c")
        ident_f = sb("ident_f", [128, 128]); ident = sb("identb", [128, 128], BF16)
        epsb = sb("epsb", [128, 1]); gpre = sb("gpre", [128, D]); gpost = sb("gpost", [128, D])
        P.dma("sp", ident_f[:], ident_d, writes=[b_c])
        P.dma("sp", gpre[:], g_pre.partition_broadcast(128), writes=[b_c])
        P.dma("sp", gpost[:], g_post.partition_broadcast(128), writes=[b_c])
        P.op("dve", lambda e: e.tensor_copy(out=ident[:], in_=ident_f[:]), reads=[b_c], writes=[b_c])
        P.op("dve", lambda e: e.memset(epsb[:], EPS), reads=[b_c], writes=[b_c])
        w1b = sb("w1b", [128, 8, DFF], BF16); w2b = sb("w2b", [128, 32, D], BF16); b_w1 = P.buf("w1"); b_w2 = P.buf("w2")
        MC = ModCalc(P, nc, sb, ps)
        wflat = MC.wst[:].rearrange("p a n -> p (a n)")
        wst = [wflat[:, 0:2048], wflat[:, 2048:4096]]; b_wst = [MC.b[4], MC.b[4]]
        w1v = w1.rearrange("(k p) n -> p k n", p=128); w2v = w2.rearrange("(k p) n -> p k n", p=128)
        i = 0
        for k in range(8):
            for hh in range(2):
                s = i % 2; i += 1
                P.dma("pool", wst[s], w1v[:, k, hh * 2048:(hh + 1) * 2048], writes=[b_wst[s]])
                P.op("pool", lambda e, k=k, hh=hh, s=s: e.tensor_copy(out=w1b[:, k, hh * 2048:(hh + 1) * 2048], in_=wst[s]), reads=[b_wst[s]], writes=[b_w1])
        for k in range(0, 32, 2):
            s = i % 2; i += 1
            P.dma("pool", wst[s].rearrange("p (a n) -> p a n", a=2), w2v[:, k:k + 2, :], writes=[b_wst[s]])
            P.op("pool", lambda e, k=k, s=s: e.tensor_copy(out=w2b[:, k:k + 2, :], in_=wst[s].rearrange("p (a n) -> p a n", a=2)), reads=[b_wst[s]], writes=[b_w2])
        M3 = sb("M3", [128, 3072]); b_M3 = P.buf("M3")
        xt = [sb("xt%d" % i, [128, D]) for i in range(G)]; b_xt = P.bufs(G, "xt")
        sq = sb("sq", [128, D]); b_sq = P.buf("sq")
        ss = sb("ss", [128, 4]); b_ss = P.buf("ss")
        hb = sb("hb", [128, D], BF16); b_hb = P.buf("hb")
        tmp = sb("tmp", [128, D]); b_tmp = P.buf("tmp")
        hT = [sb("hT%d" % i, [128, 8, G * 128], BF16) for i in range(2)]; b_hT = P.bufs(2, "hT")
        uT = sb("uT", [128, 32, G * 128], BF16); b_uT = P.buf("uT")
        ur = [sb("ur%d" % i, [128, G * 128]) for i in range(2)]; b_ur = P.bufs(2, "ur")
        tps = ps("tps", [128, 8, 128], BF16); b_tps = P.buf("tps")
        ups = [ps("ups%d" % i, [128, G * 128]) for i in range(2)]; b_ups = P.bufs(2, "ups")
        yps = [ps("yps%d" % i, [128, 2, 512]) for i in range(2)]; b_yps = P.bufs(2, "yps")
        ob1 = sb("ob1", [128, D]); ob = [ob1, ob1]; bo1 = P.buf("ob"); b_ob = [bo1, bo1]
        b_out = P.buf("out")
        gi = 0
        for seg, (t0, t1, cvec) in enumerate(((0, ntl, cv), (ntl, NT, cctx))):
            MC.set_c(cvec)
            MC.calc(w_mod, b_mod, 0, 3072, lambda j: M3[:, j * 512:(j + 1) * 512], b_M3)
            P.op("dve", lambda e: e.scalar_tensor_tensor(out=M3[:, 1024:2048], in0=M3[:, 1024:2048], scalar=1.0, in1=gpre[:], op0=ALU.add, op1=ALU.mult), reads=[b_M3, b_c], writes=[b_M3])
            P.op("dve", lambda e: e.tensor_tensor(out=M3[:, 2048:3072], in0=M3[:, 2048:3072], in1=gpost[:], op=ALU.mult), reads=[b_M3, b_c], writes=[b_M3])
            for g0 in range(t0, t1, G):
                tiles = list(range(g0, min(g0 + G, t1)))
                ng = len(tiles); W = ng * 128
                hs = gi % 2; gi += 1
                xs_ = []
                for j, t in enumerate(tiles):
                    s = j
                    xs_.append(s)
                    tok = slice(t * 128, (t + 1) * 128)
                    P.dma("sp", xt[s][:], x[tok, :], writes=[b_xt[s]])
                    P.op("act", lambda e, s=s: e.activation(out=sq[:], in_=xt[s][:], func=AF.Square, scale=1.0 / 32, accum_out=ss[:, 0:1]), reads=[b_xt[s]], writes=[b_sq, b_ss])
                    rstd_from_ss(P, ss[:, 0:1], b_ss, epsb, b_c)
                    P.op("dve", lambda e, s=s: e.scalar_tensor_tensor(out=tmp[:], in0=xt[s][:], scalar=ss[:, 0:1], in1=M3[:, 1024:2048], op0=ALU.mult, op1=ALU.mult), reads=[b_xt[s], b_ss, b_M3], writes=[b_tmp])
                    P.op("dve", lambda e: e.tensor_tensor(out=hb[:], in0=tmp[:], in1=M3[:, 0:1024], op=ALU.add), reads=[b_tmp, b_M3], writes=[b_hb])
                    for k in range(8):
                        P.op("pe", lambda e, k=k: e.transpose(tps[:, k, :], hb[:, k * 128:(k + 1) * 128], ident[:]), reads=[b_hb, b_c], writes=[b_tps])
                    P.op("act", lambda e, j=j, hs=hs: e.copy(out=hT[hs][:, :, j * 128:(j + 1) * 128], in_=tps[:]), reads=[b_tps], writes=[b_hT[hs]])
                for f in range(32):
                    u = f % 2
                    for k in range(8):
                        P.op("pe", lambda e, k=k, f=f, u=u, hs=hs, W=W: e.matmul(ups[u][:, 0:W], lhsT=w1b[:, k, f * 128:(f + 1) * 128], rhs=hT[hs][:, k, 0:W], start=(k == 0), stop=(k == 7)),
                             reads=[b_hT[hs], b_w1], writes=[b_ups[u]])
                    P.op("act", lambda e, u=u, W=W: e.activation(out=ur[u][:, 0:W], in_=ups[u][:, 0:W], func=AF.Relu), reads=[b_ups[u]], writes=[b_ur[u]])
                    P.op("dve", lambda e, u=u, f=f, W=W: e.tensor_tensor(out=uT[:, f, 0:W], in0=ur[u][:, 0:W], in1=ur[u][:, 0:W], op=ALU.mult), reads=[b_ur[u]], writes=[b_uT])
                for j, t in enumerate(tiles):
                    s = xs_[j]
                    y = j % 2
                    tok = slice(t * 128, (t + 1) * 128)
                    for c in range(2):
                        for f in range(32):
                            P.op("pe", lambda e, f=f, c=c, y=y, j=j: e.matmul(yps[y][:, c, :], lhsT=uT[:, f, j * 128:(j + 1) * 128], rhs=w2b[:, f, c * 512:(c + 1) * 512], start=(f == 0), stop=(f == 31)),
                                 reads=[b_uT, b_w2], writes=[b_yps[y]])
                    for c in range(2):
                        P.op("act", lambda e, c=c, y=y: e.activation(out=sq[:, c * 512:(c + 1) * 512], in_=yps[y][:, c, :], func=AF.Square, scale=1.0 / 32, accum_out=ss[:, 1 + c:2 + c]), reads=[b_yps[y]], writes=[b_sq, b_ss])
                    P.op("dve", lambda e: e.tensor_tensor(out=ss[:, 3:4], in0=ss[:, 1:2], in1=ss[:, 2:3], op=ALU.add), reads=[b_ss], writes=[b_ss])
                    rstd_from_ss(P, ss[:, 3:4], b_ss, epsb, b_c)
                    for c in range(2):
                        P.op("dve", lambda e, c=c, y=y: e.scalar_tensor_tensor(out=tmp[:, c * 512:(c + 1) * 512], in0=yps[y][:, c, :], scalar=ss[:, 3:4], in1=M3[:, 2048 + c * 512:2048 + (c + 1) * 512], op0=ALU.mult, op1=ALU.mult),
                             reads=[b_yps[y], b_ss, b_M3], writes=[b_tmp])
                    P.op("pool", lambda e, s=s, y=y: e.tensor_tensor(out=ob[y][:], in0=tmp[:], in1=xt[s][:], op=ALU.add), reads=[b_tmp, b_xt[s]], writes=[b_ob[y]])
                    P.dma("sp", xo[tok, :], ob[y][:], reads=[b_ob[y]], writes=[b_out])
        P.finish("sp", [b_out])
        P.emit()
    return nc


BATCH, SEQ, CTX = 4, 8192, 256
OFF_B, OFF_C, OFF_D = 776, 1544, 2056
_cache = {}
_nl = [0]


def _prog(key, fn):
    if key not in _cache:
        _cache[key] = fn()
    return _cache[key]


def _run(nc, maps, tag):
    t0 = time.time()
    res = run_bass_kernel_spmd(nc, maps, core_ids=list(range(8)))
    _nl[0] += 1
    print("[launch %d] %s %.1fs" % (_nl[0], tag, time.time() - t0), flush=True)
    return res.results


def C_(a):
    return np.ascontiguousarray(a, dtype=np.float32)


def launch_k1(x, ctx, c, c_ctx, w_mod_i, b_mod_i, g_pre_i, w_in_i):
    ntl, ntc = SEQ // 2 // 128, CTX // 2 // 128
    nc = _prog("k1", lambda: build_k1(ntl, ntc))
    ident = np.eye(128, dtype=np.float32)
    maps = []
    for k in range(8):
        b, hf = k // 2, k % 2
        xs = np.concatenate([x[b, hf * SEQ // 2:(hf + 1) * SEQ // 2], ctx[b, hf * CTX // 2:(hf + 1) * CTX // 2]], 0)
        maps.append({"x": C_(xs), "cv": C_(c[b]), "cctx": C_(c_ctx), "w_mod": C_(w_mod_i[:, 0:2048]), "b_mod": C_(b_mod_i[0:2048]),
                     "g_pre": C_(g_pre_i), "w_in": C_(w_in_i), "ident": ident})
    res = _run(nc, maps, "k1")
    proj = np.zeros((BATCH, SEQ, 2568), np.float32)
    projc = np.zeros((BATCH, CTX, 2568), np.float32)
    for k in range(8):
        b, hf = k // 2, k % 2
        o = res[k]["proj"]
        proj[b, hf * SEQ // 2:(hf + 1) * SEQ // 2] = o[:SEQ // 2]
        projc[b, hf * CTX // 2:(hf + 1) * CTX // 2] = o[SEQ // 2:]
    return proj, projc


def launch_kf1(n, w1, b1, f1, w2, b2, f2, w3, b3):
    nc = _prog(("kf1", n), lambda: build_kf1(n))
    res = _run(nc, kf1_inputs(n, C_(w1), C_(b1), C_(f1), C_(w2), C_(b2), C_(f2), C_(w3), C_(b3)), "kf1_%d" % n)
    return kf1_gather(res, n)


def launch_ka(proj, projc, conv_w, conv_b, a_log, dt_bias):
    nc = _prog("ka", lambda: build_ka(CTX, SEQ))
    cst = ka_consts()
    NT = CTX + SEQ
    maps = []
    for k in range(8):
        b, g = k // 2, k % 2
        d = dict(cst)
        seq = [np.concatenate([projc[b], proj[b]], 0), np.concatenate([projc[b][::-1], proj[b][::-1]], 0)]
        xc = slice(256 + g * 128, 256 + (g + 1) * 128); bc = slice(512 + g * 64, 512 + (g + 1) * 64); cc = slice(640 + g * 64, 640 + (g + 1) * 64)
        d["xr"] = C_(np.stack([pad_seq(s[:, xc].T, CTX, SEQ) for s in seq]))
        d["br"] = C_(np.stack([pad_seq(s[:, bc].T, CTX, SEQ) for s in seq]))
        d["cr"] = C_(np.stack([pad_seq(s[:, cc].T, CTX, SEQ) for s in seq]))
        def cwpack(idx):
            w = conv_w[:, idx].T
            bb = conv_b[idx][:, None]
            return C_(np.stack([np.concatenate([w, bb], 1), np.concatenate([w[:, ::-1], bb], 1)]))
        d["cwx"] = cwpack(np.arange(g * 128, (g + 1) * 128))
        d["cwb"] = cwpack(256 + np.arange(g * 64, (g + 1) * 64))
        d["cwc"] = cwpack(384 + np.arange(g * 64, (g + 1) * 64))
        d["dtr"] = C_(np.stack([np.stack([seq[dd][:, 768 + dd * 4 + 2 * g + h] for h in range(2)]) for dd in range(2)]))
        d["alog"] = C_(a_log[:, 2 * g:2 * g + 2]); d["dtb"] = C_(dt_bias[:, 2 * g:2 * g + 2])
        maps.append(d)
    res = _run(nc, maps, "ka")
    mk = lambda: (np.zeros((BATCH, SEQ, 256), np.float32), np.zeros((BATCH, CTX, 256), np.float32))
    yf, yb, xs = mk(), mk(), mk()
    for k in range(8):
        b, g = k // 2, k % 2
        y = res[k]["y"]; x_ = res[k]["xs"]
        cs = slice(g * 128, (g + 1) * 128)
        yf[1][b][:, cs] = y[0, :CTX]; yf[0][b][:, cs] = y[0, CTX:]
        yb[1][b][:, cs] = y[1, :CTX][::-1]; yb[0][b][:, cs] = y[1, CTX:][::-1]
        xs[1][b][:, cs] = x_[:CTX]; xs[0][b][:, cs] = x_[CTX:]
    return yf, yb, xs


def launch_kc(proj, projc, qn, kn, sink):
    nc = _prog("kc", lambda: build_kc(SEQ, CTX))
    cst = _prog("kc_consts", lambda: kc_consts(SEQ))
    maps = []
    for k in range(8):
        b, g = k // 2, k % 2
        d = dict(cst)
        full = np.concatenate([proj[b], projc[b]], 0)
        for br, off in (("w", OFF_C), ("d", OFF_D)):
            d["qT_" + br] = C_(np.stack([full[:, off + (2 * g + h) * 64: off + (2 * g + h + 1) * 64].T for h in range(2)]))
            d["kT_" + br] = C_(full[:, off + 256 + g * 64: off + 256 + (g + 1) * 64].T)
            d["v_" + br] = C_(full[:, off + 384 + g * 64: off + 384 + (g + 1) * 64])
        d["qn"] = C_(qn); d["kn"] = C_(kn); d["sink"] = C_(sink[2 * g:2 * g + 2])
        maps.append(d)
    res = _run(nc, maps, "kc")
    out = {}
    for br in "dw":
        lat = np.zeros((BATCH, SEQ, 256), np.float32); cx = np.zeros((BATCH, CTX, 256), np.float32)
        for k in range(8):
            b, g = k // 2, k % 2
            y = res[k]["y_" + br]
            lat[b][:, g * 128:(g + 1) * 128] = y[:SEQ]; cx[b][:, g * 128:(g + 1) * 128] = y[SEQ:]
        out[br] = (lat, cx)
    return out


def launch_kb(pr, n, kfilt, conv_w, conv_b, hbias):
    nc = _prog(("kb", n), lambda: build_kb(n))
    cst = _prog(("kb_consts", n), lambda: kb_consts(n))
    maps = []
    for k in range(8):
        b, g = k // 2, k % 2
        d = dict(cst)
        chs = [OFF_B + s * 256 + g * 128 + np.arange(128) for s in range(3)]
        uraw = np.stack([pr[b][:, ch].T for ch in chs])
        d["raw"] = kb_pack_raw(C_(uraw), n)
        d["full"] = kb_pack_full(C_(kfilt[:, :, g * 128:(g + 1) * 128]), n)
        cw = np.stack([np.concatenate([conv_w[:, s * 256 + g * 128: s * 256 + (g + 1) * 128].T, conv_b[s * 256 + g * 128: s * 256 + (g + 1) * 128][:, None]], 1) for s in range(3)])
        d["cw"] = C_(cw.reshape(-1)); d["hb"] = C_(hbias[:, g * 128:(g + 1) * 128].reshape(-1))
        maps.append(d)
    res = _run(nc, maps, "kb_%d" % n)
    out = np.zeros((BATCH, 256, n), np.float32)
    for k in range(8):
        b, g = k // 2, k % 2
        y = res[k]["y"]
        out[b, g * 128:(g + 1) * 128] = y.transpose(1, 0, 2).reshape(128, n)
    return out


def _tok_shard(lat, cx, k):
    b, hf = k // 2, k % 2
    return np.concatenate([lat[b, hf * SEQ // 2:(hf + 1) * SEQ // 2], cx[b, hf * CTX // 2:(hf + 1) * CTX // 2]], 0)


def _tok_gather(res, name):
    lat = np.zeros((BATCH, SEQ, 1024), np.float32); cx = np.zeros((BATCH, CTX, 1024), np.float32)
    for k in range(8):
        b, hf = k // 2, k % 2
        o = res[k][name]
        lat[b, hf * SEQ // 2:(hf + 1) * SEQ // 2] = o[:SEQ // 2]; cx[b, hf * CTX // 2:(hf + 1) * CTX // 2] = o[SEQ // 2:]
    return lat, cx


def launch_k3a(x, ctx, yf, yb, xs, z, hy, hyc, yw, yd, c, c_ctx, w_mod_i, b_mod_i, g_post, skip, ssdn, w_out_i):
    ntl, ntc = SEQ // 2 // 128, CTX // 2 // 128
    nc = _prog("k3a", lambda: build_k3a(ntl, ntc))
    ident = np.eye(128, dtype=np.float32)
    maps = []
    for k in range(8):
        b, hf = k // 2, k % 2
        ls = slice(hf * SEQ // 2, (hf + 1) * SEQ // 2); cs = slice(hf * CTX // 2, (hf + 1) * CTX // 2)
        mixT = np.concatenate([
            np.concatenate([hy[b][:, ls], hyc[b][:, cs]], 1),
            np.concatenate([yw[0][b, ls], yw[1][b, cs]], 0).T,
            np.concatenate([yd[0][b, ls], yd[1][b, cs]], 0).T], 0)
        maps.append({"x": C_(_tok_shard(x, ctx, k)), "yf": C_(_tok_shard(yf[0], yf[1], k)), "yb": C_(_tok_shard(yb[0], yb[1], k)),
                     "xs": C_(_tok_shard(xs[0], xs[1], k)), "z": C_(_tok_shard(z[0], z[1], k)), "mixT": C_(mixT),
                     "cv": C_(c[b]), "cctx": C_(c_ctx), "w_mod": C_(w_mod_i[:, 2048:3072]), "b_mod": C_(b_mod_i[2048:3072]),
                     "g_post": C_(g_post), "skip": C_(skip), "ssdn": C_(ssdn), "w_out": C_(w_out_i), "ident": ident})
    res = _run(nc, maps, "k3a")
    return _tok_gather(res, "xo")


def launch_k3b(x, ctx, c, c_ctx, w_mod_i, b_mod_i, g_pre, g_post, w1, w2):
    ntl, ntc = SEQ // 2 // 128, CTX // 2 // 128
    nc = _prog("k3b", lambda: build_k3b(ntl, ntc))
    ident = np.eye(128, dtype=np.float32)
    maps = []
    for k in range(8):
        b = k // 2
        maps.append({"x": C_(_tok_shard(x, ctx, k)), "cv": C_(c[b]), "cctx": C_(c_ctx), "w_mod": C_(w_mod_i[:, 3072:6144]), "b_mod": C_(b_mod_i[3072:6144]),
                     "g_pre": C_(g_pre), "g_post": C_(g_post), "w1": C_(w1), "w2": C_(w2), "ident": ident})
    res = _run(nc, maps, "k3b")
    return _tok_gather(res, "xo")


def forward(x, c, ctx, c_ctx, w_mod, b_mod, norm_mix_pre, norm_mix_post, norm_mlp_pre, norm_mlp_post,
            w_in, w_out, ssd_conv_w, ssd_conv_b, ssd_a_log, ssd_dt_bias, ssd_d, ssd_norm,
            hy_conv_w, hy_conv_b, hy_w1, hy_b1, hy_freq1, hy_w2, hy_b2, hy_freq2, hy_w3, hy_b3, hy_bias,
            attn_sink, q_norm, k_norm, mlp_w1, mlp_w2, depth=2, dbg=None):
    A = lambda a: np.asarray(a, dtype=np.float32)
    x = A(x); ctx = A(ctx); c = A(c); c_ctx = A(c_ctx)
    for i in range(depth):
        need_ctx = i < depth - 1
        proj, projc = launch_k1(x, ctx, c, c_ctx, A(w_mod[i]), A(b_mod[i]), A(norm_mix_pre[i]), A(w_in[i]))
        hyf = (A(hy_w1[i]), A(hy_b1[i]), A(hy_freq1[i]), A(hy_w2[i]), A(hy_b2[i]), A(hy_freq2[i]), A(hy_w3[i]), A(hy_b3[i]))
        k_lat = launch_kf1(SEQ, *hyf)
        hy = launch_kb(proj, SEQ, k_lat, A(hy_conv_w[i]), A(hy_conv_b[i]), A(hy_bias[i]))
        if need_ctx:
            k_ctx = launch_kf1(CTX, *hyf)
            hyc = launch_kb(projc, CTX, k_ctx, A(hy_conv_w[i]), A(hy_conv_b[i]), A(hy_bias[i]))
        else:
            hyc = np.zeros((BATCH, 256, CTX), np.float32)
        yf, yb, xs = launch_ka(proj, projc, A(ssd_conv_w[i]), A(ssd_conv_b[i]), A(ssd_a_log[i]), A(ssd_dt_bias[i]))
        att = launch_kc(proj, projc, A(q_norm[i]), A(k_norm[i]), A(attn_sink[i]))
        z = (proj[:, :, 0:256], projc[:, :, 0:256])
        if dbg is not None:
            dbg.update({"proj": proj, "projc": projc, "hy": hy, "hyc": hyc, "yf": yf, "yb": yb, "xs": xs, "att": att})
        x1, ctx1 = launch_k3a(x, ctx, yf, yb, xs, z, hy, hyc, att["w"], att["d"], c, c_ctx, A(w_mod[i]), A(b_mod[i]),
                              A(norm_mix_post[i]), np.repeat(A(ssd_d[i]), 64), A(ssd_norm[i]), A(w_out[i]))
        x2, ctx2 = launch_k3b(x1, ctx1, c, c_ctx, A(w_mod[i]), A(b_mod[i]), A(norm_mlp_pre[i]), A(norm_mlp_post[i]), A(mlp_w1[i]), A(mlp_w2[i]))
        if dbg is not None:
            dbg.update({"x1": x1, "ctx1": ctx1, "x2": x2, "ctx2": ctx2})
        x = x2
        if need_ctx:
            ctx = ctx2
    return x


def kernel(**inputs):
    _nl[0] = 0
    out = forward(**inputs, depth=2)
    return np.ascontiguousarray(out, dtype=np.float32)
```

```python
import os
import sys
import time
import math
from contextlib import ExitStack
import numpy as np
import concourse.bass as bass
import concourse.mybir as mybir
from concourse.bass_utils import run_bass_kernel_spmd


F32 = mybir.dt.float32
BF16 = mybir.dt.bfloat16
ALU = mybir.AluOpType
AF = mybir.ActivationFunctionType
AX = mybir.AxisListType


class Buf:
    __slots__ = ("name", "w", "r")

    def __init__(self, name):
        self.name = name
        self.w = None
        self.r = []


class Prog:
    ENG = ("pe", "act", "dve", "pool", "sp")
    NDMA = 8

    def __init__(self, nc, stack):
        self.nc = nc
        self.ops = {e: [] for e in self.ENG}
        self.sems = []
        self.semval = []
        self.known = {e: {} for e in self.ENG}
        self.esem = {}
        for e in ("pe", "act", "dve", "pool"):
            self.esem[e] = self._newsem(stack, "s_" + e)
        self.dsem = {}
        self.dcnt = {}
        for e in ("sp", "act", "pool"):
            self.dsem[e] = [self._newsem(stack, "d_%s%d" % (e, i)) for i in range(self.NDMA)]
            self.dcnt[e] = 0
        self.nbuf = 0

    def _newsem(self, stack, name):
        h = stack.enter_context(self.nc.semaphore(name))
        self.sems.append(h)
        self.semval.append(0)
        return len(self.sems) - 1

    def buf(self, name=None):
        self.nbuf += 1
        return Buf(name or "b%d" % self.nbuf)

    def bufs(self, n, name="b"):
        return [self.buf("%s%d" % (name, i)) for i in range(n)]

    def _deps(self, eng, reads, writes):
        need = {}
        def add(d):
            if d is None:
                return
            s, v = d
            if need.get(s, 0) < v:
                need[s] = v
        for b in reads:
            add(b.w)
        for b in writes:
            add(b.w)
            for d in b.r:
                add(d)
        kn = self.known[eng]
        waits = []
        own = self.esem.get(eng)
        for s, v in need.items():
            if s == own and v > self.semval[s]:
                continue
            if kn.get(s, 0) < v:
                kn[s] = v
                waits.append((s, v))
        return waits

    def op(self, eng, fn, reads=(), writes=(), inc=True):
        waits = self._deps(eng, reads, writes)
        s = self.esem[eng]
        if inc:
            self.semval[s] += 1
            tok = (s, self.semval[s])
            self.ops[eng].append((fn, waits, (s, 1)))
        else:
            tok = (s, self.semval[s] + 1)
            self.ops[eng].append((fn, waits, None))
        for b in reads:
            b.r.append(tok)
        for b in writes:
            b.w = tok
            b.r = []
        return tok

    def dma(self, q, out, in_, reads=(), writes=(), **kw):
        waits = self._deps(q, reads, writes)
        i = self.dcnt[q]
        self.dcnt[q] += 1
        s = self.dsem[q][i % self.NDMA]
        prev = self.semval[s]
        if prev > 0 and self.known[q].get(s, 0) < prev:
            self.known[q][s] = prev
            waits.append((s, prev))
        self.semval[s] += 16
        tok = (s, self.semval[s])
        self.ops[q].append((lambda e: e.dma_start(out=out, in_=in_, **kw), waits, (s, 16)))
        for b in reads:
            b.r.append(tok)
        for b in writes:
            b.w = tok
            b.r = []
        return tok

    def finish(self, eng="sp", bufs=()):
        waits = self._deps(eng, bufs, ())
        for q in self.dsem:
            for s in self.dsem[q]:
                v = self.semval[s]
                if v > 0 and self.known[eng].get(s, 0) < v:
                    self.known[eng][s] = v
                    waits.append((s, v))
        self.ops[eng].append((None, waits, None))

    def emit(self):
        nc = self.nc
        hmap = {"pe": "tensor", "act": "scalar", "dve": "vector", "pool": "gpsimd", "sp": "sync"}
        with nc.Block() as block:
            for e in self.ENG:
                ops = self.ops[e]
                sems = self.sems

                def body(h, ops=ops):
                    for fn, waits, inc in ops:
                        for s, v in waits:
                            h.wait_ge(sems[s], v)
                        if fn is not None:
                            ins = fn(h)
                            if inc is not None:
                                ins.then_inc(sems[inc[0]], inc[1])
                getattr(block, hmap[e])(body)


D = 1024
DIN = 2568
EPS = 1e-6


def load_bcast_row(P, q, dst_ap, src_row_ap, n, wbuf):
    P.dma(q, dst_ap, src_row_ap.partition_broadcast(128), writes=[wbuf])


def mod_scratch(P, nc, st, ncols):
    S = {}
    S["c_sb"] = st.enter_context(nc.sbuf_tensor("c_sb", [128, 8], F32))
    S["c_sg"] = st.enter_context(nc.sbuf_tensor("c_sg", [128, 8], F32))
    S["c_bc"] = st.enter_context(nc.sbuf_tensor("c_bc", [128, 8, 128], F32))
    S["bm"] = st.enter_context(nc.sbuf_tensor("bm", [128, ncols], F32))
    S["wst"] = [st.enter_context(nc.sbuf_tensor("wmst%d" % i, [128, 8, 512], F32)) for i in range(2)]
    S["ps"] = [st.enter_context(nc.psum_tensor("modps%d" % i, [128, 512], F32)) for i in range(2)]
    S["b"] = P.bufs(4, "modb")
    S["b_w"] = P.bufs(2, "wmst")
    S["b_ps"] = P.bufs(2, "modps")
    return S


def build_mod(P, nc, S, cvec, w_mod, b_mod, col0, ncols, out_tile, out_buf):
    c_sb, c_sg, c_bc, bm, wst, ps = S["c_sb"], S["c_sg"], S["c_bc"], S["bm"], S["wst"], S["ps"]
    b_c, b_cs, b_bc, b_bm = S["b"]
    b_w, b_ps = S["b_w"], S["b_ps"]
    P.dma("sp", c_sb[:], cvec.rearrange("(k p) -> p k", p=128), writes=[b_c], allow_slow_non_contiguous=True)
    P.dma("sp", bm[:, 0:ncols], b_mod[col0:col0 + ncols].partition_broadcast(128), writes=[b_bm])
    P.op("act", lambda e: e.activation(out=c_sg[:], in_=c_sb[:], func=AF.Sigmoid), reads=[b_c], writes=[b_cs])
    P.op("dve", lambda e: e.tensor_tensor(out=c_sg[:], in0=c_sg[:], in1=c_sb[:], op=ALU.mult), reads=[b_c, b_cs], writes=[b_cs])
    P.op("dve", lambda e: e.tensor_copy(out=c_bc[:], in_=c_sg[:].unsqueeze(2).to_broadcast([128, 8, 128])), reads=[b_cs], writes=[b_bc])
    wv = w_mod.rearrange("(k p) n -> p k n", p=128)
    for j in range(ncols // 512):
        s = j % 2
        P.dma("sp", wst[s][:], wv[:, :, col0 + j * 512: col0 + (j + 1) * 512], writes=[b_w[s]])
        for k in range(8):
            P.op("pe", lambda e, k=k, s=s: e.matmul(ps[s][:], lhsT=c_bc[:, k, :], rhs=wst[s][:, k, :], start=(k == 0), stop=(k == 7)),
                 reads=[b_bc, b_w[s]], writes=[b_ps[s]])
        P.op("dve", lambda e, j=j, s=s: e.tensor_tensor(out=out_tile[:, j * 512:(j + 1) * 512], in0=ps[s][:], in1=bm[:, j * 512:(j + 1) * 512], op=ALU.add),
             reads=[b_ps[s], b_bm], writes=[out_buf])


def build_k1(ntiles_lat, ntiles_ctx):
    nc = bass.Bass("TRN2", target_bir_lowering=False)
    NT = ntiles_lat + ntiles_ctx
    x = nc.dram_tensor("x", [NT * 128, D], F32, kind="ExternalInput").ap()
    cv = nc.dram_tensor("cv", [D], F32, kind="ExternalInput").ap()
    cctx = nc.dram_tensor("cctx", [D], F32, kind="ExternalInput").ap()
    w_mod = nc.dram_tensor("w_mod", [D, 2048], F32, kind="ExternalInput").ap()
    b_mod = nc.dram_tensor("b_mod", [2048], F32, kind="ExternalInput").ap()
    g_pre = nc.dram_tensor("g_pre", [D], F32, kind="ExternalInput").ap()
    w_in = nc.dram_tensor("w_in", [D, DIN], F32, kind="ExternalInput").ap()
    proj = nc.dram_tensor("proj", [NT * 128, DIN], F32, kind="ExternalOutput").ap()
    ident_d = nc.dram_tensor("ident", [128, 128], F32, kind="ExternalInput").ap()
    with ExitStack() as st:
        P = Prog(nc, st)
        sb = lambda name, shape, dt=F32: st.enter_context(nc.sbuf_tensor(name, shape, dt))
        ident_f = sb("ident_f", [128, 128])
        ident = sb("ident_b", [128, 128], BF16)
        b_id = P.buf("ident")
        P.dma("sp", ident_f[:], ident_d, writes=[b_id])
        P.op("dve", lambda e: e.tensor_copy(out=ident[:], in_=ident_f[:]), reads=[b_id], writes=[b_id])
        modl = sb("modl", [128, 2048]); b_modl = P.buf("modl")
        modc = sb("modc", [128, 2048]); b_modc = P.buf("modc")
        if True:
            st2 = st
            MS = mod_scratch(P, nc, st, 2048)
            build_mod(P, nc, MS, cv, w_mod, b_mod, 0, 2048, modl, b_modl)
            build_mod(P, nc, MS, cctx, w_mod, b_mod, 0, 2048, modc, b_modc)
            gp = sb("gp", [128, D]); b_gp = P.buf("gp")
            P.dma("sp", gp[:], g_pre.partition_broadcast(128), writes=[b_gp])
            for m, bm_ in ((modl, b_modl), (modc, b_modc)):
                P.op("dve", lambda e, m=m: e.scalar_tensor_tensor(out=m[:, 1024:2048], in0=m[:, 1024:2048], scalar=1.0, in1=gp[:], op0=ALU.add, op1=ALU.mult),
                     reads=[bm_, b_gp], writes=[bm_])
            w_bf = sb("w_bf", [128, 8, DIN], BF16); b_wbf = P.buf("wbf")
            wst = [st2.enter_context(nc.sbuf_tensor("wst%d" % i, [128, DIN], F32)) for i in range(2)]
            b_wst = P.bufs(2, "wst")
            wv = w_in.rearrange("(k p) n -> p k n", p=128)
            for k in range(8):
                s = k % 2
                P.dma("pool", wst[s][:], wv[:, k, :], writes=[b_wst[s]])
                P.op("pool", lambda e, k=k, s=s: e.tensor_copy(out=w_bf[:, k, :], in_=wst[s][:]), reads=[b_wst[s]], writes=[b_wbf])
        epsb = sb("epsb", [128, 1])
        b_eps = P.buf("eps")
        P.op("dve", lambda e: e.memset(epsb[:], EPS), writes=[b_eps])
        NB = 2
        xt = [sb("xt%d" % i, [128, D]) for i in range(NB)]; b_xt = P.bufs(NB, "xt")
        sq = sb("sq", [128, D]); b_sq = P.buf("sq")
        ss = [sb("ss%d" % i, [128, 1]) for i in range(NB)]; b_ss = P.bufs(NB, "ss")
        hb = [sb("hb%d" % i, [128, D], BF16) for i in range(NB)]; b_hb = P.bufs(NB, "hb")
        hT = [sb("hT%d" % i, [128, 8, 128], BF16) for i in range(NB)]; b_hT = P.bufs(NB, "hT")
        tps = [st.enter_context(nc.psum_tensor("tps%d" % i, [128, 8, 128], BF16)) for i in range(2)]; b_tps = P.bufs(2, "tps")
        ops_ = [st.enter_context(nc.psum_tensor("ops%d" % i, [128, 512], F32)) for i in range(4)]; b_ops = P.bufs(4, "ops")
        ob = [sb("ob%d" % i, [128, DIN]) for i in range(NB)]; b_ob = P.bufs(NB, "ob")
        b_out = P.buf("out")
        colch = [(c0, min(512, DIN - c0)) for c0 in range(0, DIN, 512)]
        pi = 0
        for t in range(NT):
            s = t % NB
            mod = modl if t < ntiles_lat else modc
            bmod = b_modl if t < ntiles_lat else b_modc
            P.dma("sp", xt[s][:], x[t * 128:(t + 1) * 128, :], writes=[b_xt[s]])
            P.op("act", lambda e, s=s: e.activation(out=sq[:], in_=xt[s][:], func=AF.Square, scale=float(D ** -0.5), accum_out=ss[s][:]),
                 reads=[b_xt[s]], writes=[b_sq, b_ss[s]])
            P.op("act", lambda e, s=s: e.activation(out=ss[s][:], in_=ss[s][:], func=AF.Sqrt, bias=epsb[:, 0:1]),
                 reads=[b_ss[s], b_eps], writes=[b_ss[s]])
            P.op("dve", lambda e, s=s: e.reciprocal(out=ss[s][:], in_=ss[s][:]),
                 reads=[b_ss[s]], writes=[b_ss[s]])
            P.op("dve", lambda e, s=s, mod=mod: e.scalar_tensor_tensor(out=xt[s][:], in0=xt[s][:], scalar=ss[s][:, 0:1], in1=mod[:, 1024:2048], op0=ALU.mult, op1=ALU.mult),
                 reads=[b_xt[s], b_ss[s], bmod], writes=[b_xt[s]])
            P.op("dve", lambda e, s=s, mod=mod: e.tensor_tensor(out=hb[s][:], in0=xt[s][:], in1=mod[:, 0:1024], op=ALU.add),
                 reads=[b_xt[s], bmod], writes=[b_hb[s]])
            tp = t % 2
            for k in range(8):
                P.op("pe", lambda e, k=k, s=s, tp=tp: e.transpose(tps[tp][:, k, :], hb[s][:, k * 128:(k + 1) * 128], ident[:]),
                     reads=[b_hb[s], b_id], writes=[b_tps[tp]], inc=(k == 7))
            P.op("act", lambda e, s=s, tp=tp: e.copy(out=hT[s][:], in_=tps[tp][:]), reads=[b_tps[tp]], writes=[b_hT[s]])
            for (c0, cw) in colch:
                p = pi % 4; pi += 1
                for k in range(8):
                    P.op("pe", lambda e, k=k, s=s, p=p, c0=c0, cw=cw: e.matmul(ops_[p][:, 0:cw], lhsT=hT[s][:, k, :], rhs=w_bf[:, k, c0:c0 + cw], start=(k == 0), stop=(k == 7)),
                         reads=[b_hT[s], b_wbf], writes=[b_ops[p]], inc=(k == 7))
                eng = "act" if (pi % 2) else "dve"
                if eng == "act":
                    P.op("act", lambda e, s=s, p=p, c0=c0, cw=cw: e.copy(out=ob[s][:, c0:c0 + cw], in_=ops_[p][:, 0:cw]), reads=[b_ops[p]], writes=[b_ob[s]])
                else:
                    P.op("dve", lambda e, s=s, p=p, c0=c0, cw=cw: e.tensor_copy(out=ob[s][:, c0:c0 + cw], in_=ops_[p][:, 0:cw]), reads=[b_ops[p]], writes=[b_ob[s]])
            P.dma("sp", proj[t * 128:(t + 1) * 128, :], ob[s][:], reads=[b_ob[s]], writes=[b_out])
        P.finish("sp", [b_out])
        P.emit()
    return nc


EPS = 1e-6
HD = 64


def build_kc(NL, NC):
    nc = bass.Bass("TRN2", target_bir_lowering=False)
    NT = NL + NC
    nlt, nct, nkt = NL // 128, NC // 128, NT // 128
    din = lambda n, s: nc.dram_tensor(n, s, F32, kind="ExternalInput").ap()
    qT = {b: din("qT_" + b, [2, 64, NT]) for b in "dw"}
    kT = {b: din("kT_" + b, [64, NT]) for b in "dw"}
    vv = {b: din("v_" + b, [NT, 64]) for b in "dw"}
    cos_d = din("cos2", [64, NL]); sin_d = din("sin2", [64, NL])
    Rm_d = din("Rm", [64, 64]); ones_d = din("ones64", [64, 64]); ident_d = din("ident", [128, 128])
    mprev_d = din("mprev", [128, 128]); mnext_d = din("mnext", [128, 128])
    qn_d = din("qn", [64]); kn_d = din("kn", [64]); sink_d = din("sink", [2])
    y = {b: nc.dram_tensor("y_" + b, [NT, 128], F32, kind="ExternalOutput").ap() for b in "dw"}
    with ExitStack() as st:
        P = Prog(nc, st)
        sb = lambda name, shape, dt=F32: st.enter_context(nc.sbuf_tensor("s_" + name, shape, dt))
        ps = lambda name, shape, dt=F32: st.enter_context(nc.psum_tensor("p_" + name, shape, dt))
        b_c = P.buf("consts")
        cos2 = sb("cos2", [64, NL]); sin2 = sb("sin2", [64, NL])
        Rm = sb("Rm", [64, 64]); ones64 = sb("ones64", [64, 64]); ident = sb("ident", [128, 128])
        mpn_f = sb("mpn_f", [128, 2, 128]); mpn = sb("mpn", [128, 2, 128], BF16)
        gq = sb("gq", [64, 1]); gk = sb("gk", [64, 1]); sk = sb("sk", [128, 2]); epsb = sb("epsb", [128, 1])
        P.dma("sp", cos2[:], cos_d, writes=[b_c]); P.dma("sp", sin2[:], sin_d, writes=[b_c])
        P.dma("sp", Rm[:], Rm_d, writes=[b_c]); P.dma("sp", ones64[:], ones_d, writes=[b_c]); P.dma("sp", ident[:], ident_d, writes=[b_c])
        P.dma("sp", mpn_f[:, 0, :], mprev_d, writes=[b_c]); P.dma("sp", mpn_f[:, 1, :], mnext_d, writes=[b_c])
        P.dma("sp", gq[:], qn_d.rearrange("(p o) -> p o", o=1), writes=[b_c]); P.dma("sp", gk[:], kn_d.rearrange("(p o) -> p o", o=1), writes=[b_c])
        P.dma("sp", sk[:], sink_d.partition_broadcast(128), writes=[b_c])
        P.op("dve", lambda e: e.tensor_copy(out=mpn[:], in_=mpn_f[:]), reads=[b_c], writes=[b_c])
        P.op("dve", lambda e: e.memset(epsb[:], EPS), reads=[b_c], writes=[b_c])
        P.op("act", lambda e: e.activation(out=sk[:], in_=sk[:], func=AF.Exp), reads=[b_c], writes=[b_c])
        QT = sb("QT", [64, 2, NT], BF16); b_QT = P.buf("QT")
        KT = sb("KT", [64, NT], BF16); b_KT = P.buf("KT")
        VA = sb("VA", [128, nkt, 65], BF16); b_VA = P.buf("VA")
        vst = sb("vst", [128, nkt, 64]); b_vst = P.buf("vst")
        stg = [sb("stg%d" % i, [64, 512]) for i in range(2)]; b_stg = P.bufs(2, "stg")
        sqb = sb("sqb", [64, 512]); b_sq = P.buf("sqb")
        rsb = sb("rsb", [64, 512]); b_rs = P.buf("rsb")
        t1b = sb("t1b", [64, 512]); b_t1 = P.buf("t1b")
        t2b = sb("t2b", [64, 512]); b_t2 = P.buf("t2b")
        pps = [ps("pps%d" % i, [64, 512]) for i in range(2)]; b_pps = P.bufs(2, "pps")
        sps = [ps("sps%d" % i, [128, 512]) for i in range(3)]; b_sps = P.bufs(3, "sps")
        ops_ = [ps("ops%d" % i, [128, 512]) for i in range(2)]; b_ops = P.bufs(2, "ops")
        tps = ps("tps", [128, 4, 128]); b_tps = P.buf("tps")
        ptb = [sb("ptb%d" % i, [128, 512], BF16) for i in range(3)]; b_pt = P.bufs(3, "ptb")
        osb = sb("osb", [65, 512]); b_osb = P.buf("osb")
        rd = sb("rd", [128, 4, 1]); b_rd = P.buf("rd")
        otb = [sb("otb%d" % i, [128, 4, 64]) for i in range(2)]; b_ot = P.bufs(2, "otb")
        b_y = P.buf("y")
        cnt = {"stg": 0, "s": 0, "o": 0, "ot": 0, "pp": 0}

        def prep(src_ap, dst_ap, ncols, col0, norm_gain, rope):
            for c0 in range(0, ncols, 512):
                cw = min(512, ncols - c0)
                s = cnt["stg"] % 2; cnt["stg"] += 1
                P.dma("sp", stg[s][:, 0:cw], src_ap[:, c0:c0 + cw], writes=[b_stg[s]])
                cur = stg[s]; bcur = b_stg[s]
                if norm_gain is not None:
                    pp = cnt["pp"] % 2; cnt["pp"] += 1
                    P.op("act", lambda e, s=s, cw=cw: e.activation(out=sqb[:, 0:cw], in_=stg[s][:, 0:cw], func=AF.Square), reads=[b_stg[s]], writes=[b_sq])
                    P.op("pe", lambda e, pp=pp, cw=cw: e.matmul(pps[pp][:, 0:cw], lhsT=ones64[:], rhs=sqb[:, 0:cw], start=True, stop=True), reads=[b_sq, b_c], writes=[b_pps[pp]])
                    P.op("act", lambda e, pp=pp, cw=cw: e.activation(out=rsb[:, 0:cw], in_=pps[pp][:, 0:cw], func=AF.Sqrt, bias=epsb[0:64, 0:1], scale=1.0 / 64), reads=[b_pps[pp], b_c], writes=[b_rs])
                    P.op("dve", lambda e, cw=cw: e.reciprocal(out=rsb[:, 0:cw], in_=rsb[:, 0:cw]), reads=[b_rs], writes=[b_rs])
                    P.op("dve", lambda e, s=s, cw=cw, g=norm_gain: e.scalar_tensor_tensor(out=stg[s][:, 0:cw], in0=stg[s][:, 0:cw], scalar=g[:, 0:1], in1=rsb[:, 0:cw], op0=ALU.mult, op1=ALU.mult),
                         reads=[b_stg[s], b_rs, b_c], writes=[b_stg[s]])
                if rope:
                    pp = cnt["pp"] % 2; cnt["pp"] += 1
                    P.op("pe", lambda e, pp=pp, s=s, cw=cw: e.matmul(pps[pp][:, 0:cw], lhsT=Rm[:], rhs=stg[s][:, 0:cw], start=True, stop=True), reads=[b_stg[s], b_c], writes=[b_pps[pp]])
                    P.op("pool", lambda e, s=s, cw=cw, c0=c0: e.tensor_tensor(out=t1b[:, 0:cw], in0=stg[s][:, 0:cw], in1=cos2[:, col0 + c0:col0 + c0 + cw], op=ALU.mult), reads=[b_stg[s], b_c], writes=[b_t1])
                    P.op("dve", lambda e, pp=pp, cw=cw, c0=c0: e.tensor_tensor(out=t2b[:, 0:cw], in0=pps[pp][:, 0:cw], in1=sin2[:, col0 + c0:col0 + c0 + cw], op=ALU.mult), reads=[b_pps[pp], b_c], writes=[b_t2])
                    P.op("dve", lambda e, cw=cw, c0=c0: e.tensor_tensor(out=dst_ap[:, c0:c0 + cw], in0=t1b[:, 0:cw], in1=t2b[:, 0:cw], op=ALU.add), reads=[b_t1, b_t2], writes=[dst_buf[0]])
                else:
                    P.op("dve", lambda e, s=s, cw=cw, c0=c0: e.tensor_copy(out=dst_ap[:, c0:c0 + cw], in_=stg[s][:, 0:cw]), reads=[b_stg[s]], writes=[dst_buf[0]])

        dst_buf = [None]

        def attn_group(qcols, N, ktiles, sink_ap, out_cb):
            o = cnt["o"] % 2; cnt["o"] += 1
            nk = len(ktiles)
            def s_mm(i):
                kt = ktiles[i][0]
                s = (base + i) % 3
                P.op("pe", lambda e, s=s, kt=kt: e.matmul(sps[s][:, 0:N], lhsT=kt, rhs=qcols, start=True, stop=True), reads=[b_KT, b_QT], writes=[b_sps[s]])
            base = cnt["s"]; cnt["s"] += nk
            s_mm(0)
            for i, (kt, va, mask) in enumerate(ktiles):
                s = (base + i) % 3
                if i + 1 < nk:
                    s_mm(i + 1)
                P.op("act", lambda e, s=s: e.activation(out=ptb[s][:, 0:N], in_=sps[s][:, 0:N], func=AF.Exp, scale=0.125), reads=[b_sps[s]], writes=[b_pt[s]])
                if mask is not None:
                    P.op("dve", lambda e, s=s, mask=mask: e.tensor_tensor(out=ptb[s][:, 0:N], in0=ptb[s][:, 0:N], in1=mask, op=ALU.mult), reads=[b_pt[s], b_c], writes=[b_pt[s]])
                P.op("pe", lambda e, s=s, va=va, i=i: e.matmul(ops_[o][0:65, 0:N], lhsT=va, rhs=ptb[s][:, 0:N], start=(i == 0), stop=(i == nk - 1)), reads=[b_pt[s], b_VA], writes=[b_ops[o]], inc=(i == nk - 1))
            nj = N // 128
            P.op("act", lambda e: e.copy(out=osb[:, 0:N], in_=ops_[o][0:65, 0:N]), reads=[b_ops[o]], writes=[b_osb])
            for j in range(nj):
                P.op("pe", lambda e, j=j: e.transpose(tps[:, j, 0:65], osb[0:65, j * 128:(j + 1) * 128], ident[0:65, 0:65]), reads=[b_osb, b_c], writes=[b_tps], inc=(j == nj - 1))
            if sink_ap is not None:
                P.op("dve", lambda e: e.tensor_tensor(out=rd[:, 0:nj, :], in0=tps[:, 0:nj, 64:65], in1=sink_ap, op=ALU.add), reads=[b_tps, b_c], writes=[b_rd])
                P.op("dve", lambda e: e.reciprocal(out=rd[:, 0:nj, :], in_=rd[:, 0:nj, :]), reads=[b_rd], writes=[b_rd])
            else:
                P.op("dve", lambda e: e.reciprocal(out=rd[:, 0:nj, :], in_=tps[:, 0:nj, 64:65]), reads=[b_tps], writes=[b_rd])
            t = cnt["ot"] % 2; cnt["ot"] += 1
            P.op("dve", lambda e, t=t: e.tensor_tensor(out=otb[t][:, 0:nj, :], in0=tps[:, 0:nj, 0:64], in1=rd[:, 0:nj, :].to_broadcast([128, nj, 64]), op=ALU.mult), reads=[b_tps, b_rd], writes=[b_ot[t]])
            out_cb(otb[t], b_ot[t])

        STAGE = int(os.environ.get("STAGE", "9"))
        for br in "dw":
            dst_buf[0] = b_QT
            for h in range(2):
                prep(qT[br][h][:, 0:NL], QT[:, h, 0:NL], NL, 0, gq if br == "d" else None, True)
                prep(qT[br][h][:, NL:NT], QT[:, h, NL:NT], NC, 0, gq if br == "d" else None, False)
            dst_buf[0] = b_KT
            prep(kT[br][:, 0:NL], KT[:, 0:NL], NL, 0, gk if br == "d" else None, True)
            prep(kT[br][:, NL:NT], KT[:, NL:NT], NC, 0, gk if br == "d" else None, False)
            P.dma("sp", vst[:], vv[br].rearrange("(t p) d -> p t d", p=128), writes=[b_vst])
            P.op("pool", lambda e: e.memset(VA[:, :, 64:65], 1.0), writes=[b_VA])
            P.op("pool", lambda e: e.tensor_copy(out=VA[:, :, 0:64], in_=vst[:]), reads=[b_vst], writes=[b_VA])
            yb = y[br]
            ctx_tiles = [(KT[:, NL + c * 128: NL + (c + 1) * 128], VA[:, nlt + c, :], None) for c in range(nct)]
            if STAGE < 2:
                P.dma("sp", yb[0:64, 0:64], stg[0][:, 0:64], reads=[b_stg[0]], writes=[b_y])
                continue
            if br == "d" and STAGE != 3:
                all_tiles = [(KT[:, c * 128:(c + 1) * 128], VA[:, c, :], None) for c in range(nkt)]
                for h in range(2):
                    for q0 in range(0, NL, 512):
                        def cb(ot, bo, h=h, q0=q0):
                            P.dma("sp", yb[q0:q0 + 512, h * 64:(h + 1) * 64].rearrange("(j p) d -> p j d", p=128), ot[:, 0:4, :], reads=[bo], writes=[b_y])
                        attn_group(QT[:, h, q0:q0 + 512], 512, all_tiles, None, cb)
            elif br == "w" and STAGE >= 3:
                for n in range(nlt):
                    tiles = []
                    if n > 0:
                        tiles.append((KT[:, (n - 1) * 128:n * 128], VA[:, n - 1, :], mpn[:, 0, :]))
                    tiles.append((KT[:, n * 128:(n + 1) * 128], VA[:, n, :], None))
                    if n < nlt - 1:
                        tiles.append((KT[:, (n + 1) * 128:(n + 2) * 128], VA[:, n + 1, :], mpn[:, 1, :]))
                    tiles += ctx_tiles
                    for h in range(2):
                        def cb(ot, bo, n=n, h=h):
                            P.dma("sp", yb[n * 128:(n + 1) * 128, h * 64:(h + 1) * 64], ot[:, 0, :], reads=[bo], writes=[b_y])
                        attn_group(QT[:, h, n * 128:(n + 1) * 128], 128, tiles, sk[:, h:h + 1].unsqueeze(1), cb)
            for h in range(2 if STAGE >= 4 else 0):
                def cb(ot, bo, h=h):
                    P.dma("sp", yb[NL:NT, h * 64:(h + 1) * 64].rearrange("(j p) d -> p j d", p=128), ot[:, 0:nct, :], reads=[bo], writes=[b_y])
                snk = sk[:, h:h + 1].unsqueeze(1).to_broadcast([128, nct, 1]) if br == "w" else None
                attn_group(QT[:, h, NL:NT], NC, ctx_tiles, snk, cb)
        P.finish("sp", [b_y])
        P.emit()
    return nc


def rope_tables_np(NL, GW=64):
    t = np.arange(NL)
    row = (t // GW).astype(np.float32); col = (t % GW).astype(np.float32)
    inv = (10000.0 ** (-np.arange(16, dtype=np.float32) / 16)).astype(np.float32)
    ang = np.concatenate([row[:, None] * inv, col[:, None] * inv], -1)
    return np.cos(ang).astype(np.float32), np.sin(ang).astype(np.float32)


def kc_consts(NL):
    cos, sin = rope_tables_np(NL)
    cos2 = np.ascontiguousarray(np.concatenate([cos, cos], -1).T)
    sin2 = np.ascontiguousarray(np.concatenate([sin, sin], -1).T)
    Rm = np.zeros((64, 64), np.float32)
    for m in range(32):
        Rm[m + 32, m] = -1.0
        Rm[m, m + 32] = 1.0
    kl = np.arange(128)[:, None]; ql = np.arange(128)[None, :]
    return {"cos2": cos2, "sin2": sin2, "Rm": Rm, "ones64": np.ones((64, 64), np.float32), "ident": np.eye(128, dtype=np.float32),
            "mprev": (kl >= ql).astype(np.float32), "mnext": (kl <= ql).astype(np.float32)}


NEG = -30000.0


def build_ka(NC_, NL):
    nc = bass.Bass("TRN2", target_bir_lowering=False)
    NT = NC_ + NL
    nt = NT // 128
    NP = NT + 8
    din = lambda n, s: nc.dram_tensor(n, s, F32, kind="ExternalInput").ap()
    xr = din("xr", [2, 128, NP]); br = din("br", [2, 64, NP]); cr = din("cr", [2, 64, NP])
    cwx = din("cwx", [2, 128, 6]); cwb = din("cwb", [2, 64, 6]); cwc = din("cwc", [2, 64, 6])
    dtr = din("dtr", [2, 2, NT])
    alog = din("alog", [2, 2]); dtb = din("dtb", [2, 2])
    U_d = din("U", [128, 128]); ones_d = din("ones", [128, 128]); SL_d = din("SL", [128, 128]); ident_d = din("ident", [128, 128])
    mask_d = din("masks", [4, 128, 512])
    yo = nc.dram_tensor("y", [2, NT, 128], F32, kind="ExternalOutput").ap()
    xso = nc.dram_tensor("xs", [NT, 128], F32, kind="ExternalOutput").ap()
    scr = nc.dram_tensor("scr", [4, NT], F32, kind="ExternalOutput").ap()
    segs = [(0, NC_), (NC_, NL)]
    with ExitStack() as st:
        P = Prog(nc, st)
        sb = lambda name, shape, dt=F32: st.enter_context(nc.sbuf_tensor("s_" + name, shape, dt))
        ps = lambda name, shape, dt=F32: st.enter_context(nc.psum_tensor("p_" + name, shape, dt))
        b_c = P.buf("c")
        U = sb("U", [128, 128]); ones = sb("ones", [128, 128]); SL = sb("SL", [128, 128]); ident = sb("ident", [128, 128])
        masks = sb("masks", [128, 4, 512])
        for t_, d_ in ((U, U_d), (ones, ones_d), (SL, SL_d), (ident, ident_d)):
            P.dma("sp", t_[:], d_, writes=[b_c])
        P.dma("sp", masks[:], mask_d.rearrange("r p n -> p r n"), writes=[b_c])
        bigA = sb("bigA", [128, NP]); b_bigA = P.buf("bigA")
        bigB = sb("bigB", [128, NT]); b_bigB = P.buf("bigB")
        sgm = sb("sgm", [128, 2048]); b_sgm = P.buf("sgm")
        cw = sb("cw", [128, 6]); b_cw = P.buf("cw")
        X = sb("X", [128, nt, 128]); b_X = P.buf("X")
        Xdt = sb("Xdt", [128, nt, 64], BF16); b_Xdt = P.buf("Xdt")
        BT = sb("BT", [64, NT], BF16); b_BT = P.buf("BT")
        CT = sb("CT", [64, NT], BF16); b_CT = P.buf("CT")
        dt_ = sb("dt", [128, nt]); dta = sb("dta", [128, nt]); acum = sb("acum", [128, nt]); nacum = sb("nacum", [128, nt]); dsl = sb("dsl", [128, nt])
        dtaT = sb("dtaT", [128, 128]); acT = sb("acT", [128, 128])
        b_dt = P.buf("dt")
        sc2 = sb("sc2", [128, 2]); b_sc = P.buf("sc2")
        tps = ps("tps", [128, 512]); b_tps = P.buf("tps")
        aps = ps("aps", [128, 128]); b_aps = P.buf("aps")
        sps = [ps("sps%d" % i, [128, 512]) for i in range(2)]; b_sps = P.bufs(2, "sps")
        ops_ = ps("ops", [64, 512]); b_ops = P.buf("ops")
        Lb = [sb("Lb%d" % i, [128, 512]) for i in range(2)]; b_L = P.bufs(2, "L")
        Mb = [sb("Mb%d" % i, [128, 512], BF16) for i in range(2)]; b_M = P.bufs(2, "M")
        osb = sb("osb", [64, 512]); b_osb = P.buf("osb")
        ot = [sb("ot%d" % i, [128, 4, 64]) for i in range(2)]; b_ot = P.bufs(2, "ot")
        b_y = P.buf("y"); b_scr = P.buf("scr")
        cnt = {"s": 0, "l": 0, "ot": 0}

        def conv_silu(raw_ap, cw_ap, npart, out_fn):
            P.dma("sp", bigA[0:npart, :], raw_ap, writes=[b_bigA])
            P.dma("sp", cw[0:npart, :], cw_ap, writes=[b_cw])
            for si, (t0, ln) in enumerate(segs):
                p0 = t0 + 2 + 4 * si
                for c0 in range(0, ln, 2048):
                    w = min(2048, ln - c0)
                    o = bigB[0:npart, t0 + c0:t0 + c0 + w]
                    P.op("dve", lambda e, o=o, p0=p0, c0=c0, w=w: e.tensor_scalar(out=o, in0=bigA[0:npart, p0 + c0 - 2:p0 + c0 - 2 + w], scalar1=cw[0:npart, 0:1], scalar2=cw[0:npart, 5:6], op0=ALU.mult, op1=ALU.add),
                         reads=[b_bigA, b_cw], writes=[b_bigB])
                    for k in range(1, 5):
                        P.op("dve", lambda e, o=o, p0=p0, c0=c0, w=w, k=k: e.scalar_tensor_tensor(out=o, in0=bigA[0:npart, p0 + c0 - 2 + k:p0 + c0 - 2 + k + w], scalar=cw[0:npart, k:k + 1], in1=o, op0=ALU.mult, op1=ALU.add),
                             reads=[b_bigA, b_cw, b_bigB], writes=[b_bigB])
                    P.op("act", lambda e, o=o, w=w: e.activation(out=sgm[0:npart, 0:w], in_=o, func=AF.Sigmoid), reads=[b_bigB], writes=[b_sgm])
                    P.op("dve", lambda e, o=o, w=w: e.tensor_tensor(out=o, in0=o, in1=sgm[0:npart, 0:w], op=ALU.mult), reads=[b_bigB, b_sgm], writes=[b_bigB])
            out_fn()

        for d in range(2):
            conv_silu(br[d], cwb[d], 64, lambda: P.op("pool", lambda e: e.tensor_copy(out=BT[:], in_=bigB[0:64, :]), reads=[b_bigB], writes=[b_BT]))
            conv_silu(cr[d], cwc[d], 64, lambda: P.op("pool", lambda e: e.tensor_copy(out=CT[:], in_=bigB[0:64, :]), reads=[b_bigB], writes=[b_CT]))
            def xout():
                for c in range(nt):
                    g = c % 4
                    P.op("pe", lambda e, c=c, g=g: e.transpose(tps[:, g * 128:(g + 1) * 128], bigB[:, c * 128:(c + 1) * 128], ident[:]), reads=[b_bigB, b_c], writes=[b_tps], inc=(g == 3 or c == nt - 1))
                    if g == 3 or c == nt - 1:
                        c0 = c - g
                        P.op("act", lambda e, c0=c0, g=g: e.copy(out=X[:, c0:c0 + g + 1, :], in_=tps[:, 0:(g + 1) * 128].rearrange("p (a n) -> p a n", n=128)), reads=[b_tps], writes=[b_X])
                if d == 0:
                    P.dma("sp", xso.rearrange("(c p) n -> p c n", p=128), X[:], reads=[b_X], writes=[b_y])
            conv_silu(xr[d], cwx[d], 128, xout)
            for h in range(2):
                P.dma("sp", dt_[:], dtr[d, h].rearrange("(c p) -> p c", p=128), writes=[b_dt], allow_slow_non_contiguous=True)
                P.dma("sp", sc2[:, 0:1], dtb[d, h:h + 1].partition_broadcast(128), writes=[b_sc])
                P.dma("sp", sc2[:, 1:2], alog[d, h:h + 1].partition_broadcast(128), writes=[b_sc])
                P.op("act", lambda e: e.activation(out=sc2[:, 1:2], in_=sc2[:, 1:2], func=AF.Exp), reads=[b_sc], writes=[b_sc])
                P.op("act", lambda e: e.activation(out=dt_[:], in_=dt_[:], func=AF.Exp, bias=sc2[:, 0:1]), reads=[b_dt, b_sc], writes=[b_dt])
                P.op("act", lambda e: e.activation(out=dt_[:], in_=dt_[:], func=AF.Ln, bias=ones[:, 0:1]), reads=[b_dt, b_c], writes=[b_dt])
                P.op("dve", lambda e: e.tensor_scalar(out=dta[:], in0=dt_[:], scalar1=sc2[:, 1:2], scalar2=-1.0, op0=ALU.mult, op1=ALU.mult), reads=[b_dt, b_sc], writes=[b_dt])
                P.op("pe", lambda e: e.transpose(aps[0:nt, :], dta[:], ident[:]), reads=[b_dt, b_c], writes=[b_aps])
                P.op("act", lambda e: e.copy(out=dtaT[0:nt, :], in_=aps[0:nt, :]), reads=[b_aps], writes=[b_dt])
                P.op("pe", lambda e: e.matmul(aps[:, 0:nt], lhsT=dtaT[0:nt, :], rhs=SL[0:nt, 0:nt], start=True, stop=True), reads=[b_dt, b_c], writes=[b_aps])
                P.op("act", lambda e: e.copy(out=dsl[:], in_=aps[:, 0:nt]), reads=[b_aps], writes=[b_dt])
                P.op("pe", lambda e: e.matmul(aps[:, 0:nt], lhsT=U[:], rhs=dta[:], start=True, stop=False), reads=[b_dt, b_c], writes=[b_aps])
                P.op("pe", lambda e: e.matmul(aps[:, 0:nt], lhsT=ones[:], rhs=dsl[:], start=False, stop=True), reads=[b_dt, b_c], writes=[b_aps])
                P.op("act", lambda e: e.copy(out=acum[:], in_=aps[:, 0:nt]), reads=[b_aps], writes=[b_dt])
                P.op("dve", lambda e: e.tensor_scalar(out=nacum[:], in0=acum[:], scalar1=-1.0, scalar2=0.0, op0=ALU.mult, op1=ALU.add), reads=[b_dt], writes=[b_dt])
                P.op("pe", lambda e: e.transpose(aps[0:nt, :], acum[:], ident[:]), reads=[b_dt, b_c], writes=[b_aps])
                P.op("act", lambda e: e.copy(out=acT[0:nt, :], in_=aps[0:nt, :]), reads=[b_aps], writes=[b_dt])
                si = d * 2 + h
                P.dma("sp", scr[si].rearrange("(c p) -> c p", p=128), acT[0:nt, :], reads=[b_dt], writes=[b_scr])
                P.dma("sp", bigB[:], scr[si].partition_broadcast(128), reads=[b_scr], writes=[b_bigB])
                P.op("dve", lambda e, h=h: e.tensor_tensor(out=Xdt[:], in0=X[:, :, h * 64:(h + 1) * 64], in1=dt_[:].unsqueeze(2).to_broadcast([128, nt, 64]), op=ALU.mult), reads=[b_X, b_dt], writes=[b_Xdt])
                def chunk(d, h, q0):
                    W = min(512, NT - q0)
                    cmax = (q0 + W) // 128 - 1
                    base = cnt["s"]; cnt["s"] += cmax + 1; cnt["l"] += cmax + 1

                    def s_mm(c):
                        s = (base + c) % 2
                        P.op("pe", lambda e, s=s, c=c: e.matmul(sps[s][:, 0:W], lhsT=BT[:, c * 128:(c + 1) * 128], rhs=CT[:, q0:q0 + W], start=True, stop=True), reads=[b_BT, b_CT], writes=[b_sps[s]])
                    s_mm(0)
                    for c in range(cmax + 1):
                        s = (base + c) % 2
                        l = s
                        r = c - q0 // 128
                        if r >= 0:
                            P.op("pool", lambda e, l=l, r=r: e.tensor_tensor(out=Lb[l][:, 0:W], in0=bigB[:, q0:q0 + W], in1=masks[:, r, 0:W], op=ALU.add), reads=[b_bigB, b_c], writes=[b_L[l]])
                            P.op("act", lambda e, l=l, c=c: e.activation(out=Lb[l][:, 0:W], in_=Lb[l][:, 0:W], func=AF.Exp, bias=nacum[:, c:c + 1]), reads=[b_L[l], b_dt], writes=[b_L[l]])
                        else:
                            P.op("act", lambda e, l=l, c=c: e.activation(out=Lb[l][:, 0:W], in_=bigB[:, q0:q0 + W], func=AF.Exp, bias=nacum[:, c:c + 1]), reads=[b_bigB, b_dt], writes=[b_L[l]])
                        P.op("dve", lambda e, l=l, s=s: e.tensor_tensor(out=Mb[l][:, 0:W], in0=sps[s][:, 0:W], in1=Lb[l][:, 0:W], op=ALU.mult), reads=[b_sps[s], b_L[l]], writes=[b_M[l]])
                        if c + 1 <= cmax:
                            s_mm(c + 1)
                        P.op("pe", lambda e, l=l, c=c: e.matmul(ops_[:, 0:W], lhsT=Xdt[:, c, :], rhs=Mb[l][:, 0:W], start=(c == 0), stop=(c == cmax)), reads=[b_M[l], b_Xdt], writes=[b_ops], inc=(c == cmax))
                    nj = W // 128
                    P.op("act", lambda e: e.copy(out=osb[:, 0:W], in_=ops_[:, 0:W]), reads=[b_ops], writes=[b_osb])
                    for j in range(nj):
                        P.op("pe", lambda e, j=j: e.transpose(tps[:, j * 64:(j + 1) * 64], osb[:, j * 128:(j + 1) * 128], ident[0:64, 0:64]), reads=[b_osb, b_c], writes=[b_tps], inc=(j == nj - 1))
                    t = cnt["ot"] % 2; cnt["ot"] += 1
                    P.op("dve", lambda e, t=t: e.tensor_copy(out=ot[t][:, 0:nj, :], in_=tps[:, 0:nj * 64].rearrange("p (a n) -> p a n", n=64)), reads=[b_tps], writes=[b_ot[t]])
                    P.dma("sp", yo[d, q0:q0 + W, h * 64:(h + 1) * 64].rearrange("(j p) n -> p j n", p=128), ot[t][:, 0:nj, :], reads=[b_ot[t]], writes=[b_y])
                for q0 in range(0, NT, 512):
                    chunk(d, h, q0)
        P.finish("sp", [b_y])
        P.emit()
    return nc


def ka_consts():
    k = np.arange(128)
    U = (k[:, None] <= k[None, :]).astype(np.float32)
    SL = (k[:, None] < k[None, :]).astype(np.float32)
    masks = np.zeros((4, 128, 512), np.float32)
    i = np.arange(512)
    for r in range(4):
        masks[r] = np.where(i[None, :] >= r * 128 + k[:, None], 0.0, NEG)
    return {"U": U, "SL": SL, "ones": np.ones((128, 128), np.float32), "ident": np.eye(128, dtype=np.float32), "masks": masks}


def pad_seq(a, NC_, NL):
    z = np.zeros(a.shape[:-1] + (2,), a.dtype)
    return np.concatenate([z, a[..., :NC_], z, z, a[..., NC_:], z], -1)


def build_kb(n, NCH=128, GC=16):
    nc = bass.Bass("TRN2", target_bir_lowering=False)
    NA = 2 * n // 128
    NAD = n // 128
    N = 2 * n
    QC = 4
    din = lambda nm, s: nc.dram_tensor(nm, s, F32, kind="ExternalInput").ap()
    raw_d = din("raw", [3, NAD, NCH, 130])
    cw_d = din("cw", [3 * NCH * 4]); hb_d = din("hb", [2 * NCH])
    full_d = din("full", [2, NA, NCH, 128])
    F1_d = din("F1", [NA, 2 * NA]); TW_d = din("TW", [128, 2, NA]); TWI_d = din("TWI", [NA, 2, 128])
    CS_d = din("CS", [128, 256]); nSC_d = din("nSC", [128, 256]); C_d = din("C128", [128, 128]); S_d = din("S128", [128, 128]); nS_d = din("nS128", [128, 128])
    CI_d = din("CI", [NA, NAD]); nSI_d = din("nSI", [NA, NAD])
    yo = nc.dram_tensor("y", [NAD, NCH, 128], F32, kind="ExternalOutput").ap()
    with ExitStack() as st:
        P = Prog(nc, st)
        sb = lambda name, shape, dt=F32: st.enter_context(nc.sbuf_tensor("s_" + name, shape, dt))
        ps = lambda name, shape, dt=F32: st.enter_context(nc.psum_tensor("p_" + name, shape, dt))
        b_c = P.buf("c")
        F1 = sb("F1", [NA, 2 * NA]); TW = sb("TW", [128, 2, NA]); TWI = sb("TWI", [NA, 2, 128])
        CS = sb("CS", [128, 256]); nSC = sb("nSC", [128, 256]); C128 = sb("C128", [128, 128]); S128 = sb("S128", [128, 128]); nS128 = sb("nS128", [128, 128])
        CI = sb("CI", [NA, NAD]); nSI = sb("nSI", [NA, NAD])
        cw = sb("cw", [NAD, 3, NCH, 4]); hb = sb("hb", [NAD, 2, NCH])
        for t_, d_ in ((F1, F1_d), (TW, TW_d), (TWI, TWI_d), (CS, CS_d), (nSC, nSC_d), (C128, C_d), (S128, S_d), (nS128, nS_d), (CI, CI_d), (nSI, nSI_d)):
            P.dma("sp", t_[:], d_, writes=[b_c])
        P.dma("sp", cw[:].rearrange("p a b c -> p (a b c)"), cw_d.partition_broadcast(NAD), writes=[b_c])
        P.dma("sp", hb[:].rearrange("p a b -> p (a b)"), hb_d.partition_broadcast(NAD), writes=[b_c])
        raw = sb("raw", [NAD, 3, GC, 130]); b_raw = P.buf("raw")
        u = sb("u", [NAD, 3, GC, 128]); b_u = P.buf("u")
        tmpc = sb("tmpc", [NAD, GC, 128]); b_tmpc = P.buf("tmpc")
        fl = sb("fl", [NA, 2, GC, 128]); b_fl = P.buf("fl")
        H = sb("H", [128, 2, QC, NA]); b_H = P.buf("H")
        Yp = sb("Yp", [128, 2, QC, NA]); b_Yp = P.buf("Yp")
        Zs = sb("Zs", [128, 2, QC, NA]); b_Zs = P.buf("Zs")
        Vp = sb("Vp", [NA, 2, QC, 128]); b_Vp = P.buf("Vp")
        t1 = sb("t1", [128, QC, max(NA, 128)]); t2 = sb("t2", [128, QC, max(NA, 128)]); b_t1 = P.buf("t1"); b_t2 = P.buf("t2")
        z1 = sb("z1", [NAD, GC, 128]); b_z1 = P.buf("z1")
        og = sb("og", [NAD, GC, 128]); b_og = P.buf("og")
        yps = ps("yps", [128, QC, 2, NA]); b_yps = P.buf("yps")
        xps = ps("xps", [128, 2, QC, NA]); b_xps = P.buf("xps")
        vps = ps("vps", [NA, QC, 2, 128]); b_vps = P.buf("vps")
        ops_ = ps("ops", [NAD, QC, 128]); b_ops = P.buf("ops")
        b_y = P.buf("y")

        def cmul(eng_out_re, eng_out_im, are, aim, bre, bim, conj_b, pn, w, rbufs, obuf):
            sgn_im = -1.0 if conj_b else 1.0
            P.op("dve", lambda e: e.tensor_tensor(out=t1[0:pn, :, 0:w], in0=are, in1=bre, op=ALU.mult), reads=rbufs, writes=[b_t1])
            P.op("dve", lambda e: e.tensor_tensor(out=t2[0:pn, :, 0:w], in0=aim, in1=bim, op=ALU.mult), reads=rbufs, writes=[b_t2])
            P.op("dve", lambda e: e.tensor_tensor(out=eng_out_re, in0=t1[0:pn, :, 0:w], in1=t2[0:pn, :, 0:w], op=(ALU.add if conj_b else ALU.subtract)), reads=[b_t1, b_t2], writes=[obuf])
            P.op("dve", lambda e: e.tensor_tensor(out=t1[0:pn, :, 0:w], in0=aim, in1=bre, op=ALU.mult), reads=rbufs, writes=[b_t1])
            P.op("dve", lambda e: e.tensor_tensor(out=t2[0:pn, :, 0:w], in0=are, in1=bim, op=ALU.mult), reads=rbufs, writes=[b_t2])
            P.op("dve", lambda e: e.tensor_tensor(out=eng_out_im, in0=t1[0:pn, :, 0:w], in1=t2[0:pn, :, 0:w], op=(ALU.subtract if conj_b else ALU.add)), reads=[b_t1, b_t2], writes=[obuf])

        def fwd(src_fn, K, src_bufs):
            for c in range(QC):
                P.op("pe", lambda e, c=c: e.matmul(yps[:, c, :, :].rearrange("p a b -> p (a b)"), lhsT=src_fn(c), rhs=F1[0:K, :], start=True, stop=True), reads=src_bufs + [b_c], writes=[b_yps], inc=(c == QC - 1))
            twc = TW[:, 0, :].unsqueeze(1).to_broadcast([128, QC, NA]); tws = TW[:, 1, :].unsqueeze(1).to_broadcast([128, QC, NA])
            cmul(Yp[:, 0, :, :], Yp[:, 1, :, :], yps[:, :, 0, :], yps[:, :, 1, :], twc, tws, True, 128, NA, [b_yps, b_c], b_Yp)
            yre = Yp[:, 0, :, :].rearrange("p a b -> p (a b)"); yim = Yp[:, 1, :, :].rearrange("p a b -> p (a b)")
            xre = xps[:, 0, :, :].rearrange("p a b -> p (a b)"); xim = xps[:, 1, :, :].rearrange("p a b -> p (a b)")
            P.op("pe", lambda e: e.matmul(xre, lhsT=C128[:], rhs=yre, start=True, stop=False), reads=[b_Yp, b_c], writes=[b_xps], inc=False)
            P.op("pe", lambda e: e.matmul(xre, lhsT=S128[:], rhs=yim, start=False, stop=True), reads=[b_Yp, b_c], writes=[b_xps], inc=False)
            P.op("pe", lambda e: e.matmul(xim, lhsT=C128[:], rhs=yim, start=True, stop=False), reads=[b_Yp, b_c], writes=[b_xps], inc=False)
            P.op("pe", lambda e: e.matmul(xim, lhsT=nS128[:], rhs=yre, start=False, stop=True), reads=[b_Yp, b_c], writes=[b_xps])

        def long_conv(zsrc_fn, zbufs, o, q0, gate_fn):
            fwd(lambda c: fl[:, o, q0 + c, :], NA, [b_fl])
            P.op("act", lambda e: e.copy(out=H[:], in_=xps[:]), reads=[b_xps], writes=[b_H])
            fwd(zsrc_fn, NAD, zbufs)
            cmul(Zs[:, 0, :, :], Zs[:, 1, :, :], xps[:, 0, :, :], xps[:, 1, :, :], H[:, 0, :, :], H[:, 1, :, :], False, 128, NA, [b_xps, b_H], b_Zs)
            for c in range(QC):
                vv = vps[:, c, :, :].rearrange("p a b -> p (a b)")
                P.op("pe", lambda e, c=c, vv=vv: e.matmul(vv, lhsT=Zs[:, 0, c, :], rhs=CS[:], start=True, stop=False), reads=[b_Zs, b_c], writes=[b_vps], inc=False)
                P.op("pe", lambda e, c=c, vv=vv: e.matmul(vv, lhsT=Zs[:, 1, c, :], rhs=nSC[:], start=False, stop=True), reads=[b_Zs, b_c], writes=[b_vps], inc=(c == QC - 1))
            twc = TWI[:, 0, :].unsqueeze(1).to_broadcast([NA, QC, 128]); tws = TWI[:, 1, :].unsqueeze(1).to_broadcast([NA, QC, 128])
            cmul(Vp[:, 0, :, :], Vp[:, 1, :, :], vps[:, :, 0, :], vps[:, :, 1, :], twc, tws, False, NA, 128, [b_vps, b_c], b_Vp)
            oo = ops_[:].rearrange("p a b -> p (a b)")
            P.op("pe", lambda e: e.matmul(oo, lhsT=CI[:], rhs=Vp[:, 0, :, :].rearrange("p a b -> p (a b)"), start=True, stop=False), reads=[b_Vp, b_c], writes=[b_ops], inc=False)
            P.op("pe", lambda e: e.matmul(oo, lhsT=nSI[:], rhs=Vp[:, 1, :, :].rearrange("p a b -> p (a b)"), start=False, stop=True), reads=[b_Vp, b_c], writes=[b_ops])
            gate_fn()

        def group(g0):
            for s in range(3):
                P.dma("sp", raw[:, s, :, :], raw_d[s, :, g0:g0 + GC, :], writes=[b_raw])
            P.dma("sp", fl[:], full_d[:, :, g0:g0 + GC, :].rearrange("o a c r -> a o c r"), writes=[b_fl])
            for s in range(3):
                wk = lambda k, s=s: cw[:, s, g0:g0 + GC, k:k + 1].to_broadcast([NAD, GC, 128])
                us = u[:, s, :, :]
                P.op("dve", lambda e, s=s, us=us, wk=wk: e.tensor_tensor(out=us, in0=raw[:, s, :, 0:128], in1=wk(0), op=ALU.mult), reads=[b_raw, b_c], writes=[b_u])
                for k in (1, 2):
                    P.op("pool", lambda e, s=s, k=k, wk=wk: e.tensor_tensor(out=tmpc[:], in0=raw[:, s, :, k:k + 128], in1=wk(k), op=ALU.mult), reads=[b_raw, b_c], writes=[b_tmpc])
                    P.op("dve", lambda e, us=us: e.tensor_tensor(out=us, in0=us, in1=tmpc[:], op=ALU.add), reads=[b_u, b_tmpc], writes=[b_u])
                P.op("dve", lambda e, us=us, wk=wk: e.tensor_tensor(out=us, in0=us, in1=wk(3), op=ALU.add), reads=[b_u, b_c], writes=[b_u])
            for q0 in range(0, GC, QC):
                def gate0(q0=q0):
                    bb = hb[:, 0, g0 + q0:g0 + q0 + QC].unsqueeze(2).to_broadcast([NAD, QC, 128])
                    zq = z1[:, q0:q0 + QC, :]
                    P.op("dve", lambda e: e.tensor_tensor(out=zq, in0=u[:, 0, q0:q0 + QC, :], in1=bb, op=ALU.mult), reads=[b_u, b_c], writes=[b_z1])
                    P.op("dve", lambda e: e.scalar_tensor_tensor(out=zq, in0=ops_[:], scalar=1.0 / N, in1=zq, op0=ALU.mult, op1=ALU.add), reads=[b_ops, b_z1], writes=[b_z1])
                    P.op("dve", lambda e: e.tensor_tensor(out=zq, in0=zq, in1=u[:, 1, q0:q0 + QC, :], op=ALU.mult), reads=[b_u, b_z1], writes=[b_z1])
                long_conv(lambda c, q0=q0: u[:, 0, q0 + c, :], [b_u], 0, q0, gate0)

                def gate1(q0=q0):
                    bb = hb[:, 1, g0 + q0:g0 + q0 + QC].unsqueeze(2).to_broadcast([NAD, QC, 128])
                    oq = og[:, q0:q0 + QC, :]
                    P.op("dve", lambda e: e.tensor_tensor(out=oq, in0=z1[:, q0:q0 + QC, :], in1=bb, op=ALU.mult), reads=[b_z1, b_c], writes=[b_og])
                    P.op("dve", lambda e: e.scalar_tensor_tensor(out=oq, in0=ops_[:], scalar=1.0 / N, in1=oq, op0=ALU.mult, op1=ALU.add), reads=[b_ops, b_og], writes=[b_og])
                    P.op("dve", lambda e: e.tensor_tensor(out=oq, in0=oq, in1=u[:, 2, q0:q0 + QC, :], op=ALU.mult), reads=[b_u, b_og], writes=[b_og])
                long_conv(lambda c, q0=q0: z1[:, q0 + c, :], [b_z1], 1, q0, gate1)
            P.dma("sp", yo[:, g0:g0 + GC, :], og[:], reads=[b_og], writes=[b_y])

        for g0 in range(0, NCH, GC):
            group(g0)
        P.finish("sp", [b_y])
        P.emit()
    return nc


def kb_consts(n):
    NA = 2 * n // 128; NAD = n // 128; N = 2 * n
    a = np.arange(NA)[:, None]; k1 = np.arange(NA)[None, :]
    ang = 2 * np.pi * a * k1 / NA
    F1 = np.concatenate([np.cos(ang), -np.sin(ang)], 1)
    r = np.arange(128)[:, None]
    th = 2 * np.pi * r * k1 / N
    TW = np.stack([np.cos(th), np.sin(th)], 1)
    TWI = np.stack([np.cos(th).T, np.sin(th).T], 1)
    p = np.arange(128)
    a128 = 2 * np.pi * p[:, None] * p[None, :] / 128
    C = np.cos(a128); S = np.sin(a128)
    angI = 2 * np.pi * np.arange(NA)[:, None] * np.arange(NAD)[None, :] / NA
    f = lambda x: np.ascontiguousarray(x.astype(np.float32))
    return {"F1": f(F1), "TW": f(TW), "TWI": f(TWI), "CS": f(np.concatenate([C, S], 1)), "nSC": f(np.concatenate([-S, C], 1)),
            "C128": f(C), "S128": f(S), "nS128": f(-S), "CI": f(np.cos(angI)), "nSI": f(-np.sin(angI))}


def kb_pack_raw(uraw, n):
    NAD = n // 128
    p = np.pad(uraw, ((0, 0), (0, 0), (1, 1)))
    idx = (np.arange(NAD)[:, None] * 128 + np.arange(130)[None, :])
    g = p[:, :, idx]
    return np.ascontiguousarray(g.transpose(0, 2, 1, 3))


def kb_pack_full(k, n):
    NA = 2 * n // 128
    kf, kb = k[:, 0], k[:, 1]
    full = np.concatenate([kf, np.zeros_like(kf[..., :1]), kb[..., :0:-1]], -1)
    return np.ascontiguousarray(full.reshape(2, -1, NA, 128).transpose(0, 2, 1, 3))


EPS = 1e-6
TWO_PI = 2.0 * math.pi


def build_kf1(n):
    nc = bass.Bass("TRN2", target_bir_lowering=False)
    din = lambda nm, s: nc.dram_tensor(nm, s, F32, kind="ExternalInput").ap()
    zT_d = din("zT", [33, n]); dec_d = din("decay", [128, n])
    w1_d = din("w1", [33, 64]); w2_d = din("w2", [64, 64]); w3_d = din("w3", [64, 128])
    fb1_d = din("fb1", [64, 2]); fb2_d = din("fb2", [64, 2]); b3_d = din("b3", [128, 1])
    pm_d = din("pm", [128, 128])
    ko = nc.dram_tensor("k", [128, n], F32, kind="ExternalOutput").ap()
    CW = min(512, n)
    nch = n // CW
    with ExitStack() as st:
        P = Prog(nc, st)
        sb = lambda name, shape, dt=F32: st.enter_context(nc.sbuf_tensor("s_" + name, shape, dt))
        ps = lambda name, shape, dt=F32: st.enter_context(nc.psum_tensor("p_" + name, shape, dt))
        b_c = P.buf("c")
        zT = sb("zT", [33, n]); dec = sb("dec", [128, n]); w1 = sb("w1", [33, 64]); w2 = sb("w2", [64, 64]); w3 = sb("w3", [64, 128])
        fb1 = sb("fb1", [64, 2]); fb2 = sb("fb2", [64, 2]); b3 = sb("b3", [128, 1]); pm = sb("pm", [128, 128]); epsb = sb("epsb", [128, 1])
        for t_, d_ in ((zT, zT_d), (dec, dec_d), (w1, w1_d), (w2, w2_d), (w3, w3_d), (fb1, fb1_d), (fb2, fb2_d), (b3, b3_d), (pm, pm_d)):
            P.dma("sp", t_[:], d_, writes=[b_c])
        for fb in (fb1, fb2):
            P.op("dve", lambda e, fb=fb: e.tensor_scalar(out=fb[:, 1:2], in0=fb[:, 1:2], scalar1=fb[:, 0:1], scalar2=16.0 * math.pi, op0=ALU.mult, op1=ALU.add), reads=[b_c], writes=[b_c])
        P.op("dve", lambda e: e.memset(epsb[:], EPS), reads=[b_c], writes=[b_c])
        kT = sb("kT", [128, n]); b_k = P.buf("kT")
        ssq = sb("ssq", [128, nch + 2]); b_ss = P.buf("ss")
        sq = sb("sq", [128, CW]); b_sq = P.buf("sq")
        h1 = [sb("h1_%d" % i, [64, CW]) for i in range(2)]; b_h1 = P.bufs(2, "h1")
        h2 = [sb("h2_%d" % i, [64, CW]) for i in range(2)]; b_h2 = P.bufs(2, "h2")
        pp = [ps("pp%d" % i, [128, CW]) for i in range(4)]; b_pp = P.bufs(4, "pp")
        pi = [0]

        I32 = mybir.dt.int32
        ki = sb("ki", [64, CW], I32); kf = sb("kf", [64, CW]); b_ki = P.buf("ki")

        def sin_layer(src_ps, fb, dst, b_src, b_dst, w):
            P.op("dve", lambda e: e.tensor_scalar(out=dst[:, 0:w], in0=src_ps[0:64, 0:w], scalar1=fb[:, 0:1], scalar2=fb[:, 1:2], op0=ALU.mult, op1=ALU.add), reads=[b_src, b_c], writes=[b_dst])
            P.op("dve", lambda e: e.tensor_scalar(out=ki[:, 0:w], in0=dst[:, 0:w], scalar1=1.0 / TWO_PI, scalar2=0.0, op0=ALU.mult, op1=ALU.add), reads=[b_dst], writes=[b_ki])
            P.op("dve", lambda e: e.tensor_copy(out=kf[:, 0:w], in_=ki[:, 0:w]), reads=[b_ki], writes=[b_ki])
            P.op("dve", lambda e: e.scalar_tensor_tensor(out=dst[:, 0:w], in0=kf[:, 0:w], scalar=-TWO_PI, in1=dst[:, 0:w], op0=ALU.mult, op1=ALU.add), reads=[b_ki, b_dst], writes=[b_dst])
            P.op("dve", lambda e: e.tensor_scalar(out=kf[:, 0:w], in0=dst[:, 0:w], scalar1=math.pi, scalar2=-TWO_PI, op0=ALU.is_gt, op1=ALU.mult), reads=[b_dst, b_ki], writes=[b_ki])
            P.op("dve", lambda e: e.tensor_tensor(out=dst[:, 0:w], in0=dst[:, 0:w], in1=kf[:, 0:w], op=ALU.add), reads=[b_ki, b_dst], writes=[b_dst])
            P.op("act", lambda e: e.activation(out=dst[:, 0:w], in_=dst[:, 0:w], func=AF.Sin), reads=[b_dst], writes=[b_dst])

        def chunk(j):
            c0 = j * CW
            s = j % 2
            a = pi[0] % 4; pi[0] += 1
            P.op("pe", lambda e: e.matmul(pp[a][0:64, :], lhsT=w1[:], rhs=zT[:, c0:c0 + CW], start=True, stop=True), reads=[b_c], writes=[b_pp[a]])
            sin_layer(pp[a], fb1, h1[s], b_pp[a], b_h1[s], CW)
            a2 = pi[0] % 4; pi[0] += 1
            P.op("pe", lambda e: e.matmul(pp[a2][0:64, :], lhsT=w2[:], rhs=h1[s][:], start=True, stop=True), reads=[b_c, b_h1[s]], writes=[b_pp[a2]])
            sin_layer(pp[a2], fb2, h2[s], b_pp[a2], b_h2[s], CW)
            a3 = pi[0] % 4; pi[0] += 1
            P.op("pe", lambda e: e.matmul(pp[a3][:, :], lhsT=w3[:], rhs=h2[s][:], start=True, stop=True), reads=[b_c, b_h2[s]], writes=[b_pp[a3]])
            P.op("dve", lambda e: e.scalar_tensor_tensor(out=kT[:, c0:c0 + CW], in0=pp[a3][:, :], scalar=b3[:, 0:1], in1=dec[:, c0:c0 + CW], op0=ALU.add, op1=ALU.mult), reads=[b_pp[a3], b_c], writes=[b_k])
            P.op("act", lambda e: e.activation(out=sq[:], in_=kT[:, c0:c0 + CW], func=AF.Square, accum_out=ssq[:, j:j + 1]), reads=[b_k], writes=[b_sq, b_ss])

        for j in range(nch):
            chunk(j)
        P.op("dve", lambda e: e.tensor_reduce(out=ssq[:, nch:nch + 1], in_=ssq[:, 0:nch], axis=AX.X, op=ALU.add), reads=[b_ss], writes=[b_ss])
        P.op("pe", lambda e: e.matmul(pp[0][:, 0:1], lhsT=pm[:], rhs=ssq[:, nch:nch + 1], start=True, stop=True), reads=[b_ss, b_c], writes=[b_pp[0]])
        P.op("act", lambda e: e.activation(out=ssq[:, nch + 1:nch + 2], in_=pp[0][:, 0:1], func=AF.Sqrt, bias=epsb[:, 0:1]), reads=[b_pp[0], b_c], writes=[b_ss])
        P.op("dve", lambda e: e.reciprocal(out=ssq[:, nch + 1:nch + 2], in_=ssq[:, nch + 1:nch + 2]), reads=[b_ss], writes=[b_ss])
        b_o = P.buf("o")
        for c0 in range(0, n, 2048):
            w = min(2048, n - c0)
            P.op("dve", lambda e, c0=c0, w=w: e.tensor_scalar(out=kT[:, c0:c0 + w], in0=kT[:, c0:c0 + w], scalar1=ssq[:, nch + 1:nch + 2], scalar2=0.0, op0=ALU.mult, op1=ALU.add), reads=[b_k, b_ss], writes=[b_k])
        P.dma("sp", ko, kT[:], reads=[b_k], writes=[b_o])
        P.finish("sp", [b_o])
        P.emit()
    return nc


def hy_tables(n):
    f32 = np.float32
    pos = np.arange(n, dtype=f32)
    t = np.linspace(0.0, 1.0, n, dtype=f32)
    f = np.linspace(1e-4, 15.0, 16, dtype=f32)
    ang = (f32(2.0 * math.pi) * pos[:, None] * f[None, :] / f32(n)).astype(f32)
    z = np.concatenate([t[:, None], np.cos(ang), -np.sin(ang)], -1).astype(f32)
    mx = math.log(1e-2) / 0.3; mn = math.log(1e-2) / 1.5
    deltas = np.abs(np.linspace(mn, mx, 256, dtype=f32))
    decay = np.exp(-t[:, None] * deltas[None, :]).astype(f32)
    return np.ascontiguousarray(z.T), np.ascontiguousarray(decay.T)


def kf1_inputs(n, hy_w1, hy_b1, hy_f1, hy_w2, hy_b2, hy_f2, hy_w3, hy_b3):
    zT, decT = hy_tables(n)
    k64 = np.arange(128)
    pm = (k64[:, None] % 64 == k64[None, :] % 64).astype(np.float32)
    maps = []
    for core in range(8):
        o, cr = core // 4, core % 4
        cols = np.concatenate([o * 512 + d * 256 + cr * 64 + np.arange(64) for d in range(2)])
        chs = np.concatenate([cr * 64 + np.arange(64)] * 2)
        maps.append({"zT": zT, "decay": np.ascontiguousarray(decT[chs]), "w1": np.ascontiguousarray(hy_w1), "w2": np.ascontiguousarray(hy_w2),
                     "w3": np.ascontiguousarray(hy_w3[:, cols]), "fb1": np.ascontiguousarray(np.stack([hy_f1, hy_b1], -1)),
                     "fb2": np.ascontiguousarray(np.stack([hy_f2, hy_b2], -1)), "b3": np.ascontiguousarray(hy_b3[cols][:, None]), "pm": pm})
    return maps


def kf1_gather(results, n):
    k = np.zeros((2, 2, 256, n), np.float32)
    for core in range(8):
        o, cr = core // 4, core % 4
        r = results[core]["k"]
        for d in range(2):
            k[o, d, cr * 64:(cr + 1) * 64] = r[d * 64:(d + 1) * 64]
    return k


D = 1024
DFF = 4096
EPS = 1e-6


class ModCalc:
    def __init__(self, P, nc, sb, ps):
        self.P, self.nc = P, nc
        self.c_sb = sb("mc_c", [128, 8]); self.c_sg = sb("mc_sg", [128, 8]); self.c_bc = sb("mc_bc", [128, 8, 128])
        self.bm = sb("mc_bm", [128, 512]); self.wst = sb("mc_w", [128, 8, 512])
        self.ps = ps("mc_ps", [128, 512])
        self.b = P.bufs(6, "mc")

    def set_c(self, cvec):
        P = self.P
        b_c, b_cs, b_bc = self.b[0:3]
        c_sb, c_sg, c_bc = self.c_sb, self.c_sg, self.c_bc
        P.dma("sp", c_sb[:], cvec.rearrange("(k p) -> p k", p=128), writes=[b_c], allow_slow_non_contiguous=True)
        P.op("act", lambda e: e.activation(out=c_sg[:], in_=c_sb[:], func=AF.Sigmoid), reads=[b_c], writes=[b_cs])
        P.op("dve", lambda e: e.tensor_tensor(out=c_sg[:], in0=c_sg[:], in1=c_sb[:], op=ALU.mult), reads=[b_c, b_cs], writes=[b_cs])
        P.op("dve", lambda e: e.tensor_copy(out=c_bc[:], in_=c_sg[:].unsqueeze(2).to_broadcast([128, 8, 128])), reads=[b_cs], writes=[b_bc])

    def calc(self, w_mod, b_mod, col0, ncols, out_ap_fn, out_buf):
        P = self.P
        b_bc, b_bm, b_w, b_ps = self.b[2:6]
        wv = w_mod.rearrange("(k p) n -> p k n", p=128)
        for j in range(ncols // 512):
            c = col0 + j * 512
            P.dma("sp", self.wst[:], wv[:, :, c:c + 512], writes=[b_w])
            P.dma("sp", self.bm[:], b_mod[c:c + 512].partition_broadcast(128), writes=[b_bm])
            for k in range(8):
                P.op("pe", lambda e, k=k: e.matmul(self.ps[:], lhsT=self.c_bc[:, k, :], rhs=self.wst[:, k, :], start=(k == 0), stop=(k == 7)),
                     reads=[b_bc, b_w], writes=[b_ps])
            P.op("dve", lambda e, j=j: e.tensor_tensor(out=out_ap_fn(j), in0=self.ps[:], in1=self.bm[:], op=ALU.add), reads=[b_ps, b_bm], writes=[out_buf])


def rstd_from_ss(P, ss_ap, b_ss, epsb, b_eps):
    P.op("act", lambda e: e.activation(out=ss_ap, in_=ss_ap, func=AF.Sqrt, bias=epsb[:, 0:1]), reads=[b_ss, b_eps], writes=[b_ss])
    P.op("dve", lambda e: e.reciprocal(out=ss_ap, in_=ss_ap), reads=[b_ss], writes=[b_ss])


def build_k3a(ntl, ntc):
    nc = bass.Bass("TRN2", target_bir_lowering=False)
    NT = ntl + ntc
    NTK = NT * 128
    din = lambda n, s: nc.dram_tensor(n, s, F32, kind="ExternalInput").ap()
    x = din("x", [NTK, D])
    yf = din("yf", [NTK, 256]); yb = din("yb", [NTK, 256]); xs = din("xs", [NTK, 256]); zz = din("z", [NTK, 256])
    mixT = din("mixT", [768, NTK])
    cv = din("cv", [D]); cctx = din("cctx", [D]); w_mod = din("w_mod", [D, 1024]); b_mod = din("b_mod", [1024])
    g_post = din("g_post", [D]); skip_d = din("skip", [256]); ssdn_d = din("ssdn", [256])
    w_out = din("w_out", [D, D]); ident_d = din("ident", [128, 128])
    xo = nc.dram_tensor("xo", [NTK, D], F32, kind="ExternalOutput").ap()
    with ExitStack() as st:
        P = Prog(nc, st)
        sb = lambda name, shape, dt=F32: st.enter_context(nc.sbuf_tensor("s_" + name, shape, dt))
        ps = lambda name, shape, dt=F32: st.enter_context(nc.psum_tensor("p_" + name, shape, dt))
        b_c = P.buf("c")
        ident_f = sb("ident_f", [128, 128]); ident = sb("identb", [128, 128], BF16)
        epsb = sb("epsb", [128, 1]); gp = sb("gp", [128, D]); skipb = sb("skipb", [128, 256]); ssdn = sb("ssdn", [128, 256])
        P.dma("sp", ident_f[:], ident_d, writes=[b_c])
        P.dma("sp", gp[:], g_post.partition_broadcast(128), writes=[b_c])
        P.dma("sp", skipb[:], skip_d.partition_broadcast(128), writes=[b_c])
        P.dma("sp", ssdn[:], ssdn_d.partition_broadcast(128), writes=[b_c])
        P.op("dve", lambda e: e.tensor_copy(out=ident[:], in_=ident_f[:]), reads=[b_c], writes=[b_c])
        P.op("dve", lambda e: e.memset(epsb[:], EPS), reads=[b_c], writes=[b_c])
        wob = sb("wob", [128, 8, D], BF16); b_wo = P.buf("wo")
        wst = [sb("wst%d" % i, [128, D]) for i in range(2)]; b_wst = P.bufs(2, "wst")
        wv = w_out.rearrange("(k p) n -> p k n", p=128)
        for k in range(8):
            s = k % 2
            P.dma("pool", wst[s][:], wv[:, k, :], writes=[b_wst[s]])
            P.op("pool", lambda e, k=k, s=s: e.tensor_copy(out=wob[:, k, :], in_=wst[s][:]), reads=[b_wst[s]], writes=[b_wo])
        MC = ModCalc(P, nc, sb, ps)
        G1 = sb("G1", [128, D]); b_G1 = P.buf("G1")
        xt = [sb("xt%d" % i, [128, D]) for i in range(2)]; b_xt = P.bufs(2, "xt")
        sq = sb("sq", [128, D]); b_sq = P.buf("sq")
        a4 = [[sb("a%d_%d" % (j, i), [128, 256]) for j in range(4)] for i in range(2)]; b_a4 = P.bufs(2, "a4")
        sg = sb("sg", [128, 256]); b_sg = P.buf("sg")
        ss = sb("ss", [128, 4]); b_ss = P.buf("ss")
        tnb = sb("tnb", [128, 256], BF16); b_tn = P.buf("tn")
        mst = [sb("mst%d" % i, [128, 6, 128]) for i in range(2)]; b_mst = P.bufs(2, "mst")
        mT = [sb("mT%d" % i, [128, 8, 128], BF16) for i in range(2)]; b_mT = P.bufs(2, "mT")
        tps = ps("tps", [128, 2, 128], BF16); b_tps = P.buf("tps")
        yps = [ps("yps%d" % i, [128, 2, 512]) for i in range(2)]; b_yps = P.bufs(2, "yps")
        tmp = sb("tmp", [128, D]); b_tmp = P.buf("tmp")
        ob = [sb("ob%d" % i, [128, D]) for i in range(2)]; b_ob = P.bufs(2, "ob")
        b_out = P.buf("out")
        for seg, (t0, t1, cvec) in enumerate(((0, ntl, cv), (ntl, NT, cctx))):
            MC.set_c(cvec)
            MC.calc(w_mod, b_mod, 0, 1024, lambda j: G1[:, j * 512:(j + 1) * 512], b_G1)
            P.op("dve", lambda e: e.tensor_tensor(out=G1[:], in0=G1[:], in1=gp[:], op=ALU.mult), reads=[b_G1, b_c], writes=[b_G1])
            for t in range(t0, t1):
                s = t % 2
                tok = slice(t * 128, (t + 1) * 128)
                P.dma("sp", xt[s][:], x[tok, :], writes=[b_xt[s]])
                for j, src in enumerate((xs, yf, yb, zz)):
                    P.dma("sp", a4[s][j][:], src[tok, :], writes=[b_a4[s]])
                P.dma("sp", mst[s][:], mixT[:, tok].rearrange("(k p) t -> p k t", p=128), writes=[b_mst[s]])
                A = a4[s]
                P.op("dve", lambda e, A=A: e.tensor_tensor(out=A[0][:], in0=A[0][:], in1=skipb[:], op=ALU.mult), reads=[b_a4[s], b_c], writes=[b_a4[s]])
                P.op("dve", lambda e, A=A: e.tensor_tensor(out=A[0][:], in0=A[0][:], in1=A[1][:], op=ALU.add), reads=[b_a4[s]], writes=[b_a4[s]])
                P.op("dve", lambda e, A=A: e.tensor_tensor(out=A[0][:], in0=A[0][:], in1=A[2][:], op=ALU.add), reads=[b_a4[s]], writes=[b_a4[s]])
                P.op("act", lambda e, A=A: e.activation(out=sg[:], in_=A[3][:], func=AF.Sigmoid), reads=[b_a4[s]], writes=[b_sg])
                P.op("dve", lambda e, A=A: e.tensor_tensor(out=sg[:], in0=sg[:], in1=A[3][:], op=ALU.mult), reads=[b_a4[s], b_sg], writes=[b_sg])
                P.op("dve", lambda e, A=A: e.tensor_tensor(out=A[0][:], in0=A[0][:], in1=sg[:], op=ALU.mult), reads=[b_a4[s], b_sg], writes=[b_a4[s]])
                P.op("act", lambda e, A=A: e.activation(out=sq[:, 0:256], in_=A[0][:], func=AF.Square, scale=1.0 / 16, accum_out=ss[:, 0:1]), reads=[b_a4[s]], writes=[b_sq, b_ss])
                rstd_from_ss(P, ss[:, 0:1], b_ss, epsb, b_c)
                P.op("dve", lambda e, A=A: e.scalar_tensor_tensor(out=tnb[:], in0=A[0][:], scalar=ss[:, 0:1], in1=ssdn[:], op0=ALU.mult, op1=ALU.mult), reads=[b_a4[s], b_ss, b_c], writes=[b_tn])
                for k in range(2):
                    P.op("pe", lambda e, k=k: e.transpose(tps[:, k, :], tnb[:, k * 128:(k + 1) * 128], ident[:]), reads=[b_tn, b_c], writes=[b_tps], inc=(k == 1))
                P.op("act", lambda e, s=s: e.copy(out=mT[s][:, 0:2, :], in_=tps[:]), reads=[b_tps], writes=[b_mT[s]])
                P.op("pool", lambda e, s=s: e.tensor_copy(out=mT[s][:, 2:8, :], in_=mst[s][:]), reads=[b_mst[s]], writes=[b_mT[s]])
                for c in range(2):
                    for k in range(8):
                        P.op("pe", lambda e, k=k, c=c, s=s: e.matmul(yps[s][:, c, :], lhsT=mT[s][:, k, :], rhs=wob[:, k, c * 512:(c + 1) * 512], start=(k == 0), stop=(k == 7)),
                             reads=[b_mT[s], b_wo], writes=[b_yps[s]], inc=(k == 7))
                for c in range(2):
                    P.op("act", lambda e, c=c, s=s: e.activation(out=sq[:, c * 512:(c + 1) * 512], in_=yps[s][:, c, :], func=AF.Square, scale=1.0 / 32, accum_out=ss[:, 1 + c:2 + c]), reads=[b_yps[s]], writes=[b_sq, b_ss])
                P.op("dve", lambda e: e.tensor_tensor(out=ss[:, 3:4], in0=ss[:, 1:2], in1=ss[:, 2:3], op=ALU.add), reads=[b_ss], writes=[b_ss])
                rstd_from_ss(P, ss[:, 3:4], b_ss, epsb, b_c)
                for c in range(2):
                    P.op("dve", lambda e, c=c, s=s: e.scalar_tensor_tensor(out=tmp[:, c * 512:(c + 1) * 512], in0=yps[s][:, c, :], scalar=ss[:, 3:4], in1=G1[:, c * 512:(c + 1) * 512], op0=ALU.mult, op1=ALU.mult),
                         reads=[b_yps[s], b_ss, b_G1], writes=[b_tmp])
                P.op("pool", lambda e, s=s: e.tensor_tensor(out=ob[s][:], in0=tmp[:], in1=xt[s][:], op=ALU.add), reads=[b_tmp, b_xt[s]], writes=[b_ob[s]])
                P.dma("sp", xo[tok, :], ob[s][:], reads=[b_ob[s]], writes=[b_out])
        P.finish("sp", [b_out])
        P.emit()
    return nc


def build_k3b(ntl, ntc):
    nc = bass.Bass("TRN2", target_bir_lowering=False)
    NT = ntl + ntc
    NTK = NT * 128
    din = lambda n, s: nc.dram_tensor(n, s, F32, kind="ExternalInput").ap()
    x = din("x", [NTK, D])
    cv = din("cv", [D]); cctx = din("cctx", [D]); w_mod = din("w_mod", [D, 3072]); b_mod = din("b_mod", [3072])
    g_pre = din("g_pre", [D]); g_post = din("g_post", [D])
    w1 = din("w1", [D, DFF]); w2 = din("w2", [DFF, D]); ident_d = din("ident", [128, 128])
    xo = nc.dram_tensor("xo", [NTK, D], F32, kind="ExternalOutput").ap()
    G = 1
    with ExitStack() as st:
        P = Prog(nc, st)
        sb = lambda name, shape, dt=F32: st.enter_context(nc.sbuf_tensor("s_" + name, shape, dt))
        ps = lambda name, shape, dt=F32: st.enter_context(nc.psum_tensor("p_" + name, shape, dt))
        b_c = P.buf("c")
        ident_f = sb("ident_f", [128, 128]); ident = sb("identb", [128, 128], BF16)
        epsb = sb("epsb", [128, 1]); gpre = sb("gpre", [128, D]); gpost = sb("gpost", [128, D])
        P.dma("sp", ident_f[:], ident_d, writes=[b_c])
        P.dma("sp", gpre[:], g_pre.partition_broadcast(128), writes=[b_c])
        P.dma("sp", gpost[:], g_post.partition_broadcast(128), writes=[b_c])
        P.op("dve", lambda e: e.tensor_copy(out=ident[:], in_=ident_f[:]), reads=[b_c], writes=[b_c])
        P.op("dve", lambda e: e.memset(epsb[:], EPS), reads=[b_c], writes=[b_c])
        w1b = sb("w1b", [128, 8, DFF], BF16); w2b = sb("w2b", [128, 32, D], BF16); b_w1 = P.buf("w1"); b_w2 = P.buf("w2")
        MC = ModCalc(P, nc, sb, ps)
        wflat = MC.wst[:].rearrange("p a n -> p (a n)")
        wst = [wflat[:, 0:2048], wflat[:, 2048:4096]]; b_wst = [MC.b[4], MC.b[4]]
        w1v = w1.rearrange("(k p) n -> p k n", p=128); w2v = w2.rearrange("(k p) n -> p k n", p=128)
        i = 0
        for k in range(8):
            for hh in range(2):
                s = i % 2; i += 1
                P.dma("pool", wst[s], w1v[:, k, hh * 2048:(hh + 1) * 2048], writes=[b_wst[s]])
                P.op("pool", lambda e, k=k, hh=hh, s=s: e.tensor_copy(out=w1b[:, k, hh * 2048:(hh + 1) * 2048], in_=wst[s]), reads=[b_wst[s]], writes=[b_w1])
        for k in range(0, 32, 2):
            s = i % 2; i += 1
            P.dma("pool", wst[s].rearrange("p (a n) -> p a n", a=2), w2v[:, k:k + 2, :], writes=[b_wst[s]])
            P.op("pool", lambda e, k=k, s=s: e.tensor_copy(out=w2b[:, k:k + 2, :], in_=wst[s].rearrange("p (a n) -> p a n", a=2)), reads=[b_wst[s]], writes=[b_w2])
        M3 = sb("M3", [128, 3072]); b_M3 = P.buf("M3")
        xt = [sb("xt%d" % i, [128, D]) for i in range(G)]; b_xt = P.bufs(G, "xt")
        sq = sb("sq", [128, D]); b_sq = P.buf("sq")
        ss = sb("ss", [128, 4]); b_ss = P.buf("ss")
        hb = sb("hb", [128, D], BF16); b_hb = P.buf("hb")
        tmp = sb("tmp", [128, D]); b_tmp = P.buf("tmp")
        hT = [sb("hT%d" % i, [128, 8, G * 128], BF16) for i in range(2)]; b_hT = P.bufs(2, "hT")
        uT = sb("uT", [128, 32, G * 128], BF16); b_uT = P.buf("uT")
        ur = [sb("ur%d" % i, [128, G * 128]) for i in range(2)]; b_ur = P.bufs(2, "ur")
        tps = ps("tps", [128, 8, 128], BF16); b_tps = P.buf("tps")
        ups = [ps("ups%d" % i, [128, G * 128]) for i in range(2)]; b_ups = P.bufs(2, "ups")
        yps = [ps("yps%d" % i, [128, 2, 512]) for i in range(2)]; b_yps = P.bufs(2, "yps")
        ob1 = sb("ob1", [128, D]); ob = [ob1, ob1]; bo1 = P.buf("ob"); b_ob = [bo1, bo1]
        b_out = P.buf("out")
        gi = 0
        for seg, (t0, t1, cvec) in enumerate(((0, ntl, cv), (ntl, NT, cctx))):
            MC.set_c(cvec)
            MC.calc(w_mod, b_mod, 0, 3072, lambda j: M3[:, j * 512:(j + 1) * 512], b_M3)
            P.op("dve", lambda e: e.scalar_tensor_tensor(out=M3[:, 1024:2048], in0=M3[:, 1024:2048], scalar=1.0, in1=gpre[:], op0=ALU.add, op1=ALU.mult), reads=[b_M3, b_c], writes=[b_M3])
            P.op("dve", lambda e: e.tensor_tensor(out=M3[:, 2048:3072], in0=M3[:, 2048:3072], in1=gpost[:], op=ALU.mult), reads=[b_M3, b_c], writes=[b_M3])
            for g0 in range(t0, t1, G):
                tiles = list(range(g0, min(g0 + G, t1)))
                ng = len(tiles); W = ng * 128
                hs = gi % 2; gi += 1
                xs_ = []
                for j, t in enumerate(tiles):
                    s = j
                    xs_.append(s)
                    tok = slice(t * 128, (t + 1) * 128)
                    P.dma("sp", xt[s][:], x[tok, :], writes=[b_xt[s]])
                    P.op("act", lambda e, s=s: e.activation(out=sq[:], in_=xt[s][:], func=AF.Square, scale=1.0 / 32, accum_out=ss[:, 0:1]), reads=[b_xt[s]], writes=[b_sq, b_ss])
                    rstd_from_ss(P, ss[:, 0:1], b_ss, epsb, b_c)
                    P.op("dve", lambda e, s=s: e.scalar_tensor_tensor(out=tmp[:], in0=xt[s][:], scalar=ss[:, 0:1], in1=M3[:, 1024:2048], op0=ALU.mult, op1=ALU.mult), reads=[b_xt[s], b_ss, b_M3], writes=[b_tmp])
                    P.op("dve", lambda e: e.tensor_tensor(out=hb[:], in0=tmp[:], in1=M3[:, 0:1024], op=ALU.add), reads=[b_tmp, b_M3], writes=[b_hb])
                    for k in range(8):
                        P.op("pe", lambda e, k=k: e.transpose(tps[:, k, :], hb[:, k * 128:(k + 1) * 128], ident[:]), reads=[b_hb, b_c], writes=[b_tps], inc=(k == 7))
                    P.op("act", lambda e, j=j, hs=hs: e.copy(out=hT[hs][:, :, j * 128:(j + 1) * 128], in_=tps[:]), reads=[b_tps], writes=[b_hT[hs]])
                for f in range(32):
                    u = f % 2
                    for k in range(8):
                        P.op("pe", lambda e, k=k, f=f, u=u, hs=hs, W=W: e.matmul(ups[u][:, 0:W], lhsT=w1b[:, k, f * 128:(f + 1) * 128], rhs=hT[hs][:, k, 0:W], start=(k == 0), stop=(k == 7)),
                             reads=[b_hT[hs], b_w1], writes=[b_ups[u]], inc=(k == 7))
                    P.op("act", lambda e, u=u, W=W: e.activation(out=ur[u][:, 0:W], in_=ups[u][:, 0:W], func=AF.Relu), reads=[b_ups[u]], writes=[b_ur[u]])
                    P.op("dve", lambda e, u=u, f=f, W=W: e.tensor_tensor(out=uT[:, f, 0:W], in0=ur[u][:, 0:W], in1=ur[u][:, 0:W], op=ALU.mult), reads=[b_ur[u]], writes=[b_uT])
                for j, t in enumerate(tiles):
                    s = xs_[j]
                    y = j % 2
                    tok = slice(t * 128, (t + 1) * 128)
                    for c in range(2):
                        for f in range(32):
                            P.op("pe", lambda e, f=f, c=c, y=y, j=j: e.matmul(yps[y][:, c, :], lhsT=uT[:, f, j * 128:(j + 1) * 128], rhs=w2b[:, f, c * 512:(c + 1) * 512], start=(f == 0), stop=(f == 31)),
                                 reads=[b_uT, b_w2], writes=[b_yps[y]], inc=(f == 31))
                    for c in range(2):
                        P.op("act", lambda e, c=c, y=y: e.activation(out=sq[:, c * 512:(c + 1) * 512], in_=yps[y][:, c, :], func=AF.Square, scale=1.0 / 32, accum_out=ss[:, 1 + c:2 + c]), reads=[b_yps[y]], writes=[b_sq, b_ss])
                    P.op("dve", lambda e: e.tensor_tensor(out=ss[:, 3:4], in0=ss[:, 1:2], in1=ss[:, 2:3], op=ALU.add), reads=[b_ss], writes=[b_ss])
                    rstd_from_ss(P, ss[:, 3:4], b_ss, epsb, b_c)
                    for c in range(2):
                        P.op("dve", lambda e, c=c, y=y: e.scalar_tensor_tensor(out=tmp[:, c * 512:(c + 1) * 512], in0=yps[y][:, c, :], scalar=ss[:, 3:4], in1=M3[:, 2048 + c * 512:2048 + (c + 1) * 512], op0=ALU.mult, op1=ALU.mult),
                             reads=[b_yps[y], b_ss, b_M3], writes=[b_tmp])
                    P.op("pool", lambda e, s=s, y=y: e.tensor_tensor(out=ob[y][:], in0=tmp[:], in1=xt[s][:], op=ALU.add), reads=[b_tmp, b_xt[s]], writes=[b_ob[y]])
                    P.dma("sp", xo[tok, :], ob[y][:], reads=[b_ob[y]], writes=[b_out])
        P.finish("sp", [b_out])
        P.emit()
    return nc


BATCH, SEQ, CTX = 4, 8192, 256
OFF_B, OFF_C, OFF_D = 776, 1544, 2056
_cache = {}
_nl = [0]


def _prog(key, fn):
    if key not in _cache:
        _cache[key] = fn()
    return _cache[key]


def _run(nc, maps, tag):
    t0 = time.time()
    res = run_bass_kernel_spmd(nc, maps, core_ids=list(range(8)))
    _nl[0] += 1
    print("[launch %d] %s %.1fs" % (_nl[0], tag, time.time() - t0), flush=True)
    return res.results


def C_(a):
    return np.ascontiguousarray(a, dtype=np.float32)


def launch_k1(x, ctx, c, c_ctx, w_mod_i, b_mod_i, g_pre_i, w_in_i):
    ntl, ntc = SEQ // 2 // 128, CTX // 2 // 128
    nc = _prog("k1", lambda: build_k1(ntl, ntc))
    ident = np.eye(128, dtype=np.float32)
    maps = []
    for k in range(8):
        b, hf = k // 2, k % 2
        xs = np.concatenate([x[b, hf * SEQ // 2:(hf + 1) * SEQ // 2], ctx[b, hf * CTX // 2:(hf + 1) * CTX // 2]], 0)
        maps.append({"x": C_(xs), "cv": C_(c[b]), "cctx": C_(c_ctx), "w_mod": C_(w_mod_i[:, 0:2048]), "b_mod": C_(b_mod_i[0:2048]),
                     "g_pre": C_(g_pre_i), "w_in": C_(w_in_i), "ident": ident})
    res = _run(nc, maps, "k1")
    proj = np.zeros((BATCH, SEQ, 2568), np.float32)
    projc = np.zeros((BATCH, CTX, 2568), np.float32)
    for k in range(8):
        b, hf = k // 2, k % 2
        o = res[k]["proj"]
        proj[b, hf * SEQ // 2:(hf + 1) * SEQ // 2] = o[:SEQ // 2]
        projc[b, hf * CTX // 2:(hf + 1) * CTX // 2] = o[SEQ // 2:]
    return proj, projc


def launch_kf1(n, w1, b1, f1, w2, b2, f2, w3, b3):
    nc = _prog(("kf1", n), lambda: build_kf1(n))
    res = _run(nc, kf1_inputs(n, C_(w1), C_(b1), C_(f1), C_(w2), C_(b2), C_(f2), C_(w3), C_(b3)), "kf1_%d" % n)
    return kf1_gather(res, n)


def launch_ka(proj, projc, conv_w, conv_b, a_log, dt_bias):
    nc = _prog("ka", lambda: build_ka(CTX, SEQ))
    cst = ka_consts()
    NT = CTX + SEQ
    maps = []
    for k in range(8):
        b, g = k // 2, k % 2
        d = dict(cst)
        seq = [np.concatenate([projc[b], proj[b]], 0), np.concatenate([projc[b][::-1], proj[b][::-1]], 0)]
        xc = slice(256 + g * 128, 256 + (g + 1) * 128); bc = slice(512 + g * 64, 512 + (g + 1) * 64); cc = slice(640 + g * 64, 640 + (g + 1) * 64)
        d["xr"] = C_(np.stack([pad_seq(s[:, xc].T, CTX, SEQ) for s in seq]))
        d["br"] = C_(np.stack([pad_seq(s[:, bc].T, CTX, SEQ) for s in seq]))
        d["cr"] = C_(np.stack([pad_seq(s[:, cc].T, CTX, SEQ) for s in seq]))
        def cwpack(idx):
            w = conv_w[:, idx].T
            bb = conv_b[idx][:, None]
            return C_(np.stack([np.concatenate([w, bb], 1), np.concatenate([w[:, ::-1], bb], 1)]))
        d["cwx"] = cwpack(np.arange(g * 128, (g + 1) * 128))
        d["cwb"] = cwpack(256 + np.arange(g * 64, (g + 1) * 64))
        d["cwc"] = cwpack(384 + np.arange(g * 64, (g + 1) * 64))
        d["dtr"] = C_(np.stack([np.stack([seq[dd][:, 768 + dd * 4 + 2 * g + h] for h in range(2)]) for dd in range(2)]))
        d["alog"] = C_(a_log[:, 2 * g:2 * g + 2]); d["dtb"] = C_(dt_bias[:, 2 * g:2 * g + 2])
        maps.append(d)
    res = _run(nc, maps, "ka")
    mk = lambda: (np.zeros((BATCH, SEQ, 256), np.float32), np.zeros((BATCH, CTX, 256), np.float32))
    yf, yb, xs = mk(), mk(), mk()
    for k in range(8):
        b, g = k // 2, k % 2
        y = res[k]["y"]; x_ = res[k]["xs"]
        cs = slice(g * 128, (g + 1) * 128)
        yf[1][b][:, cs] = y[0, :CTX]; yf[0][b][:, cs] = y[0, CTX:]
        yb[1][b][:, cs] = y[1, :CTX][::-1]; yb[0][b][:, cs] = y[1, CTX:][::-1]
        xs[1][b][:, cs] = x_[:CTX]; xs[0][b][:, cs] = x_[CTX:]
    return yf, yb, xs


def launch_kc(proj, projc, qn, kn, sink):
    nc = _prog("kc", lambda: build_kc(SEQ, CTX))
    cst = _prog("kc_consts", lambda: kc_consts(SEQ))
    maps = []
    for k in range(8):
        b, g = k // 2, k % 2
        d = dict(cst)
        full = np.concatenate([proj[b], projc[b]], 0)
        for br, off in (("w", OFF_C), ("d", OFF_D)):
            d["qT_" + br] = C_(np.stack([full[:, off + (2 * g + h) * 64: off + (2 * g + h + 1) * 64].T for h in range(2)]))
            d["kT_" + br] = C_(full[:, off + 256 + g * 64: off + 256 + (g + 1) * 64].T)
            d["v_" + br] = C_(full[:, off + 384 + g * 64: off + 384 + (g + 1) * 64])
        d["qn"] = C_(qn); d["kn"] = C_(kn); d["sink"] = C_(sink[2 * g:2 * g + 2])
        maps.append(d)
    res = _run(nc, maps, "kc")
    out = {}
    for br in "dw":
        lat = np.zeros((BATCH, SEQ, 256), np.float32); cx = np.zeros((BATCH, CTX, 256), np.float32)
        for k in range(8):
            b, g = k // 2, k % 2
            y = res[k]["y_" + br]
            lat[b][:, g * 128:(g + 1) * 128] = y[:SEQ]; cx[b][:, g * 128:(g + 1) * 128] = y[SEQ:]
        out[br] = (lat, cx)
    return out


def launch_kb(pr, n, kfilt, conv_w, conv_b, hbias):
    nc = _prog(("kb", n), lambda: build_kb(n))
    cst = _prog(("kb_consts", n), lambda: kb_consts(n))
    maps = []
    for k in range(8):
        b, g = k // 2, k % 2
        d = dict(cst)
        chs = [OFF_B + s * 256 + g * 128 + np.arange(128) for s in range(3)]
        uraw = np.stack([pr[b][:, ch].T for ch in chs])
        d["raw"] = kb_pack_raw(C_(uraw), n)
        d["full"] = kb_pack_full(C_(kfilt[:, :, g * 128:(g + 1) * 128]), n)
        cw = np.stack([np.concatenate([conv_w[:, s * 256 + g * 128: s * 256 + (g + 1) * 128].T, conv_b[s * 256 + g * 128: s * 256 + (g + 1) * 128][:, None]], 1) for s in range(3)])
        d["cw"] = C_(cw.reshape(-1)); d["hb"] = C_(hbias[:, g * 128:(g + 1) * 128].reshape(-1))
        maps.append(d)
    res = _run(nc, maps, "kb_%d" % n)
    out = np.zeros((BATCH, 256, n), np.float32)
    for k in range(8):
        b, g = k // 2, k % 2
        y = res[k]["y"]
        out[b, g * 128:(g + 1) * 128] = y.transpose(1, 0, 2).reshape(128, n)
    return out


def _tok_shard(lat, cx, k):
    b, hf = k // 2, k % 2
    return np.concatenate([lat[b, hf * SEQ // 2:(hf + 1) * SEQ // 2], cx[b, hf * CTX // 2:(hf + 1) * CTX // 2]], 0)


def _tok_gather(res, name):
    lat = np.zeros((BATCH, SEQ, 1024), np.float32); cx = np.zeros((BATCH, CTX, 1024), np.float32)
    for k in range(8):
        b, hf = k // 2, k % 2
        o = res[k][name]
        lat[b, hf * SEQ // 2:(hf + 1) * SEQ // 2] = o[:SEQ // 2]; cx[b, hf * CTX // 2:(hf + 1) * CTX // 2] = o[SEQ // 2:]
    return lat, cx


def launch_k3a(x, ctx, yf, yb, xs, z, hy, hyc, yw, yd, c, c_ctx, w_mod_i, b_mod_i, g_post, skip, ssdn, w_out_i):
    ntl, ntc = SEQ // 2 // 128, CTX // 2 // 128
    nc = _prog("k3a", lambda: build_k3a(ntl, ntc))
    ident = np.eye(128, dtype=np.float32)
    maps = []
    for k in range(8):
        b, hf = k // 2, k % 2
        ls = slice(hf * SEQ // 2, (hf + 1) * SEQ // 2); cs = slice(hf * CTX // 2, (hf + 1) * CTX // 2)
        mixT = np.concatenate([
            np.concatenate([hy[b][:, ls], hyc[b][:, cs]], 1),
            np.concatenate([yw[0][b, ls], yw[1][b, cs]], 0).T,
            np.concatenate([yd[0][b, ls], yd[1][b, cs]], 0).T], 0)
        maps.append({"x": C_(_tok_shard(x, ctx, k)), "yf": C_(_tok_shard(yf[0], yf[1], k)), "yb": C_(_tok_shard(yb[0], yb[1], k)),
                     "xs": C_(_tok_shard(xs[0], xs[1], k)), "z": C_(_tok_shard(z[0], z[1], k)), "mixT": C_(mixT),
                     "cv": C_(c[b]), "cctx": C_(c_ctx), "w_mod": C_(w_mod_i[:, 2048:3072]), "b_mod": C_(b_mod_i[2048:3072]),
                     "g_post": C_(g_post), "skip": C_(skip), "ssdn": C_(ssdn), "w_out": C_(w_out_i), "ident": ident})
    res = _run(nc, maps, "k3a")
    return _tok_gather(res, "xo")


def launch_k3b(x, ctx, c, c_ctx, w_mod_i, b_mod_i, g_pre, g_post, w1, w2):
    ntl, ntc = SEQ // 2 // 128, CTX // 2 // 128
    nc = _prog("k3b", lambda: build_k3b(ntl, ntc))
    ident = np.eye(128, dtype=np.float32)
    maps = []
    for k in range(8):
        b = k // 2
        maps.append({"x": C_(_tok_shard(x, ctx, k)), "cv": C_(c[b]), "cctx": C_(c_ctx), "w_mod": C_(w_mod_i[:, 3072:6144]), "b_mod": C_(b_mod_i[3072:6144]),
                     "g_pre": C_(g_pre), "g_post": C_(g_post), "w1": C_(w1), "w2": C_(w2), "ident": ident})
    res = _run(nc, maps, "k3b")
    return _tok_gather(res, "xo")


def forward(x, c, ctx, c_ctx, w_mod, b_mod, norm_mix_pre, norm_mix_post, norm_mlp_pre, norm_mlp_post,
            w_in, w_out, ssd_conv_w, ssd_conv_b, ssd_a_log, ssd_dt_bias, ssd_d, ssd_norm,
            hy_conv_w, hy_conv_b, hy_w1, hy_b1, hy_freq1, hy_w2, hy_b2, hy_freq2, hy_w3, hy_b3, hy_bias,
            attn_sink, q_norm, k_norm, mlp_w1, mlp_w2, depth=2, dbg=None):
    A = lambda a: np.asarray(a, dtype=np.float32)
    x = A(x); ctx = A(ctx); c = A(c); c_ctx = A(c_ctx)
    for i in range(depth):
        need_ctx = i < depth - 1
        proj, projc = launch_k1(x, ctx, c, c_ctx, A(w_mod[i]), A(b_mod[i]), A(norm_mix_pre[i]), A(w_in[i]))
        hyf = (A(hy_w1[i]), A(hy_b1[i]), A(hy_freq1[i]), A(hy_w2[i]), A(hy_b2[i]), A(hy_freq2[i]), A(hy_w3[i]), A(hy_b3[i]))
        k_lat = launch_kf1(SEQ, *hyf)
        hy = launch_kb(proj, SEQ, k_lat, A(hy_conv_w[i]), A(hy_conv_b[i]), A(hy_bias[i]))
        if need_ctx:
            k_ctx = launch_kf1(CTX, *hyf)
            hyc = launch_kb(projc, CTX, k_ctx, A(hy_conv_w[i]), A(hy_conv_b[i]), A(hy_bias[i]))
        else:
            hyc = np.zeros((BATCH, 256, CTX), np.float32)
        yf, yb, xs = launch_ka(proj, projc, A(ssd_conv_w[i]), A(ssd_conv_b[i]), A(ssd_a_log[i]), A(ssd_dt_bias[i]))
        att = launch_kc(proj, projc, A(q_norm[i]), A(k_norm[i]), A(attn_sink[i]))
        z = (proj[:, :, 0:256], projc[:, :, 0:256])
        if dbg is not None:
            dbg.update({"proj": proj, "projc": projc, "hy": hy, "hyc": hyc, "yf": yf, "yb": yb, "xs": xs, "att": att})
        x1, ctx1 = launch_k3a(x, ctx, yf, yb, xs, z, hy, hyc, att["w"], att["d"], c, c_ctx, A(w_mod[i]), A(b_mod[i]),
                              A(norm_mix_post[i]), np.repeat(A(ssd_d[i]), 64), A(ssd_norm[i]), A(w_out[i]))
        x2, ctx2 = launch_k3b(x1, ctx1, c, c_ctx, A(w_mod[i]), A(b_mod[i]), A(norm_mlp_pre[i]), A(norm_mlp_post[i]), A(mlp_w1[i]), A(mlp_w2[i]))
        if dbg is not None:
            dbg.update({"x1": x1, "ctx1": ctx1, "x2": x2, "ctx2": ctx2})
        x = x2
        if need_ctx:
            ctx = ctx2
    return x


def kernel(**inputs):
    _nl[0] = 0
    out = forward(**inputs, depth=2)
    return np.ascontiguousarray(out, dtype=np.float32)
```

```python
import os
import sys
import time
import math
from contextlib import ExitStack
import numpy as np
import concourse.bass as bass
import concourse.mybir as mybir
from concourse.bass_utils import run_bass_kernel_spmd


F32 = mybir.dt.float32
BF16 = mybir.dt.bfloat16
ALU = mybir.AluOpType
AF = mybir.ActivationFunctionType
AX = mybir.AxisListType


class Buf:
    __slots__ = ("name", "w", "r")

    def __init__(self, name):
        self.name = name
        self.w = None
        self.r = []


class Prog:
    ENG = ("pe", "act", "dve", "pool", "sp")
    NDMA = 8

    def __init__(self, nc, stack):
        self.nc = nc
        self.ops = {e: [] for e in self.ENG}
        self.sems = []
        self.semval = []
        self.known = {e: {} for e in self.ENG}
        self.esem = {}
        for e in ("pe", "act", "dve", "pool"):
            self.esem[e] = self._newsem(stack, "s_" + e)
        self.dsem = {}
        self.dcnt = {}
        for e in ("sp", "act", "pool"):
            self.dsem[e] = [self._newsem(stack, "d_%s%d" % (e, i)) for i in range(self.NDMA)]
            self.dcnt[e] = 0
        self.nbuf = 0

    def _newsem(self, stack, name):
        h = stack.enter_context(self.nc.semaphore(name))
        self.sems.append(h)
        self.semval.append(0)
        return len(self.sems) - 1

    def buf(self, name=None):
        self.nbuf += 1
        return Buf(name or "b%d" % self.nbuf)

    def bufs(self, n, name="b"):
        return [self.buf("%s%d" % (name, i)) for i in range(n)]

    def _deps(self, eng, reads, writes):
        need = {}
        def add(d):
            if d is None:
                return
            s, v = d
            if need.get(s, 0) < v:
                need[s] = v
        for b in reads:
            add(b.w)
        for b in writes:
            add(b.w)
            for d in b.r:
                add(d)
        kn = self.known[eng]
        waits = []
        own = self.esem.get(eng)
        for s, v in need.items():
            if s == own and v > self.semval[s]:
                continue
            if kn.get(s, 0) < v:
                kn[s] = v
                waits.append((s, v))
        return waits

    def op(self, eng, fn, reads=(), writes=(), inc=True):
        waits = self._deps(eng, reads, writes)
        s = self.esem[eng]
        if inc:
            self.semval[s] += 1
            tok = (s, self.semval[s])
            self.ops[eng].append((fn, waits, (s, 1)))
        else:
            tok = (s, self.semval[s] + 1)
            self.ops[eng].append((fn, waits, None))
        for b in reads:
            b.r.append(tok)
        for b in writes:
            b.w = tok
            b.r = []
        return tok

    def dma(self, q, out, in_, reads=(), writes=(), **kw):
        waits = self._deps(q, reads, writes)
        i = self.dcnt[q]
        self.dcnt[q] += 1
        s = self.dsem[q][i % self.NDMA]
        prev = self.semval[s]
        if prev > 0 and self.known[q].get(s, 0) < prev:
            self.known[q][s] = prev
            waits.append((s, prev))
        self.semval[s] += 16
        tok = (s, self.semval[s])
        self.ops[q].append((lambda e: e.dma_start(out=out, in_=in_, **kw), waits, (s, 16)))
        for b in reads:
            b.r.append(tok)
        for b in writes:
            b.w = tok
            b.r = []
        return tok

    def finish(self, eng="sp", bufs=()):
        waits = self._deps(eng, bufs, ())
        for q in self.dsem:
            for s in self.dsem[q]:
                v = self.semval[s]
                if v > 0 and self.known[eng].get(s, 0) < v:
                    self.known[eng][s] = v
                    waits.append((s, v))
        self.ops[eng].append((None, waits, None))

    def emit(self):
        nc = self.nc
        hmap = {"pe": "tensor", "act": "scalar", "dve": "vector", "pool": "gpsimd", "sp": "sync"}
        with nc.Block() as block:
            for e in self.ENG:
                ops = self.ops[e]
                sems = self.sems

                def body(h, ops=ops):
                    for fn, waits, inc in ops:
                        for s, v in waits:
                            h.wait_ge(sems[s], v)
                        if fn is not None:
                            ins = fn(h)
                            if inc is not None:
                                ins.then_inc(sems[inc[0]], inc[1])
                getattr(block, hmap[e])(body)


D = 1024
DIN = 2568
EPS = 1e-6


def load_bcast_row(P, q, dst_ap, src_row_ap, n, wbuf):
    P.dma(q, dst_ap, src_row_ap.partition_broadcast(128), writes=[wbuf])


def mod_scratch(P, nc, st, ncols):
    S = {}
    S["c_sb"] = st.enter_context(nc.sbuf_tensor("c_sb", [128, 8], F32))
    S["c_sg"] = st.enter_context(nc.sbuf_tensor("c_sg", [128, 8], F32))
    S["c_bc"] = st.enter_context(nc.sbuf_tensor("c_bc", [128, 8, 128], F32))
    S["bm"] = st.enter_context(nc.sbuf_tensor("bm", [128, ncols], F32))
    S["wst"] = [st.enter_context(nc.sbuf_tensor("wmst%d" % i, [128, 8, 512], F32)) for i in range(2)]
    S["ps"] = [st.enter_context(nc.psum_tensor("modps%d" % i, [128, 512], F32)) for i in range(2)]
    S["b"] = P.bufs(4, "modb")
    S["b_w"] = P.bufs(2, "wmst")
    S["b_ps"] = P.bufs(2, "modps")
    return S


def build_mod(P, nc, S, cvec, w_mod, b_mod, col0, ncols, out_tile, out_buf):
    c_sb, c_sg, c_bc, bm, wst, ps = S["c_sb"], S["c_sg"], S["c_bc"], S["bm"], S["wst"], S["ps"]
    b_c, b_cs, b_bc, b_bm = S["b"]
    b_w, b_ps = S["b_w"], S["b_ps"]
    P.dma("sp", c_sb[:], cvec.rearrange("(k p) -> p k", p=128), writes=[b_c], allow_slow_non_contiguous=True)
    P.dma("sp", bm[:, 0:ncols], b_mod[col0:col0 + ncols].partition_broadcast(128), writes=[b_bm])
    P.op("act", lambda e: e.activation(out=c_sg[:], in_=c_sb[:], func=AF.Sigmoid), reads=[b_c], writes=[b_cs])
    P.op("dve", lambda e: e.tensor_tensor(out=c_sg[:], in0=c_sg[:], in1=c_sb[:], op=ALU.mult), reads=[b_c, b_cs], writes=[b_cs])
    P.op("dve", lambda e: e.tensor_copy(out=c_bc[:], in_=c_sg[:].unsqueeze(2).to_broadcast([128, 8, 128])), reads=[b_cs], writes=[b_bc])
    wv = w_mod.rearrange("(k p) n -> p k n", p=128)
    for j in range(ncols // 512):
        s = j % 2
        P.dma("sp", wst[s][:], wv[:, :, col0 + j * 512: col0 + (j + 1) * 512], writes=[b_w[s]])
        for k in range(8):
            P.op("pe", lambda e, k=k, s=s: e.matmul(ps[s][:], lhsT=c_bc[:, k, :], rhs=wst[s][:, k, :], start=(k == 0), stop=(k == 7)),
                 reads=[b_bc, b_w[s]], writes=[b_ps[s]])
        P.op("dve", lambda e, j=j, s=s: e.tensor_tensor(out=out_tile[:, j * 512:(j + 1) * 512], in0=ps[s][:], in1=bm[:, j * 512:(j + 1) * 512], op=ALU.add),
             reads=[b_ps[s], b_bm], writes=[out_buf])


def build_k1(ntiles_lat, ntiles_ctx):
    nc = bass.Bass("TRN2", target_bir_lowering=False)
    NT = ntiles_lat + ntiles_ctx
    x = nc.dram_tensor("x", [NT * 128, D], F32, kind="ExternalInput").ap()
    cv = nc.dram_tensor("cv", [D], F32, kind="ExternalInput").ap()
    cctx = nc.dram_tensor("cctx", [D], F32, kind="ExternalInput").ap()
    w_mod = nc.dram_tensor("w_mod", [D, 2048], F32, kind="ExternalInput").ap()
    b_mod = nc.dram_tensor("b_mod", [2048], F32, kind="ExternalInput").ap()
    g_pre = nc.dram_tensor("g_pre", [D], F32, kind="ExternalInput").ap()
    w_in = nc.dram_tensor("w_in", [D, DIN], F32, kind="ExternalInput").ap()
    proj = nc.dram_tensor("proj", [NT * 128, DIN], F32, kind="ExternalOutput").ap()
    ident_d = nc.dram_tensor("ident", [128, 128], F32, kind="ExternalInput").ap()
    with ExitStack() as st:
        P = Prog(nc, st)
        sb = lambda name, shape, dt=F32: st.enter_context(nc.sbuf_tensor(name, shape, dt))
        ident_f = sb("ident_f", [128, 128])
        ident = sb("ident_b", [128, 128], BF16)
        b_id = P.buf("ident")
        P.dma("sp", ident_f[:], ident_d, writes=[b_id])
        P.op("dve", lambda e: e.tensor_copy(out=ident[:], in_=ident_f[:]), reads=[b_id], writes=[b_id])
        modl = sb("modl", [128, 2048]); b_modl = P.buf("modl")
        modc = sb("modc", [128, 2048]); b_modc = P.buf("modc")
        if True:
            st2 = st
            MS = mod_scratch(P, nc, st, 2048)
            build_mod(P, nc, MS, cv, w_mod, b_mod, 0, 2048, modl, b_modl)
            build_mod(P, nc, MS, cctx, w_mod, b_mod, 0, 2048, modc, b_modc)
            gp = sb("gp", [128, D]); b_gp = P.buf("gp")
            P.dma("sp", gp[:], g_pre.partition_broadcast(128), writes=[b_gp])
            for m, bm_ in ((modl, b_modl), (modc, b_modc)):
                P.op("dve", lambda e, m=m: e.scalar_tensor_tensor(out=m[:, 1024:2048], in0=m[:, 1024:2048], scalar=1.0, in1=gp[:], op0=ALU.add, op1=ALU.mult),
                     reads=[bm_, b_gp], writes=[bm_])
            w_bf = sb("w_bf", [128, 8, DIN], BF16); b_wbf = P.buf("wbf")
            wst = [st2.enter_context(nc.sbuf_tensor("wst%d" % i, [128, DIN], F32)) for i in range(2)]
            b_wst = P.bufs(2, "wst")
            wv = w_in.rearrange("(k p) n -> p k n", p=128)
            for k in range(8):
                s = k % 2
                P.dma("pool", wst[s][:], wv[:, k, :], writes=[b_wst[s]])
                P.op("pool", lambda e, k=k, s=s: e.tensor_copy(out=w_bf[:, k, :], in_=wst[s][:]), reads=[b_wst[s]], writes=[b_wbf])
        epsb = sb("epsb", [128, 1])
        b_eps = P.buf("eps")
        P.op("dve", lambda e: e.memset(epsb[:], EPS), writes=[b_eps])
        NB = 2
        xt = [sb("xt%d" % i, [128, D]) for i in range(NB)]; b_xt = P.bufs(NB, "xt")
        sq = sb("sq", [128, D]); b_sq = P.buf("sq")
        ss = [sb("ss%d" % i, [128, 1]) for i in range(NB)]; b_ss = P.bufs(NB, "ss")
        hb = [sb("hb%d" % i, [128, D], BF16) for i in range(NB)]; b_hb = P.bufs(NB, "hb")
        hT = [sb("hT%d" % i, [128, 8, 128], BF16) for i in range(NB)]; b_hT = P.bufs(NB, "hT")
        tps = [st.enter_context(nc.psum_tensor("tps%d" % i, [128, 8, 128], BF16)) for i in range(2)]; b_tps = P.bufs(2, "tps")
        ops_ = [st.enter_context(nc.psum_tensor("ops%d" % i, [128, 512], F32)) for i in range(4)]; b_ops = P.bufs(4, "ops")
        ob = [sb("ob%d" % i, [128, DIN]) for i in range(NB)]; b_ob = P.bufs(NB, "ob")
        b_out = P.buf("out")
        colch = [(c0, min(512, DIN - c0)) for c0 in range(0, DIN, 512)]
        pi = 0
        for t in range(NT):
            s = t % NB
            mod = modl if t < ntiles_lat else modc
            bmod = b_modl if t < ntiles_lat else b_modc
            P.dma("sp", xt[s][:], x[t * 128:(t + 1) * 128, :], writes=[b_xt[s]])
            P.op("act", lambda e, s=s: e.activation(out=sq[:], in_=xt[s][:], func=AF.Square, scale=float(D ** -0.5), accum_out=ss[s][:]),
                 reads=[b_xt[s]], writes=[b_sq, b_ss[s]])
            P.op("act", lambda e, s=s: e.activation(out=ss[s][:], in_=ss[s][:], func=AF.Sqrt, bias=epsb[:, 0:1]),
                 reads=[b_ss[s], b_eps], writes=[b_ss[s]])
            P.op("dve", lambda e, s=s: e.reciprocal(out=ss[s][:], in_=ss[s][:]),
                 reads=[b_ss[s]], writes=[b_ss[s]])
            P.op("dve", lambda e, s=s, mod=mod: e.scalar_tensor_tensor(out=xt[s][:], in0=xt[s][:], scalar=ss[s][:, 0:1], in1=mod[:, 1024:2048], op0=ALU.mult, op1=ALU.mult),
                 reads=[b_xt[s], b_ss[s], bmod], writes=[b_xt[s]])
            P.op("dve", lambda e, s=s, mod=mod: e.tensor_tensor(out=hb[s][:], in0=xt[s][:], in1=mod[:, 0:1024], op=ALU.add),
                 reads=[b_xt[s], bmod], writes=[b_hb[s]])
            tp = t % 2
            for k in range(8):
                P.op("pe", lambda e, k=k, s=s, tp=tp: e.transpose(tps[tp][:, k, :], hb[s][:, k * 128:(k + 1) * 128], ident[:]),
                     reads=[b_hb[s], b_id], writes=[b_tps[tp]], inc=(k == 7))
            P.op("act", lambda e, s=s, tp=tp: e.copy(out=hT[s][:], in_=tps[tp][:]), reads=[b_tps[tp]], writes=[b_hT[s]])
            for (c0, cw) in colch:
                p = pi % 4; pi += 1
                for k in range(8):
                    P.op("pe", lambda e, k=k, s=s, p=p, c0=c0, cw=cw: e.matmul(ops_[p][:, 0:cw], lhsT=hT[s][:, k, :], rhs=w_bf[:, k, c0:c0 + cw], start=(k == 0), stop=(k == 7)),
                         reads=[b_hT[s], b_wbf], writes=[b_ops[p]], inc=(k == 7))
                eng = "act" if (pi % 2) else "dve"
                if eng == "act":
                    P.op("act", lambda e, s=s, p=p, c0=c0, cw=cw: e.copy(out=ob[s][:, c0:c0 + cw], in_=ops_[p][:, 0:cw]), reads=[b_ops[p]], writes=[b_ob[s]])
                else:
                    P.op("dve", lambda e, s=s, p=p, c0=c0, cw=cw: e.tensor_copy(out=ob[s][:, c0:c0 + cw], in_=ops_[p][:, 0:cw]), reads=[b_ops[p]], writes=[b_ob[s]])
            P.dma("sp", proj[t * 128:(t + 1) * 128, :], ob[s][:], reads=[b_ob[s]], writes=[b_out])
        P.finish("sp", [b_out])
        P.emit()
    return nc


EPS = 1e-6
HD = 64


def build_kc(NL, NC):
    nc = bass.Bass("TRN2", target_bir_lowering=False)
    NT = NL + NC
    nlt, nct, nkt = NL // 128, NC // 128, NT // 128
    din = lambda n, s: nc.dram_tensor(n, s, F32, kind="ExternalInput").ap()
    qT = {b: din("qT_" + b, [2, 64, NT]) for b in "dw"}
    kT = {b: din("kT_" + b, [64, NT]) for b in "dw"}
    vv = {b: din("v_" + b, [NT, 64]) for b in "dw"}
    cos_d = din("cos2", [64, NL]); sin_d = din("sin2", [64, NL])
    Rm_d = din("Rm", [64, 64]); ones_d = din("ones64", [64, 64]); ident_d = din("ident", [128, 128])
    mprev_d = din("mprev", [128, 128]); mnext_d = din("mnext", [128, 128])
    qn_d = din("qn", [64]); kn_d = din("kn", [64]); sink_d = din("sink", [2])
    y = {b: nc.dram_tensor("y_" + b, [NT, 128], F32, kind="ExternalOutput").ap() for b in "dw"}
    with ExitStack() as st:
        P = Prog(nc, st)
        sb = lambda name, shape, dt=F32: st.enter_context(nc.sbuf_tensor("s_" + name, shape, dt))
        ps = lambda name, shape, dt=F32: st.enter_context(nc.psum_tensor("p_" + name, shape, dt))
        b_c = P.buf("consts")
        cos2 = sb("cos2", [64, NL]); sin2 = sb("sin2", [64, NL])
        Rm = sb("Rm", [64, 64]); ones64 = sb("ones64", [64, 64]); ident = sb("ident", [128, 128])
        mpn_f = sb("mpn_f", [128, 2, 128]); mpn = sb("mpn", [128, 2, 128], BF16)
        gq = sb("gq", [64, 1]); gk = sb("gk", [64, 1]); sk = sb("sk", [128, 2]); epsb = sb("epsb", [128, 1])
        P.dma("sp", cos2[:], cos_d, writes=[b_c]); P.dma("sp", sin2[:], sin_d, writes=[b_c])
        P.dma("sp", Rm[:], Rm_d, writes=[b_c]); P.dma("sp", ones64[:], ones_d, writes=[b_c]); P.dma("sp", ident[:], ident_d, writes=[b_c])
        P.dma("sp", mpn_f[:, 0, :], mprev_d, writes=[b_c]); P.dma("sp", mpn_f[:, 1, :], mnext_d, writes=[b_c])
        P.dma("sp", gq[:], qn_d.rearrange("(p o) -> p o", o=1), writes=[b_c]); P.dma("sp", gk[:], kn_d.rearrange("(p o) -> p o", o=1), writes=[b_c])
        P.dma("sp", sk[:], sink_d.partition_broadcast(128), writes=[b_c])
        P.op("dve", lambda e: e.tensor_copy(out=mpn[:], in_=mpn_f[:]), reads=[b_c], writes=[b_c])
        P.op("dve", lambda e: e.memset(epsb[:], EPS), reads=[b_c], writes=[b_c])
        P.op("act", lambda e: e.activation(out=sk[:], in_=sk[:], func=AF.Exp), reads=[b_c], writes=[b_c])
        QT = sb("QT", [64, 2, NT], BF16); b_QT = P.buf("QT")
        KT = sb("KT", [64, NT], BF16); b_KT = P.buf("KT")
        VA = sb("VA", [128, nkt, 65], BF16); b_VA = P.buf("VA")
        vst = sb("vst", [128, nkt, 64]); b_vst = P.buf("vst")
        stg = [sb("stg%d" % i, [64, 512]) for i in range(2)]; b_stg = P.bufs(2, "stg")
        sqb = sb("sqb", [64, 512]); b_sq = P.buf("sqb")
        rsb = sb("rsb", [64, 512]); b_rs = P.buf("rsb")
        t1b = sb("t1b", [64, 512]); b_t1 = P.buf("t1b")
        t2b = sb("t2b", [64, 512]); b_t2 = P.buf("t2b")
        pps = [ps("pps%d" % i, [64, 512]) for i in range(2)]; b_pps = P.bufs(2, "pps")
        sps = [ps("sps%d" % i, [128, 512]) for i in range(3)]; b_sps = P.bufs(3, "sps")
        ops_ = [ps("ops%d" % i, [128, 512]) for i in range(2)]; b_ops = P.bufs(2, "ops")
        tps = ps("tps", [128, 4, 128]); b_tps = P.buf("tps")
        ptb = [sb("ptb%d" % i, [128, 512], BF16) for i in range(3)]; b_pt = P.bufs(3, "ptb")
        osb = sb("osb", [65, 512]); b_osb = P.buf("osb")
        rd = sb("rd", [128, 4, 1]); b_rd = P.buf("rd")
        otb = [sb("otb%d" % i, [128, 4, 64]) for i in range(2)]; b_ot = P.bufs(2, "otb")
        b_y = P.buf("y")
        cnt = {"stg": 0, "s": 0, "o": 0, "ot": 0, "pp": 0}

        def prep(src_ap, dst_ap, ncols, col0, norm_gain, rope):
            for c0 in range(0, ncols, 512):
                cw = min(512, ncols - c0)
                s = cnt["stg"] % 2; cnt["stg"] += 1
                P.dma("sp", stg[s][:, 0:cw], src_ap[:, c0:c0 + cw], writes=[b_stg[s]])
                cur = stg[s]; bcur = b_stg[s]
                if norm_gain is not None:
                    pp = cnt["pp"] % 2; cnt["pp"] += 1
                    P.op("act", lambda e, s=s, cw=cw: e.activation(out=sqb[:, 0:cw], in_=stg[s][:, 0:cw], func=AF.Square), reads=[b_stg[s]], writes=[b_sq])
                    P.op("pe", lambda e, pp=pp, cw=cw: e.matmul(pps[pp][:, 0:cw], lhsT=ones64[:], rhs=sqb[:, 0:cw], start=True, stop=True), reads=[b_sq, b_c], writes=[b_pps[pp]])
                    P.op("act", lambda e, pp=pp, cw=cw: e.activation(out=rsb[:, 0:cw], in_=pps[pp][:, 0:cw], func=AF.Sqrt, bias=epsb[0:64, 0:1], scale=1.0 / 64), reads=[b_pps[pp], b_c], writes=[b_rs])
                    P.op("dve", lambda e, cw=cw: e.reciprocal(out=rsb[:, 0:cw], in_=rsb[:, 0:cw]), reads=[b_rs], writes=[b_rs])
                    P.op("dve", lambda e, s=s, cw=cw, g=norm_gain: e.scalar_tensor_tensor(out=stg[s][:, 0:cw], in0=stg[s][:, 0:cw], scalar=g[:, 0:1], in1=rsb[:, 0:cw], op0=ALU.mult, op1=ALU.mult),
                         reads=[b_stg[s], b_rs, b_c], writes=[b_stg[s]])
                if rope:
                    pp = cnt["pp"] % 2; cnt["pp"] += 1
                    P.op("pe", lambda e, pp=pp, s=s, cw=cw: e.matmul(pps[pp][:, 0:cw], lhsT=Rm[:], rhs=stg[s][:, 0:cw], start=True, stop=True), reads=[b_stg[s], b_c], writes=[b_pps[pp]])
                    P.op("pool", lambda e, s=s, cw=cw, c0=c0: e.tensor_tensor(out=t1b[:, 0:cw], in0=stg[s][:, 0:cw], in1=cos2[:, col0 + c0:col0 + c0 + cw], op=ALU.mult), reads=[b_stg[s], b_c], writes=[b_t1])
                    P.op("dve", lambda e, pp=pp, cw=cw, c0=c0: e.tensor_tensor(out=t2b[:, 0:cw], in0=pps[pp][:, 0:cw], in1=sin2[:, col0 + c0:col0 + c0 + cw], op=ALU.mult), reads=[b_pps[pp], b_c], writes=[b_t2])
                    P.op("dve", lambda e, cw=cw, c0=c0: e.tensor_tensor(out=dst_ap[:, c0:c0 + cw], in0=t1b[:, 0:cw], in1=t2b[:, 0:cw], op=ALU.add), reads=[b_t1, b_t2], writes=[dst_buf[0]])
                else:
                    P.op("dve", lambda e, s=s, cw=cw, c0=c0: e.tensor_copy(out=dst_ap[:, c0:c0 + cw], in_=stg[s][:, 0:cw]), reads=[b_stg[s]], writes=[dst_buf[0]])

        dst_buf = [None]

        def attn_group(qcols, N, ktiles, sink_ap, out_cb):
            o = cnt["o"] % 2; cnt["o"] += 1
            nk = len(ktiles)
            def s_mm(i):
                kt = ktiles[i][0]
                s = (base + i) % 3
                P.op("pe", lambda e, s=s, kt=kt: e.matmul(sps[s][:, 0:N], lhsT=kt, rhs=qcols, start=True, stop=True), reads=[b_KT, b_QT], writes=[b_sps[s]])
            base = cnt["s"]; cnt["s"] += nk
            s_mm(0)
            for i, (kt, va, mask) in enumerate(ktiles):
                s = (base + i) % 3
                if i + 1 < nk:
                    s_mm(i + 1)
                P.op("act", lambda e, s=s: e.activation(out=ptb[s][:, 0:N], in_=sps[s][:, 0:N], func=AF.Exp, scale=0.125), reads=[b_sps[s]], writes=[b_pt[s]])
                if mask is not None:
                    P.op("dve", lambda e, s=s, mask=mask: e.tensor_tensor(out=ptb[s][:, 0:N], in0=ptb[s][:, 0:N], in1=mask, op=ALU.mult), reads=[b_pt[s], b_c], writes=[b_pt[s]])
                P.op("pe", lambda e, s=s, va=va, i=i: e.matmul(ops_[o][0:65, 0:N], lhsT=va, rhs=ptb[s][:, 0:N], start=(i == 0), stop=(i == nk - 1)), reads=[b_pt[s], b_VA], writes=[b_ops[o]], inc=(i == nk - 1))
            nj = N // 128
            P.op("act", lambda e: e.copy(out=osb[:, 0:N], in_=ops_[o][0:65, 0:N]), reads=[b_ops[o]], writes=[b_osb])
            for j in range(nj):
                P.op("pe", lambda e, j=j: e.transpose(tps[:, j, 0:65], osb[0:65, j * 128:(j + 1) * 128], ident[0:65, 0:65]), reads=[b_osb, b_c], writes=[b_tps], inc=(j == nj - 1))
            if sink_ap is not None:
                P.op("dve", lambda e: e.tensor_tensor(out=rd[:, 0:nj, :], in0=tps[:, 0:nj, 64:65], in1=sink_ap, op=ALU.add), reads=[b_tps, b_c], writes=[b_rd])
                P.op("dve", lambda e: e.reciprocal(out=rd[:, 0:nj, :], in_=rd[:, 0:nj, :]), reads=[b_rd], writes=[b_rd])
            else:
                P.op("dve", lambda e: e.reciprocal(out=rd[:, 0:nj, :], in_=tps[:, 0:nj, 64:65]), reads=[b_tps], writes=[b_rd])
            t = cnt["ot"] % 2; cnt["ot"] += 1
            P.op("dve", lambda e, t=t: e.tensor_tensor(out=otb[t][:, 0:nj, :], in0=tps[:, 0:nj, 0:64], in1=rd[:, 0:nj, :].to_broadcast([128, nj, 64]), op=ALU.mult), reads=[b_tps, b_rd], writes=[b_ot[t]])
            out_cb(otb[t], b_ot[t])

        STAGE = int(os.environ.get("STAGE", "9"))
        for br in "dw":
            dst_buf[0] = b_QT
            for h in range(2):
                prep(qT[br][h][:, 0:NL], QT[:, h, 0:NL], NL, 0, gq if br == "d" else None, True)
                prep(qT[br][h][:, NL:NT], QT[:, h, NL:NT], NC, 0, gq if br == "d" else None, False)
            dst_buf[0] = b_KT
            prep(kT[br][:, 0:NL], KT[:, 0:NL], NL, 0, gk if br == "d" else None, True)
            prep(kT[br][:, NL:NT], KT[:, NL:NT], NC, 0, gk if br == "d" else None, False)
            P.dma("sp", vst[:], vv[br].rearrange("(t p) d -> p t d", p=128), writes=[b_vst])
            P.op("pool", lambda e: e.memset(VA[:, :, 64:65], 1.0), writes=[b_VA])
            P.op("pool", lambda e: e.tensor_copy(out=VA[:, :, 0:64], in_=vst[:]), reads=[b_vst], writes=[b_VA])
            yb = y[br]
            ctx_tiles = [(KT[:, NL + c * 128: NL + (c + 1) * 128], VA[:, nlt + c, :], None) for c in range(nct)]
            if STAGE < 2:
                P.dma("sp", yb[0:64, 0:64], stg[0][:, 0:64], reads=[b_stg[0]], writes=[b_y])
                continue
            if br == "d" and STAGE != 3:
                all_tiles = [(KT[:, c * 128:(c + 1) * 128], VA[:, c, :], None) for c in range(nkt)]
                for h in range(2):
                    for q0 in range(0, NL, 512):
                        def cb(ot, bo, h=h, q0=q0):
                            P.dma("sp", yb[q0:q0 + 512, h * 64:(h + 1) * 64].rearrange("(j p) d -> p j d", p=128), ot[:, 0:4, :], reads=[bo], writes=[b_y])
                        attn_group(QT[:, h, q0:q0 + 512], 512, all_tiles, None, cb)
            elif br == "w" and STAGE >= 3:
                for n in range(nlt):
                    tiles = []
                    if n > 0:
                        tiles.append((KT[:, (n - 1) * 128:n * 128], VA[:, n - 1, :], mpn[:, 0, :]))
                    tiles.append((KT[:, n * 128:(n + 1) * 128], VA[:, n, :], None))
                    if n < nlt - 1:
                        tiles.append((KT[:, (n + 1) * 128:(n + 2) * 128], VA[:, n + 1, :], mpn[:, 1, :]))
                    tiles += ctx_tiles
                    for h in range(2):
                        def cb(ot, bo, n=n, h=h):
                            P.dma("sp", yb[n * 128:(n + 1) * 128, h * 64:(h + 1) * 64], ot[:, 0, :], reads=[bo], writes=[b_y])
                        attn_group(QT[:, h, n * 128:(n + 1) * 128], 128, tiles, sk[:, h:h + 1].unsqueeze(1), cb)
            for h in range(2 if STAGE >= 4 else 0):
                def cb(ot, bo, h=h):
                    P.dma("sp", yb[NL:NT, h * 64:(h + 1) * 64].rearrange("(j p) d -> p j d", p=128), ot[:, 0:nct, :], reads=[bo], writes=[b_y])
                snk = sk[:, h:h + 1].unsqueeze(1).to_broadcast([128, nct, 1]) if br == "w" else None
                attn_group(QT[:, h, NL:NT], NC, ctx_tiles, snk, cb)
        P.finish("sp", [b_y])
        P.emit()
    return nc


def rope_tables_np(NL, GW=64):
    t = np.arange(NL)
    row = (t // GW).astype(np.float32); col = (t % GW).astype(np.float32)
    inv = (10000.0 ** (-np.arange(16, dtype=np.float32) / 16)).astype(np.float32)
    ang = np.concatenate([row[:, None] * inv, col[:, None] * inv], -1)
    return np.cos(ang).astype(np.float32), np.sin(ang).astype(np.float32)


def kc_consts(NL):
    cos, sin = rope_tables_np(NL)
    cos2 = np.ascontiguousarray(np.concatenate([cos, cos], -1).T)
    sin2 = np.ascontiguousarray(np.concatenate([sin, sin], -1).T)
    Rm = np.zeros((64, 64), np.float32)
    for m in range(32):
        Rm[m + 32, m] = -1.0
        Rm[m, m + 32] = 1.0
    kl = np.arange(128)[:, None]; ql = np.arange(128)[None, :]
    return {"cos2": cos2, "sin2": sin2, "Rm": Rm, "ones64": np.ones((64, 64), np.float32), "ident": np.eye(128, dtype=np.float32),
            "mprev": (kl >= ql).astype(np.float32), "mnext": (kl <= ql).astype(np.float32)}


NEG = -30000.0


def build_ka(NC_, NL):
    nc = bass.Bass("TRN2", target_bir_lowering=False)
    NT = NC_ + NL
    nt = NT // 128
    NP = NT + 8
    din = lambda n, s: nc.dram_tensor(n, s, F32, kind="ExternalInput").ap()
    xr = din("xr", [2, 128, NP]); br = din("br", [2, 64, NP]); cr = din("cr", [2, 64, NP])
    cwx = din("cwx", [2, 128, 6]); cwb = din("cwb", [2, 64, 6]); cwc = din("cwc", [2, 64, 6])
    dtr = din("dtr", [2, 2, NT])
    alog = din("alog", [2, 2]); dtb = din("dtb", [2, 2])
    U_d = din("U", [128, 128]); ones_d = din("ones", [128, 128]); SL_d = din("SL", [128, 128]); ident_d = din("ident", [128, 128])
    mask_d = din("masks", [4, 128, 512])
    yo = nc.dram_tensor("y", [2, NT, 128], F32, kind="ExternalOutput").ap()
    xso = nc.dram_tensor("xs", [NT, 128], F32, kind="ExternalOutput").ap()
    scr = nc.dram_tensor("scr", [4, NT], F32, kind="ExternalOutput").ap()
    segs = [(0, NC_), (NC_, NL)]
    with ExitStack() as st:
        P = Prog(nc, st)
        sb = lambda name, shape, dt=F32: st.enter_context(nc.sbuf_tensor("s_" + name, shape, dt))
        ps = lambda name, shape, dt=F32: st.enter_context(nc.psum_tensor("p_" + name, shape, dt))
        b_c = P.buf("c")
        U = sb("U", [128, 128]); ones = sb("ones", [128, 128]); SL = sb("SL", [128, 128]); ident = sb("ident", [128, 128])
        masks = sb("masks", [128, 4, 512])
        for t_, d_ in ((U, U_d), (ones, ones_d), (SL, SL_d), (ident, ident_d)):
            P.dma("sp", t_[:], d_, writes=[b_c])
        P.dma("sp", masks[:], mask_d.rearrange("r p n -> p r n"), writes=[b_c])
        bigA = sb("bigA", [128, NP]); b_bigA = P.buf("bigA")
        bigB = sb("bigB", [128, NT]); b_bigB = P.buf("bigB")
        sgm = sb("sgm", [128, 2048]); b_sgm = P.buf("sgm")
        cw = sb("cw", [128, 6]); b_cw = P.buf("cw")
        X = sb("X", [128, nt, 128]); b_X = P.buf("X")
        Xdt = sb("Xdt", [128, nt, 64], BF16); b_Xdt = P.buf("Xdt")
        BT = sb("BT", [64, NT], BF16); b_BT = P.buf("BT")
        CT = sb("CT", [64, NT], BF16); b_CT = P.buf("CT")
        dt_ = sb("dt", [128, nt]); dta = sb("dta", [128, nt]); acum = sb("acum", [128, nt]); nacum = sb("nacum", [128, nt]); dsl = sb("dsl", [128, nt])
        dtaT = sb("dtaT", [128, 128]); acT = sb("acT", [128, 128])
        b_dt = P.buf("dt")
        sc2 = sb("sc2", [128, 2]); b_sc = P.buf("sc2")
        tps = ps("tps", [128, 512]); b_tps = P.buf("tps")
        aps = ps("aps", [128, 128]); b_aps = P.buf("aps")
        sps = [ps("sps%d" % i, [128, 512]) for i in range(4)]; b_sps = P.bufs(4, "sps")
        ops_ = ps("ops", [64, 512]); b_ops = P.buf("ops")
        Lb = [sb("Lb%d" % i, [128, 512]) for i in range(4)]; b_L = P.bufs(4, "L")
        Mb = [sb("Mb%d" % i, [128, 512], BF16) for i in range(4)]; b_M = P.bufs(4, "M")
        osb = sb("osb", [64, 512]); b_osb = P.buf("osb")
        ot = [sb("ot%d" % i, [128, 4, 64]) for i in range(2)]; b_ot = P.bufs(2, "ot")
        b_y = P.buf("y"); b_scr = P.buf("scr")
        cnt = {"s": 0, "l": 0, "ot": 0}

        def conv_silu(raw_ap, cw_ap, npart, out_fn):
            P.dma("sp", bigA[0:npart, :], raw_ap, writes=[b_bigA])
            P.dma("sp", cw[0:npart, :], cw_ap, writes=[b_cw])
            for si, (t0, ln) in enumerate(segs):
                p0 = t0 + 2 + 4 * si
                for c0 in range(0, ln, 2048):
                    w = min(2048, ln - c0)
                    o = bigB[0:npart, t0 + c0:t0 + c0 + w]
                    P.op("dve", lambda e, o=o, p0=p0, c0=c0, w=w: e.tensor_scalar(out=o, in0=bigA[0:npart, p0 + c0 - 2:p0 + c0 - 2 + w], scalar1=cw[0:npart, 0:1], scalar2=cw[0:npart, 5:6], op0=ALU.mult, op1=ALU.add),
                         reads=[b_bigA, b_cw], writes=[b_bigB])
                    for k in range(1, 5):
                        P.op("dve", lambda e, o=o, p0=p0, c0=c0, w=w, k=k: e.scalar_tensor_tensor(out=o, in0=bigA[0:npart, p0 + c0 - 2 + k:p0 + c0 - 2 + k + w], scalar=cw[0:npart, k:k + 1], in1=o, op0=ALU.mult, op1=ALU.add),
                             reads=[b_bigA, b_cw, b_bigB], writes=[b_bigB])
                    P.op("act", lambda e, o=o, w=w: e.activation(out=sgm[0:npart, 0:w], in_=o, func=AF.Sigmoid), reads=[b_bigB], writes=[b_sgm])
                    P.op("dve", lambda e, o=o, w=w: e.tensor_tensor(out=o, in0=o, in1=sgm[0:npart, 0:w], op=ALU.mult), reads=[b_bigB, b_sgm], writes=[b_bigB])
            out_fn()

        for d in range(2):
            conv_silu(br[d], cwb[d], 64, lambda: P.op("pool", lambda e: e.tensor_copy(out=BT[:], in_=bigB[0:64, :]), reads=[b_bigB], writes=[b_BT]))
            conv_silu(cr[d], cwc[d], 64, lambda: P.op("pool", lambda e: e.tensor_copy(out=CT[:], in_=bigB[0:64, :]), reads=[b_bigB], writes=[b_CT]))
            def xout():
                for c in range(nt):
                    g = c % 4
                    P.op("pe", lambda e, c=c, g=g: e.transpose(tps[:, g * 128:(g + 1) * 128], bigB[:, c * 128:(c + 1) * 128], ident[:]), reads=[b_bigB, b_c], writes=[b_tps], inc=(g == 3 or c == nt - 1))
                    if g == 3 or c == nt - 1:
                        c0 = c - g
                        P.op("act", lambda e, c0=c0, g=g: e.copy(out=X[:, c0:c0 + g + 1, :], in_=tps[:, 0:(g + 1) * 128].rearrange("p (a n) -> p a n", n=128)), reads=[b_tps], writes=[b_X])
                if d == 0:
                    P.dma("sp", xso.rearrange("(c p) n -> p c n", p=128), X[:], reads=[b_X], writes=[b_y])
            conv_silu(xr[d], cwx[d], 128, xout)
            for h in range(2):
                P.dma("sp", dt_[:], dtr[d, h].rearrange("(c p) -> p c", p=128), writes=[b_dt], allow_slow_non_contiguous=True)
                P.dma("sp", sc2[:, 0:1], dtb[d, h:h + 1].partition_broadcast(128), writes=[b_sc])
                P.dma("sp", sc2[:, 1:2], alog[d, h:h + 1].partition_broadcast(128), writes=[b_sc])
                P.op("act", lambda e: e.activation(out=sc2[:, 1:2], in_=sc2[:, 1:2], func=AF.Exp), reads=[b_sc], writes=[b_sc])
                P.op("act", lambda e: e.activation(out=dt_[:], in_=dt_[:], func=AF.Exp, bias=sc2[:, 0:1]), reads=[b_dt, b_sc], writes=[b_dt])
                P.op("act", lambda e: e.activation(out=dt_[:], in_=dt_[:], func=AF.Ln, bias=ones[:, 0:1]), reads=[b_dt, b_c], writes=[b_dt])
                P.op("dve", lambda e: e.tensor_scalar(out=dta[:], in0=dt_[:], scalar1=sc2[:, 1:2], scalar2=-1.0, op0=ALU.mult, op1=ALU.mult), reads=[b_dt, b_sc], writes=[b_dt])
                P.op("pe", lambda e: e.transpose(aps[0:nt, :], dta[:], ident[:]), reads=[b_dt, b_c], writes=[b_aps])
                P.op("act", lambda e: e.copy(out=dtaT[0:nt, :], in_=aps[0:nt, :]), reads=[b_aps], writes=[b_dt])
                P.op("pe", lambda e: e.matmul(aps[:, 0:nt], lhsT=dtaT[0:nt, :], rhs=SL[0:nt, 0:nt], start=True, stop=True), reads=[b_dt, b_c], writes=[b_aps])
                P.op("act", lambda e: e.copy(out=dsl[:], in_=aps[:, 0:nt]), reads=[b_aps], writes=[b_dt])
                P.op("pe", lambda e: e.matmul(aps[:, 0:nt], lhsT=U[:], rhs=dta[:], start=True, stop=False), reads=[b_dt, b_c], writes=[b_aps])
                P.op("pe", lambda e: e.matmul(aps[:, 0:nt], lhsT=ones[:], rhs=dsl[:], start=False, stop=True), reads=[b_dt, b_c], writes=[b_aps])
                P.op("act", lambda e: e.copy(out=acum[:], in_=aps[:, 0:nt]), reads=[b_aps], writes=[b_dt])
                P.op("dve", lambda e: e.tensor_scalar(out=nacum[:], in0=acum[:], scalar1=-1.0, scalar2=0.0, op0=ALU.mult, op1=ALU.add), reads=[b_dt], writes=[b_dt])
                P.op("pe", lambda e: e.transpose(aps[0:nt, :], acum[:], ident[:]), reads=[b_dt, b_c], writes=[b_aps])
                P.op("act", lambda e: e.copy(out=acT[0:nt, :], in_=aps[0:nt, :]), reads=[b_aps], writes=[b_dt])
                si = d * 2 + h
                P.dma("sp", scr[si].rearrange("(c p) -> c p", p=128), acT[0:nt, :], reads=[b_dt], writes=[b_scr])
                P.dma("sp", bigB[:], scr[si].partition_broadcast(128), reads=[b_scr], writes=[b_bigB])
                P.op("dve", lambda e, h=h: e.tensor_tensor(out=Xdt[:], in0=X[:, :, h * 64:(h + 1) * 64], in1=dt_[:].unsqueeze(2).to_broadcast([128, nt, 64]), op=ALU.mult), reads=[b_X, b_dt], writes=[b_Xdt])
                def chunk(d, h, q0):
                    W = min(512, NT - q0)
                    cmax = (q0 + W) // 128 - 1
                    base = cnt["s"]; cnt["s"] += cmax + 1; cnt["l"] += cmax + 1

                    def s_mm(c):
                        s = (base + c) % 4
                        P.op("pe", lambda e, s=s, c=c: e.matmul(sps[s][:, 0:W], lhsT=BT[:, c * 128:(c + 1) * 128], rhs=CT[:, q0:q0 + W], start=True, stop=True), reads=[b_BT, b_CT], writes=[b_sps[s]])
                    SK = 3
                    for c in range(min(SK, cmax + 1)):
                        s_mm(c)
                    for c in range(cmax + 1):
                        s = (base + c) % 4
                        l = s
                        r = c - q0 // 128
                        if r >= 0:
                            P.op("pool", lambda e, l=l, r=r: e.tensor_tensor(out=Lb[l][:, 0:W], in0=bigB[:, q0:q0 + W], in1=masks[:, r, 0:W], op=ALU.add), reads=[b_bigB, b_c], writes=[b_L[l]])
                            P.op("act", lambda e, l=l, c=c: e.activation(out=Lb[l][:, 0:W], in_=Lb[l][:, 0:W], func=AF.Exp, bias=nacum[:, c:c + 1]), reads=[b_L[l], b_dt], writes=[b_L[l]])
                        else:
                            P.op("act", lambda e, l=l, c=c: e.activation(out=Lb[l][:, 0:W], in_=bigB[:, q0:q0 + W], func=AF.Exp, bias=nacum[:, c:c + 1]), reads=[b_bigB, b_dt], writes=[b_L[l]])
                        P.op("dve", lambda e, l=l, s=s: e.tensor_tensor(out=Mb[l][:, 0:W], in0=sps[s][:, 0:W], in1=Lb[l][:, 0:W], op=ALU.mult), reads=[b_sps[s], b_L[l]], writes=[b_M[l]])
                        if c + SK <= cmax:
                            s_mm(c + SK)
                        P.op("pe", lambda e, l=l, c=c: e.matmul(ops_[:, 0:W], lhsT=Xdt[:, c, :], rhs=Mb[l][:, 0:W], start=(c == 0), stop=(c == cmax)), reads=[b_M[l], b_Xdt], writes=[b_ops], inc=(c == cmax))
                    nj = W // 128
                    P.op("act", lambda e: e.copy(out=osb[:, 0:W], in_=ops_[:, 0:W]), reads=[b_ops], writes=[b_osb])
                    for j in range(nj):
                        P.op("pe", lambda e, j=j: e.transpose(tps[:, j * 64:(j + 1) * 64], osb[:, j * 128:(j + 1) * 128], ident[0:64, 0:64]), reads=[b_osb, b_c], writes=[b_tps], inc=(j == nj - 1))
                    t = cnt["ot"] % 2; cnt["ot"] += 1
                    P.op("dve", lambda e, t=t: e.tensor_copy(out=ot[t][:, 0:nj, :], in_=tps[:, 0:nj * 64].rearrange("p (a n) -> p a n", n=64)), reads=[b_tps], writes=[b_ot[t]])
                    P.dma("sp", yo[d, q0:q0 + W, h * 64:(h + 1) * 64].rearrange("(j p) n -> p j n", p=128), ot[t][:, 0:nj, :], reads=[b_ot[t]], writes=[b_y])
                for q0 in range(0, NT, 512):
                    chunk(d, h, q0)
        P.finish("sp", [b_y])
        P.emit()
    return nc


def ka_consts():
    k = np.arange(128)
    U = (k[:, None] <= k[None, :]).astype(np.float32)
    SL = (k[:, None] < k[None, :]).astype(np.float32)
    masks = np.zeros((4, 128, 512), np.float32)
    i = np.arange(512)
    for r in range(4):
        masks[r] = np.where(i[None, :] >= r * 128 + k[:, None], 0.0, NEG)
    return {"U": U, "SL": SL, "ones": np.ones((128, 128), np.float32), "ident": np.eye(128, dtype=np.float32), "masks": masks}


def pad_seq(a, NC_, NL):
    z = np.zeros(a.shape[:-1] + (2,), a.dtype)
    return np.concatenate([z, a[..., :NC_], z, z, a[..., NC_:], z], -1)


def build_kb(n, NCH=128, GC=16):
    nc = bass.Bass("TRN2", target_bir_lowering=False)
    NA = 2 * n // 128
    NAD = n // 128
    N = 2 * n
    QC = 8 if NA <= 16 else 4
    din = lambda nm, s: nc.dram_tensor(nm, s, F32, kind="ExternalInput").ap()
    raw_d = din("raw", [3, NAD, NCH, 130])
    cw_d = din("cw", [3 * NCH * 4]); hb_d = din("hb", [2 * NCH])
    full_d = din("full", [2, NA, NCH, 128])
    F1_d = din("F1", [NA, 2 * NA]); TW_d = din("TW", [128, 2, NA]); TWI_d = din("TWI", [NA, 2, 128])
    CS_d = din("CS", [128, 256]); nSC_d = din("nSC", [128, 256]); C_d = din("C128", [128, 128]); S_d = din("S128", [128, 128]); nS_d = din("nS128", [128, 128])
    CI_d = din("CI", [NA, NAD]); nSI_d = din("nSI", [NA, NAD])
    yo = nc.dram_tensor("y", [NAD, NCH, 128], F32, kind="ExternalOutput").ap()
    with ExitStack() as st:
        P = Prog(nc, st)
        sb = lambda name, shape, dt=F32: st.enter_context(nc.sbuf_tensor("s_" + name, shape, dt))
        ps = lambda name, shape, dt=F32: st.enter_context(nc.psum_tensor("p_" + name, shape, dt))
        b_c = P.buf("c")
        F1 = sb("F1", [NA, 2 * NA]); TW = sb("TW", [128, 2, NA]); TWI = sb("TWI", [NA, 2, 128])
        CS = sb("CS", [128, 256]); nSC = sb("nSC", [128, 256]); C128 = sb("C128", [128, 128]); S128 = sb("S128", [128, 128]); nS128 = sb("nS128", [128, 128])
        CI = sb("CI", [NA, NAD]); nSI = sb("nSI", [NA, NAD])
        cw = sb("cw", [NAD, 3, NCH, 4]); hb = sb("hb", [NAD, 2, NCH])
        for t_, d_ in ((F1, F1_d), (TW, TW_d), (TWI, TWI_d), (CS, CS_d), (nSC, nSC_d), (C128, C_d), (S128, S_d), (nS128, nS_d), (CI, CI_d), (nSI, nSI_d)):
            P.dma("sp", t_[:], d_, writes=[b_c])
        P.dma("sp", cw[:].rearrange("p a b c -> p (a b c)"), cw_d.partition_broadcast(NAD), writes=[b_c])
        P.dma("sp", hb[:].rearrange("p a b -> p (a b)"), hb_d.partition_broadcast(NAD), writes=[b_c])
        raw = sb("raw", [NAD, 3, GC, 130]); b_raw = P.buf("raw")
        u = sb("u", [NAD, 3, GC, 128]); b_u = P.buf("u")
        tmpc = sb("tmpc", [NAD, GC, 128]); b_tmpc = P.buf("tmpc")
        fl = sb("fl", [NA, 2, GC, 128]); b_fl = P.buf("fl")
        H = sb("H", [128, 2, QC, NA]); b_H = P.buf("H")
        Yp = sb("Yp", [128, 2, QC, NA]); b_Yp = P.buf("Yp")
        Zs = sb("Zs", [128, 2, QC, NA]); b_Zs = P.buf("Zs")
        Vp = sb("Vp", [NA, 2, QC, 128]); b_Vp = P.buf("Vp")
        t1 = sb("t1", [128, QC, max(NA, 128)]); t2 = sb("t2", [128, QC, max(NA, 128)]); b_t1 = P.buf("t1"); b_t2 = P.buf("t2")
        z1 = sb("z1", [NAD, GC, 128]); b_z1 = P.buf("z1")
        og = sb("og", [NAD, GC, 128]); b_og = P.buf("og")
        yps = ps("yps", [128, QC, 2, NA]); b_yps = P.buf("yps")
        xps = ps("xps", [128, 2, QC, NA]); b_xps = P.buf("xps")
        vps = ps("vps", [NA, QC, 2, 128]); b_vps = P.buf("vps")
        ops_ = ps("ops", [NAD, QC, 128]); b_ops = P.buf("ops")
        b_y = P.buf("y")

        def cmul(eng_out_re, eng_out_im, are, aim, bre, bim, conj_b, pn, w, rbufs, obuf):
            sgn_im = -1.0 if conj_b else 1.0
            P.op("dve", lambda e: e.tensor_tensor(out=t1[0:pn, :, 0:w], in0=are, in1=bre, op=ALU.mult), reads=rbufs, writes=[b_t1])
            P.op("dve", lambda e: e.tensor_tensor(out=t2[0:pn, :, 0:w], in0=aim, in1=bim, op=ALU.mult), reads=rbufs, writes=[b_t2])
            P.op("dve", lambda e: e.tensor_tensor(out=eng_out_re, in0=t1[0:pn, :, 0:w], in1=t2[0:pn, :, 0:w], op=(ALU.add if conj_b else ALU.subtract)), reads=[b_t1, b_t2], writes=[obuf])
            P.op("dve", lambda e: e.tensor_tensor(out=t1[0:pn, :, 0:w], in0=aim, in1=bre, op=ALU.mult), reads=rbufs, writes=[b_t1])
            P.op("dve", lambda e: e.tensor_tensor(out=t2[0:pn, :, 0:w], in0=are, in1=bim, op=ALU.mult), reads=rbufs, writes=[b_t2])
            P.op("dve", lambda e: e.tensor_tensor(out=eng_out_im, in0=t1[0:pn, :, 0:w], in1=t2[0:pn, :, 0:w], op=(ALU.subtract if conj_b else ALU.add)), reads=[b_t1, b_t2], writes=[obuf])

        def fwd(src_fn, K, src_bufs):
            for c in range(QC):
                P.op("pe", lambda e, c=c: e.matmul(yps[:, c, :, :].rearrange("p a b -> p (a b)"), lhsT=src_fn(c), rhs=F1[0:K, :], start=True, stop=True), reads=src_bufs + [b_c], writes=[b_yps], inc=(c == QC - 1))
            twc = TW[:, 0, :].unsqueeze(1).to_broadcast([128, QC, NA]); tws = TW[:, 1, :].unsqueeze(1).to_broadcast([128, QC, NA])
            cmul(Yp[:, 0, :, :], Yp[:, 1, :, :], yps[:, :, 0, :], yps[:, :, 1, :], twc, tws, True, 128, NA, [b_yps, b_c], b_Yp)
            yre = Yp[:, 0, :, :].rearrange("p a b -> p (a b)"); yim = Yp[:, 1, :, :].rearrange("p a b -> p (a b)")
            xre = xps[:, 0, :, :].rearrange("p a b -> p (a b)"); xim = xps[:, 1, :, :].rearrange("p a b -> p (a b)")
            P.op("pe", lambda e: e.matmul(xre, lhsT=C128[:], rhs=yre, start=True, stop=False), reads=[b_Yp, b_c], writes=[b_xps], inc=False)
            P.op("pe", lambda e: e.matmul(xre, lhsT=S128[:], rhs=yim, start=False, stop=True), reads=[b_Yp, b_c], writes=[b_xps], inc=False)
            P.op("pe", lambda e: e.matmul(xim, lhsT=C128[:], rhs=yim, start=True, stop=False), reads=[b_Yp, b_c], writes=[b_xps], inc=False)
            P.op("pe", lambda e: e.matmul(xim, lhsT=nS128[:], rhs=yre, start=False, stop=True), reads=[b_Yp, b_c], writes=[b_xps])

        def long_conv(zsrc_fn, zbufs, o, q0, gate_fn):
            fwd(lambda c: fl[:, o, q0 + c, :], NA, [b_fl])
            P.op("act", lambda e: e.copy(out=H[:], in_=xps[:]), reads=[b_xps], writes=[b_H])
            fwd(zsrc_fn, NAD, zbufs)
            cmul(Zs[:, 0, :, :], Zs[:, 1, :, :], xps[:, 0, :, :], xps[:, 1, :, :], H[:, 0, :, :], H[:, 1, :, :], False, 128, NA, [b_xps, b_H], b_Zs)
            for c in range(QC):
                vv = vps[:, c, :, :].rearrange("p a b -> p (a b)")
                P.op("pe", lambda e, c=c, vv=vv: e.matmul(vv, lhsT=Zs[:, 0, c, :], rhs=CS[:], start=True, stop=False), reads=[b_Zs, b_c], writes=[b_vps], inc=False)
                P.op("pe", lambda e, c=c, vv=vv: e.matmul(vv, lhsT=Zs[:, 1, c, :], rhs=nSC[:], start=False, stop=True), reads=[b_Zs, b_c], writes=[b_vps], inc=(c == QC - 1))
            twc = TWI[:, 0, :].unsqueeze(1).to_broadcast([NA, QC, 128]); tws = TWI[:, 1, :].unsqueeze(1).to_broadcast([NA, QC, 128])
            cmul(Vp[:, 0, :, :], Vp[:, 1, :, :], vps[:, :, 0, :], vps[:, :, 1, :], twc, tws, False, NA, 128, [b_vps, b_c], b_Vp)
            oo = ops_[:].rearrange("p a b -> p (a b)")
            vre = Vp[:, 0, :, :].rearrange("p a b -> p (a b)"); vim = Vp[:, 1, :, :].rearrange("p a b -> p (a b)")
            nh = (QC * 128) // 512
            for hh in range(nh):
                cs = slice(hh * 512, (hh + 1) * 512)
                P.op("pe", lambda e, cs=cs: e.matmul(oo[:, cs], lhsT=CI[:], rhs=vre[:, cs], start=True, stop=False), reads=[b_Vp, b_c], writes=[b_ops], inc=False)
                P.op("pe", lambda e, cs=cs: e.matmul(oo[:, cs], lhsT=nSI[:], rhs=vim[:, cs], start=False, stop=True), reads=[b_Vp, b_c], writes=[b_ops], inc=(hh == nh - 1))
            gate_fn()

        def group(g0):
            for s in range(3):
                P.dma("sp", raw[:, s, :, :], raw_d[s, :, g0:g0 + GC, :], writes=[b_raw])
            P.dma("sp", fl[:], full_d[:, :, g0:g0 + GC, :].rearrange("o a c r -> a o c r"), writes=[b_fl])
            for s in range(3):
                wk = lambda k, s=s: cw[:, s, g0:g0 + GC, k:k + 1].to_broadcast([NAD, GC, 128])
                us = u[:, s, :, :]
                P.op("dve", lambda e, s=s, us=us, wk=wk: e.tensor_tensor(out=us, in0=raw[:, s, :, 0:128], in1=wk(0), op=ALU.mult), reads=[b_raw, b_c], writes=[b_u])
                for k in (1, 2):
                    P.op("pool", lambda e, s=s, k=k, wk=wk: e.tensor_tensor(out=tmpc[:], in0=raw[:, s, :, k:k + 128], in1=wk(k), op=ALU.mult), reads=[b_raw, b_c], writes=[b_tmpc])
                    P.op("dve", lambda e, us=us: e.tensor_tensor(out=us, in0=us, in1=tmpc[:], op=ALU.add), reads=[b_u, b_tmpc], writes=[b_u])
                P.op("dve", lambda e, us=us, wk=wk: e.tensor_tensor(out=us, in0=us, in1=wk(3), op=ALU.add), reads=[b_u, b_c], writes=[b_u])
            for q0 in range(0, GC, QC):
                def gate0(q0=q0):
                    bb = hb[:, 0, g0 + q0:g0 + q0 + QC].unsqueeze(2).to_broadcast([NAD, QC, 128])
                    zq = z1[:, q0:q0 + QC, :]
                    P.op("dve", lambda e: e.tensor_tensor(out=zq, in0=u[:, 0, q0:q0 + QC, :], in1=bb, op=ALU.mult), reads=[b_u, b_c], writes=[b_z1])
                    P.op("dve", lambda e: e.scalar_tensor_tensor(out=zq, in0=ops_[:], scalar=1.0 / N, in1=zq, op0=ALU.mult, op1=ALU.add), reads=[b_ops, b_z1], writes=[b_z1])
                    P.op("dve", lambda e: e.tensor_tensor(out=zq, in0=zq, in1=u[:, 1, q0:q0 + QC, :], op=ALU.mult), reads=[b_u, b_z1], writes=[b_z1])
                long_conv(lambda c, q0=q0: u[:, 0, q0 + c, :], [b_u], 0, q0, gate0)

                def gate1(q0=q0):
                    bb = hb[:, 1, g0 + q0:g0 + q0 + QC].unsqueeze(2).to_broadcast([NAD, QC, 128])
                    oq = og[:, q0:q0 + QC, :]
                    P.op("dve", lambda e: e.tensor_tensor(out=oq, in0=z1[:, q0:q0 + QC, :], in1=bb, op=ALU.mult), reads=[b_z1, b_c], writes=[b_og])
                    P.op("dve", lambda e: e.scalar_tensor_tensor(out=oq, in0=ops_[:], scalar=1.0 / N, in1=oq, op0=ALU.mult, op1=ALU.add), reads=[b_ops, b_og], writes=[b_og])
                    P.op("dve", lambda e: e.tensor_tensor(out=oq, in0=oq, in1=u[:, 2, q0:q0 + QC, :], op=ALU.mult), reads=[b_u, b_og], writes=[b_og])
                long_conv(lambda c, q0=q0: z1[:, q0 + c, :], [b_z1], 1, q0, gate1)
            P.dma("sp", yo[:, g0:g0 + GC, :], og[:], reads=[b_og], writes=[b_y])

        for g0 in range(0, NCH, GC):
            group(g0)
        P.finish("sp", [b_y])
        P.emit()
    return nc


def kb_consts(n):
    NA = 2 * n // 128; NAD = n // 128; N = 2 * n
    a = np.arange(NA)[:, None]; k1 = np.arange(NA)[None, :]
    ang = 2 * np.pi * a * k1 / NA
    F1 = np.concatenate([np.cos(ang), -np.sin(ang)], 1)
    r = np.arange(128)[:, None]
    th = 2 * np.pi * r * k1 / N
    TW = np.stack([np.cos(th), np.sin(th)], 1)
    TWI = np.stack([np.cos(th).T, np.sin(th).T], 1)
    p = np.arange(128)
    a128 = 2 * np.pi * p[:, None] * p[None, :] / 128
    C = np.cos(a128); S = np.sin(a128)
    angI = 2 * np.pi * np.arange(NA)[:, None] * np.arange(NAD)[None, :] / NA
    f = lambda x: np.ascontiguousarray(x.astype(np.float32))
    return {"F1": f(F1), "TW": f(TW), "TWI": f(TWI), "CS": f(np.concatenate([C, S], 1)), "nSC": f(np.concatenate([-S, C], 1)),
            "C128": f(C), "S128": f(S), "nS128": f(-S), "CI": f(np.cos(angI)), "nSI": f(-np.sin(angI))}


def kb_pack_raw(uraw, n):
    NAD = n // 128
    p = np.pad(uraw, ((0, 0), (0, 0), (1, 1)))
    idx = (np.arange(NAD)[:, None] * 128 + np.arange(130)[None, :])
    g = p[:, :, idx]
    return np.ascontiguousarray(g.transpose(0, 2, 1, 3))


def kb_pack_full(k, n):
    NA = 2 * n // 128
    kf, kb = k[:, 0], k[:, 1]
    full = np.concatenate([kf, np.zeros_like(kf[..., :1]), kb[..., :0:-1]], -1)
    return np.ascontiguousarray(full.reshape(2, -1, NA, 128).transpose(0, 2, 1, 3))


EPS = 1e-6
TWO_PI = 2.0 * math.pi


def build_kf1(n):
    nc = bass.Bass("TRN2", target_bir_lowering=False)
    din = lambda nm, s: nc.dram_tensor(nm, s, F32, kind="ExternalInput").ap()
    zT_d = din("zT", [33, n]); dec_d = din("decay", [128, n])
    w1_d = din("w1", [33, 64]); w2_d = din("w2", [64, 64]); w3_d = din("w3", [64, 128])
    fb1_d = din("fb1", [64, 2]); fb2_d = din("fb2", [64, 2]); b3_d = din("b3", [128, 1])
    pm_d = din("pm", [128, 128])
    ko = nc.dram_tensor("k", [128, n], F32, kind="ExternalOutput").ap()
    CW = min(512, n)
    nch = n // CW
    with ExitStack() as st:
        P = Prog(nc, st)
        sb = lambda name, shape, dt=F32: st.enter_context(nc.sbuf_tensor("s_" + name, shape, dt))
        ps = lambda name, shape, dt=F32: st.enter_context(nc.psum_tensor("p_" + name, shape, dt))
        b_c = P.buf("c")
        zT = sb("zT", [33, n]); dec = sb("dec", [128, n]); w1 = sb("w1", [33, 64]); w2 = sb("w2", [64, 64]); w3 = sb("w3", [64, 128])
        fb1 = sb("fb1", [64, 2]); fb2 = sb("fb2", [64, 2]); b3 = sb("b3", [128, 1]); pm = sb("pm", [128, 128]); epsb = sb("epsb", [128, 1])
        for t_, d_ in ((zT, zT_d), (dec, dec_d), (w1, w1_d), (w2, w2_d), (w3, w3_d), (fb1, fb1_d), (fb2, fb2_d), (b3, b3_d), (pm, pm_d)):
            P.dma("sp", t_[:], d_, writes=[b_c])
        for fb in (fb1, fb2):
            P.op("dve", lambda e, fb=fb: e.tensor_scalar(out=fb[:, 1:2], in0=fb[:, 1:2], scalar1=fb[:, 0:1], scalar2=16.0 * math.pi, op0=ALU.mult, op1=ALU.add), reads=[b_c], writes=[b_c])
        P.op("dve", lambda e: e.memset(epsb[:], EPS), reads=[b_c], writes=[b_c])
        kT = sb("kT", [128, n]); b_k = P.buf("kT")
        ssq = sb("ssq", [128, nch + 2]); b_ss = P.buf("ss")
        sq = sb("sq", [128, CW]); b_sq = P.buf("sq")
        h1 = [sb("h1_%d" % i, [64, CW]) for i in range(2)]; b_h1 = P.bufs(2, "h1")
        h2 = [sb("h2_%d" % i, [64, CW]) for i in range(2)]; b_h2 = P.bufs(2, "h2")
        pp = [ps("pp%d" % i, [128, CW]) for i in range(4)]; b_pp = P.bufs(4, "pp")
        pi = [0]

        I32 = mybir.dt.int32
        ki = sb("ki", [64, CW], I32); kf = sb("kf", [64, CW]); b_ki = P.buf("ki")

        def sin_layer(src_ps, fb, dst, b_src, b_dst, w):
            P.op("dve", lambda e: e.tensor_scalar(out=dst[:, 0:w], in0=src_ps[0:64, 0:w], scalar1=fb[:, 0:1], scalar2=fb[:, 1:2], op0=ALU.mult, op1=ALU.add), reads=[b_src, b_c], writes=[b_dst])
            P.op("dve", lambda e: e.tensor_scalar(out=ki[:, 0:w], in0=dst[:, 0:w], scalar1=1.0 / TWO_PI, scalar2=0.0, op0=ALU.mult, op1=ALU.add), reads=[b_dst], writes=[b_ki])
            P.op("dve", lambda e: e.tensor_copy(out=kf[:, 0:w], in_=ki[:, 0:w]), reads=[b_ki], writes=[b_ki])
            P.op("dve", lambda e: e.scalar_tensor_tensor(out=dst[:, 0:w], in0=kf[:, 0:w], scalar=-TWO_PI, in1=dst[:, 0:w], op0=ALU.mult, op1=ALU.add), reads=[b_ki, b_dst], writes=[b_dst])
            P.op("dve", lambda e: e.tensor_scalar(out=kf[:, 0:w], in0=dst[:, 0:w], scalar1=math.pi, scalar2=-TWO_PI, op0=ALU.is_gt, op1=ALU.mult), reads=[b_dst, b_ki], writes=[b_ki])
            P.op("dve", lambda e: e.tensor_tensor(out=dst[:, 0:w], in0=dst[:, 0:w], in1=kf[:, 0:w], op=ALU.add), reads=[b_ki, b_dst], writes=[b_dst])
            P.op("act", lambda e: e.activation(out=dst[:, 0:w], in_=dst[:, 0:w], func=AF.Sin), reads=[b_dst], writes=[b_dst])

        def chunk(j):
            c0 = j * CW
            s = j % 2
            a = pi[0] % 4; pi[0] += 1
            P.op("pe", lambda e: e.matmul(pp[a][0:64, :], lhsT=w1[:], rhs=zT[:, c0:c0 + CW], start=True, stop=True), reads=[b_c], writes=[b_pp[a]])
            sin_layer(pp[a], fb1, h1[s], b_pp[a], b_h1[s], CW)
            a2 = pi[0] % 4; pi[0] += 1
            P.op("pe", lambda e: e.matmul(pp[a2][0:64, :], lhsT=w2[:], rhs=h1[s][:], start=True, stop=True), reads=[b_c, b_h1[s]], writes=[b_pp[a2]])
            sin_layer(pp[a2], fb2, h2[s], b_pp[a2], b_h2[s], CW)
            a3 = pi[0] % 4; pi[0] += 1
            P.op("pe", lambda e: e.matmul(pp[a3][:, :], lhsT=w3[:], rhs=h2[s][:], start=True, stop=True), reads=[b_c, b_h2[s]], writes=[b_pp[a3]])
            P.op("dve", lambda e: e.scalar_tensor_tensor(out=kT[:, c0:c0 + CW], in0=pp[a3][:, :], scalar=b3[:, 0:1], in1=dec[:, c0:c0 + CW], op0=ALU.add, op1=ALU.mult), reads=[b_pp[a3], b_c], writes=[b_k])
            P.op("act", lambda e: e.activation(out=sq[:], in_=kT[:, c0:c0 + CW], func=AF.Square, accum_out=ssq[:, j:j + 1]), reads=[b_k], writes=[b_sq, b_ss])

        for j in range(nch):
            chunk(j)
        P.op("dve", lambda e: e.tensor_reduce(out=ssq[:, nch:nch + 1], in_=ssq[:, 0:nch], axis=AX.X, op=ALU.add), reads=[b_ss], writes=[b_ss])
        P.op("pe", lambda e: e.matmul(pp[0][:, 0:1], lhsT=pm[:], rhs=ssq[:, nch:nch + 1], start=True, stop=True), reads=[b_ss, b_c], writes=[b_pp[0]])
        P.op("act", lambda e: e.activation(out=ssq[:, nch + 1:nch + 2], in_=pp[0][:, 0:1], func=AF.Sqrt, bias=epsb[:, 0:1]), reads=[b_pp[0], b_c], writes=[b_ss])
        P.op("dve", lambda e: e.reciprocal(out=ssq[:, nch + 1:nch + 2], in_=ssq[:, nch + 1:nch + 2]), reads=[b_ss], writes=[b_ss])
        b_o = P.buf("o")
        for c0 in range(0, n, 2048):
            w = min(2048, n - c0)
            P.op("dve", lambda e, c0=c0, w=w: e.tensor_scalar(out=kT[:, c0:c0 + w], in0=kT[:, c0:c0 + w], scalar1=ssq[:, nch + 1:nch + 2], scalar2=0.0, op0=ALU.mult, op1=ALU.add), reads=[b_k, b_ss], writes=[b_k])
        P.dma("sp", ko, kT[:], reads=[b_k], writes=[b_o])
        P.finish("sp", [b_o])
        P.emit()
    return nc


def hy_tables(n):
    f32 = np.float32
    pos = np.arange(n, dtype=f32)
    t = np.linspace(0.0, 1.0, n, dtype=f32)
    f = np.linspace(1e-4, 15.0, 16, dtype=f32)
    ang = (f32(2.0 * math.pi) * pos[:, None] * f[None, :] / f32(n)).astype(f32)
    z = np.concatenate([t[:, None], np.cos(ang), -np.sin(ang)], -1).astype(f32)
    mx = math.log(1e-2) / 0.3; mn = math.log(1e-2) / 1.5
    deltas = np.abs(np.linspace(mn, mx, 256, dtype=f32))
    decay = np.exp(-t[:, None] * deltas[None, :]).astype(f32)
    return np.ascontiguousarray(z.T), np.ascontiguousarray(decay.T)


def kf1_inputs(n, hy_w1, hy_b1, hy_f1, hy_w2, hy_b2, hy_f2, hy_w3, hy_b3):
    zT, decT = hy_tables(n)
    k64 = np.arange(128)
    pm = (k64[:, None] % 64 == k64[None, :] % 64).astype(np.float32)
    maps = []
    for core in range(8):
        o, cr = core // 4, core % 4
        cols = np.concatenate([o * 512 + d * 256 + cr * 64 + np.arange(64) for d in range(2)])
        chs = np.concatenate([cr * 64 + np.arange(64)] * 2)
        maps.append({"zT": zT, "decay": np.ascontiguousarray(decT[chs]), "w1": np.ascontiguousarray(hy_w1), "w2": np.ascontiguousarray(hy_w2),
                     "w3": np.ascontiguousarray(hy_w3[:, cols]), "fb1": np.ascontiguousarray(np.stack([hy_f1, hy_b1], -1)),
                     "fb2": np.ascontiguousarray(np.stack([hy_f2, hy_b2], -1)), "b3": np.ascontiguousarray(hy_b3[cols][:, None]), "pm": pm})
    return maps


def kf1_gather(results, n):
    k = np.zeros((2, 2, 256, n), np.float32)
    for core in range(8):
        o, cr = core // 4, core % 4
        r = results[core]["k"]
        for d in range(2):
            k[o, d, cr * 64:(cr + 1) * 64] = r[d * 64:(d + 1) * 64]
    return k


D = 1024
DFF = 4096
EPS = 1e-6


class ModCalc:
    def __init__(self, P, nc, sb, ps):
        self.P, self.nc = P, nc
        self.c_sb = sb("mc_c", [128, 8]); self.c_sg = sb("mc_sg", [128, 8]); self.c_bc = sb("mc_bc", [128, 8, 128])
        self.bm = sb("mc_bm", [128, 512]); self.wst = sb("mc_w", [128, 8, 512])
        self.ps = ps("mc_ps", [128, 512])
        self.b = P.bufs(6, "mc")

    def set_c(self, cvec):
        P = self.P
        b_c, b_cs, b_bc = self.b[0:3]
        c_sb, c_sg, c_bc = self.c_sb, self.c_sg, self.c_bc
        P.dma("sp", c_sb[:], cvec.rearrange("(k p) -> p k", p=128), writes=[b_c], allow_slow_non_contiguous=True)
        P.op("act", lambda e: e.activation(out=c_sg[:], in_=c_sb[:], func=AF.Sigmoid), reads=[b_c], writes=[b_cs])
        P.op("dve", lambda e: e.tensor_tensor(out=c_sg[:], in0=c_sg[:], in1=c_sb[:], op=ALU.mult), reads=[b_c, b_cs], writes=[b_cs])
        P.op("dve", lambda e: e.tensor_copy(out=c_bc[:], in_=c_sg[:].unsqueeze(2).to_broadcast([128, 8, 128])), reads=[b_cs], writes=[b_bc])

    def calc(self, w_mod, b_mod, col0, ncols, out_ap_fn, out_buf):
        P = self.P
        b_bc, b_bm, b_w, b_ps = self.b[2:6]
        wv = w_mod.rearrange("(k p) n -> p k n", p=128)
        for j in range(ncols // 512):
            c = col0 + j * 512
            P.dma("sp", self.wst[:], wv[:, :, c:c + 512], writes=[b_w])
            P.dma("sp", self.bm[:], b_mod[c:c + 512].partition_broadcast(128), writes=[b_bm])
            for k in range(8):
                P.op("pe", lambda e, k=k: e.matmul(self.ps[:], lhsT=self.c_bc[:, k, :], rhs=self.wst[:, k, :], start=(k == 0), stop=(k == 7)),
                     reads=[b_bc, b_w], writes=[b_ps])
            P.op("dve", lambda e, j=j: e.tensor_tensor(out=out_ap_fn(j), in0=self.ps[:], in1=self.bm[:], op=ALU.add), reads=[b_ps, b_bm], writes=[out_buf])


def rstd_from_ss(P, ss_ap, b_ss, epsb, b_eps):
    P.op("act", lambda e: e.activation(out=ss_ap, in_=ss_ap, func=AF.Sqrt, bias=epsb[:, 0:1]), reads=[b_ss, b_eps], writes=[b_ss])
    P.op("dve", lambda e: e.reciprocal(out=ss_ap, in_=ss_ap), reads=[b_ss], writes=[b_ss])


def build_k3a(ntl, ntc):
    nc = bass.Bass("TRN2", target_bir_lowering=False)
    NT = ntl + ntc
    NTK = NT * 128
    din = lambda n, s: nc.dram_tensor(n, s, F32, kind="ExternalInput").ap()
    x = din("x", [NTK, D])
    yf = din("yf", [NTK, 256]); yb = din("yb", [NTK, 256]); xs = din("xs", [NTK, 256]); zz = din("z", [NTK, 256])
    mixT = din("mixT", [768, NTK])
    cv = din("cv", [D]); cctx = din("cctx", [D]); w_mod = din("w_mod", [D, 1024]); b_mod = din("b_mod", [1024])
    g_post = din("g_post", [D]); skip_d = din("skip", [256]); ssdn_d = din("ssdn", [256])
    w_out = din("w_out", [D, D]); ident_d = din("ident", [128, 128])
    xo = nc.dram_tensor("xo", [NTK, D], F32, kind="ExternalOutput").ap()
    with ExitStack() as st:
        P = Prog(nc, st)
        sb = lambda name, shape, dt=F32: st.enter_context(nc.sbuf_tensor("s_" + name, shape, dt))
        ps = lambda name, shape, dt=F32: st.enter_context(nc.psum_tensor("p_" + name, shape, dt))
        b_c = P.buf("c")
        ident_f = sb("ident_f", [128, 128]); ident = sb("identb", [128, 128], BF16)
        epsb = sb("epsb", [128, 1]); gp = sb("gp", [128, D]); skipb = sb("skipb", [128, 256]); ssdn = sb("ssdn", [128, 256])
        P.dma("sp", ident_f[:], ident_d, writes=[b_c])
        P.dma("sp", gp[:], g_post.partition_broadcast(128), writes=[b_c])
        P.dma("sp", skipb[:], skip_d.partition_broadcast(128), writes=[b_c])
        P.dma("sp", ssdn[:], ssdn_d.partition_broadcast(128), writes=[b_c])
        P.op("dve", lambda e: e.tensor_copy(out=ident[:], in_=ident_f[:]), reads=[b_c], writes=[b_c])
        P.op("dve", lambda e: e.memset(epsb[:], EPS), reads=[b_c], writes=[b_c])
        wob = sb("wob", [128, 8, D], BF16); b_wo = P.buf("wo")
        wst = [sb("wst%d" % i, [128, D]) for i in range(2)]; b_wst = P.bufs(2, "wst")
        wv = w_out.rearrange("(k p) n -> p k n", p=128)
        for k in range(8):
            s = k % 2
            P.dma("pool", wst[s][:], wv[:, k, :], writes=[b_wst[s]])
            P.op("pool", lambda e, k=k, s=s: e.tensor_copy(out=wob[:, k, :], in_=wst[s][:]), reads=[b_wst[s]], writes=[b_wo])
        MC = ModCalc(P, nc, sb, ps)
        G1 = sb("G1", [128, D]); b_G1 = P.buf("G1")
        xt = [sb("xt%d" % i, [128, D]) for i in range(2)]; b_xt = P.bufs(2, "xt")
        sq = sb("sq", [128, D]); b_sq = P.buf("sq")
        a4 = [[sb("a%d_%d" % (j, i), [128, 256]) for j in range(4)] for i in range(2)]; b_a4 = P.bufs(2, "a4")
        sg = sb("sg", [128, 256]); b_sg = P.buf("sg")
        ss = sb("ss", [128, 4]); b_ss = P.buf("ss")
        tnb = sb("tnb", [128, 256], BF16); b_tn = P.buf("tn")
        mst = [sb("mst%d" % i, [128, 6, 128]) for i in range(2)]; b_mst = P.bufs(2, "mst")
        mT = [sb("mT%d" % i, [128, 8, 128], BF16) for i in range(2)]; b_mT = P.bufs(2, "mT")
        tps = ps("tps", [128, 2, 128], BF16); b_tps = P.buf("tps")
        yps = [ps("yps%d" % i, [128, 2, 512]) for i in range(2)]; b_yps = P.bufs(2, "yps")
        tmp = sb("tmp", [128, D]); b_tmp = P.buf("tmp")
        ob = [sb("ob%d" % i, [128, D]) for i in range(2)]; b_ob = P.bufs(2, "ob")
        b_out = P.buf("out")
        for seg, (t0, t1, cvec) in enumerate(((0, ntl, cv), (ntl, NT, cctx))):
            MC.set_c(cvec)
            MC.calc(w_mod, b_mod, 0, 1024, lambda j: G1[:, j * 512:(j + 1) * 512], b_G1)
            P.op("dve", lambda e: e.tensor_tensor(out=G1[:], in0=G1[:], in1=gp[:], op=ALU.mult), reads=[b_G1, b_c], writes=[b_G1])
            for t in range(t0, t1):
                s = t % 2
                tok = slice(t * 128, (t + 1) * 128)
                P.dma("sp", xt[s][:], x[tok, :], writes=[b_xt[s]])
                for j, src in enumerate((xs, yf, yb, zz)):
                    P.dma("sp", a4[s][j][:], src[tok, :], writes=[b_a4[s]])
                P.dma("sp", mst[s][:], mixT[:, tok].rearrange("(k p) t -> p k t", p=128), writes=[b_mst[s]])
                A = a4[s]
                P.op("dve", lambda e, A=A: e.tensor_tensor(out=A[0][:], in0=A[0][:], in1=skipb[:], op=ALU.mult), reads=[b_a4[s], b_c], writes=[b_a4[s]])
                P.op("dve", lambda e, A=A: e.tensor_tensor(out=A[0][:], in0=A[0][:], in1=A[1][:], op=ALU.add), reads=[b_a4[s]], writes=[b_a4[s]])
                P.op("dve", lambda e, A=A: e.tensor_tensor(out=A[0][:], in0=A[0][:], in1=A[2][:], op=ALU.add), reads=[b_a4[s]], writes=[b_a4[s]])
                P.op("act", lambda e, A=A: e.activation(out=sg[:], in_=A[3][:], func=AF.Sigmoid), reads=[b_a4[s]], writes=[b_sg])
                P.op("dve", lambda e, A=A: e.tensor_tensor(out=sg[:], in0=sg[:], in1=A[3][:], op=ALU.mult), reads=[b_a4[s], b_sg], writes=[b_sg])
                P.op("dve", lambda e, A=A: e.tensor_tensor(out=A[0][:], in0=A[0][:], in1=sg[:], op=ALU.mult), reads=[b_a4[s], b_sg], writes=[b_a4[s]])
                P.op("act", lambda e, A=A: e.activation(out=sq[:, 0:256], in_=A[0][:], func=AF.Square, scale=1.0 / 16, accum_out=ss[:, 0:1]), reads=[b_a4[s]], writes=[b_sq, b_ss])
                rstd_from_ss(P, ss[:, 0:1], b_ss, epsb, b_c)
                P.op("dve", lambda e, A=A: e.scalar_tensor_tensor(out=tnb[:], in0=A[0][:], scalar=ss[:, 0:1], in1=ssdn[:], op0=ALU.mult, op1=ALU.mult), reads=[b_a4[s], b_ss, b_c], writes=[b_tn])
                for k in range(2):
                    P.op("pe", lambda e, k=k: e.transpose(tps[:, k, :], tnb[:, k * 128:(k + 1) * 128], ident[:]), reads=[b_tn, b_c], writes=[b_tps], inc=(k == 1))
                P.op("act", lambda e, s=s: e.copy(out=mT[s][:, 0:2, :], in_=tps[:]), reads=[b_tps], writes=[b_mT[s]])
                P.op("pool", lambda e, s=s: e.tensor_copy(out=mT[s][:, 2:8, :], in_=mst[s][:]), reads=[b_mst[s]], writes=[b_mT[s]])
                for c in range(2):
                    for k in range(8):
                        P.op("pe", lambda e, k=k, c=c, s=s: e.matmul(yps[s][:, c, :], lhsT=mT[s][:, k, :], rhs=wob[:, k, c * 512:(c + 1) * 512], start=(k == 0), stop=(k == 7)),
                             reads=[b_mT[s], b_wo], writes=[b_yps[s]], inc=(k == 7))
                for c in range(2):
                    P.op("act", lambda e, c=c, s=s: e.activation(out=sq[:, c * 512:(c + 1) * 512], in_=yps[s][:, c, :], func=AF.Square, scale=1.0 / 32, accum_out=ss[:, 1 + c:2 + c]), reads=[b_yps[s]], writes=[b_sq, b_ss])
                P.op("dve", lambda e: e.tensor_tensor(out=ss[:, 3:4], in0=ss[:, 1:2], in1=ss[:, 2:3], op=ALU.add), reads=[b_ss], writes=[b_ss])
                rstd_from_ss(P, ss[:, 3:4], b_ss, epsb, b_c)
                for c in range(2):
                    P.op("dve", lambda e, c=c, s=s: e.scalar_tensor_tensor(out=tmp[:, c * 512:(c + 1) * 512], in0=yps[s][:, c, :], scalar=ss[:, 3:4], in1=G1[:, c * 512:(c + 1) * 512], op0=ALU.mult, op1=ALU.mult),
                         reads=[b_yps[s], b_ss, b_G1], writes=[b_tmp])
                P.op("pool", lambda e, s=s: e.tensor_tensor(out=ob[s][:], in0=tmp[:], in1=xt[s][:], op=ALU.add), reads=[b_tmp, b_xt[s]], writes=[b_ob[s]])
                P.dma("sp", xo[tok, :], ob[s][:], reads=[b_ob[s]], writes=[b_out])
        P.finish("sp", [b_out])
        P.emit()
    return nc


def build_k3b(ntl, ntc):
    nc = bass.Bass("TRN2", target_bir_lowering=False)
    NT = ntl + ntc
    NTK = NT * 128
    din = lambda n, s: nc.dram_tensor(n, s, F32, kind="ExternalInput").ap()
    x = din("x", [NTK, D])
    cv = din("cv", [D]); cctx = din("cctx", [D]); w_mod = din("w_mod", [D, 3072]); b_mod = din("b_mod", [3072])
    g_pre = din("g_pre", [D]); g_post = din("g_post", [D])
    w1 = din("w1", [D, DFF]); w2 = din("w2", [DFF, D]); ident_d = din("ident", [128, 128])
    xo = nc.dram_tensor("xo", [NTK, D], F32, kind="ExternalOutput").ap()
    G = 1
    with ExitStack() as st:
        P = Prog(nc, st)
        sb = lambda name, shape, dt=F32: st.enter_context(nc.sbuf_tensor("s_" + name, shape, dt))
        ps = lambda name, shape, dt=F32: st.enter_context(nc.psum_tensor("p_" + name, shape, dt))
        b_c = P.buf("c")
        ident_f = sb("ident_f", [128, 128]); ident = sb("identb", [128, 128], BF16)
        epsb = sb("epsb", [128, 1]); gpre = sb("gpre", [128, D]); gpost = sb("gpost", [128, D])
        P.dma("sp", ident_f[:], ident_d, writes=[b_c])
        P.dma("sp", gpre[:], g_pre.partition_broadcast(128), writes=[b_c])
        P.dma("sp", gpost[:], g_post.partition_broadcast(128), writes=[b_c])
        P.op("dve", lambda e: e.tensor_copy(out=ident[:], in_=ident_f[:]), reads=[b_c], writes=[b_c])
        P.op("dve", lambda e: e.memset(epsb[:], EPS), reads=[b_c], writes=[b_c])
        w1b = sb("w1b", [128, 8, DFF], BF16); w2b = sb("w2b", [128, 32, D], BF16); b_w1 = P.buf("w1"); b_w2 = P.buf("w2")
        MC = ModCalc(P, nc, sb, ps)
        wflat = MC.wst[:].rearrange("p a n -> p (a n)")
        wst = [wflat[:, 0:2048], wflat[:, 2048:4096]]; b_wst = [MC.b[4], MC.b[4]]
        w1v = w1.rearrange("(k p) n -> p k n", p=128); w2v = w2.rearrange("(k p) n -> p k n", p=128)
        i = 0
        for k in range(8):
            for hh in range(2):
                s = i % 2; i += 1
                P.dma("pool", wst[s], w1v[:, k, hh * 2048:(hh + 1) * 2048], writes=[b_wst[s]])
                P.op("pool", lambda e, k=k, hh=hh, s=s: e.tensor_copy(out=w1b[:, k, hh * 2048:(hh + 1) * 2048], in_=wst[s]), reads=[b_wst[s]], writes=[b_w1])
        for k in range(0, 32, 2):
            s = i % 2; i += 1
            P.dma("pool", wst[s].rearrange("p (a n) -> p a n", a=2), w2v[:, k:k + 2, :], writes=[b_wst[s]])
            P.op("pool", lambda e, k=k, s=s: e.tensor_copy(out=w2b[:, k:k + 2, :], in_=wst[s].rearrange("p (a n) -> p a n", a=2)), reads=[b_wst[s]], writes=[b_w2])
        M3 = sb("M3", [128, 3072]); b_M3 = P.buf("M3")
        xt = [sb("xt%d" % i, [128, D]) for i in range(G)]; b_xt = P.bufs(G, "xt")
        sq = sb("sq", [128, D]); b_sq = P.buf("sq")
        ss = sb("ss", [128, 4]); b_ss = P.buf("ss")
        hb = sb("hb", [128, D], BF16); b_hb = P.buf("hb")
        tmp = sb("tmp", [128, D]); b_tmp = P.buf("tmp")
        hT = [sb("hT%d" % i, [128, 8, G * 128], BF16) for i in range(2)]; b_hT = P.bufs(2, "hT")
        uT = sb("uT", [128, 32, G * 128], BF16); b_uT = P.buf("uT")
        ur = [sb("ur%d" % i, [128, G * 128]) for i in range(2)]; b_ur = P.bufs(2, "ur")
        tps = ps("tps", [128, 8, 128], BF16); b_tps = P.buf("tps")
        ups = [ps("ups%d" % i, [128, G * 128]) for i in range(2)]; b_ups = P.bufs(2, "ups")
        yps = [ps("yps%d" % i, [128, 2, 512]) for i in range(2)]; b_yps = P.bufs(2, "yps")
        ob1 = sb("ob1", [128, D]); ob = [ob1, ob1]; bo1 = P.buf("ob"); b_ob = [bo1, bo1]
        b_out = P.buf("out")
        gi = 0
        for seg, (t0, t1, cvec) in enumerate(((0, ntl, cv), (ntl, NT, cctx))):
            MC.set_c(cvec)
            MC.calc(w_mod, b_mod, 0, 3072, lambda j: M3[:, j * 512:(j + 1) * 512], b_M3)
            P.op("dve", lambda e: e.scalar_tensor_tensor(out=M3[:, 1024:2048], in0=M3[:, 1024:2048], scalar=1.0, in1=gpre[:], op0=ALU.add, op1=ALU.mult), reads=[b_M3, b_c], writes=[b_M3])
            P.op("dve", lambda e: e.tensor_tensor(out=M3[:, 2048:3072], in0=M3[:, 2048:3072], in1=gpost[:], op=ALU.mult), reads=[b_M3, b_c], writes=[b_M3])
            for g0 in range(t0, t1, G):
                tiles = list(range(g0, min(g0 + G, t1)))
                ng = len(tiles); W = ng * 128
                hs = gi % 2; gi += 1
                xs_ = []
                for j, t in enumerate(tiles):
                    s = j
                    xs_.append(s)
                    tok = slice(t * 128, (t + 1) * 128)
                    P.dma("sp", xt[s][:], x[tok, :], writes=[b_xt[s]])
                    P.op("act", lambda e, s=s: e.activation(out=sq[:], in_=xt[s][:], func=AF.Square, scale=1.0 / 32, accum_out=ss[:, 0:1]), reads=[b_xt[s]], writes=[b_sq, b_ss])
                    rstd_from_ss(P, ss[:, 0:1], b_ss, epsb, b_c)
                    P.op("dve", lambda e, s=s: e.scalar_tensor_tensor(out=tmp[:], in0=xt[s][:], scalar=ss[:, 0:1], in1=M3[:, 1024:2048], op0=ALU.mult, op1=ALU.mult), reads=[b_xt[s], b_ss, b_M3], writes=[b_tmp])
                    P.op("dve", lambda e: e.tensor_tensor(out=hb[:], in0=tmp[:], in1=M3[:, 0:1024], op=ALU.add), reads=[b_tmp, b_M3], writes=[b_hb])
                    for k in range(8):
                        P.op("pe", lambda e, k=k: e.transpose(tps[:, k, :], hb[:, k * 128:(k + 1) * 128], ident[:]), reads=[b_hb, b_c], writes=[b_tps], inc=(k == 7))
                    P.op("act", lambda e, j=j, hs=hs: e.copy(out=hT[hs][:, :, j * 128:(j + 1) * 128], in_=tps[:]), reads=[b_tps], writes=[b_hT[hs]])
                for f in range(32):
                    u = f % 2
                    for k in range(8):
                        P.op("pe", lambda e, k=k, f=f, u=u, hs=hs, W=W: e.matmul(ups[u][:, 0:W], lhsT=w1b[:, k, f * 128:(f + 1) * 128], rhs=hT[hs][:, k, 0:W], start=(k == 0), stop=(k == 7)),
                             reads=[b_hT[hs], b_w1], writes=[b_ups[u]], inc=(k == 7))
                    P.op("act", lambda e, u=u, W=W: e.activation(out=ur[u][:, 0:W], in_=ups[u][:, 0:W], func=AF.Relu), reads=[b_ups[u]], writes=[b_ur[u]])
                    P.op("dve", lambda e, u=u, f=f, W=W: e.tensor_tensor(out=uT[:, f, 0:W], in0=ur[u][:, 0:W], in1=ur[u][:, 0:W], op=ALU.mult), reads=[b_ur[u]], writes=[b_uT])
                for j, t in enumerate(tiles):
                    s = xs_[j]
                    y = j % 2
                    tok = slice(t * 128, (t + 1) * 128)
                    for c in range(2):
                        for f in range(32):
                            P.op("pe", lambda e, f=f, c=c, y=y, j=j: e.matmul(yps[y][:, c, :], lhsT=uT[:, f, j * 128:(j + 1) * 128], rhs=w2b[:, f, c * 512:(c + 1) * 512], start=(f == 0), stop=(f == 31)),
                                 reads=[b_uT, b_w2], writes=[b_yps[y]], inc=(f == 31))
                    for c in range(2):
                        P.op("act", lambda e, c=c, y=y: e.activation(out=sq[:, c * 512:(c + 1) * 512], in_=yps[y][:, c, :], func=AF.Square, scale=1.0 / 32, accum_out=ss[:, 1 + c:2 + c]), reads=[b_yps[y]], writes=[b_sq, b_ss])
                    P.op("dve", lambda e: e.tensor_tensor(out=ss[:, 3:4], in0=ss[:, 1:2], in1=ss[:, 2:3], op=ALU.add), reads=[b_ss], writes=[b_ss])
                    rstd_from_ss(P, ss[:, 3:4], b_ss, epsb, b_c)
                    for c in range(2):
                        P.op("dve", lambda e, c=c, y=y: e.scalar_tensor_tensor(out=tmp[:, c * 512:(c + 1) * 512], in0=yps[y][:, c, :], scalar=ss[:, 3:4], in1=M3[:, 2048 + c * 512:2048 + (c + 1) * 512], op0=ALU.mult, op1=ALU.mult),
                             reads=[b_yps[y], b_ss, b_M3], writes=[b_tmp])
                    P.op("pool", lambda e, s=s, y=y: e.tensor_tensor(out=ob[y][:], in0=tmp[:], in1=xt[s][:], op=ALU.add), reads=[b_tmp, b_xt[s]], writes=[b_ob[y]])
                    P.dma("sp", xo[tok, :], ob[y][:], reads=[b_ob[y]], writes=[b_out])
        P.finish("sp", [b_out])
        P.emit()
    return nc


BATCH, SEQ, CTX = 4, 8192, 256
OFF_B, OFF_C, OFF_D = 776, 1544, 2056
_cache = {}
_nl = [0]


def _prog(key, fn):
    if key not in _cache:
        _cache[key] = fn()
    return _cache[key]


def _run(nc, maps, tag):
    t0 = time.time()
    res = run_bass_kernel_spmd(nc, maps, core_ids=list(range(8)))
    _nl[0] += 1
    print("[launch %d] %s %.1fs" % (_nl[0], tag, time.time() - t0), flush=True)
    return res.results


def C_(a):
    return np.ascontiguousarray(a, dtype=np.float32)


def launch_k1(x, ctx, c, c_ctx, w_mod_i, b_mod_i, g_pre_i, w_in_i):
    ntl, ntc = SEQ // 2 // 128, CTX // 2 // 128
    nc = _prog("k1", lambda: build_k1(ntl, ntc))
    ident = np.eye(128, dtype=np.float32)
    maps = []
    for k in range(8):
        b, hf = k // 2, k % 2
        xs = np.concatenate([x[b, hf * SEQ // 2:(hf + 1) * SEQ // 2], ctx[b, hf * CTX // 2:(hf + 1) * CTX // 2]], 0)
        maps.append({"x": C_(xs), "cv": C_(c[b]), "cctx": C_(c_ctx), "w_mod": C_(w_mod_i[:, 0:2048]), "b_mod": C_(b_mod_i[0:2048]),
                     "g_pre": C_(g_pre_i), "w_in": C_(w_in_i), "ident": ident})
    res = _run(nc, maps, "k1")
    proj = np.zeros((BATCH, SEQ, 2568), np.float32)
    projc = np.zeros((BATCH, CTX, 2568), np.float32)
    for k in range(8):
        b, hf = k // 2, k % 2
        o = res[k]["proj"]
        proj[b, hf * SEQ // 2:(hf + 1) * SEQ // 2] = o[:SEQ // 2]
        projc[b, hf * CTX // 2:(hf + 1) * CTX // 2] = o[SEQ // 2:]
    return proj, projc


def launch_kf1(n, w1, b1, f1, w2, b2, f2, w3, b3):
    nc = _prog(("kf1", n), lambda: build_kf1(n))
    res = _run(nc, kf1_inputs(n, C_(w1), C_(b1), C_(f1), C_(w2), C_(b2), C_(f2), C_(w3), C_(b3)), "kf1_%d" % n)
    return kf1_gather(res, n)


def launch_ka(proj, projc, conv_w, conv_b, a_log, dt_bias):
    nc = _prog("ka", lambda: build_ka(CTX, SEQ))
    cst = ka_consts()
    NT = CTX + SEQ
    maps = []
    for k in range(8):
        b, g = k // 2, k % 2
        d = dict(cst)
        seq = [np.concatenate([projc[b], proj[b]], 0), np.concatenate([projc[b][::-1], proj[b][::-1]], 0)]
        xc = slice(256 + g * 128, 256 + (g + 1) * 128); bc = slice(512 + g * 64, 512 + (g + 1) * 64); cc = slice(640 + g * 64, 640 + (g + 1) * 64)
        d["xr"] = C_(np.stack([pad_seq(s[:, xc].T, CTX, SEQ) for s in seq]))
        d["br"] = C_(np.stack([pad_seq(s[:, bc].T, CTX, SEQ) for s in seq]))
        d["cr"] = C_(np.stack([pad_seq(s[:, cc].T, CTX, SEQ) for s in seq]))
        def cwpack(idx):
            w = conv_w[:, idx].T
            bb = conv_b[idx][:, None]
            return C_(np.stack([np.concatenate([w, bb], 1), np.concatenate([w[:, ::-1], bb], 1)]))
        d["cwx"] = cwpack(np.arange(g * 128, (g + 1) * 128))
        d["cwb"] = cwpack(256 + np.arange(g * 64, (g + 1) * 64))
        d["cwc"] = cwpack(384 + np.arange(g * 64, (g + 1) * 64))
        d["dtr"] = C_(np.stack([np.stack([seq[dd][:, 768 + dd * 4 + 2 * g + h] for h in range(2)]) for dd in range(2)]))
        d["alog"] = C_(a_log[:, 2 * g:2 * g + 2]); d["dtb"] = C_(dt_bias[:, 2 * g:2 * g + 2])
        maps.append(d)
    res = _run(nc, maps, "ka")
    mk = lambda: (np.zeros((BATCH, SEQ, 256), np.float32), np.zeros((BATCH, CTX, 256), np.float32))
    yf, yb, xs = mk(), mk(), mk()
    for k in range(8):
        b, g = k // 2, k % 2
        y = res[k]["y"]; x_ = res[k]["xs"]
        cs = slice(g * 128, (g + 1) * 128)
        yf[1][b][:, cs] = y[0, :CTX]; yf[0][b][:, cs] = y[0, CTX:]
        yb[1][b][:, cs] = y[1, :CTX][::-1]; yb[0][b][:, cs] = y[1, CTX:][::-1]
        xs[1][b][:, cs] = x_[:CTX]; xs[0][b][:, cs] = x_[CTX:]
    return yf, yb, xs


def launch_kc(proj, projc, qn, kn, sink):
    nc = _prog("kc", lambda: build_kc(SEQ, CTX))
    cst = _prog("kc_consts", lambda: kc_consts(SEQ))
    maps = []
    for k in range(8):
        b, g = k // 2, k % 2
        d = dict(cst)
        full = np.concatenate([proj[b], projc[b]], 0)
        for br, off in (("w", OFF_C), ("d", OFF_D)):
            d["qT_" + br] = C_(np.stack([full[:, off + (2 * g + h) * 64: off + (2 * g + h + 1) * 64].T for h in range(2)]))
            d["kT_" + br] = C_(full[:, off + 256 + g * 64: off + 256 + (g + 1) * 64].T)
            d["v_" + br] = C_(full[:, off + 384 + g * 64: off + 384 + (g + 1) * 64])
        d["qn"] = C_(qn); d["kn"] = C_(kn); d["sink"] = C_(sink[2 * g:2 * g + 2])
        maps.append(d)
    res = _run(nc, maps, "kc")
    out = {}
    for br in "dw":
        lat = np.zeros((BATCH, SEQ, 256), np.float32); cx = np.zeros((BATCH, CTX, 256), np.float32)
        for k in range(8):
            b, g = k // 2, k % 2
            y = res[k]["y_" + br]
            lat[b][:, g * 128:(g + 1) * 128] = y[:SEQ]; cx[b][:, g * 128:(g + 1) * 128] = y[SEQ:]
        out[br] = (lat, cx)
    return out


def launch_kb(pr, n, kfilt, conv_w, conv_b, hbias):
    nc = _prog(("kb", n), lambda: build_kb(n))
    cst = _prog(("kb_consts", n), lambda: kb_consts(n))
    maps = []
    for k in range(8):
        b, g = k // 2, k % 2
        d = dict(cst)
        chs = [OFF_B + s * 256 + g * 128 + np.arange(128) for s in range(3)]
        uraw = np.stack([pr[b][:, ch].T for ch in chs])
        d["raw"] = kb_pack_raw(C_(uraw), n)
        d["full"] = kb_pack_full(C_(kfilt[:, :, g * 128:(g + 1) * 128]), n)
        cw = np.stack([np.concatenate([conv_w[:, s * 256 + g * 128: s * 256 + (g + 1) * 128].T, conv_b[s * 256 + g * 128: s * 256 + (g + 1) * 128][:, None]], 1) for s in range(3)])
        d["cw"] = C_(cw.reshape(-1)); d["hb"] = C_(hbias[:, g * 128:(g + 1) * 128].reshape(-1))
        maps.append(d)
    res = _run(nc, maps, "kb_%d" % n)
    out = np.zeros((BATCH, 256, n), np.float32)
    for k in range(8):
        b, g = k // 2, k % 2
        y = res[k]["y"]
        out[b, g * 128:(g + 1) * 128] = y.transpose(1, 0, 2).reshape(128, n)
    return out


def _tok_shard(lat, cx, k):
    b, hf = k // 2, k % 2
    return np.concatenate([lat[b, hf * SEQ // 2:(hf + 1) * SEQ // 2], cx[b, hf * CTX // 2:(hf + 1) * CTX // 2]], 0)


def _tok_gather(res, name):
    lat = np.zeros((BATCH, SEQ, 1024), np.float32); cx = np.zeros((BATCH, CTX, 1024), np.float32)
    for k in range(8):
        b, hf = k // 2, k % 2
        o = res[k][name]
        lat[b, hf * SEQ // 2:(hf + 1) * SEQ // 2] = o[:SEQ // 2]; cx[b, hf * CTX // 2:(hf + 1) * CTX // 2] = o[SEQ // 2:]
    return lat, cx


def launch_k3a(x, ctx, yf, yb, xs, z, hy, hyc, yw, yd, c, c_ctx, w_mod_i, b_mod_i, g_post, skip, ssdn, w_out_i):
    ntl, ntc = SEQ // 2 // 128, CTX // 2 // 128
    nc = _prog("k3a", lambda: build_k3a(ntl, ntc))
    ident = np.eye(128, dtype=np.float32)
    maps = []
    for k in range(8):
        b, hf = k // 2, k % 2
        ls = slice(hf * SEQ // 2, (hf + 1) * SEQ // 2); cs = slice(hf * CTX // 2, (hf + 1) * CTX // 2)
        mixT = np.concatenate([
            np.concatenate([hy[b][:, ls], hyc[b][:, cs]], 1),
            np.concatenate([yw[0][b, ls], yw[1][b, cs]], 0).T,
            np.concatenate([yd[0][b, ls], yd[1][b, cs]], 0).T], 0)
        maps.append({"x": C_(_tok_shard(x, ctx, k)), "yf": C_(_tok_shard(yf[0], yf[1], k)), "yb": C_(_tok_shard(yb[0], yb[1], k)),
                     "xs": C_(_tok_shard(xs[0], xs[1], k)), "z": C_(_tok_shard(z[0], z[1], k)), "mixT": C_(mixT),
                     "cv": C_(c[b]), "cctx": C_(c_ctx), "w_mod": C_(w_mod_i[:, 2048:3072]), "b_mod": C_(b_mod_i[2048:3072]),
                     "g_post": C_(g_post), "skip": C_(skip), "ssdn": C_(ssdn), "w_out": C_(w_out_i), "ident": ident})
    res = _run(nc, maps, "k3a")
    return _tok_gather(res, "xo")


def launch_k3b(x, ctx, c, c_ctx, w_mod_i, b_mod_i, g_pre, g_post, w1, w2):
    ntl, ntc = SEQ // 2 // 128, CTX // 2 // 128
    nc = _prog("k3b", lambda: build_k3b(ntl, ntc))
    ident = np.eye(128, dtype=np.float32)
    maps = []
    for k in range(8):
        b = k // 2
        maps.append({"x": C_(_tok_shard(x, ctx, k)), "cv": C_(c[b]), "cctx": C_(c_ctx), "w_mod": C_(w_mod_i[:, 3072:6144]), "b_mod": C_(b_mod_i[3072:6144]),
                     "g_pre": C_(g_pre), "g_post": C_(g_post), "w1": C_(w1), "w2": C_(w2), "ident": ident})
    res = _run(nc, maps, "k3b")
    return _tok_gather(res, "xo")


def forward(x, c, ctx, c_ctx, w_mod, b_mod, norm_mix_pre, norm_mix_post, norm_mlp_pre, norm_mlp_post,
            w_in, w_out, ssd_conv_w, ssd_conv_b, ssd_a_log, ssd_dt_bias, ssd_d, ssd_norm,
            hy_conv_w, hy_conv_b, hy_w1, hy_b1, hy_freq1, hy_w2, hy_b2, hy_freq2, hy_w3, hy_b3, hy_bias,
            attn_sink, q_norm, k_norm, mlp_w1, mlp_w2, depth=2, dbg=None):
    A = lambda a: np.asarray(a, dtype=np.float32)
    x = A(x); ctx = A(ctx); c = A(c); c_ctx = A(c_ctx)
    for i in range(depth):
        need_ctx = i < depth - 1
        proj, projc = launch_k1(x, ctx, c, c_ctx, A(w_mod[i]), A(b_mod[i]), A(norm_mix_pre[i]), A(w_in[i]))
        hyf = (A(hy_w1[i]), A(hy_b1[i]), A(hy_freq1[i]), A(hy_w2[i]), A(hy_b2[i]), A(hy_freq2[i]), A(hy_w3[i]), A(hy_b3[i]))
        k_lat = launch_kf1(SEQ, *hyf)
        hy = launch_kb(proj, SEQ, k_lat, A(hy_conv_w[i]), A(hy_conv_b[i]), A(hy_bias[i]))
        if need_ctx:
            k_ctx = launch_kf1(CTX, *hyf)
            hyc = launch_kb(projc, CTX, k_ctx, A(hy_conv_w[i]), A(hy_conv_b[i]), A(hy_bias[i]))
        else:
            hyc = np.zeros((BATCH, 256, CTX), np.float32)
        yf, yb, xs = launch_ka(proj, projc, A(ssd_conv_w[i]), A(ssd_conv_b[i]), A(ssd_a_log[i]), A(ssd_dt_bias[i]))
        att = launch_kc(proj, projc, A(q_norm[i]), A(k_norm[i]), A(attn_sink[i]))
        z = (proj[:, :, 0:256], projc[:, :, 0:256])
        if dbg is not None:
            dbg.update({"proj": proj, "projc": projc, "hy": hy, "hyc": hyc, "yf": yf, "yb": yb, "xs": xs, "att": att})
        x1, ctx1 = launch_k3a(x, ctx, yf, yb, xs, z, hy, hyc, att["w"], att["d"], c, c_ctx, A(w_mod[i]), A(b_mod[i]),
                              A(norm_mix_post[i]), np.repeat(A(ssd_d[i]), 64), A(ssd_norm[i]), A(w_out[i]))
        x2, ctx2 = launch_k3b(x1, ctx1, c, c_ctx, A(w_mod[i]), A(b_mod[i]), A(norm_mlp_pre[i]), A(norm_mlp_post[i]), A(mlp_w1[i]), A(mlp_w2[i]))
        if dbg is not None:
            dbg.update({"x1": x1, "ctx1": ctx1, "x2": x2, "ctx2": ctx2})
        x = x2
        if need_ctx:
            ctx = ctx2
    return x


def kernel(**inputs):
    _nl[0] = 0
    out = forward(**inputs, depth=2)
    return np.ascontiguousarray(out, dtype=np.float32)
```

```python
import os
import sys
import time
import math
from contextlib import ExitStack
import numpy as np
import concourse.bass as bass
import concourse.mybir as mybir
from concourse.bass_utils import run_bass_kernel_spmd


F32 = mybir.dt.float32
BF16 = mybir.dt.bfloat16
ALU = mybir.AluOpType
AF = mybir.ActivationFunctionType
AX = mybir.AxisListType


class Buf:
    __slots__ = ("name", "w", "r")

    def __init__(self, name):
        self.name = name
        self.w = None
        self.r = []


class Prog:
    ENG = ("pe", "act", "dve", "pool", "sp")
    NDMA = 8

    def __init__(self, nc, stack):
        self.nc = nc
        self.ops = {e: [] for e in self.ENG}
        self.sems = []
        self.semval = []
        self.known = {e: {} for e in self.ENG}
        self.esem = {}
        for e in ("pe", "act", "dve", "pool"):
            self.esem[e] = self._newsem(stack, "s_" + e)
        self.dsem = {}
        self.dcnt = {}
        for e in ("sp", "act", "pool"):
            self.dsem[e] = [self._newsem(stack, "d_%s%d" % (e, i)) for i in range(self.NDMA)]
            self.dcnt[e] = 0
        self.nbuf = 0

    def _newsem(self, stack, name):
        h = stack.enter_context(self.nc.semaphore(name))
        self.sems.append(h)
        self.semval.append(0)
        return len(self.sems) - 1

    def buf(self, name=None):
        self.nbuf += 1
        return Buf(name or "b%d" % self.nbuf)

    def bufs(self, n, name="b"):
        return [self.buf("%s%d" % (name, i)) for i in range(n)]

    def _deps(self, eng, reads, writes):
        need = {}
        def add(d):
            if d is None:
                return
            s, v = d
            if need.get(s, 0) < v:
                need[s] = v
        for b in reads:
            add(b.w)
        for b in writes:
            add(b.w)
            for d in b.r:
                add(d)
        kn = self.known[eng]
        waits = []
        own = self.esem.get(eng)
        for s, v in need.items():
            if s == own and v > self.semval[s]:
                continue
            if kn.get(s, 0) < v:
                kn[s] = v
                waits.append((s, v))
        return waits

    def op(self, eng, fn, reads=(), writes=(), inc=True):
        waits = self._deps(eng, reads, writes)
        s = self.esem[eng]
        if inc:
            self.semval[s] += 1
            tok = (s, self.semval[s])
            self.ops[eng].append((fn, waits, (s, 1)))
        else:
            tok = (s, self.semval[s] + 1)
            self.ops[eng].append((fn, waits, None))
        for b in reads:
            b.r.append(tok)
        for b in writes:
            b.w = tok
            b.r = []
        return tok

    def dma(self, q, out, in_, reads=(), writes=(), **kw):
        waits = self._deps(q, reads, writes)
        i = self.dcnt[q]
        self.dcnt[q] += 1
        s = self.dsem[q][i % self.NDMA]
        prev = self.semval[s]
        if prev > 0 and self.known[q].get(s, 0) < prev:
            self.known[q][s] = prev
            waits.append((s, prev))
        self.semval[s] += 16
        tok = (s, self.semval[s])
        self.ops[q].append((lambda e: e.dma_start(out=out, in_=in_, **kw), waits, (s, 16)))
        for b in reads:
            b.r.append(tok)
        for b in writes:
            b.w = tok
            b.r = []
        return tok

    def finish(self, eng="sp", bufs=()):
        waits = self._deps(eng, bufs, ())
        for q in self.dsem:
            for s in self.dsem[q]:
                v = self.semval[s]
                if v > 0 and self.known[eng].get(s, 0) < v:
                    self.known[eng][s] = v
                    waits.append((s, v))
        self.ops[eng].append((None, waits, None))

    def emit(self):
        nc = self.nc
        hmap = {"pe": "tensor", "act": "scalar", "dve": "vector", "pool": "gpsimd", "sp": "sync"}
        with nc.Block() as block:
            for e in self.ENG:
                ops = self.ops[e]
                sems = self.sems

                def body(h, ops=ops):
                    for fn, waits, inc in ops:
                        for s, v in waits:
                            h.wait_ge(sems[s], v)
                        if fn is not None:
                            ins = fn(h)
                            if inc is not None:
                                ins.then_inc(sems[inc[0]], inc[1])
                getattr(block, hmap[e])(body)


D = 1024
DIN = 2568
EPS = 1e-6


def load_bcast_row(P, q, dst_ap, src_row_ap, n, wbuf):
    P.dma(q, dst_ap, src_row_ap.partition_broadcast(128), writes=[wbuf])


def mod_scratch(P, nc, st, ncols):
    S = {}
    S["c_sb"] = st.enter_context(nc.sbuf_tensor("c_sb", [128, 8], F32))
    S["c_sg"] = st.enter_context(nc.sbuf_tensor("c_sg", [128, 8], F32))
    S["c_bc"] = st.enter_context(nc.sbuf_tensor("c_bc", [128, 8, 128], F32))
    S["bm"] = st.enter_context(nc.sbuf_tensor("bm", [128, ncols], F32))
    S["wst"] = [st.enter_context(nc.sbuf_tensor("wmst%d" % i, [128, 8, 512], F32)) for i in range(2)]
    S["ps"] = [st.enter_context(nc.psum_tensor("modps%d" % i, [128, 512], F32)) for i in range(2)]
    S["b"] = P.bufs(4, "modb")
    S["b_w"] = P.bufs(2, "wmst")
    S["b_ps"] = P.bufs(2, "modps")
    return S


def build_mod(P, nc, S, cvec, w_mod, b_mod, col0, ncols, out_tile, out_buf):
    c_sb, c_sg, c_bc, bm, wst, ps = S["c_sb"], S["c_sg"], S["c_bc"], S["bm"], S["wst"], S["ps"]
    b_c, b_cs, b_bc, b_bm = S["b"]
    b_w, b_ps = S["b_w"], S["b_ps"]
    P.dma("sp", c_sb[:], cvec.rearrange("(k p) -> p k", p=128), writes=[b_c], allow_slow_non_contiguous=True)
    P.dma("sp", bm[:, 0:ncols], b_mod[col0:col0 + ncols].partition_broadcast(128), writes=[b_bm])
    P.op("act", lambda e: e.activation(out=c_sg[:], in_=c_sb[:], func=AF.Sigmoid), reads=[b_c], writes=[b_cs])
    P.op("dve", lambda e: e.tensor_tensor(out=c_sg[:], in0=c_sg[:], in1=c_sb[:], op=ALU.mult), reads=[b_c, b_cs], writes=[b_cs])
    P.op("dve", lambda e: e.tensor_copy(out=c_bc[:], in_=c_sg[:].unsqueeze(2).to_broadcast([128, 8, 128])), reads=[b_cs], writes=[b_bc])
    wv = w_mod.rearrange("(k p) n -> p k n", p=128)
    for j in range(ncols // 512):
        s = j % 2
        P.dma("sp", wst[s][:], wv[:, :, col0 + j * 512: col0 + (j + 1) * 512], writes=[b_w[s]])
        for k in range(8):
            P.op("pe", lambda e, k=k, s=s: e.matmul(ps[s][:], lhsT=c_bc[:, k, :], rhs=wst[s][:, k, :], start=(k == 0), stop=(k == 7)),
                 reads=[b_bc, b_w[s]], writes=[b_ps[s]])
        P.op("dve", lambda e, j=j, s=s: e.tensor_tensor(out=out_tile[:, j * 512:(j + 1) * 512], in0=ps[s][:], in1=bm[:, j * 512:(j + 1) * 512], op=ALU.add),
             reads=[b_ps[s], b_bm], writes=[out_buf])


def build_k1(ntiles_lat, ntiles_ctx):
    nc = bass.Bass("TRN2", target_bir_lowering=False)
    NT = ntiles_lat + ntiles_ctx
    x = nc.dram_tensor("x", [NT * 128, D], F32, kind="ExternalInput").ap()
    cv = nc.dram_tensor("cv", [D], F32, kind="ExternalInput").ap()
    cctx = nc.dram_tensor("cctx", [D], F32, kind="ExternalInput").ap()
    w_mod = nc.dram_tensor("w_mod", [D, 2048], F32, kind="ExternalInput").ap()
    b_mod = nc.dram_tensor("b_mod", [2048], F32, kind="ExternalInput").ap()
    g_pre = nc.dram_tensor("g_pre", [D], F32, kind="ExternalInput").ap()
    w_in = nc.dram_tensor("w_in", [D, DIN], F32, kind="ExternalInput").ap()
    proj = nc.dram_tensor("proj", [NT * 128, DIN], F32, kind="ExternalOutput").ap()
    ident_d = nc.dram_tensor("ident", [128, 128], F32, kind="ExternalInput").ap()
    with ExitStack() as st:
        P = Prog(nc, st)
        sb = lambda name, shape, dt=F32: st.enter_context(nc.sbuf_tensor(name, shape, dt))
        ident_f = sb("ident_f", [128, 128])
        ident = sb("ident_b", [128, 128], BF16)
        b_id = P.buf("ident")
        P.dma("sp", ident_f[:], ident_d, writes=[b_id])
        P.op("dve", lambda e: e.tensor_copy(out=ident[:], in_=ident_f[:]), reads=[b_id], writes=[b_id])
        modl = sb("modl", [128, 2048]); b_modl = P.buf("modl")
        modc = sb("modc", [128, 2048]); b_modc = P.buf("modc")
        if True:
            st2 = st
            MS = mod_scratch(P, nc, st, 2048)
            build_mod(P, nc, MS, cv, w_mod, b_mod, 0, 2048, modl, b_modl)
            build_mod(P, nc, MS, cctx, w_mod, b_mod, 0, 2048, modc, b_modc)
            gp = sb("gp", [128, D]); b_gp = P.buf("gp")
            P.dma("sp", gp[:], g_pre.partition_broadcast(128), writes=[b_gp])
            for m, bm_ in ((modl, b_modl), (modc, b_modc)):
                P.op("dve", lambda e, m=m: e.scalar_tensor_tensor(out=m[:, 1024:2048], in0=m[:, 1024:2048], scalar=1.0, in1=gp[:], op0=ALU.add, op1=ALU.mult),
                     reads=[bm_, b_gp], writes=[bm_])
            w_bf = sb("w_bf", [128, 8, DIN], BF16); b_wbf = P.buf("wbf")
            wst = [st2.enter_context(nc.sbuf_tensor("wst%d" % i, [128, DIN], F32)) for i in range(2)]
            b_wst = P.bufs(2, "wst")
            wv = w_in.rearrange("(k p) n -> p k n", p=128)
            for k in range(8):
                s = k % 2
                P.dma("pool", wst[s][:], wv[:, k, :], writes=[b_wst[s]])
                P.op("pool", lambda e, k=k, s=s: e.tensor_copy(out=w_bf[:, k, :], in_=wst[s][:]), reads=[b_wst[s]], writes=[b_wbf])
        epsb = sb("epsb", [128, 1])
        b_eps = P.buf("eps")
        P.op("dve", lambda e: e.memset(epsb[:], EPS), writes=[b_eps])
        NB = 2
        xt = [sb("xt%d" % i, [128, D]) for i in range(NB)]; b_xt = P.bufs(NB, "xt")
        sq = sb("sq", [128, D]); b_sq = P.buf("sq")
        ss = [sb("ss%d" % i, [128, 1]) for i in range(NB)]; b_ss = P.bufs(NB, "ss")
        hb = [sb("hb%d" % i, [128, D], BF16) for i in range(NB)]; b_hb = P.bufs(NB, "hb")
        hT = [sb("hT%d" % i, [128, 8, 128], BF16) for i in range(NB)]; b_hT = P.bufs(NB, "hT")
        tps = [st.enter_context(nc.psum_tensor("tps%d" % i, [128, 8, 128], BF16)) for i in range(2)]; b_tps = P.bufs(2, "tps")
        ops_ = [st.enter_context(nc.psum_tensor("ops%d" % i, [128, 512], F32)) for i in range(4)]; b_ops = P.bufs(4, "ops")
        ob = [sb("ob%d" % i, [128, DIN]) for i in range(NB)]; b_ob = P.bufs(NB, "ob")
        b_out = P.buf("out")
        colch = [(c0, min(512, DIN - c0)) for c0 in range(0, DIN, 512)]
        pi = 0
        for t in range(NT):
            s = t % NB
            mod = modl if t < ntiles_lat else modc
            bmod = b_modl if t < ntiles_lat else b_modc
            P.dma("sp", xt[s][:], x[t * 128:(t + 1) * 128, :], writes=[b_xt[s]])
            P.op("act", lambda e, s=s: e.activation(out=sq[:], in_=xt[s][:], func=AF.Square, scale=float(D ** -0.5), accum_out=ss[s][:]),
                 reads=[b_xt[s]], writes=[b_sq, b_ss[s]])
            P.op("act", lambda e, s=s: e.activation(out=ss[s][:], in_=ss[s][:], func=AF.Sqrt, bias=epsb[:, 0:1]),
                 reads=[b_ss[s], b_eps], writes=[b_ss[s]])
            P.op("dve", lambda e, s=s: e.reciprocal(out=ss[s][:], in_=ss[s][:]),
                 reads=[b_ss[s]], writes=[b_ss[s]])
            P.op("dve", lambda e, s=s, mod=mod: e.scalar_tensor_tensor(out=xt[s][:], in0=xt[s][:], scalar=ss[s][:, 0:1], in1=mod[:, 1024:2048], op0=ALU.mult, op1=ALU.mult),
                 reads=[b_xt[s], b_ss[s], bmod], writes=[b_xt[s]])
            P.op("dve", lambda e, s=s, mod=mod: e.tensor_tensor(out=hb[s][:], in0=xt[s][:], in1=mod[:, 0:1024], op=ALU.add),
                 reads=[b_xt[s], bmod], writes=[b_hb[s]])
            tp = t % 2
            for k in range(8):
                P.op("pe", lambda e, k=k, s=s, tp=tp: e.transpose(tps[tp][:, k, :], hb[s][:, k * 128:(k + 1) * 128], ident[:]),
                     reads=[b_hb[s], b_id], writes=[b_tps[tp]], inc=(k == 7))
            P.op("act", lambda e, s=s, tp=tp: e.copy(out=hT[s][:], in_=tps[tp][:]), reads=[b_tps[tp]], writes=[b_hT[s]])
            for (c0, cw) in colch:
                p = pi % 4; pi += 1
                for k in range(8):
                    P.op("pe", lambda e, k=k, s=s, p=p, c0=c0, cw=cw: e.matmul(ops_[p][:, 0:cw], lhsT=hT[s][:, k, :], rhs=w_bf[:, k, c0:c0 + cw], start=(k == 0), stop=(k == 7)),
                         reads=[b_hT[s], b_wbf], writes=[b_ops[p]], inc=(k == 7))
                eng = "act" if (pi % 2) else "dve"
                if eng == "act":
                    P.op("act", lambda e, s=s, p=p, c0=c0, cw=cw: e.copy(out=ob[s][:, c0:c0 + cw], in_=ops_[p][:, 0:cw]), reads=[b_ops[p]], writes=[b_ob[s]])
                else:
                    P.op("dve", lambda e, s=s, p=p, c0=c0, cw=cw: e.tensor_copy(out=ob[s][:, c0:c0 + cw], in_=ops_[p][:, 0:cw]), reads=[b_ops[p]], writes=[b_ob[s]])
            P.dma("sp", proj[t * 128:(t + 1) * 128, :], ob[s][:], reads=[b_ob[s]], writes=[b_out])
        P.finish("sp", [b_out])
        P.emit()
    return nc


EPS = 1e-6
HD = 64


def build_kc(NL, NC):
    nc = bass.Bass("TRN2", target_bir_lowering=False)
    NT = NL + NC
    nlt, nct, nkt = NL // 128, NC // 128, NT // 128
    din = lambda n, s: nc.dram_tensor(n, s, F32, kind="ExternalInput").ap()
    qT = {b: din("qT_" + b, [2, 64, NT]) for b in "dw"}
    kT = {b: din("kT_" + b, [64, NT]) for b in "dw"}
    vv = {b: din("v_" + b, [NT, 64]) for b in "dw"}
    cos_d = din("cos2", [64, NL]); sin_d = din("sin2", [64, NL])
    Rm_d = din("Rm", [64, 64]); ones_d = din("ones64", [64, 64]); ident_d = din("ident", [128, 128])
    mprev_d = din("mprev", [128, 128]); mnext_d = din("mnext", [128, 128])
    qn_d = din("qn", [64]); kn_d = din("kn", [64]); sink_d = din("sink", [2])
    y = {b: nc.dram_tensor("y_" + b, [NT, 128], F32, kind="ExternalOutput").ap() for b in "dw"}
    with ExitStack() as st:
        P = Prog(nc, st)
        sb = lambda name, shape, dt=F32: st.enter_context(nc.sbuf_tensor("s_" + name, shape, dt))
        ps = lambda name, shape, dt=F32: st.enter_context(nc.psum_tensor("p_" + name, shape, dt))
        b_c = P.buf("consts")
        cos2 = sb("cos2", [64, NL]); sin2 = sb("sin2", [64, NL])
        Rm = sb("Rm", [64, 64]); ones64 = sb("ones64", [64, 64]); ident = sb("ident", [128, 128])
        mpn_f = sb("mpn_f", [128, 2, 128]); mpn = sb("mpn", [128, 2, 128], BF16)
        gq = sb("gq", [64, 1]); gk = sb("gk", [64, 1]); sk = sb("sk", [128, 2]); epsb = sb("epsb", [128, 1])
        P.dma("sp", cos2[:], cos_d, writes=[b_c]); P.dma("sp", sin2[:], sin_d, writes=[b_c])
        P.dma("sp", Rm[:], Rm_d, writes=[b_c]); P.dma("sp", ones64[:], ones_d, writes=[b_c]); P.dma("sp", ident[:], ident_d, writes=[b_c])
        P.dma("sp", mpn_f[:, 0, :], mprev_d, writes=[b_c]); P.dma("sp", mpn_f[:, 1, :], mnext_d, writes=[b_c])
        P.dma("sp", gq[:], qn_d.rearrange("(p o) -> p o", o=1), writes=[b_c]); P.dma("sp", gk[:], kn_d.rearrange("(p o) -> p o", o=1), writes=[b_c])
        P.dma("sp", sk[:], sink_d.partition_broadcast(128), writes=[b_c])
        P.op("dve", lambda e: e.tensor_copy(out=mpn[:], in_=mpn_f[:]), reads=[b_c], writes=[b_c])
        P.op("dve", lambda e: e.memset(epsb[:], EPS), reads=[b_c], writes=[b_c])
        P.op("act", lambda e: e.activation(out=sk[:], in_=sk[:], func=AF.Exp), reads=[b_c], writes=[b_c])
        QT = sb("QT", [64, 2, NT], BF16); b_QT = P.buf("QT")
        KT = sb("KT", [64, NT], BF16); b_KT = P.buf("KT")
        VA = sb("VA", [128, nkt, 65], BF16); b_VA = P.buf("VA")
        vst = sb("vst", [128, nkt, 64]); b_vst = P.buf("vst")
        stg = [sb("stg%d" % i, [64, 512]) for i in range(2)]; b_stg = P.bufs(2, "stg")
        sqb = sb("sqb", [64, 512]); b_sq = P.buf("sqb")
        rsb = sb("rsb", [64, 512]); b_rs = P.buf("rsb")
        t1b = sb("t1b", [64, 512]); b_t1 = P.buf("t1b")
        t2b = sb("t2b", [64, 512]); b_t2 = P.buf("t2b")
        pps = [ps("pps%d" % i, [64, 512]) for i in range(2)]; b_pps = P.bufs(2, "pps")
        sps = [ps("sps%d" % i, [128, 512]) for i in range(3)]; b_sps = P.bufs(3, "sps")
        ops_ = [ps("ops%d" % i, [128, 512]) for i in range(2)]; b_ops = P.bufs(2, "ops")
        tps = ps("tps", [128, 4, 128]); b_tps = P.buf("tps")
        ptb = [sb("ptb%d" % i, [128, 512], BF16) for i in range(3)]; b_pt = P.bufs(3, "ptb")
        osb = sb("osb", [65, 512]); b_osb = P.buf("osb")
        rd = sb("rd", [128, 4, 1]); b_rd = P.buf("rd")
        otb = [sb("otb%d" % i, [128, 4, 64]) for i in range(2)]; b_ot = P.bufs(2, "otb")
        b_y = P.buf("y")
        cnt = {"stg": 0, "s": 0, "o": 0, "ot": 0, "pp": 0}

        def prep(src_ap, dst_ap, ncols, col0, norm_gain, rope):
            for c0 in range(0, ncols, 512):
                cw = min(512, ncols - c0)
                s = cnt["stg"] % 2; cnt["stg"] += 1
                P.dma("sp", stg[s][:, 0:cw], src_ap[:, c0:c0 + cw], writes=[b_stg[s]])
                cur = stg[s]; bcur = b_stg[s]
                if norm_gain is not None:
                    pp = cnt["pp"] % 2; cnt["pp"] += 1
                    P.op("act", lambda e, s=s, cw=cw: e.activation(out=sqb[:, 0:cw], in_=stg[s][:, 0:cw], func=AF.Square), reads=[b_stg[s]], writes=[b_sq])
                    P.op("pe", lambda e, pp=pp, cw=cw: e.matmul(pps[pp][:, 0:cw], lhsT=ones64[:], rhs=sqb[:, 0:cw], start=True, stop=True), reads=[b_sq, b_c], writes=[b_pps[pp]])
                    P.op("act", lambda e, pp=pp, cw=cw: e.activation(out=rsb[:, 0:cw], in_=pps[pp][:, 0:cw], func=AF.Sqrt, bias=epsb[0:64, 0:1], scale=1.0 / 64), reads=[b_pps[pp], b_c], writes=[b_rs])
                    P.op("dve", lambda e, cw=cw: e.reciprocal(out=rsb[:, 0:cw], in_=rsb[:, 0:cw]), reads=[b_rs], writes=[b_rs])
                    P.op("dve", lambda e, s=s, cw=cw, g=norm_gain: e.scalar_tensor_tensor(out=stg[s][:, 0:cw], in0=stg[s][:, 0:cw], scalar=g[:, 0:1], in1=rsb[:, 0:cw], op0=ALU.mult, op1=ALU.mult),
                         reads=[b_stg[s], b_rs, b_c], writes=[b_stg[s]])
                if rope:
                    pp = cnt["pp"] % 2; cnt["pp"] += 1
                    P.op("pe", lambda e, pp=pp, s=s, cw=cw: e.matmul(pps[pp][:, 0:cw], lhsT=Rm[:], rhs=stg[s][:, 0:cw], start=True, stop=True), reads=[b_stg[s], b_c], writes=[b_pps[pp]])
                    P.op("pool", lambda e, s=s, cw=cw, c0=c0: e.tensor_tensor(out=t1b[:, 0:cw], in0=stg[s][:, 0:cw], in1=cos2[:, col0 + c0:col0 + c0 + cw], op=ALU.mult), reads=[b_stg[s], b_c], writes=[b_t1])
                    P.op("dve", lambda e, pp=pp, cw=cw, c0=c0: e.tensor_tensor(out=t2b[:, 0:cw], in0=pps[pp][:, 0:cw], in1=sin2[:, col0 + c0:col0 + c0 + cw], op=ALU.mult), reads=[b_pps[pp], b_c], writes=[b_t2])
                    P.op("dve", lambda e, cw=cw, c0=c0: e.tensor_tensor(out=dst_ap[:, c0:c0 + cw], in0=t1b[:, 0:cw], in1=t2b[:, 0:cw], op=ALU.add), reads=[b_t1, b_t2], writes=[dst_buf[0]])
                else:
                    P.op("dve", lambda e, s=s, cw=cw, c0=c0: e.tensor_copy(out=dst_ap[:, c0:c0 + cw], in_=stg[s][:, 0:cw]), reads=[b_stg[s]], writes=[dst_buf[0]])

        dst_buf = [None]

        def attn_group(qcols, N, ktiles, sink_ap, out_cb):
            o = cnt["o"] % 2; cnt["o"] += 1
            nk = len(ktiles)
            def s_mm(i):
                kt = ktiles[i][0]
                s = (base + i) % 3
                P.op("pe", lambda e, s=s, kt=kt: e.matmul(sps[s][:, 0:N], lhsT=kt, rhs=qcols, start=True, stop=True), reads=[b_KT, b_QT], writes=[b_sps[s]])
            base = cnt["s"]; cnt["s"] += nk
            s_mm(0)
            for i, (kt, va, mask) in enumerate(ktiles):
                s = (base + i) % 3
                if i + 1 < nk:
                    s_mm(i + 1)
                P.op("act", lambda e, s=s: e.activation(out=ptb[s][:, 0:N], in_=sps[s][:, 0:N], func=AF.Exp, scale=0.125), reads=[b_sps[s]], writes=[b_pt[s]])
                if mask is not None:
                    P.op("dve", lambda e, s=s, mask=mask: e.tensor_tensor(out=ptb[s][:, 0:N], in0=ptb[s][:, 0:N], in1=mask, op=ALU.mult), reads=[b_pt[s], b_c], writes=[b_pt[s]])
                P.op("pe", lambda e, s=s, va=va, i=i: e.matmul(ops_[o][0:65, 0:N], lhsT=va, rhs=ptb[s][:, 0:N], start=(i == 0), stop=(i == nk - 1)), reads=[b_pt[s], b_VA], writes=[b_ops[o]], inc=(i == nk - 1))
            nj = N // 128
            P.op("act", lambda e: e.copy(out=osb[:, 0:N], in_=ops_[o][0:65, 0:N]), reads=[b_ops[o]], writes=[b_osb])
            for j in range(nj):
                P.op("pe", lambda e, j=j: e.transpose(tps[:, j, 0:65], osb[0:65, j * 128:(j + 1) * 128], ident[0:65, 0:65]), reads=[b_osb, b_c], writes=[b_tps], inc=(j == nj - 1))
            if sink_ap is not None:
                P.op("dve", lambda e: e.tensor_tensor(out=rd[:, 0:nj, :], in0=tps[:, 0:nj, 64:65], in1=sink_ap, op=ALU.add), reads=[b_tps, b_c], writes=[b_rd])
                P.op("dve", lambda e: e.reciprocal(out=rd[:, 0:nj, :], in_=rd[:, 0:nj, :]), reads=[b_rd], writes=[b_rd])
            else:
                P.op("dve", lambda e: e.reciprocal(out=rd[:, 0:nj, :], in_=tps[:, 0:nj, 64:65]), reads=[b_tps], writes=[b_rd])
            t = cnt["ot"] % 2; cnt["ot"] += 1
            P.op("dve", lambda e, t=t: e.tensor_tensor(out=otb[t][:, 0:nj, :], in0=tps[:, 0:nj, 0:64], in1=rd[:, 0:nj, :].to_broadcast([128, nj, 64]), op=ALU.mult), reads=[b_tps, b_rd], writes=[b_ot[t]])
            out_cb(otb[t], b_ot[t])

        STAGE = int(os.environ.get("STAGE", "9"))
        for br in "dw":
            dst_buf[0] = b_QT
            for h in range(2):
                prep(qT[br][h][:, 0:NL], QT[:, h, 0:NL], NL, 0, gq if br == "d" else None, True)
                prep(qT[br][h][:, NL:NT], QT[:, h, NL:NT], NC, 0, gq if br == "d" else None, False)
            dst_buf[0] = b_KT
            prep(kT[br][:, 0:NL], KT[:, 0:NL], NL, 0, gk if br == "d" else None, True)
            prep(kT[br][:, NL:NT], KT[:, NL:NT], NC, 0, gk if br == "d" else None, False)
            P.dma("sp", vst[:], vv[br].rearrange("(t p) d -> p t d", p=128), writes=[b_vst])
            P.op("pool", lambda e: e.memset(VA[:, :, 64:65], 1.0), writes=[b_VA])
            P.op("pool", lambda e: e.tensor_copy(out=VA[:, :, 0:64], in_=vst[:]), reads=[b_vst], writes=[b_VA])
            yb = y[br]
            ctx_tiles = [(KT[:, NL + c * 128: NL + (c + 1) * 128], VA[:, nlt + c, :], None) for c in range(nct)]
            if STAGE < 2:
                P.dma("sp", yb[0:64, 0:64], stg[0][:, 0:64], reads=[b_stg[0]], writes=[b_y])
                continue
            if br == "d" and STAGE != 3:
                all_tiles = [(KT[:, c * 128:(c + 1) * 128], VA[:, c, :], None) for c in range(nkt)]
                for h in range(2):
                    for q0 in range(0, NL, 512):
                        def cb(ot, bo, h=h, q0=q0):
                            P.dma("sp", yb[q0:q0 + 512, h * 64:(h + 1) * 64].rearrange("(j p) d -> p j d", p=128), ot[:, 0:4, :], reads=[bo], writes=[b_y])
                        attn_group(QT[:, h, q0:q0 + 512], 512, all_tiles, None, cb)
            elif br == "w" and STAGE >= 3:
                for n in range(nlt):
                    tiles = []
                    if n > 0:
                        tiles.append((KT[:, (n - 1) * 128:n * 128], VA[:, n - 1, :], mpn[:, 0, :]))
                    tiles.append((KT[:, n * 128:(n + 1) * 128], VA[:, n, :], None))
                    if n < nlt - 1:
                        tiles.append((KT[:, (n + 1) * 128:(n + 2) * 128], VA[:, n + 1, :], mpn[:, 1, :]))
                    tiles += ctx_tiles
                    for h in range(2):
                        def cb(ot, bo, n=n, h=h):
                            P.dma("sp", yb[n * 128:(n + 1) * 128, h * 64:(h + 1) * 64], ot[:, 0, :], reads=[bo], writes=[b_y])
                        attn_group(QT[:, h, n * 128:(n + 1) * 128], 128, tiles, sk[:, h:h + 1].unsqueeze(1), cb)
            for h in range(2 if STAGE >= 4 else 0):
                def cb(ot, bo, h=h):
                    P.dma("sp", yb[NL:NT, h * 64:(h + 1) * 64].rearrange("(j p) d -> p j d", p=128), ot[:, 0:nct, :], reads=[bo], writes=[b_y])
                snk = sk[:, h:h + 1].unsqueeze(1).to_broadcast([128, nct, 1]) if br == "w" else None
                attn_group(QT[:, h, NL:NT], NC, ctx_tiles, snk, cb)
        P.finish("sp", [b_y])
        P.emit()
    return nc


def rope_tables_np(NL, GW=64):
    t = np.arange(NL)
    row = (t // GW).astype(np.float32); col = (t % GW).astype(np.float32)
    inv = (10000.0 ** (-np.arange(16, dtype=np.float32) / 16)).astype(np.float32)
    ang = np.concatenate([row[:, None] * inv, col[:, None] * inv], -1)
    return np.cos(ang).astype(np.float32), np.sin(ang).astype(np.float32)


def kc_consts(NL):
    cos, sin = rope_tables_np(NL)
    cos2 = np.ascontiguousarray(np.concatenate([cos, cos], -1).T)
    sin2 = np.ascontiguousarray(np.concatenate([sin, sin], -1).T)
    Rm = np.zeros((64, 64), np.float32)
    for m in range(32):
        Rm[m + 32, m] = -1.0
        Rm[m, m + 32] = 1.0
    kl = np.arange(128)[:, None]; ql = np.arange(128)[None, :]
    return {"cos2": cos2, "sin2": sin2, "Rm": Rm, "ones64": np.ones((64, 64), np.float32), "ident": np.eye(128, dtype=np.float32),
            "mprev": (kl >= ql).astype(np.float32), "mnext": (kl <= ql).astype(np.float32)}


NEG = -30000.0


def build_ka(NC_, NL):
    nc = bass.Bass("TRN2", target_bir_lowering=False)
    NT = NC_ + NL
    nt = NT // 128
    NP = NT + 8
    din = lambda n, s: nc.dram_tensor(n, s, F32, kind="ExternalInput").ap()
    xr = din("xr", [2, 128, NP]); br = din("br", [2, 64, NP]); cr = din("cr", [2, 64, NP])
    cwx = din("cwx", [2, 128, 6]); cwb = din("cwb", [2, 64, 6]); cwc = din("cwc", [2, 64, 6])
    dtr = din("dtr", [2, 2, NT])
    alog = din("alog", [2, 2]); dtb = din("dtb", [2, 2])
    U_d = din("U", [128, 128]); ones_d = din("ones", [128, 128]); SL_d = din("SL", [128, 128]); ident_d = din("ident", [128, 128])
    mask_d = din("masks", [4, 128, 512])
    yo = nc.dram_tensor("y", [2, NT, 128], F32, kind="ExternalOutput").ap()
    xso = nc.dram_tensor("xs", [NT, 128], F32, kind="ExternalOutput").ap()
    scr = nc.dram_tensor("scr", [4, NT], F32, kind="ExternalOutput").ap()
    segs = [(0, NC_), (NC_, NL)]
    with ExitStack() as st:
        P = Prog(nc, st)
        sb = lambda name, shape, dt=F32: st.enter_context(nc.sbuf_tensor("s_" + name, shape, dt))
        ps = lambda name, shape, dt=F32: st.enter_context(nc.psum_tensor("p_" + name, shape, dt))
        b_c = P.buf("c")
        U = sb("U", [128, 128]); ones = sb("ones", [128, 128]); SL = sb("SL", [128, 128]); ident = sb("ident", [128, 128])
        masks = sb("masks", [128, 4, 512])
        for t_, d_ in ((U, U_d), (ones, ones_d), (SL, SL_d), (ident, ident_d)):
            P.dma("sp", t_[:], d_, writes=[b_c])
        P.dma("sp", masks[:], mask_d.rearrange("r p n -> p r n"), writes=[b_c])
        bigA = sb("bigA", [128, NP]); b_bigA = P.buf("bigA")
        bigB = sb("bigB", [128, NT]); b_bigB = P.buf("bigB")
        sgm = sb("sgm", [128, 2048]); b_sgm = P.buf("sgm")
        cw = sb("cw", [128, 6]); b_cw = P.buf("cw")
        X = sb("X", [128, nt, 128]); b_X = P.buf("X")
        Xdt = sb("Xdt", [128, nt, 64], BF16); b_Xdt = P.buf("Xdt")
        BT = sb("BT", [64, NT], BF16); b_BT = P.buf("BT")
        CT = sb("CT", [64, NT], BF16); b_CT = P.buf("CT")
        dt_ = sb("dt", [128, nt]); dta = sb("dta", [128, nt]); acum = sb("acum", [128, nt]); nacum = sb("nacum", [128, nt]); dsl = sb("dsl", [128, nt])
        dtaT = sb("dtaT", [128, 128]); acT = sb("acT", [128, 128])
        b_dt = P.buf("dt")
        sc2 = sb("sc2", [128, 2]); b_sc = P.buf("sc2")
        tps = ps("tps", [128, 512]); b_tps = P.buf("tps")
        aps = ps("aps", [128, 128]); b_aps = P.buf("aps")
        sps = [ps("sps%d" % i, [128, 512]) for i in range(4)]; b_sps = P.bufs(4, "sps")
        ops_ = ps("ops", [64, 512]); b_ops = P.buf("ops")
        Lb = [sb("Lb%d" % i, [128, 512]) for i in range(4)]; b_L = P.bufs(4, "L")
        Mb = [sb("Mb%d" % i, [128, 512], BF16) for i in range(4)]; b_M = P.bufs(4, "M")
        osb = sb("osb", [64, 512]); b_osb = P.buf("osb")
        ot = [sb("ot%d" % i, [128, 4, 64]) for i in range(2)]; b_ot = P.bufs(2, "ot")
        b_y = P.buf("y"); b_scr = P.buf("scr")
        cnt = {"s": 0, "l": 0, "ot": 0}

        def conv_silu(raw_ap, cw_ap, npart, out_fn):
            P.dma("sp", bigA[0:npart, :], raw_ap, writes=[b_bigA])
            P.dma("sp", cw[0:npart, :], cw_ap, writes=[b_cw])
            for si, (t0, ln) in enumerate(segs):
                p0 = t0 + 2 + 4 * si
                for c0 in range(0, ln, 2048):
                    w = min(2048, ln - c0)
                    o = bigB[0:npart, t0 + c0:t0 + c0 + w]
                    P.op("dve", lambda e, o=o, p0=p0, c0=c0, w=w: e.tensor_scalar(out=o, in0=bigA[0:npart, p0 + c0 - 2:p0 + c0 - 2 + w], scalar1=cw[0:npart, 0:1], scalar2=cw[0:npart, 5:6], op0=ALU.mult, op1=ALU.add),
                         reads=[b_bigA, b_cw], writes=[b_bigB])
                    for k in range(1, 5):
                        P.op("dve", lambda e, o=o, p0=p0, c0=c0, w=w, k=k: e.scalar_tensor_tensor(out=o, in0=bigA[0:npart, p0 + c0 - 2 + k:p0 + c0 - 2 + k + w], scalar=cw[0:npart, k:k + 1], in1=o, op0=ALU.mult, op1=ALU.add),
                             reads=[b_bigA, b_cw, b_bigB], writes=[b_bigB])
                    P.op("act", lambda e, o=o, w=w: e.activation(out=sgm[0:npart, 0:w], in_=o, func=AF.Sigmoid), reads=[b_bigB], writes=[b_sgm])
                    P.op("dve", lambda e, o=o, w=w: e.tensor_tensor(out=o, in0=o, in1=sgm[0:npart, 0:w], op=ALU.mult), reads=[b_bigB, b_sgm], writes=[b_bigB])
            out_fn()

        for d in range(2):
            conv_silu(br[d], cwb[d], 64, lambda: P.op("pool", lambda e: e.tensor_copy(out=BT[:], in_=bigB[0:64, :]), reads=[b_bigB], writes=[b_BT]))
            conv_silu(cr[d], cwc[d], 64, lambda: P.op("pool", lambda e: e.tensor_copy(out=CT[:], in_=bigB[0:64, :]), reads=[b_bigB], writes=[b_CT]))
            def xout():
                for c in range(nt):
                    g = c % 4
                    P.op("pe", lambda e, c=c, g=g: e.transpose(tps[:, g * 128:(g + 1) * 128], bigB[:, c * 128:(c + 1) * 128], ident[:]), reads=[b_bigB, b_c], writes=[b_tps], inc=(g == 3 or c == nt - 1))
                    if g == 3 or c == nt - 1:
                        c0 = c - g
                        P.op("act", lambda e, c0=c0, g=g: e.copy(out=X[:, c0:c0 + g + 1, :], in_=tps[:, 0:(g + 1) * 128].rearrange("p (a n) -> p a n", n=128)), reads=[b_tps], writes=[b_X])
                if d == 0:
                    P.dma("sp", xso.rearrange("(c p) n -> p c n", p=128), X[:], reads=[b_X], writes=[b_y])
            conv_silu(xr[d], cwx[d], 128, xout)
            for h in range(2):
                P.dma("sp", dt_[:], dtr[d, h].rearrange("(c p) -> p c", p=128), writes=[b_dt], allow_slow_non_contiguous=True)
                P.dma("sp", sc2[:, 0:1], dtb[d, h:h + 1].partition_broadcast(128), writes=[b_sc])
                P.dma("sp", sc2[:, 1:2], alog[d, h:h + 1].partition_broadcast(128), writes=[b_sc])
                P.op("act", lambda e: e.activation(out=sc2[:, 1:2], in_=sc2[:, 1:2], func=AF.Exp), reads=[b_sc], writes=[b_sc])
                P.op("act", lambda e: e.activation(out=dt_[:], in_=dt_[:], func=AF.Exp, bias=sc2[:, 0:1]), reads=[b_dt, b_sc], writes=[b_dt])
                P.op("act", lambda e: e.activation(out=dt_[:], in_=dt_[:], func=AF.Ln, bias=ones[:, 0:1]), reads=[b_dt, b_c], writes=[b_dt])
                P.op("dve", lambda e: e.tensor_scalar(out=dta[:], in0=dt_[:], scalar1=sc2[:, 1:2], scalar2=-1.0, op0=ALU.mult, op1=ALU.mult), reads=[b_dt, b_sc], writes=[b_dt])
                P.op("pe", lambda e: e.transpose(aps[0:nt, :], dta[:], ident[:]), reads=[b_dt, b_c], writes=[b_aps])
                P.op("act", lambda e: e.copy(out=dtaT[0:nt, :], in_=aps[0:nt, :]), reads=[b_aps], writes=[b_dt])
                P.op("pe", lambda e: e.matmul(aps[:, 0:nt], lhsT=dtaT[0:nt, :], rhs=SL[0:nt, 0:nt], start=True, stop=True), reads=[b_dt, b_c], writes=[b_aps])
                P.op("act", lambda e: e.copy(out=dsl[:], in_=aps[:, 0:nt]), reads=[b_aps], writes=[b_dt])
                P.op("pe", lambda e: e.matmul(aps[:, 0:nt], lhsT=U[:], rhs=dta[:], start=True, stop=False), reads=[b_dt, b_c], writes=[b_aps])
                P.op("pe", lambda e: e.matmul(aps[:, 0:nt], lhsT=ones[:], rhs=dsl[:], start=False, stop=True), reads=[b_dt, b_c], writes=[b_aps])
                P.op("act", lambda e: e.copy(out=acum[:], in_=aps[:, 0:nt]), reads=[b_aps], writes=[b_dt])
                P.op("dve", lambda e: e.tensor_scalar(out=nacum[:], in0=acum[:], scalar1=-1.0, scalar2=0.0, op0=ALU.mult, op1=ALU.add), reads=[b_dt], writes=[b_dt])
                P.op("pe", lambda e: e.transpose(aps[0:nt, :], acum[:], ident[:]), reads=[b_dt, b_c], writes=[b_aps])
                P.op("act", lambda e: e.copy(out=acT[0:nt, :], in_=aps[0:nt, :]), reads=[b_aps], writes=[b_dt])
                si = d * 2 + h
                P.dma("sp", scr[si].rearrange("(c p) -> c p", p=128), acT[0:nt, :], reads=[b_dt], writes=[b_scr])
                P.dma("sp", bigB[:], scr[si].partition_broadcast(128), reads=[b_scr], writes=[b_bigB])
                P.op("dve", lambda e, h=h: e.tensor_tensor(out=Xdt[:], in0=X[:, :, h * 64:(h + 1) * 64], in1=dt_[:].unsqueeze(2).to_broadcast([128, nt, 64]), op=ALU.mult), reads=[b_X, b_dt], writes=[b_Xdt])
                def chunk(d, h, q0):
                    W = min(512, NT - q0)
                    cmax = (q0 + W) // 128 - 1
                    base = cnt["s"]; cnt["s"] += cmax + 1; cnt["l"] += cmax + 1

                    def s_mm(c):
                        s = (base + c) % 4
                        P.op("pe", lambda e, s=s, c=c: e.matmul(sps[s][:, 0:W], lhsT=BT[:, c * 128:(c + 1) * 128], rhs=CT[:, q0:q0 + W], start=True, stop=True), reads=[b_BT, b_CT], writes=[b_sps[s]])
                    SK = 3
                    for c in range(min(SK, cmax + 1)):
                        s_mm(c)
                    for c in range(cmax + 1):
                        s = (base + c) % 4
                        l = s
                        r = c - q0 // 128
                        if r >= 0:
                            P.op("pool", lambda e, l=l, r=r: e.tensor_tensor(out=Lb[l][:, 0:W], in0=bigB[:, q0:q0 + W], in1=masks[:, r, 0:W], op=ALU.add), reads=[b_bigB, b_c], writes=[b_L[l]])
                            P.op("act", lambda e, l=l, c=c: e.activation(out=Lb[l][:, 0:W], in_=Lb[l][:, 0:W], func=AF.Exp, bias=nacum[:, c:c + 1]), reads=[b_L[l], b_dt], writes=[b_L[l]])
                        else:
                            P.op("act", lambda e, l=l, c=c: e.activation(out=Lb[l][:, 0:W], in_=bigB[:, q0:q0 + W], func=AF.Exp, bias=nacum[:, c:c + 1]), reads=[b_bigB, b_dt], writes=[b_L[l]])
                        P.op("dve", lambda e, l=l, s=s: e.tensor_tensor(out=Mb[l][:, 0:W], in0=sps[s][:, 0:W], in1=Lb[l][:, 0:W], op=ALU.mult), reads=[b_sps[s], b_L[l]], writes=[b_M[l]])
                        if c + SK <= cmax:
                            s_mm(c + SK)
                        P.op("pe", lambda e, l=l, c=c: e.matmul(ops_[:, 0:W], lhsT=Xdt[:, c, :], rhs=Mb[l][:, 0:W], start=(c == 0), stop=(c == cmax)), reads=[b_M[l], b_Xdt], writes=[b_ops], inc=(c == cmax))
                    nj = W // 128
                    P.op("act", lambda e: e.copy(out=osb[:, 0:W], in_=ops_[:, 0:W]), reads=[b_ops], writes=[b_osb])
                    for j in range(nj):
                        P.op("pe", lambda e, j=j: e.transpose(tps[:, j * 64:(j + 1) * 64], osb[:, j * 128:(j + 1) * 128], ident[0:64, 0:64]), reads=[b_osb, b_c], writes=[b_tps], inc=(j == nj - 1))
                    t = cnt["ot"] % 2; cnt["ot"] += 1
                    P.op("dve", lambda e, t=t: e.tensor_copy(out=ot[t][:, 0:nj, :], in_=tps[:, 0:nj * 64].rearrange("p (a n) -> p a n", n=64)), reads=[b_tps], writes=[b_ot[t]])
                    P.dma("sp", yo[d, q0:q0 + W, h * 64:(h + 1) * 64].rearrange("(j p) n -> p j n", p=128), ot[t][:, 0:nj, :], reads=[b_ot[t]], writes=[b_y])
                for q0 in range(0, NT, 512):
                    chunk(d, h, q0)
        P.finish("sp", [b_y])
        P.emit()
    return nc


def ka_consts():
    k = np.arange(128)
    U = (k[:, None] <= k[None, :]).astype(np.float32)
    SL = (k[:, None] < k[None, :]).astype(np.float32)
    masks = np.zeros((4, 128, 512), np.float32)
    i = np.arange(512)
    for r in range(4):
        masks[r] = np.where(i[None, :] >= r * 128 + k[:, None], 0.0, NEG)
    return {"U": U, "SL": SL, "ones": np.ones((128, 128), np.float32), "ident": np.eye(128, dtype=np.float32), "masks": masks}


def pad_seq(a, NC_, NL):
    z = np.zeros(a.shape[:-1] + (2,), a.dtype)
    return np.concatenate([z, a[..., :NC_], z, z, a[..., NC_:], z], -1)


def build_kb(n, NCH=128, GC=16):
    nc = bass.Bass("TRN2", target_bir_lowering=False)
    NA = 2 * n // 128
    NAD = n // 128
    N = 2 * n
    QC = 8 if NA <= 16 else 4
    din = lambda nm, s: nc.dram_tensor(nm, s, F32, kind="ExternalInput").ap()
    raw_d = din("raw", [3, NAD, NCH, 130])
    cw_d = din("cw", [3 * NCH * 4]); hb_d = din("hb", [2 * NCH])
    full_d = din("full", [2, NA, NCH, 128])
    F1_d = din("F1", [NA, 2 * NA]); TW_d = din("TW", [128, 2, NA]); TWI_d = din("TWI", [NA, 2, 128])
    CS_d = din("CS", [128, 256]); nSC_d = din("nSC", [128, 256]); C_d = din("C128", [128, 128]); S_d = din("S128", [128, 128]); nS_d = din("nS128", [128, 128])
    CI_d = din("CI", [NA, NAD]); nSI_d = din("nSI", [NA, NAD])
    yo = nc.dram_tensor("y", [NAD, NCH, 128], F32, kind="ExternalOutput").ap()
    with ExitStack() as st:
        P = Prog(nc, st)
        sb = lambda name, shape, dt=F32: st.enter_context(nc.sbuf_tensor("s_" + name, shape, dt))
        ps = lambda name, shape, dt=F32: st.enter_context(nc.psum_tensor("p_" + name, shape, dt))
        b_c = P.buf("c")
        F1 = sb("F1", [NA, 2 * NA]); TW = sb("TW", [128, 2, NA]); TWI = sb("TWI", [NA, 2, 128])
        CS = sb("CS", [128, 256]); nSC = sb("nSC", [128, 256]); C128 = sb("C128", [128, 128]); S128 = sb("S128", [128, 128]); nS128 = sb("nS128", [128, 128])
        CI = sb("CI", [NA, NAD]); nSI = sb("nSI", [NA, NAD])
        cw = sb("cw", [NAD, 3, NCH, 4]); hb = sb("hb", [NAD, 2, NCH])
        for t_, d_ in ((F1, F1_d), (TW, TW_d), (TWI, TWI_d), (CS, CS_d), (nSC, nSC_d), (C128, C_d), (S128, S_d), (nS128, nS_d), (CI, CI_d), (nSI, nSI_d)):
            P.dma("sp", t_[:], d_, writes=[b_c])
        P.dma("sp", cw[:].rearrange("p a b c -> p (a b c)"), cw_d.partition_broadcast(NAD), writes=[b_c])
        P.dma("sp", hb[:].rearrange("p a b -> p (a b)"), hb_d.partition_broadcast(NAD), writes=[b_c])
        raw = sb("raw", [NAD, 3, GC, 130]); b_raw = P.buf("raw")
        u = sb("u", [NAD, 3, GC, 128]); b_u = P.buf("u")
        tmpc = sb("tmpc", [NAD, GC, 128]); b_tmpc = P.buf("tmpc")
        fl = sb("fl", [NA, 2, GC, 128]); b_fl = P.buf("fl")
        H = sb("H", [128, 2, QC, NA]); b_H = P.buf("H")
        Yp = sb("Yp", [128, 2, QC, NA]); b_Yp = P.buf("Yp")
        Zs = sb("Zs", [128, 2, QC, NA]); b_Zs = P.buf("Zs")
        Vp = sb("Vp", [NA, 2, QC, 128]); b_Vp = P.buf("Vp")
        t1 = sb("t1", [128, QC, max(NA, 128)]); t2 = sb("t2", [128, QC, max(NA, 128)]); b_t1 = P.buf("t1"); b_t2 = P.buf("t2")
        z1 = sb("z1", [NAD, GC, 128]); b_z1 = P.buf("z1")
        og = sb("og", [NAD, GC, 128]); b_og = P.buf("og")
        yps = ps("yps", [128, QC, 2, NA]); b_yps = P.buf("yps")
        xps = ps("xps", [128, 2, QC, NA]); b_xps = P.buf("xps")
        vps = ps("vps", [NA, QC, 2, 128]); b_vps = P.buf("vps")
        ops_ = ps("ops", [NAD, QC, 128]); b_ops = P.buf("ops")
        b_y = P.buf("y")

        def cmul(eng_out_re, eng_out_im, are, aim, bre, bim, conj_b, pn, w, rbufs, obuf):
            sgn_im = -1.0 if conj_b else 1.0
            P.op("dve", lambda e: e.tensor_tensor(out=t1[0:pn, :, 0:w], in0=are, in1=bre, op=ALU.mult), reads=rbufs, writes=[b_t1])
            P.op("dve", lambda e: e.tensor_tensor(out=t2[0:pn, :, 0:w], in0=aim, in1=bim, op=ALU.mult), reads=rbufs, writes=[b_t2])
            P.op("dve", lambda e: e.tensor_tensor(out=eng_out_re, in0=t1[0:pn, :, 0:w], in1=t2[0:pn, :, 0:w], op=(ALU.add if conj_b else ALU.subtract)), reads=[b_t1, b_t2], writes=[obuf])
            P.op("dve", lambda e: e.tensor_tensor(out=t1[0:pn, :, 0:w], in0=aim, in1=bre, op=ALU.mult), reads=rbufs, writes=[b_t1])
            P.op("dve", lambda e: e.tensor_tensor(out=t2[0:pn, :, 0:w], in0=are, in1=bim, op=ALU.mult), reads=rbufs, writes=[b_t2])
            P.op("dve", lambda e: e.tensor_tensor(out=eng_out_im, in0=t1[0:pn, :, 0:w], in1=t2[0:pn, :, 0:w], op=(ALU.subtract if conj_b else ALU.add)), reads=[b_t1, b_t2], writes=[obuf])

        def fwd(src_fn, K, src_bufs):
            for c in range(QC):
                P.op("pe", lambda e, c=c: e.matmul(yps[:, c, :, :].rearrange("p a b -> p (a b)"), lhsT=src_fn(c), rhs=F1[0:K, :], start=True, stop=True), reads=src_bufs + [b_c], writes=[b_yps], inc=(c == QC - 1))
            twc = TW[:, 0, :].unsqueeze(1).to_broadcast([128, QC, NA]); tws = TW[:, 1, :].unsqueeze(1).to_broadcast([128, QC, NA])
            cmul(Yp[:, 0, :, :], Yp[:, 1, :, :], yps[:, :, 0, :], yps[:, :, 1, :], twc, tws, True, 128, NA, [b_yps, b_c], b_Yp)
            yre = Yp[:, 0, :, :].rearrange("p a b -> p (a b)"); yim = Yp[:, 1, :, :].rearrange("p a b -> p (a b)")
            xre = xps[:, 0, :, :].rearrange("p a b -> p (a b)"); xim = xps[:, 1, :, :].rearrange("p a b -> p (a b)")
            P.op("pe", lambda e: e.matmul(xre, lhsT=C128[:], rhs=yre, start=True, stop=False), reads=[b_Yp, b_c], writes=[b_xps], inc=False)
            P.op("pe", lambda e: e.matmul(xre, lhsT=S128[:], rhs=yim, start=False, stop=True), reads=[b_Yp, b_c], writes=[b_xps], inc=False)
            P.op("pe", lambda e: e.matmul(xim, lhsT=C128[:], rhs=yim, start=True, stop=False), reads=[b_Yp, b_c], writes=[b_xps], inc=False)
            P.op("pe", lambda e: e.matmul(xim, lhsT=nS128[:], rhs=yre, start=False, stop=True), reads=[b_Yp, b_c], writes=[b_xps])

        def long_conv(zsrc_fn, zbufs, o, q0, gate_fn):
            fwd(lambda c: fl[:, o, q0 + c, :], NA, [b_fl])
            P.op("act", lambda e: e.copy(out=H[:], in_=xps[:]), reads=[b_xps], writes=[b_H])
            fwd(zsrc_fn, NAD, zbufs)
            cmul(Zs[:, 0, :, :], Zs[:, 1, :, :], xps[:, 0, :, :], xps[:, 1, :, :], H[:, 0, :, :], H[:, 1, :, :], False, 128, NA, [b_xps, b_H], b_Zs)
            for c in range(QC):
                vv = vps[:, c, :, :].rearrange("p a b -> p (a b)")
                P.op("pe", lambda e, c=c, vv=vv: e.matmul(vv, lhsT=Zs[:, 0, c, :], rhs=CS[:], start=True, stop=False), reads=[b_Zs, b_c], writes=[b_vps], inc=False)
                P.op("pe", lambda e, c=c, vv=vv: e.matmul(vv, lhsT=Zs[:, 1, c, :], rhs=nSC[:], start=False, stop=True), reads=[b_Zs, b_c], writes=[b_vps], inc=(c == QC - 1))
            twc = TWI[:, 0, :].unsqueeze(1).to_broadcast([NA, QC, 128]); tws = TWI[:, 1, :].unsqueeze(1).to_broadcast([NA, QC, 128])
            cmul(Vp[:, 0, :, :], Vp[:, 1, :, :], vps[:, :, 0, :], vps[:, :, 1, :], twc, tws, False, NA, 128, [b_vps, b_c], b_Vp)
            oo = ops_[:].rearrange("p a b -> p (a b)")
            vre = Vp[:, 0, :, :].rearrange("p a b -> p (a b)"); vim = Vp[:, 1, :, :].rearrange("p a b -> p (a b)")
            nh = (QC * 128) // 512
            for hh in range(nh):
                cs = slice(hh * 512, (hh + 1) * 512)
                P.op("pe", lambda e, cs=cs: e.matmul(oo[:, cs], lhsT=CI[:], rhs=vre[:, cs], start=True, stop=False), reads=[b_Vp, b_c], writes=[b_ops], inc=False)
                P.op("pe", lambda e, cs=cs: e.matmul(oo[:, cs], lhsT=nSI[:], rhs=vim[:, cs], start=False, stop=True), reads=[b_Vp, b_c], writes=[b_ops], inc=(hh == nh - 1))
            gate_fn()

        def group(g0):
            for s in range(3):
                P.dma("sp", raw[:, s, :, :], raw_d[s, :, g0:g0 + GC, :], writes=[b_raw])
            P.dma("sp", fl[:], full_d[:, :, g0:g0 + GC, :].rearrange("o a c r -> a o c r"), writes=[b_fl])
            for s in range(3):
                wk = lambda k, s=s: cw[:, s, g0:g0 + GC, k:k + 1].to_broadcast([NAD, GC, 128])
                us = u[:, s, :, :]
                P.op("dve", lambda e, s=s, us=us, wk=wk: e.tensor_tensor(out=us, in0=raw[:, s, :, 0:128], in1=wk(0), op=ALU.mult), reads=[b_raw, b_c], writes=[b_u])
                for k in (1, 2):
                    P.op("pool", lambda e, s=s, k=k, wk=wk: e.tensor_tensor(out=tmpc[:], in0=raw[:, s, :, k:k + 128], in1=wk(k), op=ALU.mult), reads=[b_raw, b_c], writes=[b_tmpc])
                    P.op("dve", lambda e, us=us: e.tensor_tensor(out=us, in0=us, in1=tmpc[:], op=ALU.add), reads=[b_u, b_tmpc], writes=[b_u])
                P.op("dve", lambda e, us=us, wk=wk: e.tensor_tensor(out=us, in0=us, in1=wk(3), op=ALU.add), reads=[b_u, b_c], writes=[b_u])
            for q0 in range(0, GC, QC):
                def gate0(q0=q0):
                    bb = hb[:, 0, g0 + q0:g0 + q0 + QC].unsqueeze(2).to_broadcast([NAD, QC, 128])
                    zq = z1[:, q0:q0 + QC, :]
                    P.op("dve", lambda e: e.tensor_tensor(out=zq, in0=u[:, 0, q0:q0 + QC, :], in1=bb, op=ALU.mult), reads=[b_u, b_c], writes=[b_z1])
                    P.op("dve", lambda e: e.scalar_tensor_tensor(out=zq, in0=ops_[:], scalar=1.0 / N, in1=zq, op0=ALU.mult, op1=ALU.add), reads=[b_ops, b_z1], writes=[b_z1])
                    P.op("dve", lambda e: e.tensor_tensor(out=zq, in0=zq, in1=u[:, 1, q0:q0 + QC, :], op=ALU.mult), reads=[b_u, b_z1], writes=[b_z1])
                long_conv(lambda c, q0=q0: u[:, 0, q0 + c, :], [b_u], 0, q0, gate0)

                def gate1(q0=q0):
                    bb = hb[:, 1, g0 + q0:g0 + q0 + QC].unsqueeze(2).to_broadcast([NAD, QC, 128])
                    oq = og[:, q0:q0 + QC, :]
                    P.op("dve", lambda e: e.tensor_tensor(out=oq, in0=z1[:, q0:q0 + QC, :], in1=bb, op=ALU.mult), reads=[b_z1, b_c], writes=[b_og])
                    P.op("dve", lambda e: e.scalar_tensor_tensor(out=oq, in0=ops_[:], scalar=1.0 / N, in1=oq, op0=ALU.mult, op1=ALU.add), reads=[b_ops, b_og], writes=[b_og])
                    P.op("dve", lambda e: e.tensor_tensor(out=oq, in0=oq, in1=u[:, 2, q0:q0 + QC, :], op=ALU.mult), reads=[b_u, b_og], writes=[b_og])
                long_conv(lambda c, q0=q0: z1[:, q0 + c, :], [b_z1], 1, q0, gate1)
            P.dma("sp", yo[:, g0:g0 + GC, :], og[:], reads=[b_og], writes=[b_y])

        for g0 in range(0, NCH, GC):
            group(g0)
        P.finish("sp", [b_y])
        P.emit()
    return nc


def kb_consts(n):
    NA = 2 * n // 128; NAD = n // 128; N = 2 * n
    a = np.arange(NA)[:, None]; k1 = np.arange(NA)[None, :]
    ang = 2 * np.pi * a * k1 / NA
    F1 = np.concatenate([np.cos(ang), -np.sin(ang)], 1)
    r = np.arange(128)[:, None]
    th = 2 * np.pi * r * k1 / N
    TW = np.stack([np.cos(th), np.sin(th)], 1)
    TWI = np.stack([np.cos(th).T, np.sin(th).T], 1)
    p = np.arange(128)
    a128 = 2 * np.pi * p[:, None] * p[None, :] / 128
    C = np.cos(a128); S = np.sin(a128)
    angI = 2 * np.pi * np.arange(NA)[:, None] * np.arange(NAD)[None, :] / NA
    f = lambda x: np.ascontiguousarray(x.astype(np.float32))
    return {"F1": f(F1), "TW": f(TW), "TWI": f(TWI), "CS": f(np.concatenate([C, S], 1)), "nSC": f(np.concatenate([-S, C], 1)),
            "C128": f(C), "S128": f(S), "nS128": f(-S), "CI": f(np.cos(angI)), "nSI": f(-np.sin(angI))}


def kb_pack_raw(uraw, n):
    NAD = n // 128
    p = np.pad(uraw, ((0, 0), (0, 0), (1, 1)))
    idx = (np.arange(NAD)[:, None] * 128 + np.arange(130)[None, :])
    g = p[:, :, idx]
    return np.ascontiguousarray(g.transpose(0, 2, 1, 3))


def kb_pack_full(k, n):
    NA = 2 * n // 128
    kf, kb = k[:, 0], k[:, 1]
    full = np.concatenate([kf, np.zeros_like(kf[..., :1]), kb[..., :0:-1]], -1)
    return np.ascontiguousarray(full.reshape(2, -1, NA, 128).transpose(0, 2, 1, 3))


EPS = 1e-6
TWO_PI = 2.0 * math.pi


def build_kf1(n):
    nc = bass.Bass("TRN2", target_bir_lowering=False)
    din = lambda nm, s: nc.dram_tensor(nm, s, F32, kind="ExternalInput").ap()
    zT_d = din("zT", [33, n]); dec_d = din("decay", [128, n])
    w1_d = din("w1", [33, 64]); w2_d = din("w2", [64, 64]); w3_d = din("w3", [64, 128])
    fb1_d = din("fb1", [64, 2]); fb2_d = din("fb2", [64, 2]); b3_d = din("b3", [128, 1])
    pm_d = din("pm", [128, 128])
    ko = nc.dram_tensor("k", [128, n], F32, kind="ExternalOutput").ap()
    CW = min(512, n)
    nch = n // CW
    with ExitStack() as st:
        P = Prog(nc, st)
        sb = lambda name, shape, dt=F32: st.enter_context(nc.sbuf_tensor("s_" + name, shape, dt))
        ps = lambda name, shape, dt=F32: st.enter_context(nc.psum_tensor("p_" + name, shape, dt))
        b_c = P.buf("c")
        zT = sb("zT", [33, n]); dec = sb("dec", [128, n]); w1 = sb("w1", [33, 64]); w2 = sb("w2", [64, 64]); w3 = sb("w3", [64, 128])
        fb1 = sb("fb1", [64, 2]); fb2 = sb("fb2", [64, 2]); b3 = sb("b3", [128, 1]); pm = sb("pm", [128, 128]); epsb = sb("epsb", [128, 1])
        for t_, d_ in ((zT, zT_d), (dec, dec_d), (w1, w1_d), (w2, w2_d), (w3, w3_d), (fb1, fb1_d), (fb2, fb2_d), (b3, b3_d), (pm, pm_d)):
            P.dma("sp", t_[:], d_, writes=[b_c])
        for fb in (fb1, fb2):
            P.op("dve", lambda e, fb=fb: e.tensor_scalar(out=fb[:, 1:2], in0=fb[:, 1:2], scalar1=fb[:, 0:1], scalar2=16.0 * math.pi, op0=ALU.mult, op1=ALU.add), reads=[b_c], writes=[b_c])
        P.op("dve", lambda e: e.memset(epsb[:], EPS), reads=[b_c], writes=[b_c])
        kT = sb("kT", [128, n]); b_k = P.buf("kT")
        ssq = sb("ssq", [128, nch + 2]); b_ss = P.buf("ss")
        sq = sb("sq", [128, CW]); b_sq = P.buf("sq")
        h1 = [sb("h1_%d" % i, [64, CW]) for i in range(2)]; b_h1 = P.bufs(2, "h1")
        h2 = [sb("h2_%d" % i, [64, CW]) for i in range(2)]; b_h2 = P.bufs(2, "h2")
        pp = [ps("pp%d" % i, [128, CW]) for i in range(4)]; b_pp = P.bufs(4, "pp")
        pi = [0]

        I32 = mybir.dt.int32
        ki = sb("ki", [64, CW], I32); kf = sb("kf", [64, CW]); b_ki = P.buf("ki")

        def sin_layer(src_ps, fb, dst, b_src, b_dst, w):
            P.op("dve", lambda e: e.tensor_scalar(out=dst[:, 0:w], in0=src_ps[0:64, 0:w], scalar1=fb[:, 0:1], scalar2=fb[:, 1:2], op0=ALU.mult, op1=ALU.add), reads=[b_src, b_c], writes=[b_dst])
            P.op("dve", lambda e: e.tensor_scalar(out=ki[:, 0:w], in0=dst[:, 0:w], scalar1=1.0 / TWO_PI, scalar2=0.0, op0=ALU.mult, op1=ALU.add), reads=[b_dst], writes=[b_ki])
            P.op("dve", lambda e: e.tensor_copy(out=kf[:, 0:w], in_=ki[:, 0:w]), reads=[b_ki], writes=[b_ki])
            P.op("dve", lambda e: e.scalar_tensor_tensor(out=dst[:, 0:w], in0=kf[:, 0:w], scalar=-TWO_PI, in1=dst[:, 0:w], op0=ALU.mult, op1=ALU.add), reads=[b_ki, b_dst], writes=[b_dst])
            P.op("dve", lambda e: e.tensor_scalar(out=kf[:, 0:w], in0=dst[:, 0:w], scalar1=math.pi, scalar2=-TWO_PI, op0=ALU.is_gt, op1=ALU.mult), reads=[b_dst, b_ki], writes=[b_ki])
            P.op("dve", lambda e: e.tensor_tensor(out=dst[:, 0:w], in0=dst[:, 0:w], in1=kf[:, 0:w], op=ALU.add), reads=[b_ki, b_dst], writes=[b_dst])
            P.op("act", lambda e: e.activation(out=dst[:, 0:w], in_=dst[:, 0:w], func=AF.Sin), reads=[b_dst], writes=[b_dst])

        def chunk(j):
            c0 = j * CW
            s = j % 2
            a = pi[0] % 4; pi[0] += 1
            P.op("pe", lambda e: e.matmul(pp[a][0:64, :], lhsT=w1[:], rhs=zT[:, c0:c0 + CW], start=True, stop=True), reads=[b_c], writes=[b_pp[a]])
            sin_layer(pp[a], fb1, h1[s], b_pp[a], b_h1[s], CW)
            a2 = pi[0] % 4; pi[0] += 1
            P.op("pe", lambda e: e.matmul(pp[a2][0:64, :], lhsT=w2[:], rhs=h1[s][:], start=True, stop=True), reads=[b_c, b_h1[s]], writes=[b_pp[a2]])
            sin_layer(pp[a2], fb2, h2[s], b_pp[a2], b_h2[s], CW)
            a3 = pi[0] % 4; pi[0] += 1
            P.op("pe", lambda e: e.matmul(pp[a3][:, :], lhsT=w3[:], rhs=h2[s][:], start=True, stop=True), reads=[b_c, b_h2[s]], writes=[b_pp[a3]])
            P.op("dve", lambda e: e.scalar_tensor_tensor(out=kT[:, c0:c0 + CW], in0=pp[a3][:, :], scalar=b3[:, 0:1], in1=dec[:, c0:c0 + CW], op0=ALU.add, op1=ALU.mult), reads=[b_pp[a3], b_c], writes=[b_k])
            P.op("act", lambda e: e.activation(out=sq[:], in_=kT[:, c0:c0 + CW], func=AF.Square, accum_out=ssq[:, j:j + 1]), reads=[b_k], writes=[b_sq, b_ss])

        for j in range(nch):
            chunk(j)
        P.op("dve", lambda e: e.tensor_reduce(out=ssq[:, nch:nch + 1], in_=ssq[:, 0:nch], axis=AX.X, op=ALU.add), reads=[b_ss], writes=[b_ss])
        P.op("pe", lambda e: e.matmul(pp[0][:, 0:1], lhsT=pm[:], rhs=ssq[:, nch:nch + 1], start=True, stop=True), reads=[b_ss, b_c], writes=[b_pp[0]])
        P.op("act", lambda e: e.activation(out=ssq[:, nch + 1:nch + 2], in_=pp[0][:, 0:1], func=AF.Sqrt, bias=epsb[:, 0:1]), reads=[b_pp[0], b_c], writes=[b_ss])
        P.op("dve", lambda e: e.reciprocal(out=ssq[:, nch + 1:nch + 2], in_=ssq[:, nch + 1:nch + 2]), reads=[b_ss], writes=[b_ss])
        b_o = P.buf("o")
        for c0 in range(0, n, 2048):
            w = min(2048, n - c0)
            P.op("dve", lambda e, c0=c0, w=w: e.tensor_scalar(out=kT[:, c0:c0 + w], in0=kT[:, c0:c0 + w], scalar1=ssq[:, nch + 1:nch + 2], scalar2=0.0, op0=ALU.mult, op1=ALU.add), reads=[b_k, b_ss], writes=[b_k])
        P.dma("sp", ko, kT[:], reads=[b_k], writes=[b_o])
        P.finish("sp", [b_o])
        P.emit()
    return nc


def hy_tables(n):
    f32 = np.float32
    pos = np.arange(n, dtype=f32)
    t = np.linspace(0.0, 1.0, n, dtype=f32)
    f = np.linspace(1e-4, 15.0, 16, dtype=f32)
    ang = (f32(2.0 * math.pi) * pos[:, None] * f[None, :] / f32(n)).astype(f32)
    z = np.concatenate([t[:, None], np.cos(ang), -np.sin(ang)], -1).astype(f32)
    mx = math.log(1e-2) / 0.3; mn = math.log(1e-2) / 1.5
    deltas = np.abs(np.linspace(mn, mx, 256, dtype=f32))
    decay = np.exp(-t[:, None] * deltas[None, :]).astype(f32)
    return np.ascontiguousarray(z.T), np.ascontiguousarray(decay.T)


def kf1_inputs(n, hy_w1, hy_b1, hy_f1, hy_w2, hy_b2, hy_f2, hy_w3, hy_b3):
    zT, decT = hy_tables(n)
    k64 = np.arange(128)
    pm = (k64[:, None] % 64 == k64[None, :] % 64).astype(np.float32)
    maps = []
    for core in range(8):
        o, cr = core // 4, core % 4
        cols = np.concatenate([o * 512 + d * 256 + cr * 64 + np.arange(64) for d in range(2)])
        chs = np.concatenate([cr * 64 + np.arange(64)] * 2)
        maps.append({"zT": zT, "decay": np.ascontiguousarray(decT[chs]), "w1": np.ascontiguousarray(hy_w1), "w2": np.ascontiguousarray(hy_w2),
                     "w3": np.ascontiguousarray(hy_w3[:, cols]), "fb1": np.ascontiguousarray(np.stack([hy_f1, hy_b1], -1)),
                     "fb2": np.ascontiguousarray(np.stack([hy_f2, hy_b2], -1)), "b3": np.ascontiguousarray(hy_b3[cols][:, None]), "pm": pm})
    return maps


def kf1_gather(results, n):
    k = np.zeros((2, 2, 256, n), np.float32)
    for core in range(8):
        o, cr = core // 4, core % 4
        r = results[core]["k"]
        for d in range(2):
            k[o, d, cr * 64:(cr + 1) * 64] = r[d * 64:(d + 1) * 64]
    return k


D = 1024
DFF = 4096
EPS = 1e-6


class ModCalc:
    def __init__(self, P, nc, sb, ps):
        self.P, self.nc = P, nc
        self.c_sb = sb("mc_c", [128, 8]); self.c_sg = sb("mc_sg", [128, 8]); self.c_bc = sb("mc_bc", [128, 8, 128])
        self.bm = sb("mc_bm", [128, 512]); self.wst = sb("mc_w", [128, 8, 512])
        self.ps = ps("mc_ps", [128, 512])
        self.b = P.bufs(6, "mc")

    def set_c(self, cvec):
        P = self.P
        b_c, b_cs, b_bc = self.b[0:3]
        c_sb, c_sg, c_bc = self.c_sb, self.c_sg, self.c_bc
        P.dma("sp", c_sb[:], cvec.rearrange("(k p) -> p k", p=128), writes=[b_c], allow_slow_non_contiguous=True)
        P.op("act", lambda e: e.activation(out=c_sg[:], in_=c_sb[:], func=AF.Sigmoid), reads=[b_c], writes=[b_cs])
        P.op("dve", lambda e: e.tensor_tensor(out=c_sg[:], in0=c_sg[:], in1=c_sb[:], op=ALU.mult), reads=[b_c, b_cs], writes=[b_cs])
        P.op("dve", lambda e: e.tensor_copy(out=c_bc[:], in_=c_sg[:].unsqueeze(2).to_broadcast([128, 8, 128])), reads=[b_cs], writes=[b_bc])

    def calc(self, w_mod, b_mod, col0, ncols, out_ap_fn, out_buf):
        P = self.P
        b_bc, b_bm, b_w, b_ps = self.b[2:6]
        wv = w_mod.rearrange("(k p) n -> p k n", p=128)
        for j in range(ncols // 512):
            c = col0 + j * 512
            P.dma("sp", self.wst[:], wv[:, :, c:c + 512], writes=[b_w])
            P.dma("sp", self.bm[:], b_mod[c:c + 512].partition_broadcast(128), writes=[b_bm])
            for k in range(8):
                P.op("pe", lambda e, k=k: e.matmul(self.ps[:], lhsT=self.c_bc[:, k, :], rhs=self.wst[:, k, :], start=(k == 0), stop=(k == 7)),
                     reads=[b_bc, b_w], writes=[b_ps])
            P.op("dve", lambda e, j=j: e.tensor_tensor(out=out_ap_fn(j), in0=self.ps[:], in1=self.bm[:], op=ALU.add), reads=[b_ps, b_bm], writes=[out_buf])


def rstd_from_ss(P, ss_ap, b_ss, epsb, b_eps):
    P.op("act", lambda e: e.activation(out=ss_ap, in_=ss_ap, func=AF.Sqrt, bias=epsb[:, 0:1]), reads=[b_ss, b_eps], writes=[b_ss])
    P.op("dve", lambda e: e.reciprocal(out=ss_ap, in_=ss_ap), reads=[b_ss], writes=[b_ss])


def build_k3a(ntl, ntc):
    nc = bass.Bass("TRN2", target_bir_lowering=False)
    NT = ntl + ntc
    NTK = NT * 128
    din = lambda n, s: nc.dram_tensor(n, s, F32, kind="ExternalInput").ap()
    x = din("x", [NTK, D])
    yf = din("yf", [NTK, 256]); yb = din("yb", [NTK, 256]); xs = din("xs", [NTK, 256]); zz = din("z", [NTK, 256])
    mixT = din("mixT", [768, NTK])
    cv = din("cv", [D]); cctx = din("cctx", [D]); w_mod = din("w_mod", [D, 1024]); b_mod = din("b_mod", [1024])
    g_post = din("g_post", [D]); skip_d = din("skip", [256]); ssdn_d = din("ssdn", [256])
    w_out = din("w_out", [D, D]); ident_d = din("ident", [128, 128])
    xo = nc.dram_tensor("xo", [NTK, D], F32, kind="ExternalOutput").ap()
    with ExitStack() as st:
        P = Prog(nc, st)
        sb = lambda name, shape, dt=F32: st.enter_context(nc.sbuf_tensor("s_" + name, shape, dt))
        ps = lambda name, shape, dt=F32: st.enter_context(nc.psum_tensor("p_" + name, shape, dt))
        b_c = P.buf("c")
        ident_f = sb("ident_f", [128, 128]); ident = sb("identb", [128, 128], BF16)
        epsb = sb("epsb", [128, 1]); gp = sb("gp", [128, D]); skipb = sb("skipb", [128, 256]); ssdn = sb("ssdn", [128, 256])
        P.dma("sp", ident_f[:], ident_d, writes=[b_c])
        P.dma("sp", gp[:], g_post.partition_broadcast(128), writes=[b_c])
        P.dma("sp", skipb[:], skip_d.partition_broadcast(128), writes=[b_c])
        P.dma("sp", ssdn[:], ssdn_d.partition_broadcast(128), writes=[b_c])
        P.op("dve", lambda e: e.tensor_copy(out=ident[:], in_=ident_f[:]), reads=[b_c], writes=[b_c])
        P.op("dve", lambda e: e.memset(epsb[:], EPS), reads=[b_c], writes=[b_c])
        wob = sb("wob", [128, 8, D], BF16); b_wo = P.buf("wo")
        wst = [sb("wst%d" % i, [128, D]) for i in range(2)]; b_wst = P.bufs(2, "wst")
        wv = w_out.rearrange("(k p) n -> p k n", p=128)
        for k in range(8):
            s = k % 2
            P.dma("pool", wst[s][:], wv[:, k, :], writes=[b_wst[s]])
            P.op("pool", lambda e, k=k, s=s: e.tensor_copy(out=wob[:, k, :], in_=wst[s][:]), reads=[b_wst[s]], writes=[b_wo])
        MC = ModCalc(P, nc, sb, ps)
        G1 = sb("G1", [128, D]); b_G1 = P.buf("G1")
        xt = [sb("xt%d" % i, [128, D]) for i in range(2)]; b_xt = P.bufs(2, "xt")
        sq = sb("sq", [128, D]); b_sq = P.buf("sq")
        a4 = [[sb("a%d_%d" % (j, i), [128, 256]) for j in range(4)] for i in range(2)]; b_a4 = P.bufs(2, "a4")
        sg = sb("sg", [128, 256]); b_sg = P.buf("sg")
        ss = sb("ss", [128, 4]); b_ss = P.buf("ss")
        tnb = sb("tnb", [128, 256], BF16); b_tn = P.buf("tn")
        mst = [sb("mst%d" % i, [128, 6, 128]) for i in range(2)]; b_mst = P.bufs(2, "mst")
        mT = [sb("mT%d" % i, [128, 8, 128], BF16) for i in range(2)]; b_mT = P.bufs(2, "mT")
        tps = ps("tps", [128, 2, 128], BF16); b_tps = P.buf("tps")
        yps = [ps("yps%d" % i, [128, 2, 512]) for i in range(2)]; b_yps = P.bufs(2, "yps")
        tmp = sb("tmp", [128, D]); b_tmp = P.buf("tmp")
        ob = [sb("ob%d" % i, [128, D]) for i in range(2)]; b_ob = P.bufs(2, "ob")
        b_out = P.buf("out")
        for seg, (t0, t1, cvec) in enumerate(((0, ntl, cv), (ntl, NT, cctx))):
            MC.set_c(cvec)
            MC.calc(w_mod, b_mod, 0, 1024, lambda j: G1[:, j * 512:(j + 1) * 512], b_G1)
            P.op("dve", lambda e: e.tensor_tensor(out=G1[:], in0=G1[:], in1=gp[:], op=ALU.mult), reads=[b_G1, b_c], writes=[b_G1])
            for t in range(t0, t1):
                s = t % 2
                tok = slice(t * 128, (t + 1) * 128)
                P.dma("sp", xt[s][:], x[tok, :], writes=[b_xt[s]])
                for j, src in enumerate((xs, yf, yb, zz)):
                    P.dma("sp", a4[s][j][:], src[tok, :], writes=[b_a4[s]])
                P.dma("sp", mst[s][:], mixT[:, tok].rearrange("(k p) t -> p k t", p=128), writes=[b_mst[s]])
                A = a4[s]
                P.op("dve", lambda e, A=A: e.tensor_tensor(out=A[0][:], in0=A[0][:], in1=skipb[:], op=ALU.mult), reads=[b_a4[s], b_c], writes=[b_a4[s]])
                P.op("dve", lambda e, A=A: e.tensor_tensor(out=A[0][:], in0=A[0][:], in1=A[1][:], op=ALU.add), reads=[b_a4[s]], writes=[b_a4[s]])
                P.op("dve", lambda e, A=A: e.tensor_tensor(out=A[0][:], in0=A[0][:], in1=A[2][:], op=ALU.add), reads=[b_a4[s]], writes=[b_a4[s]])
                P.op("act", lambda e, A=A: e.activation(out=sg[:], in_=A[3][:], func=AF.Sigmoid), reads=[b_a4[s]], writes=[b_sg])
                P.op("dve", lambda e, A=A: e.tensor_tensor(out=sg[:], in0=sg[:], in1=A[3][:], op=ALU.mult), reads=[b_a4[s], b_sg], writes=[b_sg])
                P.op("dve", lambda e, A=A: e.tensor_tensor(out=A[0][:], in0=A[0][:], in1=sg[:], op=ALU.mult), reads=[b_a4[s], b_sg], writes=[b_a4[s]])
                P.op("act", lambda e, A=A: e.activation(out=sq[:, 0:256], in_=A[0][:], func=AF.Square, scale=1.0 / 16, accum_out=ss[:, 0:1]), reads=[b_a4[s]], writes=[b_sq, b_ss])
                rstd_from_ss(P, ss[:, 0:1], b_ss, epsb, b_c)
                P.op("dve", lambda e, A=A: e.scalar_tensor_tensor(out=tnb[:], in0=A[0][:], scalar=ss[:, 0:1], in1=ssdn[:], op0=ALU.mult, op1=ALU.mult), reads=[b_a4[s], b_ss, b_c], writes=[b_tn])
                for k in range(2):
                    P.op("pe", lambda e, k=k: e.transpose(tps[:, k, :], tnb[:, k * 128:(k + 1) * 128], ident[:]), reads=[b_tn, b_c], writes=[b_tps], inc=(k == 1))
                P.op("act", lambda e, s=s: e.copy(out=mT[s][:, 0:2, :], in_=tps[:]), reads=[b_tps], writes=[b_mT[s]])
                P.op("pool", lambda e, s=s: e.tensor_copy(out=mT[s][:, 2:8, :], in_=mst[s][:]), reads=[b_mst[s]], writes=[b_mT[s]])
                for c in range(2):
                    for k in range(8):
                        P.op("pe", lambda e, k=k, c=c, s=s: e.matmul(yps[s][:, c, :], lhsT=mT[s][:, k, :], rhs=wob[:, k, c * 512:(c + 1) * 512], start=(k == 0), stop=(k == 7)),
                             reads=[b_mT[s], b_wo], writes=[b_yps[s]], inc=(k == 7))
                for c in range(2):
                    P.op("act", lambda e, c=c, s=s: e.activation(out=sq[:, c * 512:(c + 1) * 512], in_=yps[s][:, c, :], func=AF.Square, scale=1.0 / 32, accum_out=ss[:, 1 + c:2 + c]), reads=[b_yps[s]], writes=[b_sq, b_ss])
                P.op("dve", lambda e: e.tensor_tensor(out=ss[:, 3:4], in0=ss[:, 1:2], in1=ss[:, 2:3], op=ALU.add), reads=[b_ss], writes=[b_ss])
                rstd_from_ss(P, ss[:, 3:4], b_ss, epsb, b_c)
                for c in range(2):
                    P.op("dve", lambda e, c=c, s=s: e.scalar_tensor_tensor(out=tmp[:, c * 512:(c + 1) * 512], in0=yps[s][:, c, :], scalar=ss[:, 3:4], in1=G1[:, c * 512:(c + 1) * 512], op0=ALU.mult, op1=ALU.mult),
                         reads=[b_yps[s], b_ss, b_G1], writes=[b_tmp])
                P.op("pool", lambda e, s=s: e.tensor_tensor(out=ob[s][:], in0=tmp[:], in1=xt[s][:], op=ALU.add), reads=[b_tmp, b_xt[s]], writes=[b_ob[s]])
                P.dma("sp", xo[tok, :], ob[s][:], reads=[b_ob[s]], writes=[b_out])
        P.finish("sp", [b_out])
        P.emit()
    return nc


def build_k3b(ntl, ntc):
    nc = bass.Bass("TRN2", target_bir_lowering=False)
    NT = ntl + ntc
    NTK = NT * 128
    din = lambda n, s: nc.dram_tensor(n, s, F32, kind="ExternalInput").ap()
    x = din("x", [NTK, D])
    cv = din("cv", [D]); cctx = din("cctx", [D]); w_mod = din("w_mod", [D, 3072]); b_mod = din("b_mod", [3072])
    g_pre = din("g_pre", [D]); g_post = din("g_post", [D])
    w1 = din("w1", [D, DFF]); w2 = din("w2", [DFF, D]); ident_d = din("ident", [128, 128])
    xo = nc.dram_tensor("xo", [NTK, D], F32, kind="ExternalOutput").ap()
    G = 2
    with ExitStack() as st:
        P = Prog(nc, st)
        sb = lambda name, shape, dt=F32: st.enter_context(nc.sbuf_tensor("s_" + name, shape, dt))
        ps = lambda name, shape, dt=F32: st.enter_context(nc.psum_tensor("p_" + name, shape, dt))
        b_c = P.buf("c")
        ident_f = sb("ident_f", [128, 128]); ident = sb("identb", [128, 128], BF16)
        epsb = sb("epsb", [128, 1])
        P.dma("sp", ident_f[:], ident_d, writes=[b_c])
        P.op("dve", lambda e: e.tensor_copy(out=ident[:], in_=ident_f[:]), reads=[b_c], writes=[b_c])
        P.op("dve", lambda e: e.memset(epsb[:], EPS), reads=[b_c], writes=[b_c])
        w1b = sb("w1b", [128, 8, DFF], BF16); w2b = sb("w2b", [128, 32, D], BF16); b_w1 = P.buf("w1"); b_w2 = P.buf("w2")
        MC = ModCalc(P, nc, sb, ps)
        wflat = MC.wst[:].rearrange("p a n -> p (a n)")
        wst = [wflat[:, 0:2048], wflat[:, 2048:4096]]; b_wst = [MC.b[4], MC.b[4]]
        w1v = w1.rearrange("(k p) n -> p k n", p=128); w2v = w2.rearrange("(k p) n -> p k n", p=128)
        i = 0
        for k in range(8):
            for hh in range(2):
                s = i % 2; i += 1
                P.dma("pool", wst[s], w1v[:, k, hh * 2048:(hh + 1) * 2048], writes=[b_wst[s]])
                P.op("pool", lambda e, k=k, hh=hh, s=s: e.tensor_copy(out=w1b[:, k, hh * 2048:(hh + 1) * 2048], in_=wst[s]), reads=[b_wst[s]], writes=[b_w1])
        for k in range(0, 32, 2):
            s = i % 2; i += 1
            P.dma("pool", wst[s].rearrange("p (a n) -> p a n", a=2), w2v[:, k:k + 2, :], writes=[b_wst[s]])
            P.op("pool", lambda e, k=k, s=s: e.tensor_copy(out=w2b[:, k:k + 2, :], in_=wst[s].rearrange("p (a n) -> p a n", a=2)), reads=[b_wst[s]], writes=[b_w2])
        M3 = sb("M3", [128, 3072]); b_M3 = P.buf("M3")
        xt = [sb("xt%d" % i, [128, D]) for i in range(G)]; b_xt = P.bufs(G, "xt")
        sq = sb("sq", [128, D]); b_sq = P.buf("sq")
        ss = sb("ss", [128, 4]); b_ss = P.buf("ss")
        hb = sb("hb", [128, D], BF16); b_hb = P.buf("hb")
        tmp = sb("tmp", [128, D]); b_tmp = P.buf("tmp")
        hT = [sb("hT%d" % i, [128, 8, G * 128], BF16) for i in range(2)]; b_hT = P.bufs(2, "hT")
        uT = sb("uT", [128, 32, G * 128], BF16); b_uT = P.buf("uT")
        ur = [sb("ur%d" % i, [128, G * 128]) for i in range(2)]; b_ur = P.bufs(2, "ur")
        tps = ps("tps", [128, 8, 128], BF16); b_tps = P.buf("tps")
        ups = [ps("ups%d" % i, [128, G * 128]) for i in range(2)]; b_ups = P.bufs(2, "ups")
        yps = [ps("yps%d" % i, [128, 2, 512]) for i in range(2)]; b_yps = P.bufs(2, "yps")
        ob = [tmp, tmp]; b_ob = [b_tmp, b_tmp]
        b_out = P.buf("out")
        gi = 0
        for seg, (t0, t1, cvec) in enumerate(((0, ntl, cv), (ntl, NT, cctx))):
            MC.set_c(cvec)
            MC.calc(w_mod, b_mod, 0, 3072, lambda j: M3[:, j * 512:(j + 1) * 512], b_M3)
            P.dma("sp", sq[:], g_pre.partition_broadcast(128), writes=[b_sq])
            P.op("dve", lambda e: e.scalar_tensor_tensor(out=M3[:, 1024:2048], in0=M3[:, 1024:2048], scalar=1.0, in1=sq[:], op0=ALU.add, op1=ALU.mult), reads=[b_M3, b_sq], writes=[b_M3])
            P.dma("sp", sq[:], g_post.partition_broadcast(128), writes=[b_sq])
            P.op("dve", lambda e: e.tensor_tensor(out=M3[:, 2048:3072], in0=M3[:, 2048:3072], in1=sq[:], op=ALU.mult), reads=[b_M3, b_sq], writes=[b_M3])
            for g0 in range(t0, t1, G):
                tiles = list(range(g0, min(g0 + G, t1)))
                ng = len(tiles); W = ng * 128
                hs = gi % 2; gi += 1
                xs_ = []
                for j, t in enumerate(tiles):
                    s = j
                    xs_.append(s)
                    tok = slice(t * 128, (t + 1) * 128)
                    P.dma("sp", xt[s][:], x[tok, :], writes=[b_xt[s]])
                    P.op("act", lambda e, s=s: e.activation(out=sq[:], in_=xt[s][:], func=AF.Square, scale=1.0 / 32, accum_out=ss[:, 0:1]), reads=[b_xt[s]], writes=[b_sq, b_ss])
                    rstd_from_ss(P, ss[:, 0:1], b_ss, epsb, b_c)
                    P.op("dve", lambda e, s=s: e.scalar_tensor_tensor(out=tmp[:], in0=xt[s][:], scalar=ss[:, 0:1], in1=M3[:, 1024:2048], op0=ALU.mult, op1=ALU.mult), reads=[b_xt[s], b_ss, b_M3], writes=[b_tmp])
                    P.op("dve", lambda e: e.tensor_tensor(out=hb[:], in0=tmp[:], in1=M3[:, 0:1024], op=ALU.add), reads=[b_tmp, b_M3], writes=[b_hb])
                    for k in range(8):
                        P.op("pe", lambda e, k=k: e.transpose(tps[:, k, :], hb[:, k * 128:(k + 1) * 128], ident[:]), reads=[b_hb, b_c], writes=[b_tps], inc=(k == 7))
                    P.op("act", lambda e, j=j, hs=hs: e.copy(out=hT[hs][:, :, j * 128:(j + 1) * 128], in_=tps[:]), reads=[b_tps], writes=[b_hT[hs]])
                for f in range(32):
                    u = f % 2
                    for k in range(8):
                        P.op("pe", lambda e, k=k, f=f, u=u, hs=hs, W=W: e.matmul(ups[u][:, 0:W], lhsT=w1b[:, k, f * 128:(f + 1) * 128], rhs=hT[hs][:, k, 0:W], start=(k == 0), stop=(k == 7)),
                             reads=[b_hT[hs], b_w1], writes=[b_ups[u]], inc=(k == 7))
                    P.op("act", lambda e, u=u, W=W: e.activation(out=ur[u][:, 0:W], in_=ups[u][:, 0:W], func=AF.Relu), reads=[b_ups[u]], writes=[b_ur[u]])
                    P.op("dve", lambda e, u=u, f=f, W=W: e.tensor_tensor(out=uT[:, f, 0:W], in0=ur[u][:, 0:W], in1=ur[u][:, 0:W], op=ALU.mult), reads=[b_ur[u]], writes=[b_uT])
                for j, t in enumerate(tiles):
                    s = xs_[j]
                    y = j % 2
                    tok = slice(t * 128, (t + 1) * 128)
                    for c in range(2):
                        for f in range(32):
                            P.op("pe", lambda e, f=f, c=c, y=y, j=j: e.matmul(yps[y][:, c, :], lhsT=uT[:, f, j * 128:(j + 1) * 128], rhs=w2b[:, f, c * 512:(c + 1) * 512], start=(f == 0), stop=(f == 31)),
                                 reads=[b_uT, b_w2], writes=[b_yps[y]], inc=(f == 31))
                    for c in range(2):
                        P.op("act", lambda e, c=c, y=y: e.activation(out=sq[:, c * 512:(c + 1) * 512], in_=yps[y][:, c, :], func=AF.Square, scale=1.0 / 32, accum_out=ss[:, 1 + c:2 + c]), reads=[b_yps[y]], writes=[b_sq, b_ss])
                    P.op("dve", lambda e: e.tensor_tensor(out=ss[:, 3:4], in0=ss[:, 1:2], in1=ss[:, 2:3], op=ALU.add), reads=[b_ss], writes=[b_ss])
                    rstd_from_ss(P, ss[:, 3:4], b_ss, epsb, b_c)
                    for c in range(2):
                        P.op("dve", lambda e, c=c, y=y: e.scalar_tensor_tensor(out=tmp[:, c * 512:(c + 1) * 512], in0=yps[y][:, c, :], scalar=ss[:, 3:4], in1=M3[:, 2048 + c * 512:2048 + (c + 1) * 512], op0=ALU.mult, op1=ALU.mult),
                             reads=[b_yps[y], b_ss, b_M3], writes=[b_tmp])
                    P.op("pool", lambda e, s=s, y=y: e.tensor_tensor(out=tmp[:], in0=tmp[:], in1=xt[s][:], op=ALU.add), reads=[b_tmp, b_xt[s]], writes=[b_tmp])
                    P.dma("sp", xo[tok, :], ob[y][:], reads=[b_ob[y]], writes=[b_out])
        P.finish("sp", [b_out])
        P.emit()
    return nc


BATCH, SEQ, CTX = 4, 8192, 256
OFF_B, OFF_C, OFF_D = 776, 1544, 2056
_cache = {}
_nl = [0]


def _prog(key, fn):
    if key not in _cache:
        _cache[key] = fn()
    return _cache[key]


def _run(nc, maps, tag):
    t0 = time.time()
    res = run_bass_kernel_spmd(nc, maps, core_ids=list(range(8)))
    _nl[0] += 1
    print("[launch %d] %s %.1fs" % (_nl[0], tag, time.time() - t0), flush=True)
    return res.results


def C_(a):
    return np.ascontiguousarray(a, dtype=np.float32)


def launch_k1(x, ctx, c, c_ctx, w_mod_i, b_mod_i, g_pre_i, w_in_i):
    ntl, ntc = SEQ // 2 // 128, CTX // 2 // 128
    nc = _prog("k1", lambda: build_k1(ntl, ntc))
    ident = np.eye(128, dtype=np.float32)
    maps = []
    for k in range(8):
        b, hf = k // 2, k % 2
        xs = np.concatenate([x[b, hf * SEQ // 2:(hf + 1) * SEQ // 2], ctx[b, hf * CTX // 2:(hf + 1) * CTX // 2]], 0)
        maps.append({"x": C_(xs), "cv": C_(c[b]), "cctx": C_(c_ctx), "w_mod": C_(w_mod_i[:, 0:2048]), "b_mod": C_(b_mod_i[0:2048]),
                     "g_pre": C_(g_pre_i), "w_in": C_(w_in_i), "ident": ident})
    res = _run(nc, maps, "k1")
    proj = np.zeros((BATCH, SEQ, 2568), np.float32)
    projc = np.zeros((BATCH, CTX, 2568), np.float32)
    for k in range(8):
        b, hf = k // 2, k % 2
        o = res[k]["proj"]
        proj[b, hf * SEQ // 2:(hf + 1) * SEQ // 2] = o[:SEQ // 2]
        projc[b, hf * CTX // 2:(hf + 1) * CTX // 2] = o[SEQ // 2:]
    return proj, projc


def launch_kf1(n, w1, b1, f1, w2, b2, f2, w3, b3):
    nc = _prog(("kf1", n), lambda: build_kf1(n))
    res = _run(nc, kf1_inputs(n, C_(w1), C_(b1), C_(f1), C_(w2), C_(b2), C_(f2), C_(w3), C_(b3)), "kf1_%d" % n)
    return kf1_gather(res, n)


def launch_ka(proj, projc, conv_w, conv_b, a_log, dt_bias):
    nc = _prog("ka", lambda: build_ka(CTX, SEQ))
    cst = ka_consts()
    NT = CTX + SEQ
    maps = []
    for k in range(8):
        b, g = k // 2, k % 2
        d = dict(cst)
        seq = [np.concatenate([projc[b], proj[b]], 0), np.concatenate([projc[b][::-1], proj[b][::-1]], 0)]
        xc = slice(256 + g * 128, 256 + (g + 1) * 128); bc = slice(512 + g * 64, 512 + (g + 1) * 64); cc = slice(640 + g * 64, 640 + (g + 1) * 64)
        d["xr"] = C_(np.stack([pad_seq(s[:, xc].T, CTX, SEQ) for s in seq]))
        d["br"] = C_(np.stack([pad_seq(s[:, bc].T, CTX, SEQ) for s in seq]))
        d["cr"] = C_(np.stack([pad_seq(s[:, cc].T, CTX, SEQ) for s in seq]))
        def cwpack(idx):
            w = conv_w[:, idx].T
            bb = conv_b[idx][:, None]
            return C_(np.stack([np.concatenate([w, bb], 1), np.concatenate([w[:, ::-1], bb], 1)]))
        d["cwx"] = cwpack(np.arange(g * 128, (g + 1) * 128))
        d["cwb"] = cwpack(256 + np.arange(g * 64, (g + 1) * 64))
        d["cwc"] = cwpack(384 + np.arange(g * 64, (g + 1) * 64))
        d["dtr"] = C_(np.stack([np.stack([seq[dd][:, 768 + dd * 4 + 2 * g + h] for h in range(2)]) for dd in range(2)]))
        d["alog"] = C_(a_log[:, 2 * g:2 * g + 2]); d["dtb"] = C_(dt_bias[:, 2 * g:2 * g + 2])
        maps.append(d)
    res = _run(nc, maps, "ka")
    mk = lambda: (np.zeros((BATCH, SEQ, 256), np.float32), np.zeros((BATCH, CTX, 256), np.float32))
    yf, yb, xs = mk(), mk(), mk()
    for k in range(8):
        b, g = k // 2, k % 2
        y = res[k]["y"]; x_ = res[k]["xs"]
        cs = slice(g * 128, (g + 1) * 128)
        yf[1][b][:, cs] = y[0, :CTX]; yf[0][b][:, cs] = y[0, CTX:]
        yb[1][b][:, cs] = y[1, :CTX][::-1]; yb[0][b][:, cs] = y[1, CTX:][::-1]
        xs[1][b][:, cs] = x_[:CTX]; xs[0][b][:, cs] = x_[CTX:]
    return yf, yb, xs


def launch_kc(proj, projc, qn, kn, sink):
    nc = _prog("kc", lambda: build_kc(SEQ, CTX))
    cst = _prog("kc_consts", lambda: kc_consts(SEQ))
    maps = []
    for k in range(8):
        b, g = k // 2, k % 2
        d = dict(cst)
        full = np.concatenate([proj[b], projc[b]], 0)
        for br, off in (("w", OFF_C), ("d", OFF_D)):
            d["qT_" + br] = C_(np.stack([full[:, off + (2 * g + h) * 64: off + (2 * g + h + 1) * 64].T for h in range(2)]))
            d["kT_" + br] = C_(full[:, off + 256 + g * 64: off + 256 + (g + 1) * 64].T)
            d["v_" + br] = C_(full[:, off + 384 + g * 64: off + 384 + (g + 1) * 64])
        d["qn"] = C_(qn); d["kn"] = C_(kn); d["sink"] = C_(sink[2 * g:2 * g + 2])
        maps.append(d)
    res = _run(nc, maps, "kc")
    out = {}
    for br in "dw":
        lat = np.zeros((BATCH, SEQ, 256), np.float32); cx = np.zeros((BATCH, CTX, 256), np.float32)
        for k in range(8):
            b, g = k // 2, k % 2
            y = res[k]["y_" + br]
            lat[b][:, g * 128:(g + 1) * 128] = y[:SEQ]; cx[b][:, g * 128:(g + 1) * 128] = y[SEQ:]
        out[br] = (lat, cx)
    return out


def launch_kb(pr, n, kfilt, conv_w, conv_b, hbias):
    nc = _prog(("kb", n), lambda: build_kb(n))
    cst = _prog(("kb_consts", n), lambda: kb_consts(n))
    maps = []
    for k in range(8):
        b, g = k // 2, k % 2
        d = dict(cst)
        chs = [OFF_B + s * 256 + g * 128 + np.arange(128) for s in range(3)]
        uraw = np.stack([pr[b][:, ch].T for ch in chs])
        d["raw"] = kb_pack_raw(C_(uraw), n)
        d["full"] = kb_pack_full(C_(kfilt[:, :, g * 128:(g + 1) * 128]), n)
        cw = np.stack([np.concatenate([conv_w[:, s * 256 + g * 128: s * 256 + (g + 1) * 128].T, conv_b[s * 256 + g * 128: s * 256 + (g + 1) * 128][:, None]], 1) for s in range(3)])
        d["cw"] = C_(cw.reshape(-1)); d["hb"] = C_(hbias[:, g * 128:(g + 1) * 128].reshape(-1))
        maps.append(d)
    res = _run(nc, maps, "kb_%d" % n)
    out = np.zeros((BATCH, 256, n), np.float32)
    for k in range(8):
        b, g = k // 2, k % 2
        y = res[k]["y"]
        out[b, g * 128:(g + 1) * 128] = y.transpose(1, 0, 2).reshape(128, n)
    return out


def _tok_shard(lat, cx, k):
    b, hf = k // 2, k % 2
    return np.concatenate([lat[b, hf * SEQ // 2:(hf + 1) * SEQ // 2], cx[b, hf * CTX // 2:(hf + 1) * CTX // 2]], 0)


def _tok_gather(res, name):
    lat = np.zeros((BATCH, SEQ, 1024), np.float32); cx = np.zeros((BATCH, CTX, 1024), np.float32)
    for k in range(8):
        b, hf = k // 2, k % 2
        o = res[k][name]
        lat[b, hf * SEQ // 2:(hf + 1) * SEQ // 2] = o[:SEQ // 2]; cx[b, hf * CTX // 2:(hf + 1) * CTX // 2] = o[SEQ // 2:]
    return lat, cx


def launch_k3a(x, ctx, yf, yb, xs, z, hy, hyc, yw, yd, c, c_ctx, w_mod_i, b_mod_i, g_post, skip, ssdn, w_out_i):
    ntl, ntc = SEQ // 2 // 128, CTX // 2 // 128
    nc = _prog("k3a", lambda: build_k3a(ntl, ntc))
    ident = np.eye(128, dtype=np.float32)
    maps = []
    for k in range(8):
        b, hf = k // 2, k % 2
        ls = slice(hf * SEQ // 2, (hf + 1) * SEQ // 2); cs = slice(hf * CTX // 2, (hf + 1) * CTX // 2)
        mixT = np.concatenate([
            np.concatenate([hy[b][:, ls], hyc[b][:, cs]], 1),
            np.concatenate([yw[0][b, ls], yw[1][b, cs]], 0).T,
            np.concatenate([yd[0][b, ls], yd[1][b, cs]], 0).T], 0)
        maps.append({"x": C_(_tok_shard(x, ctx, k)), "yf": C_(_tok_shard(yf[0], yf[1], k)), "yb": C_(_tok_shard(yb[0], yb[1], k)),
                     "xs": C_(_tok_shard(xs[0], xs[1], k)), "z": C_(_tok_shard(z[0], z[1], k)), "mixT": C_(mixT),
                     "cv": C_(c[b]), "cctx": C_(c_ctx), "w_mod": C_(w_mod_i[:, 2048:3072]), "b_mod": C_(b_mod_i[2048:3072]),
                     "g_post": C_(g_post), "skip": C_(skip), "ssdn": C_(ssdn), "w_out": C_(w_out_i), "ident": ident})
    res = _run(nc, maps, "k3a")
    return _tok_gather(res, "xo")


def launch_k3b(x, ctx, c, c_ctx, w_mod_i, b_mod_i, g_pre, g_post, w1, w2):
    ntl, ntc = SEQ // 2 // 128, CTX // 2 // 128
    nc = _prog("k3b", lambda: build_k3b(ntl, ntc))
    ident = np.eye(128, dtype=np.float32)
    maps = []
    for k in range(8):
        b = k // 2
        maps.append({"x": C_(_tok_shard(x, ctx, k)), "cv": C_(c[b]), "cctx": C_(c_ctx), "w_mod": C_(w_mod_i[:, 3072:6144]), "b_mod": C_(b_mod_i[3072:6144]),
                     "g_pre": C_(g_pre), "g_post": C_(g_post), "w1": C_(w1), "w2": C_(w2), "ident": ident})
    res = _run(nc, maps, "k3b")
    return _tok_gather(res, "xo")


def forward(x, c, ctx, c_ctx, w_mod, b_mod, norm_mix_pre, norm_mix_post, norm_mlp_pre, norm_mlp_post,
            w_in, w_out, ssd_conv_w, ssd_conv_b, ssd_a_log, ssd_dt_bias, ssd_d, ssd_norm,
            hy_conv_w, hy_conv_b, hy_w1, hy_b1, hy_freq1, hy_w2, hy_b2, hy_freq2, hy_w3, hy_b3, hy_bias,
            attn_sink, q_norm, k_norm, mlp_w1, mlp_w2, depth=2, dbg=None):
    A = lambda a: np.asarray(a, dtype=np.float32)
    x = A(x); ctx = A(ctx); c = A(c); c_ctx = A(c_ctx)
    for i in range(depth):
        need_ctx = i < depth - 1
        proj, projc = launch_k1(x, ctx, c, c_ctx, A(w_mod[i]), A(b_mod[i]), A(norm_mix_pre[i]), A(w_in[i]))
        hyf = (A(hy_w1[i]), A(hy_b1[i]), A(hy_freq1[i]), A(hy_w2[i]), A(hy_b2[i]), A(hy_freq2[i]), A(hy_w3[i]), A(hy_b3[i]))
        k_lat = launch_kf1(SEQ, *hyf)
        hy = launch_kb(proj, SEQ, k_lat, A(hy_conv_w[i]), A(hy_conv_b[i]), A(hy_bias[i]))
        if need_ctx:
            k_ctx = launch_kf1(CTX, *hyf)
            hyc = launch_kb(projc, CTX, k_ctx, A(hy_conv_w[i]), A(hy_conv_b[i]), A(hy_bias[i]))
        else:
            hyc = np.zeros((BATCH, 256, CTX), np.float32)
        yf, yb, xs = launch_ka(proj, projc, A(ssd_conv_w[i]), A(ssd_conv_b[i]), A(ssd_a_log[i]), A(ssd_dt_bias[i]))
        att = launch_kc(proj, projc, A(q_norm[i]), A(k_norm[i]), A(attn_sink[i]))
        z = (proj[:, :, 0:256], projc[:, :, 0:256])
        if dbg is not None:
            dbg.update({"proj": proj, "projc": projc, "hy": hy, "hyc": hyc, "yf": yf, "yb": yb, "xs": xs, "att": att})
        x1, ctx1 = launch_k3a(x, ctx, yf, yb, xs, z, hy, hyc, att["w"], att["d"], c, c_ctx, A(w_mod[i]), A(b_mod[i]),
                              A(norm_mix_post[i]), np.repeat(A(ssd_d[i]), 64), A(ssd_norm[i]), A(w_out[i]))
        x2, ctx2 = launch_k3b(x1, ctx1, c, c_ctx, A(w_mod[i]), A(b_mod[i]), A(norm_mlp_pre[i]), A(norm_mlp_post[i]), A(mlp_w1[i]), A(mlp_w2[i]))
        if dbg is not None:
            dbg.update({"x1": x1, "ctx1": ctx1, "x2": x2, "ctx2": ctx2})
        x = x2
        if need_ctx:
            ctx = ctx2
    return x


def kernel(**inputs):
    _nl[0] = 0
    out = forward(**inputs, depth=2)
    return np.ascontiguousarray(out, dtype=np.float32)
```
